# Optimizing a Trainium2 kernel written in Bass

```python
import math
import jax, jax.numpy as jnp
from jax import lax
import numpy as np

D_MODEL = 2048
BATCH = 2
SEQ = 4096
DEPTH = 4
DEC_BATCH = 8
DEC_SEQ = 4
PAST_LEN = 16384
PAGE_SIZE = 128

N_ATTN_LAYERS = (DEPTH + 1) // 2
N_SSM_LAYERS = DEPTH // 2
N_HEADS = 16
N_KV_HEADS = 4
HEAD_DIM = 128
ATTN_WIDTH = N_HEADS * HEAD_DIM
KV_GROUP = N_HEADS // N_KV_HEADS
N_IDX_HEADS = 16
IDX_DIM = 64
TOPK_MAX = 256
Q_BLOCK = 128
ATTN_SPLITS = (ATTN_WIDTH, N_KV_HEADS * HEAD_DIM, N_KV_HEADS * HEAD_DIM, ATTN_WIDTH,
               N_IDX_HEADS * IDX_DIM, IDX_DIM, N_IDX_HEADS)
ATTN_IN_COLS = sum(ATTN_SPLITS)
NUM_BUCKETS = 32
MAX_EXACT = NUM_BUCKETS // 2
MAX_DISTANCE = 128
SSM_WIDTH = D_MODEL
SSM_GROUP = 16
SSM_GROUPS = SSM_WIDTH // SSM_GROUP
SSM_STATE = 64
SSM_BLOCK = 128
DT_MIN = 0.001
DT_MAX = 0.1
PLE_DIM = 256
RMS_EPS = 1e-6

kernel_name = "dsa_s5_hybrid_decode_step"

F32 = jnp.float32


def rms_norm(x, w):
    x32 = x.astype(F32)
    y = x32 * lax.rsqrt(jnp.mean(x32 * x32, axis=-1, keepdims=True) + RMS_EPS)
    return (y * w.astype(F32)).astype(x.dtype)


def t5_bucket(n):
    nf = jnp.maximum(n, 1).astype(F32)
    large = MAX_EXACT + (jnp.log(nf / MAX_EXACT) / math.log(MAX_DISTANCE / MAX_EXACT)
                         * (NUM_BUCKETS - MAX_EXACT)).astype(jnp.int32)
    large = jnp.minimum(large, NUM_BUCKETS - 1)
    return jnp.where(n < MAX_EXACT, n, large)


def split_attn(z):
    cuts = [int(c) for c in np.cumsum(ATTN_SPLITS)[:-1]]
    return jnp.split(z, cuts, axis=-1)


def dsa_attend(q, q_idx, w_idx, t_pos, k_idx_all, gather_kv, top_k, rel_bias):
    b_, t_ = q.shape[:2]
    L = k_idx_all.shape[1]
    qk = jnp.einsum('bthd,bsd->bths', q_idx.astype(F32), k_idx_all.astype(F32)) * (IDX_DIM ** -0.5)
    score = jnp.einsum('bths,bth->bts', jax.nn.relu(qk), w_idx.astype(F32) * (N_IDX_HEADS ** -0.5))
    admissible = jnp.arange(L, dtype=jnp.int32)[None, :] <= t_pos[:, None]
    score = jnp.where(admissible[None], score, -jnp.inf)
    _, idx = lax.top_k(score, top_k)
    k_sel, v_sel = gather_kv(idx)
    dist = t_pos[None, :, None] - idx
    valid = dist >= 0
    bias = rel_bias[t5_bucket(jnp.maximum(dist, 0))].astype(F32)
    bias = bias.reshape(b_, t_, top_k, N_KV_HEADS, KV_GROUP).transpose(0, 1, 3, 4, 2)
    qg = q.reshape(b_, t_, N_KV_HEADS, KV_GROUP, HEAD_DIM)
    logits = jnp.einsum('btgrd,btkgd->btgrk', qg, k_sel).astype(F32) * (HEAD_DIM ** -0.5) + bias
    logits = jnp.where(valid[:, :, None, None, :], logits, -jnp.inf)
    probs = jax.nn.softmax(logits, axis=-1).astype(v_sel.dtype)
    o = jnp.einsum('btgrk,btkgd->btgrd', probs, v_sel)
    return o.reshape(b_, t_, ATTN_WIDTH)


def attn_project(xn, w_in):
    b_, t_, _ = xn.shape
    q, k, v, gate, qi, ki, wi = split_attn(xn @ w_in)
    q = q.reshape(b_, t_, N_HEADS, HEAD_DIM)
    k = k.reshape(b_, t_, N_KV_HEADS, HEAD_DIM)
    v = v.reshape(b_, t_, N_KV_HEADS, HEAD_DIM)
    qi = qi.reshape(b_, t_, N_IDX_HEADS, IDX_DIM)
    return q, k, v, gate, qi, ki, wi


def take_rows(a, i):
    return jax.vmap(lambda ab, ib: ab[ib])(a, i)


def attn_prompt(xn, w_in, w_out, rel_bias):
    b_, s_, _ = xn.shape
    q, k, v, gate, qi, ki, wi = attn_project(xn, w_in)
    top_k = min(TOPK_MAX, s_ // 4)
    nb = s_ // Q_BLOCK

    def gather_kv(idx):
        return take_rows(k, idx), take_rows(v, idx)

    def block(args):
        qb, qib, wib, t0 = args
        t_pos = t0 + jnp.arange(Q_BLOCK, dtype=jnp.int32)
        return dsa_attend(qb, qib, wib, t_pos, ki, gather_kv, top_k, rel_bias)

    def to_blocks(a):
        return a.reshape((b_, nb, Q_BLOCK) + a.shape[2:]).swapaxes(0, 1)

    o = lax.map(block, (to_blocks(q), to_blocks(qi), to_blocks(wi),
                        jnp.arange(nb, dtype=jnp.int32) * Q_BLOCK))
    o = o.swapaxes(0, 1).reshape(b_, s_, ATTN_WIDTH)
    out = (o * jax.nn.silu(gate)) @ w_out
    return out, (k, v, ki)


def attn_sample(xn, w_in, w_out, rel_bias, ck, cv, cki, page_table):
    b_, t_, _ = xn.shape
    q, k, v, gate, qi, ki, wi = attn_project(xn, w_in)
    past = page_table.shape[1] * PAGE_SIZE
    past_ki = cki[page_table].reshape(b_, past, IDX_DIM)
    ki_all = jnp.concatenate([past_ki, ki.astype(past_ki.dtype)], axis=1)
    t_pos = past + jnp.arange(t_, dtype=jnp.int32)
    top_k = min(TOPK_MAX, (past + t_) // 4)

    def gather_kv(idx):
        flat = idx.reshape(b_, t_ * top_k)
        pi = jnp.clip(flat, 0, past - 1)
        page = jnp.take_along_axis(page_table, pi // PAGE_SIZE, axis=1)
        off = pi % PAGE_SIZE
        ni = jnp.clip(flat - past, 0, t_ - 1)
        is_past = (flat < past)[:, :, None, None]

        def pick(cache, new):
            sel = jnp.where(is_past, cache[page, off], take_rows(new, ni).astype(cache.dtype))
            return sel.reshape(b_, t_, top_k, N_KV_HEADS, HEAD_DIM)
        return pick(ck, k), pick(cv, v)

    o = dsa_attend(q, qi, wi, t_pos, ki_all, gather_kv, top_k, rel_bias)
    out = (o * jax.nn.silu(gate)) @ w_out
    return out, (k, v, ki)


def complex_combine(e1, e2):
    a1r, a1i, b1r, b1i = e1
    a2r, a2i, b2r, b2i = e2
    return (a2r * a1r - a2i * a1i,
            a2r * a1i + a2i * a1r,
            a2r * b1r - a2i * b1i + b2r,
            a2r * b1i + a2i * b1r + b2i)


def s5_scan(u_g, h0_re, h0_im, a_re, a_im, bb_re, bb_im, c_re, c_im, d_g):
    b_, t_ = u_g.shape[:2]
    blk = SSM_BLOCK if t_ % SSM_BLOCK == 0 else t_
    nb = t_ // blk
    ub = u_g.reshape(b_, nb, blk, SSM_GROUPS, SSM_GROUP).swapaxes(0, 1)

    def step(carry, u_blk):
        hr, hi = carry
        br = jnp.einsum('btgj,gnj->btgn', u_blk, bb_re)
        bi = jnp.einsum('btgj,gnj->btgn', u_blk, bb_im)
        br = br.at[:, 0].add(a_re * hr - a_im * hi)
        bi = bi.at[:, 0].add(a_re * hi + a_im * hr)
        ar = jnp.broadcast_to(a_re, br.shape)
        ai = jnp.broadcast_to(a_im, bi.shape)
        _, _, xr, xi = lax.associative_scan(complex_combine, (ar, ai, br, bi), axis=1)
        y = (jnp.einsum('btgn,gjn->btgj', xr, c_re) - jnp.einsum('btgn,gjn->btgj', xi, c_im)
             + d_g * u_blk)
        return (xr[:, -1], xi[:, -1]), y

    (hr, hi), ys = lax.scan(step, (h0_re.astype(F32), h0_im.astype(F32)), ub)
    y = ys.swapaxes(0, 1).reshape(b_, t_, SSM_GROUPS, SSM_GROUP)
    return y, hr, hi


def s5_mixer(xn, h0_re, h0_im, w_in, lam_re, lam_im, log_dt, b_re, b_im, c_re, c_im, d,
             w_glu, b_glu, w_out):
    b_, t_, _ = xn.shape
    u, gate = jnp.split(xn @ w_in, 2, axis=-1)
    lre = jnp.minimum(lam_re.astype(F32), -1e-4)
    lim = lam_im.astype(F32)
    dt = jnp.exp(log_dt.astype(F32))[:, None]
    mag = jnp.exp(lre * dt)
    ang = lim * dt
    a_re = mag * jnp.cos(ang)
    a_im = mag * jnp.sin(ang)
    den = lre * lre + lim * lim
    coef_re = ((a_re - 1.0) * lre + a_im * lim) / den
    coef_im = (a_im * lre - (a_re - 1.0) * lim) / den
    br32, bi32 = b_re.astype(F32), b_im.astype(F32)
    bb_re = coef_re[..., None] * br32 - coef_im[..., None] * bi32
    bb_im = coef_re[..., None] * bi32 + coef_im[..., None] * br32
    u_g = u.astype(F32).reshape(b_, t_, SSM_GROUPS, SSM_GROUP)
    y, hr, hi = s5_scan(u_g, h0_re, h0_im, a_re, a_im, bb_re, bb_im,
                        c_re.astype(F32), c_im.astype(F32),
                        d.astype(F32).reshape(SSM_GROUPS, SSM_GROUP))
    y = jax.nn.gelu(y.reshape(b_, t_, SSM_WIDTH).astype(xn.dtype))
    y = y * jax.nn.sigmoid(y @ w_glu + b_glu)
    out = (y * jax.nn.silu(gate)) @ w_out
    return out, hr, hi


def setup_inputs(seed: int = 0) -> dict:
    key = jax.random.key(seed)
    ks = iter(jax.random.split(key, 40))

    def nrm(shape, scale=1.0):
        return jax.random.normal(next(ks), shape, F32) * scale

    n_pages = PAST_LEN // PAGE_SIZE
    n_used = DEC_BATCH * n_pages
    n_pool = n_used + max(1, n_used // 4)
    x_prompt = nrm((BATCH, SEQ, D_MODEL))
    x_sample = nrm((DEC_BATCH, DEC_SEQ, D_MODEL))
    cache_k = nrm((N_ATTN_LAYERS, n_pool, PAGE_SIZE, N_KV_HEADS, HEAD_DIM))
    cache_v = nrm((N_ATTN_LAYERS, n_pool, PAGE_SIZE, N_KV_HEADS, HEAD_DIM))
    cache_kidx = nrm((N_ATTN_LAYERS, n_pool, PAGE_SIZE, IDX_DIM))
    state_ssm_re = nrm((N_SSM_LAYERS, DEC_BATCH, SSM_GROUPS, SSM_STATE), 0.5)
    state_ssm_im = nrm((N_SSM_LAYERS, DEC_BATCH, SSM_GROUPS, SSM_STATE), 0.5)
    page_table = jax.random.permutation(next(ks), n_pool)[:n_used].reshape(DEC_BATCH, n_pages).astype(jnp.int32)
    p_prompt = nrm((DEPTH, BATCH, SEQ, PLE_DIM))
    p_sample = nrm((DEPTH, DEC_BATCH, DEC_SEQ, PLE_DIM))
    norm_w = 1.0 + nrm((DEPTH, D_MODEL), 0.01)
    final_norm_w = 1.0 + nrm((D_MODEL,), 0.01)
    rel_bias = nrm((NUM_BUCKETS, N_HEADS), 0.5)
    attn_w_in = nrm((N_ATTN_LAYERS, D_MODEL, ATTN_IN_COLS), D_MODEL ** -0.5)
    attn_w_out = nrm((N_ATTN_LAYERS, ATTN_WIDTH, D_MODEL), ATTN_WIDTH ** -0.5)
    ssm_w_in = nrm((N_SSM_LAYERS, D_MODEL, 2 * SSM_WIDTH), D_MODEL ** -0.5)
    ssm_lambda_re = -0.5 + nrm((N_SSM_LAYERS, SSM_GROUPS, SSM_STATE), 0.01)
    ssm_lambda_im = jnp.broadcast_to(math.pi * jnp.arange(SSM_STATE, dtype=F32),
                                     (N_SSM_LAYERS, SSM_GROUPS, SSM_STATE)) + nrm((N_SSM_LAYERS, SSM_GROUPS, SSM_STATE), 0.01)
    ssm_log_dt = jax.random.uniform(next(ks), (N_SSM_LAYERS, SSM_GROUPS), F32,
                                    minval=math.log(DT_MIN), maxval=math.log(DT_MAX))
    ssm_b_re = nrm((N_SSM_LAYERS, SSM_GROUPS, SSM_STATE, SSM_GROUP), (2 * SSM_GROUP) ** -0.5)
    ssm_b_im = nrm((N_SSM_LAYERS, SSM_GROUPS, SSM_STATE, SSM_GROUP), (2 * SSM_GROUP) ** -0.5)
    ssm_c_re = nrm((N_SSM_LAYERS, SSM_GROUPS, SSM_GROUP, SSM_STATE), (2 * SSM_STATE) ** -0.5)
    ssm_c_im = nrm((N_SSM_LAYERS, SSM_GROUPS, SSM_GROUP, SSM_STATE), (2 * SSM_STATE) ** -0.5)
    ssm_d = nrm((N_SSM_LAYERS, SSM_WIDTH))
    ssm_w_glu = nrm((N_SSM_LAYERS, SSM_WIDTH, SSM_WIDTH), SSM_WIDTH ** -0.5)
    ssm_b_glu = nrm((N_SSM_LAYERS, SSM_WIDTH), 0.01)
    ssm_w_out = nrm((N_SSM_LAYERS, SSM_WIDTH, D_MODEL), SSM_WIDTH ** -0.5)
    ple_norm_w = 1.0 + nrm((DEPTH, D_MODEL), 0.01)
    ple_w_gate = nrm((DEPTH, D_MODEL, D_MODEL), D_MODEL ** -0.5)
    ple_w_proj = nrm((DEPTH, PLE_DIM, D_MODEL), PLE_DIM ** -0.5)
    return {"x_prompt": x_prompt, "x_sample": x_sample, "cache_k": cache_k, "cache_v": cache_v,
            "cache_kidx": cache_kidx, "state_ssm_re": state_ssm_re, "state_ssm_im": state_ssm_im,
            "page_table": page_table, "p_prompt": p_prompt, "p_sample": p_sample,
            "norm_w": norm_w, "final_norm_w": final_norm_w, "rel_bias": rel_bias,
            "attn_w_in": attn_w_in, "attn_w_out": attn_w_out, "ssm_w_in": ssm_w_in,
            "ssm_lambda_re": ssm_lambda_re, "ssm_lambda_im": ssm_lambda_im, "ssm_log_dt": ssm_log_dt,
            "ssm_b_re": ssm_b_re, "ssm_b_im": ssm_b_im, "ssm_c_re": ssm_c_re, "ssm_c_im": ssm_c_im,
            "ssm_d": ssm_d, "ssm_w_glu": ssm_w_glu, "ssm_b_glu": ssm_b_glu, "ssm_w_out": ssm_w_out,
            "ple_norm_w": ple_norm_w, "ple_w_gate": ple_w_gate, "ple_w_proj": ple_w_proj}


def reference(x_prompt, x_sample, cache_k, cache_v, cache_kidx, state_ssm_re, state_ssm_im, page_table,
              p_prompt, p_sample, norm_w, final_norm_w, rel_bias, attn_w_in, attn_w_out, ssm_w_in,
              ssm_lambda_re, ssm_lambda_im, ssm_log_dt, ssm_b_re, ssm_b_im, ssm_c_re, ssm_c_im, ssm_d,
              ssm_w_glu, ssm_b_glu, ssm_w_out, ple_norm_w, ple_w_gate, ple_w_proj):

    def run(x, p, attn_apply, h0_re, h0_im):
        h = x
        ks_, vs_, kis_, hrs, his = [], [], [], [], []
        for i in range(DEPTH):
            xn = rms_norm(h, norm_w[i])
            l = i // 2
            if i % 2 == 0:
                out, (k_new, v_new, ki_new) = attn_apply(xn, l)
                ks_.append(k_new); vs_.append(v_new); kis_.append(ki_new)
            else:
                out, hr, hi = s5_mixer(xn, h0_re[l], h0_im[l], ssm_w_in[l], ssm_lambda_re[l],
                                       ssm_lambda_im[l], ssm_log_dt[l], ssm_b_re[l], ssm_b_im[l],
                                       ssm_c_re[l], ssm_c_im[l], ssm_d[l], ssm_w_glu[l], ssm_b_glu[l],
                                       ssm_w_out[l])
                hrs.append(hr); his.append(hi)
            h = h + out
            g = jax.nn.sigmoid(rms_norm(h, ple_norm_w[i]) @ ple_w_gate[i])
            h = h + g * (p[i].astype(h.dtype) @ ple_w_proj[i])
        y = rms_norm(h, final_norm_w)
        return y, jnp.stack(ks_), jnp.stack(vs_), jnp.stack(kis_), jnp.stack(hrs), jnp.stack(his)

    def prompt_attn(xn, l):
        return attn_prompt(xn, attn_w_in[l], attn_w_out[l], rel_bias)

    def sample_attn(xn, l):
        return attn_sample(xn, attn_w_in[l], attn_w_out[l], rel_bias,
                           cache_k[l], cache_v[l], cache_kidx[l], page_table)

    zeros = jnp.zeros((N_SSM_LAYERS, x_prompt.shape[0], SSM_GROUPS, SSM_STATE), F32)
    y_prompt, k_prompt, v_prompt, kidx_prompt, ssm_re_prompt, ssm_im_prompt = run(
        x_prompt, p_prompt, prompt_attn, zeros, zeros)
    y_sample, k_sample, v_sample, kidx_sample, ssm_re_sample, ssm_im_sample = run(
        x_sample, p_sample, sample_attn, state_ssm_re, state_ssm_im)
    return (y_prompt, y_sample, k_prompt, v_prompt, kidx_prompt, ssm_re_prompt, ssm_im_prompt,
            k_sample, v_sample, kidx_sample, ssm_re_sample, ssm_im_sample)
```

```python
import math
from contextlib import ExitStack

import numpy as np
import concourse.bass as bass
import concourse.mybir as mybir
from concourse.bass_utils import run_bass_kernel_spmd

F32 = mybir.dt.float32
BF16 = mybir.dt.bfloat16
I32 = mybir.dt.int32
AF = mybir.ActivationFunctionType
ALU = mybir.AluOpType
AX = mybir.AxisListType

NCORES = 8
EPOCH = 30000


class Prog:
    ENG = ("pe", "act", "dve", "pool", "sp")

    def __init__(self, nc, es):
        self.nc = nc
        self.es = es
        self.ops = {e: [] for e in self.ENG}
        self.cnt = {e: 0 for e in self.ENG}
        self.lanes = {}
        self.nuniq = 0

    def sb(self, shape, dt, name=None):
        self.nuniq += 1
        return self.es.enter_context(self.nc.sbuf_tensor(name or f"sb{self.nuniq}", list(shape), dt))

    def ps(self, shape, dt=F32, name=None):
        self.nuniq += 1
        return self.es.enter_context(self.nc.psum_tensor(name or f"ps{self.nuniq}", list(shape), dt))

    def op(self, eng, fn, deps=(), inc=True):
        deps = tuple(d for d in deps if d is not None)
        tok = None
        if inc:
            tok = ("c", eng, self.cnt[eng])
            self.cnt[eng] += 1
        self.ops[eng].append((fn, deps, tok))
        return tok

    def dma(self, eng, lane, out, in_, deps=()):
        deps = tuple(d for d in deps if d is not None)
        n = self.lanes.get(lane, 0)
        self.lanes[lane] = n + 1
        tok = ("d", lane, n)
        self.ops[eng].append((lambda e: e.dma_start(out=out, in_=in_), deps, tok))
        return tok

    def emit(self):
        nc, es = self.nc, self.es
        csem = {}
        for e in self.ENG:
            n_ep = (self.cnt[e] + EPOCH - 1) // EPOCH
            csem[e] = [es.enter_context(nc.semaphore(f"s_{e}{i}")) for i in range(max(1, n_ep))]
        lsem = {}
        for ln, n in self.lanes.items():
            n_ep = (n * 16 + EPOCH - 1) // EPOCH + 1
            lsem[ln] = [es.enter_context(nc.semaphore(f"l_{ln}_{i}")) for i in range(n_ep)]
        per = EPOCH // 16

        def semval(tok):
            if tok[0] == "c":
                _, e, i = tok
                return csem[e][i // EPOCH], (i % EPOCH) + 1
            _, ln, i = tok
            return lsem[ln][i // per], ((i % per) + 1) * 16

        block = es.enter_context(nc.Block())

        def make(ename):
            def body(eng):
                waited = {}
                for fn, deps, tok in self.ops[ename]:
                    for d in deps:
                        s, v = semval(d)
                        key = id(s)
                        if waited.get(key, 0) >= v:
                            continue
                        waited[key] = v
                        eng.wait_ge(s, v)
                    if fn is None:
                        continue
                    ins = fn(eng)
                    if tok is not None:
                        s, v = semval(tok)
                        ins.then_inc(s, 16 if tok[0] == "d" else 1)
            return body

        block.tensor(make("pe"))
        block.scalar(make("act"))
        block.vector(make("dve"))
        block.gpsimd(make("pool"))
        block.sync(make("sp"))


class Ring:
    def __init__(self, bufs):
        self.bufs = bufs
        self.i = 0
        self.readers = [[] for _ in bufs]

    def next(self):
        k = self.i % len(self.bufs)
        self.i += 1
        deps = self.readers[k]
        self.readers[k] = []
        return k, self.bufs[k], deps

    def done(self, k, *toks):
        self.readers[k].extend(t for t in toks if t is not None)


def build_linear(RT, D, N, pro, epi, eps=1e-6):
    nc = bass.Bass("TRN2", target_bir_lowering=False)
    R = RT * 128
    KC = D // 128
    NT = (N + 511) // 512
    x = nc.dram_tensor("x", [R, D], F32, kind="ExternalInput").ap()
    w = nc.dram_tensor("w", [D, N], F32, kind="ExternalInput").ap()
    ident_d = nc.dram_tensor("ident", [128, 128], F32, kind="ExternalInput").ap()
    x2 = nc.dram_tensor("x2", [R, D], F32, kind="ExternalInput").ap() if pro == "gate" else None
    nw = nc.dram_tensor("nw", [1, D], F32, kind="ExternalInput").ap() if pro == "rms" else None
    e1 = nc.dram_tensor("e1", [R, N], F32, kind="ExternalInput").ap() if epi in ("res", "gres", "glu") else None
    e2 = nc.dram_tensor("e2", [R, N], F32, kind="ExternalInput").ap() if epi == "gres" else None
    bv = nc.dram_tensor("b", [1, N], F32, kind="ExternalInput").ap() if epi == "glu" else None
    y = nc.dram_tensor("y", [R, N], F32, kind="ExternalOutput").ap()

    with ExitStack() as es:
        p = Prog(nc, es)
        ident_f = p.sb([128, 128], F32)
        ident = p.sb([128, 128], BF16)
        XT = p.sb([128, KC, R], BF16, "XT")
        xbufs = Ring([p.sb([128, D], F32) for _ in range(2)])
        x2bufs = Ring([p.sb([128, D], F32) for _ in range(2)]) if pro == "gate" else None
        tmpf = Ring([p.sb([128, D], F32) for _ in range(2)]) if pro in ("gate", "gelu", "rms") else None
        xpb = Ring([p.sb([128, D], BF16) for _ in range(2)])
        nwb = p.sb([128, D], F32) if pro == "rms" else None
        stat = Ring([p.sb([128, 2], F32) for _ in range(2)]) if pro == "rms" else None
        pT = Ring([p.ps([128, 4, 128], BF16) for _ in range(2)])
        wb = Ring([p.sb([128, KC, 512], BF16) for _ in range(2)])
        acc = Ring([p.ps([128, 512], F32) for _ in range(3)])
        ob = Ring([p.sb([128, 512], F32) for _ in range(3)])
        e1b = Ring([p.sb([128, 512], F32) for _ in range(2)]) if e1 is not None else None
        e2b = Ring([p.sb([128, 512], F32) for _ in range(2)]) if e2 is not None else None
        bb = Ring([p.sb([128, 512], F32) for _ in range(2)]) if bv is not None else None
        tb = Ring([p.sb([128, 512], F32) for _ in range(2)]) if epi in ("gres", "glu") else None

        t_id = p.dma("sp", "ident", ident_f[:], ident_d)
        t_ident = p.op("dve", lambda e: e.tensor_copy(out=ident[:], in_=ident_f[:]), [t_id])
        t_nw = None
        if pro == "rms":
            t_nw = p.dma("sp", "nw", nwb[:], nw.partition_broadcast(128))

        xt_toks = []
        for r in range(RT):
            rows = slice(r * 128, (r + 1) * 128)
            kx, xb, dx = xbufs.next()
            t_x = p.dma("sp", f"x{kx}", xb[:], x[rows, :], dx)
            kp, xp, dxp = xpb.next()
            if pro == "rms":
                kt, tf, dt_ = tmpf.next()
                ks, st, ds = stat.next()
                t_sq = p.op("act", lambda e, tf=tf, xb=xb, st=st: e.activation(
                    out=tf[:], in_=xb[:], func=AF.Square, accum_out=st[:, 0:1]), [t_x] + dt_ + ds)
                t_r1 = p.op("dve", lambda e, st=st: e.tensor_scalar(
                    out=st[:, 1:2], in0=st[:, 0:1], scalar1=1.0 / D, scalar2=eps, op0=ALU.mult, op1=ALU.add), [t_sq])
                t_r1b = p.op("act", lambda e, st=st: e.activation(out=st[:, 1:2], in_=st[:, 1:2], func=AF.Sqrt), [t_r1])
                t_r2 = p.op("dve", lambda e, st=st: e.reciprocal(out=st[:, 1:2], in_=st[:, 1:2]), [t_r1b])
                t_xp = p.op("dve", lambda e, xp=xp, xb=xb, st=st: e.scalar_tensor_tensor(
                    out=xp[:], in0=xb[:], scalar=st[:, 1:2], in1=nwb[:], op0=ALU.mult, op1=ALU.mult),
                    [t_r2, t_nw] + dxp)
                tmpf.done(kt, t_sq)
                stat.done(ks, t_xp)
                xbufs.done(kx, t_xp)
            elif pro == "gate":
                k2, x2b, d2 = x2bufs.next()
                t_x2 = p.dma("act", f"x2{k2}", x2b[:], x2[rows, :], d2)
                kt, tf, dt_ = tmpf.next()
                t_s = p.op("act", lambda e, tf=tf, x2b=x2b: e.activation(out=tf[:], in_=x2b[:], func=AF.Silu),
                           [t_x2] + dt_)
                t_xp = p.op("dve", lambda e, xp=xp, xb=xb, tf=tf: e.tensor_tensor(
                    out=xp[:], in0=xb[:], in1=tf[:], op=ALU.mult), [t_x, t_s] + dxp)
                x2bufs.done(k2, t_s)
                tmpf.done(kt, t_xp)
                xbufs.done(kx, t_xp)
            elif pro == "gelu":
                kt, tf, dt_ = tmpf.next()
                t_a = p.op("dve", lambda e, tf=tf, xb=xb: e.tensor_tensor(out=tf[:], in0=xb[:], in1=xb[:], op=ALU.mult),
                           [t_x] + dt_)
                t_b = p.op("dve", lambda e, tf=tf: e.tensor_scalar(
                    out=tf[:], in0=tf[:], scalar1=0.044715, scalar2=1.0, op0=ALU.mult, op1=ALU.add), [t_a])
                t_c = p.op("dve", lambda e, tf=tf, xb=xb: e.tensor_tensor(out=tf[:], in0=tf[:], in1=xb[:], op=ALU.mult),
                           [t_b])
                t_d = p.op("act", lambda e, tf=tf: e.activation(
                    out=tf[:], in_=tf[:], func=AF.Sigmoid, scale=2.0 * math.sqrt(2.0 / math.pi)), [t_c])
                t_xp = p.op("dve", lambda e, xp=xp, xb=xb, tf=tf: e.tensor_tensor(
                    out=xp[:], in0=xb[:], in1=tf[:], op=ALU.mult), [t_d] + dxp)
                tmpf.done(kt, t_xp)
                xbufs.done(kx, t_xp)
            else:
                t_xp = p.op("dve", lambda e, xp=xp, xb=xb: e.tensor_copy(out=xp[:], in_=xb[:]), [t_x] + dxp)
                xbufs.done(kx, t_xp)
            last = []
            for k0 in range(0, KC, 4):
                nk = min(4, KC - k0)
                kq, pt, dq = pT.next()
                tt = None
                for j in range(nk):
                    tt = p.op("pe", lambda e, pt=pt, xp=xp, j=j, k0=k0: e.transpose(
                        out=pt[:, j, :], in_=xp[:, (k0 + j) * 128:(k0 + j + 1) * 128], identity=ident[:]),
                        [t_xp, t_ident] + dq, inc=(j == nk - 1))
                eng = "act" if (k0 // 4) % 2 == 0 else "dve"
                if eng == "act":
                    t_cp = p.op("act", lambda e, pt=pt, k0=k0, nk=nk, rows=rows: e.copy(
                        out=XT[:, k0:k0 + nk, rows], in_=pt[:, 0:nk, :]), [tt])
                else:
                    t_cp = p.op("dve", lambda e, pt=pt, k0=k0, nk=nk, rows=rows: e.tensor_copy(
                        out=XT[:, k0:k0 + nk, rows], in_=pt[:, 0:nk, :]), [tt])
                pT.done(kq, t_cp)
                last.append(t_cp)
                xpb.done(kp, tt)
            xt_toks.append(last)

        wv = w.rearrange("(k p) n -> p k n", p=128)
        for n in range(NT):
            n0 = n * 512
            ns = min(512, N - n0)
            kw, wt, dw = wb.next()
            t_w = p.dma("pool", f"w{kw}", wt[:, :, 0:ns], wv[:, :, n0:n0 + ns], dw)
            t_bb = None
            if bv is not None:
                kb, bt, db = bb.next()
                t_bb = p.dma("act", f"b{kb}", bt[:, 0:ns], bv[0:1, n0:n0 + ns].partition_broadcast(128), db)
            mm_last = []
            for r in range(RT):
                rows = slice(r * 128, (r + 1) * 128)
                ka, ac, da = acc.next()
                tm = None
                for k in range(KC):
                    tm = p.op("pe", lambda e, ac=ac, wt=wt, k=k, rows=rows, ns=ns: e.matmul(
                        ac[:, 0:ns], lhsT=XT[:, k, rows], rhs=wt[:, k, 0:ns], start=(k == 0), stop=(k == KC - 1)),
                        ([t_w] + xt_toks[r] + da) if k == 0 else [], inc=(k == KC - 1))
                mm_last.append(tm)
                ko, ot, do = ob.next()
                if epi == "none":
                    t_o = p.op("act", lambda e, ot=ot, ac=ac, ns=ns: e.copy(out=ot[:, 0:ns], in_=ac[:, 0:ns]),
                               [tm] + do)
                    acc.done(ka, t_o)
                elif epi == "sigmoid":
                    t_o = p.op("act", lambda e, ot=ot, ac=ac, ns=ns: e.activation(
                        out=ot[:, 0:ns], in_=ac[:, 0:ns], func=AF.Sigmoid), [tm] + do)
                    acc.done(ka, t_o)
                elif epi == "res":
                    k1, et, d1 = e1b.next()
                    t_e = p.dma("sp", f"e1{k1}", et[:, 0:ns], e1[rows, n0:n0 + ns], d1)
                    t_o = p.op("dve", lambda e, ot=ot, ac=ac, et=et, ns=ns: e.tensor_tensor(
                        out=ot[:, 0:ns], in0=ac[:, 0:ns], in1=et[:, 0:ns], op=ALU.add), [tm, t_e] + do)
                    acc.done(ka, t_o)
                    e1b.done(k1, t_o)
                elif epi == "gres":
                    k1, et, d1 = e1b.next()
                    t_e = p.dma("sp", f"e1{k1}", et[:, 0:ns], e1[rows, n0:n0 + ns], d1)
                    k2, et2, d2 = e2b.next()
                    t_e2 = p.dma("sp", f"e2{k2}", et2[:, 0:ns], e2[rows, n0:n0 + ns], d2)
                    kt, tt_, dtt = tb.next()
                    t_m = p.op("dve", lambda e, tt_=tt_, ac=ac, et2=et2, ns=ns: e.tensor_tensor(
                        out=tt_[:, 0:ns], in0=ac[:, 0:ns], in1=et2[:, 0:ns], op=ALU.mult), [tm, t_e2] + dtt)
                    t_o = p.op("dve", lambda e, ot=ot, tt_=tt_, et=et, ns=ns: e.tensor_tensor(
                        out=ot[:, 0:ns], in0=tt_[:, 0:ns], in1=et[:, 0:ns], op=ALU.add), [t_m, t_e] + do)
                    acc.done(ka, t_m)
                    e1b.done(k1, t_o)
                    e2b.done(k2, t_m)
                    tb.done(kt, t_o)
                elif epi == "glu":
                    k1, et, d1 = e1b.next()
                    t_e = p.dma("sp", f"e1{k1}", et[:, 0:ns], e1[rows, n0:n0 + ns], d1)
                    kt, tt_, dtt = tb.next()
                    t_a = p.op("dve", lambda e, tt_=tt_, et=et, ns=ns: e.tensor_tensor(
                        out=tt_[:, 0:ns], in0=et[:, 0:ns], in1=et[:, 0:ns], op=ALU.mult), [t_e] + dtt)
                    t_b = p.op("dve", lambda e, tt_=tt_, ns=ns: e.tensor_scalar(
                        out=tt_[:, 0:ns], in0=tt_[:, 0:ns], scalar1=0.044715, scalar2=1.0, op0=ALU.mult, op1=ALU.add),
                        [t_a])
                    t_c = p.op("dve", lambda e, tt_=tt_, et=et, ns=ns: e.tensor_tensor(
                        out=tt_[:, 0:ns], in0=tt_[:, 0:ns], in1=et[:, 0:ns], op=ALU.mult), [t_b])
                    t_d = p.op("act", lambda e, tt_=tt_, ns=ns: e.activation(
                        out=tt_[:, 0:ns], in_=tt_[:, 0:ns], func=AF.Sigmoid, scale=2.0 * math.sqrt(2.0 / math.pi)),
                        [t_c])
                    t_g = p.op("dve", lambda e, tt_=tt_, et=et, ns=ns: e.tensor_tensor(
                        out=tt_[:, 0:ns], in0=tt_[:, 0:ns], in1=et[:, 0:ns], op=ALU.mult), [t_d])
                    t_s1 = p.op("dve", lambda e, ot=ot, ac=ac, bt=bt, ns=ns: e.tensor_tensor(
                        out=ot[:, 0:ns], in0=ac[:, 0:ns], in1=bt[:, 0:ns], op=ALU.add), [tm, t_bb] + do)
                    t_s2 = p.op("act", lambda e, ot=ot, ns=ns: e.activation(
                        out=ot[:, 0:ns], in_=ot[:, 0:ns], func=AF.Sigmoid), [t_s1])
                    t_o = p.op("dve", lambda e, ot=ot, tt_=tt_, ns=ns: e.tensor_tensor(
                        out=ot[:, 0:ns], in0=ot[:, 0:ns], in1=tt_[:, 0:ns], op=ALU.mult), [t_s2, t_g])
                    acc.done(ka, t_s1)
                    e1b.done(k1, t_g)
                    tb.done(kt, t_o)
                else:
                    raise ValueError(epi)
                t_st = p.dma("act", f"o{ko}", y[rows, n0:n0 + ns], ot[:, 0:ns], [t_o])
                ob.done(ko, t_st)
            wb.done(kw, *mm_last)
            if bv is not None:
                bb.done(kb, *mm_last)
        fin = []
        for k in range(len(ob.bufs)):
            fin.extend(ob.readers[k])
        p.op("sp", None, fin, inc=False)
        p.emit()
    return nc


_IDENT = np.eye(128, dtype=np.float32)
_PROG_CACHE = {}


def run_linear(xs, w, pro, epi, x2s=None, nw=None, e1s=None, e2s=None, b=None):
    R, D = xs[0].shape
    N = w.shape[1]
    key = ("lin", R, D, N, pro, epi)
    if key not in _PROG_CACHE:
        _PROG_CACHE[key] = build_linear(R // 128, D, N, pro, epi)
    nc = _PROG_CACHE[key]
    w = np.ascontiguousarray(w, dtype=np.float32)
    in_maps = []
    for c in range(NCORES):
        m = {"x": np.ascontiguousarray(xs[c]), "w": w, "ident": _IDENT}
        if pro == "gate":
            m["x2"] = np.ascontiguousarray(x2s[c])
        if pro == "rms":
            m["nw"] = np.ascontiguousarray(nw.reshape(1, D))
        if e1s is not None:
            m["e1"] = np.ascontiguousarray(e1s[c])
        if e2s is not None:
            m["e2"] = np.ascontiguousarray(e2s[c])
        if b is not None:
            m["b"] = np.ascontiguousarray(b.reshape(1, N))
        in_maps.append(m)
    res = run_bass_kernel_spmd(nc, in_maps, core_ids=list(range(NCORES)))
    return [r["y"] for r in res.results]


NEG = -1.0e30
TOPK = 256
BIS_LO, BIS_HI, BIS_IT = -8192.0, 8192.0, 28


def build_attn(NJ, NT, base, smax, NBLK):
    nc = bass.Bass("TRN2", target_bir_lowering=False)
    HN = 16 * NT
    NU = sum(smax)
    qT_d = nc.dram_tensor("qT", [NJ, 128, HN], F32, kind="ExternalInput").ap()
    qiT_d = nc.dram_tensor("qiT", [NJ, 64, HN], F32, kind="ExternalInput").ap()
    wT_d = nc.dram_tensor("wT", [NJ, 1, HN], F32, kind="ExternalInput").ap()
    KT_d = nc.dram_tensor("KT", [NBLK, 128, 512], F32, kind="ExternalInput").ap()
    V_d = nc.dram_tensor("V", [NBLK, 128, 512], F32, kind="ExternalInput").ap()
    kiT_d = nc.dram_tensor("kiT", [NBLK, 64, 128], F32, kind="ExternalInput").ap()
    vt_d = nc.dram_tensor("vt", [1, NU], F32, kind="ExternalInput").ap()
    D_d = nc.dram_tensor("D", [2, 128, HN], F32, kind="ExternalInput").ap()
    cb_d = nc.dram_tensor("cb", [1, 16], F32, kind="ExternalInput").ap()
    C0_d = nc.dram_tensor("C0", [128, NT], F32, kind="ExternalInput").ap()
    oT_d = nc.dram_tensor("oT", [NJ, 4, 128, 4 * NT], F32, kind="ExternalOutput").ap()
    HPM = min(16, 512 // NT)
    NMM = 16 // HPM
    SMX = max(smax)
    scale = 128.0 ** -0.5

    with ExitStack() as es:
        p = Prog(nc, es)
        ones = p.sb([128, 128], BF16)
        vt = p.sb([128, NU], F32)
        Dt = p.sb([128, 2, HN], F32)
        cb = p.sb([128, 16], F32)
        C0 = p.sb([128, NT], F32)
        qT = Ring([p.sb([128, HN], BF16) for _ in range(2)])
        qiT = Ring([p.sb([64, HN], BF16) for _ in range(2)])
        wb = Ring([p.sb([128, HN], F32) for _ in range(2)])
        SC = p.sb([128, SMX, NT], F32, "SC")
        CM = p.sb([128, SMX, NT], BF16, "CM")
        MK = p.sb([128, SMX, NT], BF16, "MK")
        kib = Ring([p.sb([64, 128], BF16) for _ in range(3)])
        rb = Ring([p.sb([128, HN], F32) for _ in range(2)])
        ktb = Ring([p.sb([128, 512], BF16) for _ in range(3)])
        vb = Ring([p.sb([128, 512], BF16) for _ in range(3)])
        banks = Ring([p.ps([128, 512], F32) for _ in range(4)])
        ps_cnt = p.ps([128, 512], F32)
        ps_o = p.ps([128, 512], F32)
        ps_d = p.ps([128, 512], F32)
        lo = p.sb([128, NT], F32)
        hi = p.sb([128, NT], F32)
        mid = p.sb([128, NT], F32)
        ge = p.sb([128, NT], F32)
        tmpa = p.sb([128, NT], F32)
        lgt = Ring([p.sb([128, 4 * NT], F32) for _ in range(2)])
        pb = Ring([p.sb([128, 4 * NT], BF16) for _ in range(2)])
        pmb = Ring([p.sb([128, 4 * NT], BF16) for _ in range(2)])
        rdb = p.sb([128, 4 * NT], F32)
        ob = Ring([p.sb([128, 4 * NT], F32) for _ in range(2)])

        t_ones = p.op("dve", lambda e: e.memset(ones[:], 1.0))
        t_vt = p.dma("sp", "vt", vt[:], vt_d.partition_broadcast(128))
        t_D = p.dma("sp", "D", Dt[:], D_d.rearrange("a p n -> p a n"))
        t_cb = p.dma("sp", "cb", cb[:], cb_d.partition_broadcast(128))
        t_C0 = p.dma("sp", "C0", C0[:], C0_d)
        consts = [t_ones, t_vt, t_D, t_cb, t_C0]

        u0 = 0
        sc_free = []
        cnt_free = []
        acc_free = []
        out_toks = []
        for j in range(NJ):
            S = smax[j]
            kq, qt, dq = qT.next()
            t_q = p.dma("pool", f"q{kq}", qt[:], qT_d[j], dq)
            kqi, qit, dqi = qiT.next()
            t_qi = p.dma("pool", f"qi{kqi}", qit[:], qiT_d[j], dqi)
            kw, wt, dw_ = wb.next()
            t_w = p.dma("sp", f"wb{kw}", wt[:], wT_d[j].partition_broadcast(128), dw_)
            sc_toks = []
            for dl in range(S):
                blk = base[j] + dl
                kk, kit, dk = kib.next()
                t_ki = p.dma("pool", f"ki{kk}", kit[:], kiT_d[blk], dk)
                kr, rt, dr = rb.next()
                t_rs = []
                mm_toks = []
                for m in range(NMM):
                    kb_, bk, dbk = banks.next()
                    n_ = HPM * NT
                    t_mm = p.op("pe", lambda e, bk=bk, kit=kit, qit=qit, m=m, n_=n_: e.matmul(
                        bk[:, 0:n_], lhsT=kit[:, :], rhs=qit[:, m * n_:(m + 1) * n_], start=True, stop=True),
                        [t_ki, t_qi] + dbk)
                    t_r = p.op("dve", lambda e, bk=bk, rt=rt, wt=wt, m=m, n_=n_: e.scalar_tensor_tensor(
                        out=rt[:, m * n_:(m + 1) * n_], in0=bk[:, 0:n_], scalar=0.0, in1=wt[:, m * n_:(m + 1) * n_],
                        op0=ALU.max, op1=ALU.mult), [t_mm, t_w] + (dr if m == 0 else []))
                    banks.done(kb_, t_r)
                    t_rs.append(t_r)
                    mm_toks.append(t_mm)
                kib.done(kk, *mm_toks)
                t_red = p.op("dve", lambda e, rt=rt, dl=dl: e.tensor_reduce(
                    out=SC[:, dl, :], in_=rt[:, :].rearrange("p (h t) -> p t h", h=16), axis=AX.X, op=ALU.add),
                    t_rs + (sc_free if dl == 0 else []))
                rb.done(kr, t_red)
                u = u0 + dl
                t_v = p.op("dve", lambda e, dl=dl, u=u: e.tensor_scalar(
                    out=SC[:, dl, :], in0=SC[:, dl, :], scalar1=vt[:, u:u + 1], scalar2=None, op0=ALU.add),
                    [t_red, t_vt])
                if dl == 0:
                    t_v = p.op("dve", lambda e: e.tensor_tensor(
                        out=SC[:, 0, :], in0=SC[:, 0, :], in1=C0[:, :], op=ALU.add), [t_v, t_C0])
                sc_toks.append(t_v)
            qiT.done(kqi, *mm_toks)
            wb.done(kw, *t_rs)
            t_lo = p.op("dve", lambda e: e.memset(lo[:], BIS_LO), sc_free)
            t_hi = p.op("dve", lambda e: e.memset(hi[:], BIS_HI), sc_free)
            t_prev = [t_lo, t_hi]
            t_cm_free = list(cnt_free)
            for it in range(BIS_IT):
                t_mid = p.op("dve", lambda e: e.tensor_tensor(out=mid[:], in0=lo[:], in1=hi[:], op=ALU.add), t_prev)
                t_mid = p.op("dve", lambda e: e.tensor_scalar(
                    out=mid[:], in0=mid[:], scalar1=0.5, scalar2=None, op0=ALU.mult), [t_mid])
                t_cmp = p.op("dve", lambda e, S=S: e.tensor_tensor(
                    out=CM[:, 0:S, :], in0=SC[:, 0:S, :], in1=mid[:, :].unsqueeze(1).to_broadcast([128, S, NT]),
                    op=ALU.is_ge), [t_mid] + sc_toks + t_cm_free)
                t_c = None
                for dl in range(S):
                    t_c = p.op("pe", lambda e, dl=dl, S=S: e.matmul(
                        ps_cnt[:, 0:NT], lhsT=ones[:, :], rhs=CM[:, dl, :], start=(dl == 0), stop=(dl == S - 1)),
                        ([t_cmp, t_ones] + t_prev) if dl == 0 else [], inc=(dl == S - 1))
                t_ge = p.op("dve", lambda e: e.tensor_scalar(
                    out=ge[:], in0=ps_cnt[:, 0:NT], scalar1=float(TOPK) - 0.5, scalar2=None, op0=ALU.is_ge), [t_c])
                t_a = p.op("dve", lambda e: e.tensor_tensor(out=tmpa[:], in0=mid[:], in1=lo[:], op=ALU.subtract), [t_ge])
                t_a = p.op("dve", lambda e: e.tensor_tensor(out=tmpa[:], in0=tmpa[:], in1=ge[:], op=ALU.mult), [t_a])
                t_lo2 = p.op("dve", lambda e: e.tensor_tensor(out=lo[:], in0=lo[:], in1=tmpa[:], op=ALU.add), [t_a])
                t_b = p.op("dve", lambda e: e.tensor_tensor(out=tmpa[:], in0=hi[:], in1=mid[:], op=ALU.subtract), [t_lo2])
                t_b = p.op("dve", lambda e: e.tensor_tensor(out=tmpa[:], in0=tmpa[:], in1=ge[:], op=ALU.mult), [t_b])
                t_hi2 = p.op("dve", lambda e: e.tensor_tensor(out=hi[:], in0=mid[:], in1=tmpa[:], op=ALU.add), [t_b])
                t_prev = [t_lo2, t_hi2]
                t_cm_free = [t_c]
            cnt_free = [t_prev[1]]
            t_mk = p.op("dve", lambda e, S=S: e.tensor_tensor(
                out=MK[:, 0:S, :], in0=SC[:, 0:S, :], in1=lo[:, :].unsqueeze(1).to_broadcast([128, S, NT]),
                op=ALU.is_ge), t_prev + sc_free)
            last_readers = []
            for g in range(4):
                t_last_o = None
                t_last_d = None
                for dl in range(S):
                    blk = base[j] + dl
                    kk, ktt, dkt = ktb.next()
                    t_kt = p.dma("pool", f"kt{kk}", ktt[:, 0:128], KT_d[blk][:, g * 128:(g + 1) * 128], dkt)
                    kv, vtl, dv = vb.next()
                    t_vv = p.dma("pool", f"v{kv}", vtl[:, 0:128], V_d[blk][:, g * 128:(g + 1) * 128], dv)
                    kb_, bk, dbk = banks.next()
                    n4 = 4 * NT
                    t_s = p.op("pe", lambda e, bk=bk, ktt=ktt, qt=qt, g=g, n4=n4: e.matmul(
                        bk[:, 0:n4], lhsT=ktt[:, 0:128], rhs=qt[:, g * n4:(g + 1) * n4], start=True, stop=True),
                        [t_kt, t_q] + dbk)
                    ktb.done(kk, t_s)
                    kp_, pt, dp = pb.next()
                    if dl >= 2:
                        t_e = None
                        for r in range(4):
                            h = 4 * g + r
                            t_e = p.op("act", lambda e, pt=pt, bk=bk, r=r, h=h: e.activation(
                                out=pt[:, r * NT:(r + 1) * NT], in_=bk[:, r * NT:(r + 1) * NT], func=AF.Exp,
                                bias=cb[:, h:h + 1], scale=scale), [t_s, t_cb] + (dp if r == 0 else []))
                        banks.done(kb_, t_e)
                    else:
                        kl, lt, dlg = lgt.next()
                        t_l = p.op("dve", lambda e, lt=lt, bk=bk, dl=dl, g=g, n4=n4: e.scalar_tensor_tensor(
                            out=lt[:, 0:n4], in0=bk[:, 0:n4], scalar=scale, in1=Dt[:, dl, g * n4:(g + 1) * n4],
                            op0=ALU.mult, op1=ALU.add), [t_s, t_D] + dlg)
                        banks.done(kb_, t_l)
                        t_e = p.op("act", lambda e, pt=pt, lt=lt, n4=n4: e.activation(
                            out=pt[:, 0:n4], in_=lt[:, 0:n4], func=AF.Exp), [t_l] + dp)
                        lgt.done(kl, t_e)
                    km, pmt, dpm = pmb.next()
                    t_pm = p.op("dve", lambda e, pmt=pmt, pt=pt, dl=dl: e.tensor_tensor(
                        out=pmt[:, :].rearrange("p (r t) -> p r t", r=4),
                        in0=pt[:, :].rearrange("p (r t) -> p r t", r=4),
                        in1=MK[:, dl, :].unsqueeze(1).to_broadcast([128, 4, NT]), op=ALU.mult),
                        [t_e, t_mk] + dpm)
                    pb.done(kp_, t_pm)
                    first = (dl == 0)
                    lastf = (dl == S - 1)
                    t_last_o = p.op("pe", lambda e, vtl=vtl, pmt=pmt, first=first, lastf=lastf, n4=n4: e.matmul(
                        ps_o[:, 0:n4], lhsT=vtl[:, 0:128], rhs=pmt[:, 0:n4], start=first, stop=lastf),
                        [t_vv, t_pm] + (acc_free if first else []))
                    t_last_d = p.op("pe", lambda e, pmt=pmt, first=first, lastf=lastf, n4=n4: e.matmul(
                        ps_d[:, 0:n4], lhsT=ones[:, :], rhs=pmt[:, 0:n4], start=first, stop=lastf), [t_pm])
                    vb.done(kv, t_last_o)
                    pmb.done(km, t_last_d)
                t_rd = p.op("dve", lambda e: e.reciprocal(out=rdb[:], in_=ps_d[:, 0:4 * NT]), [t_last_d] + acc_free)
                ko, ot, do = ob.next()
                t_o = p.op("dve", lambda e, ot=ot: e.tensor_tensor(
                    out=ot[:], in0=ps_o[:, 0:4 * NT], in1=rdb[:], op=ALU.mult), [t_rd, t_last_o] + do)
                acc_free = [t_o]
                t_st = p.dma("act", f"o{ko}", oT_d[j, g], ot[:], [t_o])
                ob.done(ko, t_st)
                out_toks.append(t_st)
                last_readers.append(t_o)
            qT.done(kq, t_last_o)
            sc_free = [t_mk, t_pm]
            u0 += S
        p.op("sp", None, out_toks[-2:], inc=False)
        p.emit()
    return nc


def _t5_bucket_np(n):
    n = np.asarray(n, dtype=np.int32)
    nf = np.maximum(n, 1).astype(np.float32)
    large = 16 + (np.log(nf / np.float32(16)) / np.float32(math.log(128 / 16)) * np.float32(16)).astype(np.int32)
    large = np.minimum(large, 31)
    return np.where(n < 16, n, large)


def _bias_tiles(rel_bias, NT):
    s_l = np.arange(128)[:, None]
    t_l = np.arange(NT)[None, :]
    d0 = t_l - s_l
    b0 = rel_bias[_t5_bucket_np(np.maximum(d0, 0))]
    b0 = np.where((d0 >= 0)[:, :, None], b0, np.float32(NEG))
    b1 = rel_bias[_t5_bucket_np(128 + d0)]
    D = np.stack([b0, b1]).transpose(0, 1, 3, 2).reshape(2, 128, 16 * NT)
    C0 = np.where(d0 >= 0, np.float32(0), np.float32(NEG)).astype(np.float32)
    return np.ascontiguousarray(D, dtype=np.float32), C0


def _zz_block(k, j):
    m = j // 2
    return 8 * m + k if j % 2 == 0 else 8 * m + 7 - k


P_BASE = [8 * (3 - j // 2) if j % 2 == 0 else 28 + 8 * (3 - j // 2) for j in range(8)]
P_SMAX = [8 * (j // 2) + 4 if j % 2 == 0 else 8 * (j // 2) + 8 for j in range(8)]


def attn_prompt_dev(q, k, v, qi, ki, wi, rel_bias):
    key = ("attn_p",)
    if key not in _PROG_CACHE:
        _PROG_CACHE[key] = build_attn(8, 128, P_BASE, P_SMAX, 60)
    nc = _PROG_CACHE[key]
    D, C0 = _bias_tiles(rel_bias, 128)
    cb = np.ascontiguousarray(rel_bias[31].reshape(1, 16))
    in_maps = []
    for c in range(NCORES):
        b, kk = c // 4, c % 4
        Kb = k[b].reshape(32, 128, 4, 128).transpose(0, 3, 2, 1).reshape(32, 128, 512)
        Vb = v[b].reshape(32, 128, 512)
        Kib = ki[b].reshape(32, 128, 64).transpose(0, 2, 1)
        tops = [24 + kk] * 28 + [31 - kk] * 32
        idx = [tops[i] - (i if i < 28 else i - 28) for i in range(60)]
        sel = np.array([max(x, 0) for x in idx])
        KT = np.ascontiguousarray(Kb[sel])
        VV = np.ascontiguousarray(Vb[sel])
        KiT = np.ascontiguousarray(Kib[sel])
        T = [_zz_block(kk, j) for j in range(8)]
        rows = np.concatenate([np.arange(t * 128, (t + 1) * 128) for t in T])
        qT = q[b][rows].reshape(8, 128, 16, 128).transpose(0, 3, 2, 1).reshape(8, 128, 2048)
        qiT = qi[b][rows].reshape(8, 128, 16, 64).transpose(0, 3, 2, 1).reshape(8, 64, 2048)
        wT = wi[b][rows].reshape(8, 128, 16).transpose(0, 2, 1).reshape(8, 1, 2048)
        vt = np.concatenate([np.where(np.arange(P_SMAX[j]) <= T[j], 0.0, NEG) for j in range(8)])
        in_maps.append({"qT": np.ascontiguousarray(qT), "qiT": np.ascontiguousarray(qiT),
                        "wT": np.ascontiguousarray(wT), "KT": KT, "V": VV, "kiT": KiT,
                        "vt": vt.reshape(1, -1).astype(np.float32), "D": D, "cb": cb, "C0": C0})
    res = run_bass_kernel_spmd(nc, in_maps, core_ids=list(range(NCORES)))
    o = np.zeros((2, 4096, 2048), np.float32)
    for c in range(NCORES):
        b, kk = c // 4, c % 4
        oT = res.results[c]["oT"].reshape(8, 4, 128, 4, 128)
        ot = oT.transpose(0, 4, 1, 3, 2).reshape(8, 128, 2048)
        for j in range(8):
            t = _zz_block(kk, j)
            o[b, t * 128:(t + 1) * 128] = ot[j]
    return o


TWO_PI = 2.0 * math.pi


def _sincos(p, out_t, x_t, tmp_t, deps, cos, width, ki_t=None, kf_t=None):
    shp = list(tmp_t.shape)
    if ki_t is None:
        ki_t = p.sb(shp, I32)[:] if len(shp) == 2 else p.sb(shp, I32)[:]
        kf_t = p.sb(shp, F32)[:]
    off = (math.pi / 2) if cos else 0.0
    t1 = p.op("dve", lambda e: e.tensor_scalar(out=tmp_t, in0=x_t, scalar1=off, scalar2=1.0 / TWO_PI,
                                               op0=ALU.add, op1=ALU.mult), deps)
    t2 = p.op("dve", lambda e: e.tensor_copy(out=ki_t, in_=tmp_t), [t1])
    t3 = p.op("dve", lambda e: e.tensor_copy(out=kf_t, in_=ki_t), [t2])
    t4 = p.op("dve", lambda e: e.tensor_scalar(out=tmp_t, in0=x_t, scalar1=off, scalar2=None, op0=ALU.add), [t3])
    t5 = p.op("dve", lambda e: e.scalar_tensor_tensor(out=tmp_t, in0=kf_t, scalar=-TWO_PI, in1=tmp_t,
                                                      op0=ALU.mult, op1=ALU.add), [t4])
    t6 = p.op("dve", lambda e: e.tensor_scalar(out=kf_t, in0=tmp_t, scalar1=math.pi, scalar2=-TWO_PI,
                                               op0=ALU.is_gt, op1=ALU.mult), [t5])
    t7 = p.op("dve", lambda e: e.tensor_tensor(out=tmp_t, in0=tmp_t, in1=kf_t, op=ALU.add), [t6])
    t8 = p.op("dve", lambda e: e.tensor_scalar(out=kf_t, in0=tmp_t, scalar1=-math.pi, scalar2=TWO_PI,
                                               op0=ALU.is_lt, op1=ALU.mult), [t7])
    t9 = p.op("dve", lambda e: e.tensor_tensor(out=tmp_t, in0=tmp_t, in1=kf_t, op=ALU.add), [t8])
    t10 = p.op("act", lambda e: e.activation(out=out_t, in_=tmp_t, func=AF.Sin), [t9])
    return t10


def build_s5_params():
    nc = bass.Bass("TRN2", target_bir_lowering=False)
    G = 16
    lre_d = nc.dram_tensor("lre", [64, G], F32, kind="ExternalInput").ap()
    lim_d = nc.dram_tensor("lim", [64, G], F32, kind="ExternalInput").ap()
    ldt_d = nc.dram_tensor("ldt", [64, G], F32, kind="ExternalInput").ap()
    bre_d = nc.dram_tensor("bre", [64, G, 16], F32, kind="ExternalInput").ap()
    bim_d = nc.dram_tensor("bim", [64, G, 16], F32, kind="ExternalInput").ap()
    out_d = nc.dram_tensor("pout", [64, 4, G], F32, kind="ExternalOutput").ap()
    bb_d = nc.dram_tensor("bb", [64, 2, G, 16], F32, kind="ExternalOutput").ap()
    with ExitStack() as es:
        p = Prog(nc, es)
        T = lambda: p.sb([64, G], F32)
        lre, lim, ldt, dt, lmag, ang, mag, c, s, are1, aim, den, cre, cim, t1, t2, tmp = [T() for _ in range(17)]
        bre = p.sb([64, G, 16], F32)
        bim = p.sb([64, G, 16], F32)
        o4 = p.sb([64, 4, G], F32)
        bb = p.sb([64, 2, G, 16], F32)
        tb1 = p.sb([64, G, 16], F32)
        d1 = p.dma("sp", "a", lre[:], lre_d)
        d2 = p.dma("sp", "b", lim[:], lim_d)
        d3 = p.dma("sp", "c", ldt[:], ldt_d)
        d4 = p.dma("sp", "d", bre[:], bre_d)
        d5 = p.dma("sp", "e", bim[:], bim_d)
        V = lambda f, deps: p.op("dve", f, deps)
        a = p.op("act", lambda e: e.activation(out=dt[:], in_=ldt[:], func=AF.Exp), [d3])
        b = V(lambda e: e.tensor_scalar(out=lre[:], in0=lre[:], scalar1=-1e-4, scalar2=None, op0=ALU.min), [d1])
        c1 = V(lambda e: e.tensor_tensor(out=lmag[:], in0=lre[:], in1=dt[:], op=ALU.mult), [a, b])
        c2 = V(lambda e: e.tensor_tensor(out=ang[:], in0=lim[:], in1=dt[:], op=ALU.mult), [a, d2])
        m = p.op("act", lambda e: e.activation(out=mag[:], in_=lmag[:], func=AF.Exp), [c1])
        ts = _sincos(p, s[:], ang[:], t1[:], [c2], False, G)
        tc = _sincos(p, c[:], ang[:], t2[:], [c2], True, G)
        x1 = V(lambda e: e.tensor_tensor(out=are1[:], in0=mag[:], in1=c[:], op=ALU.mult), [m, tc])
        x1 = V(lambda e: e.tensor_scalar(out=are1[:], in0=are1[:], scalar1=-1.0, scalar2=None, op0=ALU.add), [x1])
        x2 = V(lambda e: e.tensor_tensor(out=aim[:], in0=mag[:], in1=s[:], op=ALU.mult), [m, ts])
        y1 = V(lambda e: e.tensor_tensor(out=den[:], in0=lre[:], in1=lre[:], op=ALU.mult), [b])
        y2 = V(lambda e: e.tensor_tensor(out=tmp[:], in0=lim[:], in1=lim[:], op=ALU.mult), [d2])
        y3 = V(lambda e: e.tensor_tensor(out=den[:], in0=den[:], in1=tmp[:], op=ALU.add), [y1, y2])
        y4 = V(lambda e: e.reciprocal(out=den[:], in_=den[:]), [y3])
        z1 = V(lambda e: e.tensor_tensor(out=cre[:], in0=are1[:], in1=lre[:], op=ALU.mult), [x1, y4])
        z2 = V(lambda e: e.tensor_tensor(out=tmp[:], in0=aim[:], in1=lim[:], op=ALU.mult), [x2, y3])
        z3 = V(lambda e: e.tensor_tensor(out=cre[:], in0=cre[:], in1=tmp[:], op=ALU.add), [z1, z2])
        z4 = V(lambda e: e.tensor_tensor(out=cre[:], in0=cre[:], in1=den[:], op=ALU.mult), [z3])
        w1 = V(lambda e: e.tensor_tensor(out=cim[:], in0=aim[:], in1=lre[:], op=ALU.mult), [z3])
        w2 = V(lambda e: e.tensor_tensor(out=tmp[:], in0=are1[:], in1=lim[:], op=ALU.mult), [w1])
        w3 = V(lambda e: e.tensor_tensor(out=cim[:], in0=cim[:], in1=tmp[:], op=ALU.subtract), [w2])
        w4 = V(lambda e: e.tensor_tensor(out=cim[:], in0=cim[:], in1=den[:], op=ALU.mult), [w3])
        bc = lambda t: t[:, :].unsqueeze(2).to_broadcast([64, G, 16])
        q1 = V(lambda e: e.tensor_tensor(out=bb[:, 0], in0=bre[:], in1=bc(cre), op=ALU.mult), [z4, d4])
        q2 = V(lambda e: e.tensor_tensor(out=tb1[:], in0=bim[:], in1=bc(cim), op=ALU.mult), [w4, d5])
        q3 = V(lambda e: e.tensor_tensor(out=bb[:, 0], in0=bb[:, 0], in1=tb1[:], op=ALU.subtract), [q1, q2])
        q4 = V(lambda e: e.tensor_tensor(out=bb[:, 1], in0=bim[:], in1=bc(cre), op=ALU.mult), [q3])
        q5 = V(lambda e: e.tensor_tensor(out=tb1[:], in0=bre[:], in1=bc(cim), op=ALU.mult), [q4])
        q6 = V(lambda e: e.tensor_tensor(out=bb[:, 1], in0=bb[:, 1], in1=tb1[:], op=ALU.add), [q5])
        r1 = V(lambda e: e.tensor_copy(out=o4[:, 0], in_=lmag[:]), [c1])
        r2 = V(lambda e: e.tensor_copy(out=o4[:, 1], in_=ang[:]), [c2])
        r3 = V(lambda e: e.tensor_copy(out=o4[:, 2], in_=cre[:]), [z4])
        r4 = V(lambda e: e.tensor_copy(out=o4[:, 3], in_=cim[:]), [w4])
        s1 = p.dma("sp", "o1", out_d, o4[:], [r1, r2, r3, r4])
        s2 = p.dma("sp", "o2", bb_d, bb[:], [q6])
        p.op("sp", None, [s1, s2], inc=False)
        p.emit()
    return nc


def s5_params_dev(lam_re, lam_im, log_dt, b_re, b_im):
    key = ("s5p",)
    if key not in _PROG_CACHE:
        _PROG_CACHE[key] = build_s5_params()
    nc = _PROG_CACHE[key]
    in_maps = []
    for c in range(NCORES):
        g = slice(16 * c, 16 * c + 16)
        in_maps.append({
            "lre": np.ascontiguousarray(lam_re[g].T), "lim": np.ascontiguousarray(lam_im[g].T),
            "ldt": np.ascontiguousarray(np.broadcast_to(log_dt[g][None, :], (64, 16))),
            "bre": np.ascontiguousarray(b_re[g].transpose(1, 0, 2)),
            "bim": np.ascontiguousarray(b_im[g].transpose(1, 0, 2))})
    res = run_bass_kernel_spmd(nc, in_maps, core_ids=list(range(NCORES)))
    lmag = np.concatenate([r["pout"][:, 0].T for r in res.results])
    ang = np.concatenate([r["pout"][:, 1].T for r in res.results])
    bbre = np.concatenate([r["bb"][:, 0].transpose(1, 0, 2) for r in res.results])
    bbim = np.concatenate([r["bb"][:, 1].transpose(1, 0, 2) for r in res.results])
    return lmag, ang, bbre, bbim


def build_s5_scan(seqs, NTOK):
    nc = bass.Bass("TRN2", target_bir_lowering=False)
    NS = len(seqs)
    NI = sum(1 for s_ in seqs if s_[3])
    uT_d = nc.dram_tensor("uT", [256, NTOK], F32, kind="ExternalInput").ap()
    lmr_d = nc.dram_tensor("lmag_row", [1, 1024], F32, kind="ExternalInput").ap()
    anr_d = nc.dram_tensor("ang_row", [1, 1024], F32, kind="ExternalInput").ap()
    lmT_d = nc.dram_tensor("lmagT2", [128, 16], F32, kind="ExternalInput").ap()
    anT_d = nc.dram_tensor("angT2", [128, 16], F32, kind="ExternalInput").ap()
    BDA_d = nc.dram_tensor("BDA", [2, 128, 1024], F32, kind="ExternalInput").ap()
    BDB_d = nc.dram_tensor("BDB", [2, 128, 1024], F32, kind="ExternalInput").ap()
    CcP_d = nc.dram_tensor("CcP", [2, 128, 1024], F32, kind="ExternalInput").ap()
    Ccc_d = nc.dram_tensor("Ccc", [128, 256], F32, kind="ExternalInput").ap()
    dv_d = nc.dram_tensor("dvec", [128, 2], F32, kind="ExternalInput").ap()
    H0A_d = nc.dram_tensor("H0A", [max(NI, 1), 128, 16], F32, kind="ExternalInput").ap()
    H0B_d = nc.dram_tensor("H0B", [max(NI, 1), 128, 16], F32, kind="ExternalInput").ap()
    tri_d = nc.dram_tensor("tri", [128, 128], F32, kind="ExternalInput").ap()
    cst_d = nc.dram_tensor("cst", [128, 4], F32, kind="ExternalInput").ap()
    tv_d = nc.dram_tensor("tvec", [1, 129], F32, kind="ExternalInput").ap()
    yT_d = nc.dram_tensor("yT", [256, NTOK], F32, kind="ExternalOutput").ap()
    HF_d = nc.dram_tensor("HF", [NS, 128, 16], F32, kind="ExternalOutput").ap()

    with ExitStack() as es:
        p = Prog(nc, es)
        V = lambda f, deps=(): p.op("dve", f, list(deps))
        cst = p.sb([128, 4], F32)
        tvb = p.sb([128, 129], F32)
        tri = p.sb([128, 128], BF16)
        lmb = p.sb([128, 1024], F32)
        anb = p.sb([128, 1024], F32)
        lmT = p.sb([128, 16], F32)
        anT = p.sb([128, 16], F32)
        BDA = p.sb([128, 2, 1024], BF16)
        BDB = p.sb([128, 2, 1024], BF16)
        CcP = p.sb([128, 2, 1024], BF16)
        CcPf = p.sb([128, 2, 1024], F32)
        Ccc = p.sb([128, 256], F32)
        dv = p.sb([128, 2], F32)
        Pa = p.sb([128, 16, 128], F32)
        Pb = p.sb([128, 16, 128], F32)
        s1 = p.sb([128, 1024], F32)
        s2 = p.sb([128, 1024], F32)
        s3 = p.sb([128, 1024], F32)
        s4 = p.sb([128, 1024], F32)
        si = p.sb([128, 1024], I32)
        Qa = p.sb([128, 16, 129], F32)
        Qb = p.sb([128, 16, 129], F32)
        Qa16 = p.sb([128, 16, 129], BF16)
        Qb16 = p.sb([128, 16, 129], BF16)
        q1 = p.sb([128, 16 * 129], F32)
        q2 = p.sb([128, 16 * 129], F32)
        q3 = p.sb([128, 16 * 129], F32)
        q4 = p.sb([128, 16 * 129], F32)
        qi_ = p.sb([128, 16 * 129], I32)

        d_c = p.dma("sp", "c0", cst[:], cst_d)
        d_tv = p.dma("sp", "c1", tvb[:], tv_d.partition_broadcast(128))
        d_tri = p.dma("pool", "c2", tri[:], tri_d)
        d_lm = p.dma("sp", "c3", lmb[:], lmr_d.partition_broadcast(128))
        d_an = p.dma("sp", "c4", anb[:], anr_d.partition_broadcast(128))
        d_lmT = p.dma("sp", "c5", lmT[:], lmT_d)
        d_anT = p.dma("sp", "c6", anT[:], anT_d)
        d_bda = p.dma("pool", "c7", BDA[:], BDA_d.rearrange("c p n -> p c n"))
        d_bdb = p.dma("pool", "c8", BDB[:], BDB_d.rearrange("c p n -> p c n"))
        d_ccp = p.dma("sp", "c9", CcPf[:], CcP_d.rearrange("c p n -> p c n"))
        d_ccc = p.dma("sp", "c10", Ccc[:], Ccc_d)
        d_dv = p.dma("sp", "c11", dv[:], dv_d)
        t_ccp = V(lambda e: e.tensor_scalar(out=CcP[:], in0=CcPf[:], scalar1=cst[:, 3:4], scalar2=None, op0=ALU.mult),
                  [d_ccp, d_c])
        t_ccc = V(lambda e: e.tensor_scalar(out=Ccc[:], in0=Ccc[:], scalar1=cst[:, 3:4], scalar2=None, op0=ALU.mult),
                  [d_ccc, d_c])
        a1 = V(lambda e: e.tensor_scalar(out=s1[:], in0=lmb[:], scalar1=cst[:, 1:2], scalar2=None, op0=ALU.mult),
               [d_lm, d_c])
        a2 = p.op("act", lambda e: e.activation(out=s1[:], in_=s1[:], func=AF.Exp), [a1])
        a3 = V(lambda e: e.tensor_scalar(out=s2[:], in0=anb[:], scalar1=cst[:, 0:1], scalar2=None, op0=ALU.mult),
               [d_an, d_c])
        a4 = _sincos(p, s3[:], s2[:], s4[:], [a3], True, 1024, si[:], anb[:])
        a5 = V(lambda e: e.tensor_tensor(out=s3[:], in0=s3[:], in1=s1[:], op=ALU.mult), [a4, a2])
        s1v = lambda t: t[:, :].rearrange("p (g n) -> p g n", n=64)
        a6 = V(lambda e: e.tensor_copy(out=Pa[:, :, 0:64], in_=s1v(s3)), [a5])
        a7 = V(lambda e: e.tensor_copy(out=Pa[:, :, 64:128], in_=s1v(s3)), [a6])
        a8 = _sincos(p, s3[:], s2[:], s4[:], [a7], False, 1024, si[:], anb[:])
        a9 = V(lambda e: e.tensor_tensor(out=s3[:], in0=s3[:], in1=s1[:], op=ALU.mult), [a8])
        a10 = V(lambda e: e.tensor_copy(out=Pb[:, :, 0:64], in_=s1v(s3)), [a9])
        a11 = V(lambda e: e.tensor_scalar(out=Pb[:, :, 64:128], in0=s1v(s3), scalar1=-1.0, scalar2=None,
                                          op0=ALU.mult), [a10])
        qv = lambda t: t[:, :].rearrange("p (g t) -> p g t", t=129)
        b1 = V(lambda e: e.tensor_tensor(out=qv(q1), in0=tvb[:, :].unsqueeze(1).to_broadcast([128, 16, 129]),
                                         in1=lmT[:, :].unsqueeze(2).to_broadcast([128, 16, 129]), op=ALU.mult),
               [d_tv, d_lmT])
        b2 = p.op("act", lambda e: e.activation(out=q1[:], in_=q1[:], func=AF.Exp), [b1])
        b3 = V(lambda e: e.tensor_tensor(out=qv(q2), in0=tvb[:, :].unsqueeze(1).to_broadcast([128, 16, 129]),
                                         in1=anT[:, :].unsqueeze(2).to_broadcast([128, 16, 129]), op=ALU.mult),
               [d_tv, d_anT])
        kfq = p.sb([128, 16 * 129], F32)
        b4 = _sincos(p, q3[:], q2[:], q4[:], [b3], True, 0, qi_[:], kfq[:])
        b5 = V(lambda e: e.tensor_tensor(out=Qa[:, :, :], in0=qv(q3), in1=qv(q1), op=ALU.mult), [b4, b2])
        b6 = _sincos(p, q3[:], q2[:], q4[:], [b5], False, 0, qi_[:], kfq[:])
        b7 = V(lambda e: e.tensor_tensor(out=q3[:], in0=q3[:], in1=q1[:], op=ALU.mult), [b6])
        b8 = V(lambda e: e.tensor_scalar(out=Qb[:, :, :], in0=qv(q3), scalar1=cst[:, 2:3], scalar2=None, op0=ALU.mult),
               [b7, d_c])
        b9 = V(lambda e: e.tensor_copy(out=Qa16[:], in_=Qa[:]), [b5])
        b10 = V(lambda e: e.tensor_copy(out=Qb16[:], in_=Qb[:]), [b8])
        tabs = [a7, a11, b9, b10, t_ccp, t_ccc, d_bda, d_bdb, d_tri, d_dv]

        uf = Ring([p.sb([128, 128], F32) for _ in range(3)])
        ub = Ring([p.sb([128, 128], BF16) for _ in range(3)])
        banks = Ring([p.ps([128, 512], F32) for _ in range(7)])
        ps_y = p.ps([128, 512], F32)
        t1b = Ring([p.sb([128, 512], F32) for _ in range(2)])
        t2b = Ring([p.sb([128, 512], F32) for _ in range(2)])
        Vb = Ring([p.sb([128, 512], BF16) for _ in range(2)])
        x1b = Ring([p.sb([128, 4, 128], F32) for _ in range(2)])
        x2b = Ring([p.sb([128, 4, 128], F32) for _ in range(2)])
        Xb = Ring([p.sb([128, 8, 128], BF16) for _ in range(2)])
        PadA = Ring([p.sb([128, 8, 128], BF16) for _ in range(2)])
        PadB = Ring([p.sb([128, 8, 128], BF16) for _ in range(2)])
        yo = Ring([p.sb([128, 128], F32) for _ in range(3)])
        EA = p.sb([128, 16], F32)
        EB = p.sb([128, 16], F32)
        e1t = p.sb([128, 16], F32)
        HA = [p.sb([128, 16], F32) for _ in range(2)]
        HB = [p.sb([128, 16], F32) for _ in range(2)]
        n1 = p.sb([128, 16], F32)
        n2 = p.sb([128, 16], F32)
        t_z = []
        for pad in PadA.bufs + PadB.bufs:
            t_z.append(V(lambda e, pad=pad: e.memset(pad[:], 0.0)))
        hcur = 0
        t_H = None
        t_Hread = []
        t_E_read = []
        outs = []
        ii = 0
        y_free = []
        for si_, (tok0, nb, L, has_init) in enumerate(seqs):
            if has_init:
                ta = p.dma("sp", "h0a", HA[hcur][:], H0A_d[ii], t_Hread)
                tb_ = p.dma("sp", "h0b", HB[hcur][:], H0B_d[ii], t_Hread)
                ii += 1
                t_H = [ta, tb_]
            else:
                ta = V(lambda e, h=HA[hcur]: e.memset(h[:], 0.0), t_Hread)
                tb_ = V(lambda e, h=HB[hcur]: e.memset(h[:], 0.0), t_Hread)
                t_H = [ta, tb_]
            for b in range(nb):
                t0 = tok0 + b * L
                e_toks = []
                pad_readers = []
                for ck in range(2):
                    kf_, uft, duf = uf.next()
                    t_u = p.dma("sp", f"u{kf_}", uft[:, 0:L], uT_d[ck * 128:(ck + 1) * 128, t0:t0 + L], duf)
                    kb_, ubt, dub = ub.next()
                    t_ub = p.op("act", lambda e, ubt=ubt, uft=uft: e.copy(out=ubt[:, 0:L], in_=uft[:, 0:L]), [t_u] + dub)
                    kx, Xt, dX = Xb.next()
                    x_toks = []
                    for hc in range(2):
                        gg0 = ck * 8 + hc * 4
                        rows = slice(hc * 64, (hc + 1) * 64)
                        cols = slice(hc * 512, (hc + 1) * 512)
                        kA, bA, dA = banks.next()
                        mA = p.op("pe", lambda e, bA=bA, ubt=ubt, rows=rows, cols=cols, ck=ck: e.matmul(
                            bA[0:L, :], lhsT=ubt[rows, 0:L], rhs=BDA[rows, ck, cols], start=True, stop=True),
                            [t_ub] + tabs + dA)
                        kB, bB, dB = banks.next()
                        mB = p.op("pe", lambda e, bB=bB, ubt=ubt, rows=rows, cols=cols, ck=ck: e.matmul(
                            bB[0:L, :], lhsT=ubt[rows, 0:L], rhs=BDB[rows, ck, cols], start=True, stop=True),
                            [t_ub] + dB)
                        k1, t1t, d1_ = t1b.next()
                        k2, t2t, d2_ = t2b.next()
                        kv, Vt, dV = Vb.next()
                        pv = lambda t, gg0=gg0: t[0:L, gg0:gg0 + 4, :]
                        v3 = lambda t: t[0:L, :].rearrange("p (g n) -> p g n", n=128)
                        o1 = V(lambda e, t1t=t1t, bA=bA, pv=pv, v3=v3: e.tensor_tensor(
                            out=v3(t1t), in0=v3(bA), in1=pv(Pa), op=ALU.mult), [mA] + d1_)
                        o2 = V(lambda e, t2t=t2t, bB=bB, pv=pv, v3=v3: e.tensor_tensor(
                            out=v3(t2t), in0=v3(bB), in1=pv(Pb), op=ALU.mult), [mB] + d2_)
                        banks.done(kA, o1)
                        banks.done(kB, o2)
                        o3 = V(lambda e, Vt=Vt, t1t=t1t, t2t=t2t: e.tensor_tensor(
                            out=Vt[0:L, :], in0=t1t[0:L, :], in1=t2t[0:L, :], op=ALU.add), [o1, o2] + dV)
                        t1b.done(k1, o3)
                        t2b.done(k2, o3)
                        kcA, cA, dcA = banks.next()
                        kcB, cB, dcB = banks.next()
                        mc = None
                        for g in range(4):
                            mc = p.op("pe", lambda e, cA=cA, Vt=Vt, g=g: e.matmul(
                                cA[:, g * 128:g * 128 + L], lhsT=Vt[0:L, g * 128:(g + 1) * 128], rhs=tri[0:L, 0:L],
                                start=True, stop=True), ([o3] + dcA + dcB) if g == 0 else [], inc=False)
                            mc = p.op("pe", lambda e, cB=cB, Vt=Vt, g=g: e.matmul(
                                cB[0:64, g * 128:g * 128 + L], lhsT=Vt[0:L, g * 128 + 64:(g + 1) * 128],
                                rhs=tri[0:L, 0:L], start=True, stop=True), [], inc=False)
                            mc = p.op("pe", lambda e, cB=cB, Vt=Vt, g=g: e.matmul(
                                cB[64:128, g * 128:g * 128 + L], lhsT=Vt[0:L, g * 128:g * 128 + 64],
                                rhs=tri[0:L, 0:L], start=True, stop=True), [], inc=(g == 3))
                        Vb.done(kv, mc)
                        c3 = lambda t: t[:, :].rearrange("p (g n) -> p g n", n=128)[:, :, 0:L]
                        qv_ = lambda t, gg0=gg0: t[:, gg0:gg0 + 4, 0:L]
                        kx1, x1t, dx1 = x1b.next()
                        kx2, x2t, dx2 = x2b.next()
                        r1 = V(lambda e, x1t=x1t, cA=cA, c3=c3, qv_=qv_: e.tensor_tensor(
                            out=x1t[:, :, 0:L], in0=c3(cA), in1=qv_(Qa), op=ALU.mult), [mc] + dx1)
                        r2 = V(lambda e, x2t=x2t, cB=cB, c3=c3, qv_=qv_: e.tensor_tensor(
                            out=x2t[:, :, 0:L], in0=c3(cB), in1=qv_(Qb), op=ALU.mult), [mc] + dx2)
                        r3 = V(lambda e, Xt=Xt, x1t=x1t, x2t=x2t, hc=hc: e.tensor_tensor(
                            out=Xt[:, hc * 4:(hc + 1) * 4, 0:L], in0=x1t[:, :, 0:L], in1=x2t[:, :, 0:L], op=ALU.add),
                            [r1, r2] + (dX if hc == 0 else []))
                        r4 = V(lambda e, x1t=x1t, x2t=x2t, gg0=gg0: e.tensor_tensor(
                            out=EA[:, gg0:gg0 + 4], in0=x1t[:, :, L - 1], in1=x2t[:, :, L - 1], op=ALU.add),
                            [r1, r2] + t_E_read)
                        r5 = V(lambda e, cB=cB, gg0=gg0: e.tensor_tensor(
                            out=e1t[:, gg0:gg0 + 4], in0=cB[:, :].rearrange("p (g n) -> p g n", n=128)[:, :, L - 1],
                            in1=Qa[:, gg0:gg0 + 4, L - 1], op=ALU.mult), [mc] + t_E_read)
                        r6 = V(lambda e, cA=cA, gg0=gg0: e.tensor_tensor(
                            out=EB[:, gg0:gg0 + 4], in0=cA[:, :].rearrange("p (g n) -> p g n", n=128)[:, :, L - 1],
                            in1=Qb[:, gg0:gg0 + 4, L - 1], op=ALU.mult), [mc] + t_E_read)
                        r7 = V(lambda e, gg0=gg0: e.tensor_tensor(
                            out=EB[:, gg0:gg0 + 4], in0=e1t[:, gg0:gg0 + 4], in1=EB[:, gg0:gg0 + 4], op=ALU.subtract),
                            [r5, r6])
                        banks.done(kcA, r1, r6)
                        banks.done(kcB, r2, r5)
                        x1b.done(kx1, r3, r4)
                        x2b.done(kx2, r3, r4)
                        x_toks.append(r3)
                        e_toks += [r4, r7]
                    ub.done(kb_, mB)
                    kpa, pa, dpa = PadA.next()
                    kpb, pbt, dpb = PadB.next()
                    tp = None
                    for g in range(8):
                        G = ck * 8 + g
                        tp = V(lambda e, pa=pa, g=g, G=G, h=HA[hcur]: e.tensor_scalar(
                            out=pa[:, g, g * 16:(g + 1) * 16], in0=Ccc[:, G * 16:(G + 1) * 16], scalar1=h[:, G:G + 1],
                            scalar2=None, op0=ALU.mult), (t_H + [t_ccc] + dpa + t_z) if g == 0 else [])
                        tp = V(lambda e, pbt=pbt, g=g, G=G, h=HB[hcur]: e.tensor_scalar(
                            out=pbt[:, g, g * 16:(g + 1) * 16], in0=Ccc[:, G * 16:(G + 1) * 16], scalar1=h[:, G:G + 1],
                            scalar2=None, op0=ALU.mult), dpb if g == 0 else [])
                    my = None
                    for g in range(8):
                        G = ck * 8 + g
                        my = p.op("pe", lambda e, Xt=Xt, g=g, ck=ck: e.matmul(
                            ps_y[:, 0:L], lhsT=CcP[:, ck, g * 128:(g + 1) * 128], rhs=Xt[:, g, 0:L],
                            start=(g == 0), stop=False), (x_toks + [tp] + y_free) if g == 0 else [], inc=False)
                        my = p.op("pe", lambda e, pa=pa, g=g, G=G: e.matmul(
                            ps_y[:, 0:L], lhsT=pa[:, g, :], rhs=Qa16[:, G, 1:L + 1], start=False, stop=False),
                            [], inc=False)
                        my = p.op("pe", lambda e, pbt=pbt, g=g, G=G: e.matmul(
                            ps_y[:, 0:L], lhsT=pbt[:, g, :], rhs=Qb16[:, G, 1:L + 1], start=False, stop=(g == 7)),
                            [], inc=(g == 7))
                    Xb.done(kx, my)
                    PadA.done(kpa, my)
                    PadB.done(kpb, my)
                    pad_readers.append(tp)
                    ko, yot, dyo = yo.next()
                    ty = V(lambda e, yot=yot, uft=uft, ck=ck: e.scalar_tensor_tensor(
                        out=yot[:, 0:L], in0=uft[:, 0:L], scalar=dv[:, ck:ck + 1], in1=ps_y[:, 0:L],
                        op0=ALU.mult, op1=ALU.add), [my, t_u] + dyo)
                    y_free = [ty]
                    uf.done(kf_, ty)
                    tst = p.dma("act", f"y{ko}", yT_d[ck * 128:(ck + 1) * 128, t0:t0 + L], yot[:, 0:L], [ty])
                    yo.done(ko, tst)
                    outs.append(tst)
                hn = 1 - hcur
                QaL = Qa[:, :, L]
                QbL = Qb[:, :, L]
                dep0 = t_H + e_toks + pad_readers + t_Hread
                u1 = V(lambda e, h=HA[hcur]: e.tensor_tensor(out=n1[:], in0=h[:], in1=QaL, op=ALU.mult), dep0)
                u2 = V(lambda e, h=HB[hcur]: e.tensor_tensor(out=n2[:], in0=h[:], in1=QbL, op=ALU.mult), [u1])
                u3 = V(lambda e: e.tensor_tensor(out=n1[:], in0=n1[:], in1=n2[:], op=ALU.add), [u2])
                u4 = V(lambda e, h=HA[hn]: e.tensor_tensor(out=h[:], in0=n1[:], in1=EA[:], op=ALU.add), [u3])
                u5 = V(lambda e, h=HB[hcur]: e.tensor_tensor(out=n1[:], in0=h[:], in1=QaL, op=ALU.mult), [u4])
                u6 = V(lambda e, h=HA[hcur]: e.tensor_tensor(out=n2[:], in0=h[:], in1=QbL, op=ALU.mult), [u5])
                u7 = V(lambda e: e.tensor_tensor(out=n1[:], in0=n1[:], in1=n2[:], op=ALU.subtract), [u6])
                u8 = V(lambda e, h=HB[hn]: e.tensor_tensor(out=h[:], in0=n1[:], in1=EB[:], op=ALU.add), [u7])
                t_E_read = [u8]
                t_Hread = [u8]
                t_H = [u4, u8]
                hcur = hn
            tf = p.dma("sp", "hf", HF_d[si_], HA[hcur][:], t_H)
            outs.append(tf)
            t_Hread = t_Hread + [tf]
        p.op("sp", None, outs[-8:], inc=False)
        p.emit()
    return nc


_TRI = np.triu(np.ones((128, 128), np.float32))
_CST = np.stack([np.arange(128), -np.arange(128), np.where(np.arange(128) < 64, -1.0, 1.0),
                 np.where(np.arange(128) < 64, 1.0, -1.0)], axis=1).astype(np.float32)
_TVEC = np.arange(129, dtype=np.float32).reshape(1, 129)


def s5_scan_dev(u, seqs, lmag, ang, bbre, bbim, c_re, c_im, d, h0_re=None, h0_im=None):
    NTOK = u.shape[0]
    key = ("s5s", tuple(seqs), NTOK)
    if key not in _PROG_CACHE:
        _PROG_CACHE[key] = build_s5_scan(list(seqs), NTOK)
    nc = _PROG_CACHE[key]
    NI = sum(1 for s_ in seqs if s_[3])
    in_maps = []
    for c in range(NCORES):
        gs = slice(16 * c, 16 * c + 16)
        BDA = np.zeros((2, 128, 1024), np.float32)
        BDB = np.zeros((2, 128, 1024), np.float32)
        CcP = np.zeros((2, 128, 8, 128), np.float32)
        Ccc = np.zeros((128, 16, 16), np.float32)
        for G in range(16):
            ck, g = G // 8, G % 8
            gi = 16 * c + G
            BDA[ck, g * 16:(g + 1) * 16, g * 128:g * 128 + 64] = bbre[gi].T
            BDA[ck, g * 16:(g + 1) * 16, g * 128 + 64:(g + 1) * 128] = bbim[gi].T
            BDB[ck, g * 16:(g + 1) * 16, g * 128:g * 128 + 64] = bbim[gi].T
            BDB[ck, g * 16:(g + 1) * 16, g * 128 + 64:(g + 1) * 128] = bbre[gi].T
            cc = np.concatenate([c_re[gi].T, c_im[gi].T], axis=0)
            CcP[ck, :, g, g * 16:(g + 1) * 16] = cc
            Ccc[:, G, :] = cc
        m = {"uT": np.ascontiguousarray(u[:, 256 * c:256 * (c + 1)].T),
             "lmag_row": np.ascontiguousarray(lmag[gs].reshape(1, 1024)),
             "ang_row": np.ascontiguousarray(ang[gs].reshape(1, 1024)),
             "lmagT2": np.ascontiguousarray(np.concatenate([lmag[gs].T, lmag[gs].T], axis=0)),
             "angT2": np.ascontiguousarray(np.concatenate([ang[gs].T, ang[gs].T], axis=0)),
             "BDA": BDA, "BDB": BDB, "CcP": CcP.reshape(2, 128, 1024), "Ccc": Ccc.reshape(128, 256),
             "dvec": np.ascontiguousarray(d[256 * c:256 * (c + 1)].reshape(2, 128).T),
             "tri": _TRI, "cst": _CST, "tvec": _TVEC}
        if NI:
            m["H0A"] = np.ascontiguousarray(np.concatenate(
                [h0_re[:, gs].transpose(0, 2, 1), h0_im[:, gs].transpose(0, 2, 1)], axis=1))
            m["H0B"] = np.ascontiguousarray(np.concatenate(
                [h0_im[:, gs].transpose(0, 2, 1), h0_re[:, gs].transpose(0, 2, 1)], axis=1))
        else:
            m["H0A"] = np.zeros((1, 128, 16), np.float32)
            m["H0B"] = np.zeros((1, 128, 16), np.float32)
        in_maps.append(m)
    res = run_bass_kernel_spmd(nc, in_maps, core_ids=list(range(NCORES)))
    y = np.concatenate([r["yT"].T for r in res.results], axis=1)
    HFre = np.concatenate([r["HF"][:, 0:64, :].transpose(0, 2, 1) for r in res.results], axis=1)
    HFim = np.concatenate([r["HF"][:, 64:128, :].transpose(0, 2, 1) for r in res.results], axis=1)
    return y, HFre, HFim


def build_gather():
    nc = bass.Bass("TRN2", target_bir_lowering=False)
    ck_d = nc.dram_tensor("ck", [10240, 8192], F32, kind="ExternalInput").ap()
    cv_d = nc.dram_tensor("cv", [10240, 8192], F32, kind="ExternalInput").ap()
    ci_d = nc.dram_tensor("ci", [1280, 8192], F32, kind="ExternalInput").ap()
    pt_d = nc.dram_tensor("pt", [128, 1], I32, kind="ExternalInput").ap()
    Kp_d = nc.dram_tensor("Kp", [128, 8, 8192], F32, kind="ExternalOutput").ap()
    Vp_d = nc.dram_tensor("Vp", [128, 8, 8192], F32, kind="ExternalOutput").ap()
    ip_d = nc.dram_tensor("ip", [128, 8192], F32, kind="ExternalOutput").ap()
    with ExitStack() as es:
        p = Prog(nc, es)
        pt = p.sb([128, 1], I32)
        idx = p.sb([128, 8], I32)
        bufs = Ring([p.sb([128, 8192], F32) for _ in range(3)])
        t_pt = p.dma("sp", "pt", pt[:], pt_d)
        t_i = None
        for e_ in range(8):
            t_i = p.op("dve", lambda e, e_=e_: e.tensor_scalar(
                out=idx[:, e_:e_ + 1], in0=pt[:], scalar1=8, scalar2=e_, op0=ALU.mult, op1=ALU.add), [t_pt])
        outs = []
        jobs = [(ci_d, pt, 0, ip_d)] + [(ck_d, idx, e_, Kp_d[:, e_, :]) for e_ in range(8)] + \
               [(cv_d, idx, e_, Vp_d[:, e_, :]) for e_ in range(8)]
        for n, (src, it, col, dst) in enumerate(jobs):
            kb, bt, db = bufs.next()
            lane = f"g{kb}"
            cnt = p.lanes.get(lane, 0)
            p.lanes[lane] = cnt + 1
            tok = ("d", lane, cnt)
            p.ops["pool"].append((lambda e, bt=bt, src=src, it=it, col=col: e.indirect_dma_start(
                out=bt[:], out_offset=None, in_=src,
                in_offset=bass.IndirectOffsetOnAxis(ap=it[:, col:col + 1], axis=0)),
                tuple(d for d in ([t_i, t_pt] + db) if d is not None), tok))
            t_o = p.dma("sp", f"go{kb}", dst, bt[:], [tok])
            bufs.done(kb, t_o)
            outs.append(t_o)
        p.op("sp", None, outs[-3:], inc=False)
        p.emit()
    return nc


def gather_dev(cache_k_l, cache_v_l, cache_kidx_l, page_table):
    key = ("gather",)
    if key not in _PROG_CACHE:
        _PROG_CACHE[key] = build_gather()
    nc = _PROG_CACHE[key]
    ck = cache_k_l.reshape(10240, 8192)
    cv = cache_v_l.reshape(10240, 8192)
    ci = cache_kidx_l.reshape(1280, 8192)
    in_maps = [{"ck": ck, "cv": cv, "ci": ci, "pt": np.ascontiguousarray(page_table[c].reshape(128, 1))}
               for c in range(NCORES)]
    res = run_bass_kernel_spmd(nc, in_maps, core_ids=list(range(NCORES)))
    Kp = np.stack([r["Kp"].reshape(128, 128, 4, 128) for r in res.results])
    Vp = np.stack([r["Vp"].reshape(128, 128, 4, 128) for r in res.results])
    ip = np.stack([r["ip"].reshape(128, 128, 64) for r in res.results])
    return Kp, Vp, ip


def attn_sample_dev(q, k, v, qi, ki, wi, Kp, Vp, ip, rel_bias):
    key = ("attn_s",)
    if key not in _PROG_CACHE:
        _PROG_CACHE[key] = build_attn(1, 4, [0], [129], 129)
    nc = _PROG_CACHE[key]
    D, C0 = _bias_tiles(rel_bias, 4)
    cb = np.ascontiguousarray(rel_bias[31].reshape(1, 16))
    in_maps = []
    for c in range(NCORES):
        KT = np.zeros((129, 128, 512), np.float32)
        VV = np.zeros((129, 128, 512), np.float32)
        KiT = np.zeros((129, 64, 128), np.float32)
        kn = np.zeros((128, 4, 128), np.float32)
        kn[0:4] = k[c].reshape(4, 4, 128)
        vn = np.zeros((128, 512), np.float32)
        vn[0:4] = v[c]
        kin = np.zeros((128, 64), np.float32)
        kin[0:4] = ki[c]
        KT[0] = kn.transpose(2, 1, 0).reshape(128, 512)
        VV[0] = vn
        KiT[0] = kin.T
        KT[1:] = Kp[c][::-1].transpose(0, 3, 2, 1).reshape(128, 128, 512)
        VV[1:] = Vp[c][::-1].reshape(128, 128, 512)
        KiT[1:] = ip[c][::-1].transpose(0, 2, 1)
        qT = q[c].reshape(4, 16, 128).transpose(2, 1, 0).reshape(1, 128, 64)
        qiT = qi[c].reshape(4, 16, 64).transpose(2, 1, 0).reshape(1, 64, 64)
        wT = wi[c].T.reshape(1, 1, 64)
        in_maps.append({"qT": np.ascontiguousarray(qT), "qiT": np.ascontiguousarray(qiT),
                        "wT": np.ascontiguousarray(wT), "KT": KT, "V": VV, "kiT": KiT,
                        "vt": np.zeros((1, 129), np.float32), "D": D, "cb": cb, "C0": C0})
    res = run_bass_kernel_spmd(nc, in_maps, core_ids=list(range(NCORES)))
    o = np.zeros((8, 4, 2048), np.float32)
    for c in range(NCORES):
        oT = res.results[c]["oT"].reshape(4, 128, 4, 4)
        o[c] = oT.transpose(3, 0, 2, 1).reshape(4, 2048)
    return o


RPC = 1152


def _to_rows(xp, xs):
    F_ = xp.shape[-1]
    xpf = xp.reshape(8192, F_)
    out = []
    for c in range(NCORES):
        r = np.zeros((RPC, F_), np.float32)
        r[0:1024] = xpf[c * 1024:(c + 1) * 1024]
        r[1024:1028] = xs[c]
        out.append(r)
    return out


def _from_rows(rows):
    xp = np.concatenate([r[0:1024] for r in rows]).reshape(2, 4096, -1)
    xs = np.stack([r[1024:1028] for r in rows])
    return xp, xs


def kernel(x_prompt, x_sample, cache_k, cache_v, cache_kidx, state_ssm_re, state_ssm_im, page_table,
           p_prompt, p_sample, norm_w, final_norm_w, rel_bias, attn_w_in, attn_w_out, ssm_w_in,
           ssm_lambda_re, ssm_lambda_im, ssm_log_dt, ssm_b_re, ssm_b_im, ssm_c_re, ssm_c_im, ssm_d,
           ssm_w_glu, ssm_b_glu, ssm_w_out, ple_norm_w, ple_w_gate, ple_w_proj):
    A = lambda a: np.asarray(a)
    (x_prompt, x_sample, cache_k, cache_v, cache_kidx, state_ssm_re, state_ssm_im, page_table, p_prompt, p_sample,
     norm_w, final_norm_w, rel_bias, attn_w_in, attn_w_out, ssm_w_in, ssm_lambda_re, ssm_lambda_im, ssm_log_dt,
     ssm_b_re, ssm_b_im, ssm_c_re, ssm_c_im, ssm_d, ssm_w_glu, ssm_b_glu, ssm_w_out, ple_norm_w, ple_w_gate,
     ple_w_proj) = [A(a) for a in (
        x_prompt, x_sample, cache_k, cache_v, cache_kidx, state_ssm_re, state_ssm_im, page_table, p_prompt,
        p_sample, norm_w, final_norm_w, rel_bias, attn_w_in, attn_w_out, ssm_w_in, ssm_lambda_re, ssm_lambda_im,
        ssm_log_dt, ssm_b_re, ssm_b_im, ssm_c_re, ssm_c_im, ssm_d, ssm_w_glu, ssm_b_glu, ssm_w_out, ple_norm_w,
        ple_w_gate, ple_w_proj)]
    h = _to_rows(x_prompt, x_sample)
    kp_, vp_, kip_, ks_, vs_, kis_ = [], [], [], [], [], []
    hrp, hip, hrs, his = [], [], [], []
    cuts = np.cumsum([2048, 512, 512, 2048, 1024, 64, 16])[:-1]
    for i in range(4):
        l = i // 2
        if i % 2 == 0:
            z = run_linear(h, attn_w_in[l], "rms", "none", nw=norm_w[i])
            zp, zs = _from_rows(z)
            q, k, v, gate, qi, ki, wi = np.split(zp, cuts, axis=-1)
            qs, ks, vs, gates, qis, kis, wis = np.split(zs, cuts, axis=-1)
            kp_.append(k.reshape(2, 4096, 4, 128)); vp_.append(v.reshape(2, 4096, 4, 128)); kip_.append(ki)
            ks_.append(ks.reshape(8, 4, 4, 128)); vs_.append(vs.reshape(8, 4, 4, 128)); kis_.append(kis)
            o_p = attn_prompt_dev(q, k, v, qi, ki, wi, rel_bias)
            Kp, Vp, ip = gather_dev(cache_k[l], cache_v[l], cache_kidx[l], page_table)
            o_s = attn_sample_dev(qs, ks, vs, qis, kis, wis, Kp, Vp, ip, rel_bias)
            o_rows = _to_rows(o_p, o_s)
            g_rows = _to_rows(gate, gates)
            h1 = run_linear(o_rows, attn_w_out[l], "gate", "res", x2s=g_rows, e1s=h)
        else:
            z = run_linear(h, ssm_w_in[l], "rms", "none", nw=norm_w[i])
            zp, zs = _from_rows(z)
            u_p, gate = zp[..., :2048], zp[..., 2048:]
            u_s, gates = zs[..., :2048], zs[..., 2048:]
            lmag, ang, bbre, bbim = s5_params_dev(ssm_lambda_re[l], ssm_lambda_im[l], ssm_log_dt[l],
                                                  ssm_b_re[l], ssm_b_im[l])
            y_p, Hr, Hi = s5_scan_dev(np.ascontiguousarray(u_p.reshape(8192, 2048)),
                                      ((0, 32, 128, False), (4096, 32, 128, False)),
                                      lmag, ang, bbre, bbim, ssm_c_re[l], ssm_c_im[l], ssm_d[l])
            hrp.append(Hr); hip.append(Hi)
            y_s, Hr, Hi = s5_scan_dev(np.ascontiguousarray(u_s.reshape(32, 2048)),
                                      tuple((4 * b, 1, 4, True) for b in range(8)),
                                      lmag, ang, bbre, bbim, ssm_c_re[l], ssm_c_im[l], ssm_d[l],
                                      state_ssm_re[l], state_ssm_im[l])
            hrs.append(Hr); his.append(Hi)
            y_rows = _to_rows(y_p.reshape(2, 4096, 2048), y_s.reshape(8, 4, 2048))
            g_rows = _to_rows(gate, gates)
            y3 = run_linear(y_rows, ssm_w_glu[l], "gelu", "glu", e1s=y_rows, b=ssm_b_glu[l])
            h1 = run_linear(y3, ssm_w_out[l], "gate", "res", x2s=g_rows, e1s=h)
        g = run_linear(h1, ple_w_gate[i], "rms", "sigmoid", nw=ple_norm_w[i])
        p_rows = _to_rows(p_prompt[i], p_sample[i])
        h = run_linear(p_rows, ple_w_proj[i], "plain", "gres", e1s=h1, e2s=g)
    yr = run_linear(h, np.eye(2048, dtype=np.float32), "rms", "none", nw=final_norm_w)
    y_p, y_s = _from_rows(yr)
    f = lambda a: np.ascontiguousarray(np.stack(a), dtype=np.float32)
    return (np.ascontiguousarray(y_p), np.ascontiguousarray(y_s), f(kp_), f(vp_), f(kip_), f(hrp), f(hip),
            f(ks_), f(vs_), f(kis_), f(hrs), f(his))
```

```python
import math
from contextlib import ExitStack, contextmanager
import numpy as np
import concourse.bass as bass
import concourse.mybir as mybir
from concourse.bass_utils import run_bass_kernel_spmd

F32 = mybir.dt.float32
BF16 = mybir.dt.bfloat16
I32 = mybir.dt.int32
AF = mybir.ActivationFunctionType
ALU = mybir.AluOpType
AX = mybir.AxisListType
NCORES = 8
EPOCH = 30000
PER = EPOCH // 16


class Prog:
    ENG = ("pe", "act", "dve", "pool", "sp")

    def __init__(self, nc, es):
        self.nc = nc
        self.es = es
        self.cnt = {e: 0 for e in self.ENG}
        self.lanes = {}
        self.csem = {e: [] for e in self.ENG}
        self.lsem = {}
        self.lane_inc = {}
        self.waited = {e: {} for e in self.ENG}
        self.nuniq = 0
        self.st = None
        self.ops = None
        self.barrier = []
        self.seen = set()

    @contextmanager
    def stage(self, name=""):
        with ExitStack() as st:
            self.st = st
            self.ops = {e: [] for e in self.ENG}
            self.seen = set()
            self.stage_lane_map = {}
            yield self
            self._emit()
            bar = []
            for e in self.ENG:
                if self.cnt[e] > 0:
                    bar.append(("c", e, self.cnt[e] - 1))
            for ln, (ep, val) in self.lanes.items():
                if val > 0:
                    bar.append(("d", ln, ep, val))
            self.barrier = bar
            self.st = None

    def sb(self, shape, dt, name=None):
        self.nuniq += 1
        return self.st.enter_context(self.nc.sbuf_tensor(name or f"sb{self.nuniq}", list(shape), dt))

    def ps(self, shape, dt=F32, name=None):
        self.nuniq += 1
        return self.st.enter_context(self.nc.psum_tensor(name or f"ps{self.nuniq}", list(shape), dt))

    def _deps(self, eng, deps):
        deps = [d for d in deps if d is not None]
        if eng not in self.seen:
            self.seen.add(eng)
            deps = list(self.barrier) + deps
        return tuple(deps)

    def op(self, eng, fn, deps=(), inc=True):
        deps = self._deps(eng, deps)
        tok = None
        if inc:
            tok = ("c", eng, self.cnt[eng])
            self.cnt[eng] += 1
        self.ops[eng].append((fn, deps, tok, 1))
        return tok

    def dma(self, eng, lane, out, in_, deps=()):
        return self.lane_op(eng, lane, lambda e: e.dma_start(out=out, in_=in_), deps)

    def lane_op(self, eng, lane, fn, deps=(), incv=16):
        deps = self._deps(eng, deps)
        if incv == 16:
            lane = "L%d" % self.stage_lane_map.setdefault(lane, len(self.stage_lane_map))
        ep, val = self.lanes.get(lane, (0, 0))
        if val + incv > EPOCH:
            ep, val = ep + 1, 0
        val += incv
        self.lanes[lane] = (ep, val)
        tok = ("d", lane, ep, val)
        self.ops[eng].append((fn, deps, tok, incv))
        return tok

    def _semval(self, tok):
        if tok[0] == "c":
            _, e, i = tok
            ep = i // EPOCH
            while len(self.csem[e]) <= ep:
                self.csem[e].append(self.es.enter_context(self.nc.semaphore(f"s_{e}{len(self.csem[e])}")))
            return self.csem[e][ep], (i % EPOCH) + 1
        _, ln, ep, val = tok
        lst = self.lsem.setdefault(ln, [])
        while len(lst) <= ep:
            lst.append(self.es.enter_context(self.nc.semaphore(f"l_{ln}_{len(lst)}")))
        return lst[ep], val

    def _emit(self):
        nc = self.nc
        with nc.Block() as block:
            def make(ename):
                def body(eng):
                    waited = self.waited[ename]
                    for fn, deps, tok, incv in self.ops[ename]:
                        for d in deps:
                            s, v = self._semval(d)
                            key = id(s)
                            if waited.get(key, 0) >= v:
                                continue
                            waited[key] = v
                            eng.wait_ge(s, v)
                        if fn is None:
                            continue
                        ins = fn(eng)
                        if tok is not None:
                            s, v = self._semval(tok)
                            ins.then_inc(s, incv if tok[0] == "d" else 1)
                return body
            block.tensor(make("pe"))
            block.scalar(make("act"))
            block.vector(make("dve"))
            block.gpsimd(make("pool"))
            block.sync(make("sp"))


class Ring:
    def __init__(self, bufs):
        self.bufs = bufs
        self.i = 0
        self.readers = [[] for _ in bufs]

    def next(self):
        k = self.i % len(self.bufs)
        self.i += 1
        deps = self.readers[k]
        self.readers[k] = []
        return k, self.bufs[k], deps

    def done(self, k, *toks):
        self.readers[k].extend(t for t in toks if t is not None)
def emit_linear(p, RT, D, N, pro, epi, x, w, y, ident_d, x2=None, nw=None, e1=None, e2=None, bv=None, eps=1e-6):
    R = RT * 128
    KC = D // 128
    NT = (N + 511) // 512
    ident_f = p.sb([128, 128], F32)
    ident = p.sb([128, 128], BF16)
    XT = p.sb([128, KC, R], BF16)
    xbufs = Ring([p.sb([128, D], F32) for _ in range(2)])
    x2bufs = Ring([p.sb([128, D], F32) for _ in range(2)]) if pro == "gate" else None
    tmpf = Ring([p.sb([128, D], F32) for _ in range(2)]) if pro in ("gate", "gelu", "rms") else None
    xpb = Ring([p.sb([128, D], BF16) for _ in range(2)])
    nwb = p.sb([128, D], F32) if pro == "rms" else None
    stat = Ring([p.sb([128, 2], F32) for _ in range(2)]) if pro == "rms" else None
    pT = Ring([p.ps([128, 4, 128], BF16) for _ in range(2)])
    wb = Ring([p.sb([128, KC, 512], BF16) for _ in range(2)])
    acc = Ring([p.ps([128, 512], F32) for _ in range(3)])
    ob = Ring([p.sb([128, 512], F32) for _ in range(3)])
    e1b = Ring([p.sb([128, 512], F32) for _ in range(2)]) if e1 is not None else None
    e2b = Ring([p.sb([128, 512], F32) for _ in range(2)]) if e2 is not None else None
    bb = Ring([p.sb([128, 512], F32) for _ in range(2)]) if bv is not None else None
    tb = Ring([p.sb([128, 512], F32) for _ in range(2)]) if epi in ("gres", "glu") else None
    GC = 2.0 * math.sqrt(2.0 / math.pi)

    t_id = p.dma("sp", "ident", ident_f[:], ident_d)
    t_ident = p.op("dve", lambda e: e.tensor_copy(out=ident[:], in_=ident_f[:]), [t_id])
    t_nw = p.dma("sp", "nw", nwb[:], nw.partition_broadcast(128)) if pro == "rms" else None
    xt_toks = []
    for r in range(RT):
        rows = slice(r * 128, (r + 1) * 128)
        kx, xb, dx = xbufs.next()
        t_x = p.dma("sp", f"x{kx}", xb[:], x[rows, :], dx)
        kp, xp, dxp = xpb.next()
        if pro == "rms":
            kt, tf, dt_ = tmpf.next()
            ks, st, ds = stat.next()
            t_sq = p.op("act", lambda e, tf=tf, xb=xb, st=st: e.activation(
                out=tf[:], in_=xb[:], func=AF.Square, accum_out=st[:, 0:1]), [t_x] + dt_ + ds)
            t_r1 = p.op("dve", lambda e, st=st: e.tensor_scalar(
                out=st[:, 1:2], in0=st[:, 0:1], scalar1=1.0 / D, scalar2=eps, op0=ALU.mult, op1=ALU.add), [t_sq])
            t_r1b = p.op("act", lambda e, st=st: e.activation(out=st[:, 1:2], in_=st[:, 1:2], func=AF.Sqrt), [t_r1])
            t_r2 = p.op("dve", lambda e, st=st: e.reciprocal(out=st[:, 1:2], in_=st[:, 1:2]), [t_r1b])
            t_xp = p.op("dve", lambda e, xp=xp, xb=xb, st=st: e.scalar_tensor_tensor(
                out=xp[:], in0=xb[:], scalar=st[:, 1:2], in1=nwb[:], op0=ALU.mult, op1=ALU.mult),
                [t_r2, t_nw] + dxp)
            tmpf.done(kt, t_sq); stat.done(ks, t_xp); xbufs.done(kx, t_xp)
        elif pro == "gate":
            k2, x2b, d2 = x2bufs.next()
            t_x2 = p.dma("act", f"x2{k2}", x2b[:], x2[rows, :], d2)
            kt, tf, dt_ = tmpf.next()
            t_s = p.op("act", lambda e, tf=tf, x2b=x2b: e.activation(out=tf[:], in_=x2b[:], func=AF.Silu), [t_x2] + dt_)
            t_xp = p.op("dve", lambda e, xp=xp, xb=xb, tf=tf: e.tensor_tensor(
                out=xp[:], in0=xb[:], in1=tf[:], op=ALU.mult), [t_x, t_s] + dxp)
            x2bufs.done(k2, t_s); tmpf.done(kt, t_xp); xbufs.done(kx, t_xp)
        elif pro == "gelu":
            kt, tf, dt_ = tmpf.next()
            t_a = p.op("dve", lambda e, tf=tf, xb=xb: e.tensor_tensor(out=tf[:], in0=xb[:], in1=xb[:], op=ALU.mult), [t_x] + dt_)
            t_b = p.op("dve", lambda e, tf=tf: e.tensor_scalar(
                out=tf[:], in0=tf[:], scalar1=0.044715, scalar2=1.0, op0=ALU.mult, op1=ALU.add), [t_a])
            t_c = p.op("dve", lambda e, tf=tf, xb=xb: e.tensor_tensor(out=tf[:], in0=tf[:], in1=xb[:], op=ALU.mult), [t_b])
            t_d = p.op("act", lambda e, tf=tf: e.activation(out=tf[:], in_=tf[:], func=AF.Sigmoid, scale=GC), [t_c])
            t_xp = p.op("dve", lambda e, xp=xp, xb=xb, tf=tf: e.tensor_tensor(
                out=xp[:], in0=xb[:], in1=tf[:], op=ALU.mult), [t_d] + dxp)
            tmpf.done(kt, t_xp); xbufs.done(kx, t_xp)
        else:
            t_xp = p.op("dve", lambda e, xp=xp, xb=xb: e.tensor_copy(out=xp[:], in_=xb[:]), [t_x] + dxp)
            xbufs.done(kx, t_xp)
        last = []
        for k0 in range(0, KC, 4):
            nk = min(4, KC - k0)
            kq, pt, dq = pT.next()
            tt = None
            for j in range(nk):
                tt = p.op("pe", lambda e, pt=pt, xp=xp, j=j, k0=k0: e.transpose(
                    out=pt[:, j, :], in_=xp[:, (k0 + j) * 128:(k0 + j + 1) * 128], identity=ident[:]),
                    [t_xp, t_ident] + dq, inc=(j == nk - 1))
            if (k0 // 4) % 2 == 0:
                t_cp = p.op("act", lambda e, pt=pt, k0=k0, nk=nk, rows=rows: e.copy(
                    out=XT[:, k0:k0 + nk, rows], in_=pt[:, 0:nk, :]), [tt])
            else:
                t_cp = p.op("dve", lambda e, pt=pt, k0=k0, nk=nk, rows=rows: e.tensor_copy(
                    out=XT[:, k0:k0 + nk, rows], in_=pt[:, 0:nk, :]), [tt])
            pT.done(kq, t_cp)
            last.append(t_cp)
            xpb.done(kp, tt)
        xt_toks.append(last)
    wv = w.rearrange("(k p) n -> p k n", p=128)
    fin = []
    for n in range(NT):
        n0 = n * 512
        ns = min(512, N - n0)
        kw, wt, dw = wb.next()
        t_w = p.dma("pool", f"w{kw}", wt[:, :, 0:ns], wv[:, :, n0:n0 + ns], dw)
        t_bb = None
        if bv is not None:
            kb, bt, db = bb.next()
            t_bb = p.dma("act", f"b{kb}", bt[:, 0:ns], bv[0:1, n0:n0 + ns].partition_broadcast(128), db)
        mm_last = []
        for r in range(RT):
            rows = slice(r * 128, (r + 1) * 128)
            ka, ac, da = acc.next()
            tm = None
            for k in range(KC):
                tm = p.op("pe", lambda e, ac=ac, wt=wt, k=k, rows=rows, ns=ns: e.matmul(
                    ac[:, 0:ns], lhsT=XT[:, k, rows], rhs=wt[:, k, 0:ns], start=(k == 0), stop=(k == KC - 1)),
                    ([t_w] + xt_toks[r] + da) if k == 0 else [], inc=(k == KC - 1))
            mm_last.append(tm)
            ko, ot, do = ob.next()
            if epi == "none":
                t_o = p.op("act", lambda e, ot=ot, ac=ac, ns=ns: e.copy(out=ot[:, 0:ns], in_=ac[:, 0:ns]), [tm] + do)
                acc.done(ka, t_o)
            elif epi == "sigmoid":
                t_o = p.op("act", lambda e, ot=ot, ac=ac, ns=ns: e.activation(
                    out=ot[:, 0:ns], in_=ac[:, 0:ns], func=AF.Sigmoid), [tm] + do)
                acc.done(ka, t_o)
            elif epi == "res":
                k1, et, d1 = e1b.next()
                t_e = p.dma("sp", f"e1{k1}", et[:, 0:ns], e1[rows, n0:n0 + ns], d1)
                t_o = p.op("dve", lambda e, ot=ot, ac=ac, et=et, ns=ns: e.tensor_tensor(
                    out=ot[:, 0:ns], in0=ac[:, 0:ns], in1=et[:, 0:ns], op=ALU.add), [tm, t_e] + do)
                acc.done(ka, t_o); e1b.done(k1, t_o)
            elif epi == "gres":
                k1, et, d1 = e1b.next()
                t_e = p.dma("sp", f"e1{k1}", et[:, 0:ns], e1[rows, n0:n0 + ns], d1)
                k2, et2, d2 = e2b.next()
                t_e2 = p.dma("sp", f"e2{k2}", et2[:, 0:ns], e2[rows, n0:n0 + ns], d2)
                kt, tt_, dtt = tb.next()
                t_m = p.op("dve", lambda e, tt_=tt_, ac=ac, et2=et2, ns=ns: e.tensor_tensor(
                    out=tt_[:, 0:ns], in0=ac[:, 0:ns], in1=et2[:, 0:ns], op=ALU.mult), [tm, t_e2] + dtt)
                t_o = p.op("dve", lambda e, ot=ot, tt_=tt_, et=et, ns=ns: e.tensor_tensor(
                    out=ot[:, 0:ns], in0=tt_[:, 0:ns], in1=et[:, 0:ns], op=ALU.add), [t_m, t_e] + do)
                acc.done(ka, t_m); e1b.done(k1, t_o); e2b.done(k2, t_m); tb.done(kt, t_o)
            elif epi == "glu":
                k1, et, d1 = e1b.next()
                t_e = p.dma("sp", f"e1{k1}", et[:, 0:ns], e1[rows, n0:n0 + ns], d1)
                kt, tt_, dtt = tb.next()
                t_a = p.op("dve", lambda e, tt_=tt_, et=et, ns=ns: e.tensor_tensor(
                    out=tt_[:, 0:ns], in0=et[:, 0:ns], in1=et[:, 0:ns], op=ALU.mult), [t_e] + dtt)
                t_b = p.op("dve", lambda e, tt_=tt_, ns=ns: e.tensor_scalar(
                    out=tt_[:, 0:ns], in0=tt_[:, 0:ns], scalar1=0.044715, scalar2=1.0, op0=ALU.mult, op1=ALU.add), [t_a])
                t_c = p.op("dve", lambda e, tt_=tt_, et=et, ns=ns: e.tensor_tensor(
                    out=tt_[:, 0:ns], in0=tt_[:, 0:ns], in1=et[:, 0:ns], op=ALU.mult), [t_b])
                t_d = p.op("act", lambda e, tt_=tt_, ns=ns: e.activation(
                    out=tt_[:, 0:ns], in_=tt_[:, 0:ns], func=AF.Sigmoid, scale=GC), [t_c])
                t_g = p.op("dve", lambda e, tt_=tt_, et=et, ns=ns: e.tensor_tensor(
                    out=tt_[:, 0:ns], in0=tt_[:, 0:ns], in1=et[:, 0:ns], op=ALU.mult), [t_d])
                t_s1 = p.op("dve", lambda e, ot=ot, ac=ac, bt=bt, ns=ns: e.tensor_tensor(
                    out=ot[:, 0:ns], in0=ac[:, 0:ns], in1=bt[:, 0:ns], op=ALU.add), [tm, t_bb] + do)
                t_s2 = p.op("act", lambda e, ot=ot, ns=ns: e.activation(
                    out=ot[:, 0:ns], in_=ot[:, 0:ns], func=AF.Sigmoid), [t_s1])
                t_o = p.op("dve", lambda e, ot=ot, tt_=tt_, ns=ns: e.tensor_tensor(
                    out=ot[:, 0:ns], in0=ot[:, 0:ns], in1=tt_[:, 0:ns], op=ALU.mult), [t_s2, t_g])
                acc.done(ka, t_s1); e1b.done(k1, t_g); tb.done(kt, t_o)
            else:
                raise ValueError(epi)
            t_st = p.dma("act", f"o{ko}", y[rows, n0:n0 + ns], ot[:, 0:ns], [t_o])
            ob.done(ko, t_st)
            fin.append(t_st)
        wb.done(kw, *mm_last)
        if bv is not None:
            bb.done(kb, *mm_last)
    return fin[-3:]


def emit_transpose(p, src, dst, R, C, ident_d):
    identf = p.sb([128, 128], F32)
    t_id = p.dma("sp", "ident", identf[:], ident_d)
    inb = Ring([p.sb([128, 512], F32) for _ in range(3)])
    pst = Ring([p.ps([128, 512], F32) for _ in range(3)])
    outb = Ring([p.sb([128, 512], F32) for _ in range(3)])
    fin = []
    for r0 in range(0, R, 512):
        nr = min(512, R - r0)
        nrt = nr // 128
        for c0 in range(0, C, 128):
            cw = min(128, C - c0)
            ki, it, di = inb.next()
            t_in = p.dma("sp", f"ti{ki}", it[:, 0:nrt * 128].rearrange("p (q c) -> p q c", c=128)[:, :, 0:cw],
                         src[r0:r0 + nr, c0:c0 + cw].rearrange("(q p) c -> p q c", p=128), di)
            kp, pt, dp = pst.next()
            tt = None
            for q in range(nrt):
                tt = p.op("pe", lambda e, pt=pt, it=it, q=q, cw=cw: e.transpose(
                    out=pt[0:cw, q * 128:(q + 1) * 128], in_=it[:, q * 128:q * 128 + cw], identity=identf[:]),
                    [t_in, t_id] + dp, inc=(q == nrt - 1))
            inb.done(ki, tt)
            ko, ot, do = outb.next()
            t_cp = p.op("act" if (c0 // 128) % 2 == 0 else "dve",
                        (lambda e, ot=ot, pt=pt, cw=cw, nr=nr: e.copy(out=ot[0:cw, 0:nr], in_=pt[0:cw, 0:nr]))
                        if (c0 // 128) % 2 == 0 else
                        (lambda e, ot=ot, pt=pt, cw=cw, nr=nr: e.tensor_copy(out=ot[0:cw, 0:nr], in_=pt[0:cw, 0:nr])),
                        [tt] + do)
            pst.done(kp, t_cp)
            t_st = p.dma("act", f"to{ko}", dst[c0:c0 + cw, r0:r0 + nr], ot[0:cw, 0:nr], [t_cp])
            outb.done(ko, t_st)
            fin.append(t_st)
    return fin[-3:]
NEG = -1.0e30
TOPK = 256
BIS_LO, BIS_HI, BIS_IT = -8192.0, 8192.0, 28


def emit_attn(p, NJ, NT, smax, src, vt_d, D_d, cb_d, C0_d, oT_dst):
    HN = 16 * NT
    NU = sum(smax)
    HPM = min(16, 512 // NT)
    NMM = 16 // HPM
    SMX = max(smax)
    scale = 128.0 ** -0.5
    ones = p.sb([128, 128], BF16)
    vt = p.sb([128, NU], F32)
    Dt = p.sb([128, 2, HN], F32)
    cb = p.sb([128, 16], F32)
    C0 = p.sb([128, NT], F32)
    qT = Ring([p.sb([128, HN], BF16) for _ in range(2)])
    qiT = Ring([p.sb([64, HN], BF16) for _ in range(2)])
    wb = Ring([p.sb([128, HN], F32) for _ in range(2)])
    SC = p.sb([128, SMX, NT], F32)
    CM = p.sb([128, SMX, NT], BF16)
    MK = p.sb([128, SMX, NT], BF16)
    kib = Ring([p.sb([128, 128], BF16) for _ in range(3)])
    rb = Ring([p.sb([128, HN], F32) for _ in range(2)])
    ktb = Ring([p.sb([128, 128], BF16) for _ in range(3)])
    vb = Ring([p.sb([128, 128], BF16) for _ in range(3)])
    banks = Ring([p.ps([128, 512], F32) for _ in range(4)])
    ps_cnt = p.ps([128, 512], F32)
    ps_o = p.ps([128, 512], F32)
    ps_d = p.ps([128, 512], F32)
    lo = p.sb([128, NT], F32)
    hi = p.sb([128, NT], F32)
    mid = p.sb([128, NT], F32)
    ge = p.sb([128, NT], F32)
    tmpa = p.sb([128, NT], F32)
    lgt = Ring([p.sb([128, 4 * NT], F32) for _ in range(2)])
    pb = Ring([p.sb([128, 4 * NT], BF16) for _ in range(2)])
    pmb = Ring([p.sb([128, 4 * NT], BF16) for _ in range(2)])
    rdb = p.sb([128, 4 * NT], F32)
    ob = Ring([p.sb([128, 4 * NT], F32) for _ in range(2)])

    t_ones = p.op("dve", lambda e: e.memset(ones[:], 1.0))
    t_vt = p.dma("sp", "vt", vt[:], vt_d.partition_broadcast(128))
    t_D = p.dma("sp", "D", Dt[:], D_d.rearrange("a p n -> p a n"))
    t_cb = p.dma("sp", "cb", cb[:], cb_d.partition_broadcast(128))
    t_C0 = p.dma("sp", "C0", C0[:], C0_d)
    u0 = 0
    sc_free = []
    cnt_free = []
    acc_free = []
    out_toks = []
    for j in range(NJ):
        S = smax[j]
        kq, qt, dq = qT.next()
        t_q = src.load_q(j, qt, dq)
        kqi, qit, dqi = qiT.next()
        t_qi = src.load_qi(j, qit, dqi)
        kw, wt, dw_ = wb.next()
        t_w = src.load_w(j, wt, dw_)
        sc_toks = []
        for dl in range(S):
            kk, kit, dk = kib.next()
            t_ki = src.load_ki(j, dl, kit, dk)
            kr, rt, dr = rb.next()
            t_rs = []
            mm_toks = []
            for m in range(NMM):
                kb_, bk, dbk = banks.next()
                n_ = HPM * NT
                t_mm = p.op("pe", lambda e, bk=bk, kit=kit, qit=qit, m=m, n_=n_: e.matmul(
                    bk[:, 0:n_], lhsT=kit[0:64, :], rhs=qit[:, m * n_:(m + 1) * n_], start=True, stop=True),
                    [t_ki, t_qi] + dbk)
                t_r = p.op("dve", lambda e, bk=bk, rt=rt, wt=wt, m=m, n_=n_: e.scalar_tensor_tensor(
                    out=rt[:, m * n_:(m + 1) * n_], in0=bk[:, 0:n_], scalar=0.0, in1=wt[:, m * n_:(m + 1) * n_],
                    op0=ALU.max, op1=ALU.mult), [t_mm, t_w] + (dr if m == 0 else []))
                banks.done(kb_, t_r)
                t_rs.append(t_r)
                mm_toks.append(t_mm)
            kib.done(kk, *mm_toks)
            t_red = p.op("dve", lambda e, rt=rt, dl=dl: e.tensor_reduce(
                out=SC[:, dl, :], in_=rt[:, :].rearrange("p (h t) -> p t h", h=16), axis=AX.X, op=ALU.add),
                t_rs + (sc_free if dl == 0 else []))
            rb.done(kr, t_red)
            u = u0 + dl
            t_v = p.op("dve", lambda e, dl=dl, u=u: e.tensor_scalar(
                out=SC[:, dl, :], in0=SC[:, dl, :], scalar1=vt[:, u:u + 1], scalar2=None, op0=ALU.add), [t_red, t_vt])
            if dl == 0:
                t_v = p.op("dve", lambda e: e.tensor_tensor(
                    out=SC[:, 0, :], in0=SC[:, 0, :], in1=C0[:, :], op=ALU.add), [t_v, t_C0])
            sc_toks.append(t_v)
        qiT.done(kqi, *mm_toks)
        wb.done(kw, *t_rs)
        t_lo = p.op("dve", lambda e: e.memset(lo[:], BIS_LO), sc_free)
        t_hi = p.op("dve", lambda e: e.memset(hi[:], BIS_HI), sc_free)
        t_prev = [t_lo, t_hi]
        t_cm_free = list(cnt_free)
        for it in range(BIS_IT):
            t_mid = p.op("dve", lambda e: e.tensor_tensor(out=mid[:], in0=lo[:], in1=hi[:], op=ALU.add), t_prev)
            t_mid = p.op("dve", lambda e: e.tensor_scalar(
                out=mid[:], in0=mid[:], scalar1=0.5, scalar2=None, op0=ALU.mult), [t_mid])
            t_cmp = p.op("dve", lambda e, S=S: e.tensor_tensor(
                out=CM[:, 0:S, :], in0=SC[:, 0:S, :], in1=mid[:, :].unsqueeze(1).to_broadcast([128, S, NT]),
                op=ALU.is_ge), [t_mid] + sc_toks + t_cm_free)
            t_c = None
            for dl in range(S):
                t_c = p.op("pe", lambda e, dl=dl, S=S: e.matmul(
                    ps_cnt[:, 0:NT], lhsT=ones[:, :], rhs=CM[:, dl, :], start=(dl == 0), stop=(dl == S - 1)),
                    ([t_cmp, t_ones] + t_prev) if dl == 0 else [], inc=(dl == S - 1))
            t_ge = p.op("dve", lambda e: e.tensor_scalar(
                out=ge[:], in0=ps_cnt[:, 0:NT], scalar1=float(TOPK) - 0.5, scalar2=None, op0=ALU.is_ge), [t_c])
            t_a = p.op("dve", lambda e: e.tensor_tensor(out=tmpa[:], in0=mid[:], in1=lo[:], op=ALU.subtract), [t_ge])
            t_a = p.op("dve", lambda e: e.tensor_tensor(out=tmpa[:], in0=tmpa[:], in1=ge[:], op=ALU.mult), [t_a])
            t_lo2 = p.op("dve", lambda e: e.tensor_tensor(out=lo[:], in0=lo[:], in1=tmpa[:], op=ALU.add), [t_a])
            t_b = p.op("dve", lambda e: e.tensor_tensor(out=tmpa[:], in0=hi[:], in1=mid[:], op=ALU.subtract), [t_lo2])
            t_b = p.op("dve", lambda e: e.tensor_tensor(out=tmpa[:], in0=tmpa[:], in1=ge[:], op=ALU.mult), [t_b])
            t_hi2 = p.op("dve", lambda e: e.tensor_tensor(out=hi[:], in0=mid[:], in1=tmpa[:], op=ALU.add), [t_b])
            t_prev = [t_lo2, t_hi2]
            t_cm_free = [t_c]
        cnt_free = [t_prev[1]]
        t_mk = p.op("dve", lambda e, S=S: e.tensor_tensor(
            out=MK[:, 0:S, :], in0=SC[:, 0:S, :], in1=lo[:, :].unsqueeze(1).to_broadcast([128, S, NT]),
            op=ALU.is_ge), t_prev + sc_free)
        t_pm = None
        for g in range(4):
            t_last_o = None
            t_last_d = None
            for dl in range(S):
                kk, ktt, dkt = ktb.next()
                t_kt = src.load_k(j, g, dl, ktt, dkt)
                kv, vtl, dv = vb.next()
                t_vv = src.load_v(j, g, dl, vtl, dv)
                kb_, bk, dbk = banks.next()
                n4 = 4 * NT
                t_s = p.op("pe", lambda e, bk=bk, ktt=ktt, qt=qt, g=g, n4=n4: e.matmul(
                    bk[:, 0:n4], lhsT=ktt[:, 0:128], rhs=qt[:, g * n4:(g + 1) * n4], start=True, stop=True),
                    [t_kt, t_q] + dbk)
                ktb.done(kk, t_s)
                kp_, pt, dp = pb.next()
                if dl >= 2:
                    t_e = None
                    for r in range(4):
                        h = 4 * g + r
                        t_e = p.op("act", lambda e, pt=pt, bk=bk, r=r, h=h: e.activation(
                            out=pt[:, r * NT:(r + 1) * NT], in_=bk[:, r * NT:(r + 1) * NT], func=AF.Exp,
                            bias=cb[:, h:h + 1], scale=scale), [t_s, t_cb] + (dp if r == 0 else []))
                    banks.done(kb_, t_e)
                else:
                    kl, lt, dlg = lgt.next()
                    t_l = p.op("dve", lambda e, lt=lt, bk=bk, dl=dl, g=g, n4=n4: e.scalar_tensor_tensor(
                        out=lt[:, 0:n4], in0=bk[:, 0:n4], scalar=scale, in1=Dt[:, dl, g * n4:(g + 1) * n4],
                        op0=ALU.mult, op1=ALU.add), [t_s, t_D] + dlg)
                    banks.done(kb_, t_l)
                    t_e = p.op("act", lambda e, pt=pt, lt=lt, n4=n4: e.activation(
                        out=pt[:, 0:n4], in_=lt[:, 0:n4], func=AF.Exp), [t_l] + dp)
                    lgt.done(kl, t_e)
                km, pmt, dpm = pmb.next()
                t_pm = p.op("dve", lambda e, pmt=pmt, pt=pt, dl=dl: e.tensor_tensor(
                    out=pmt[:, :].rearrange("p (r t) -> p r t", r=4),
                    in0=pt[:, :].rearrange("p (r t) -> p r t", r=4),
                    in1=MK[:, dl, :].unsqueeze(1).to_broadcast([128, 4, NT]), op=ALU.mult), [t_e, t_mk] + dpm)
                pb.done(kp_, t_pm)
                first = (dl == 0)
                lastf = (dl == S - 1)
                t_last_o = p.op("pe", lambda e, vtl=vtl, pmt=pmt, first=first, lastf=lastf, n4=n4: e.matmul(
                    ps_o[:, 0:n4], lhsT=vtl[:, 0:128], rhs=pmt[:, 0:n4], start=first, stop=lastf),
                    [t_vv, t_pm] + (acc_free if first else []))
                t_last_d = p.op("pe", lambda e, pmt=pmt, first=first, lastf=lastf, n4=n4: e.matmul(
                    ps_d[:, 0:n4], lhsT=ones[:, :], rhs=pmt[:, 0:n4], start=first, stop=lastf), [t_pm])
                vb.done(kv, t_last_o)
                pmb.done(km, t_last_d)
            t_rd = p.op("dve", lambda e: e.reciprocal(out=rdb[:], in_=ps_d[:, 0:4 * NT]), [t_last_d] + acc_free)
            ko, ot, do = ob.next()
            t_o = p.op("dve", lambda e, ot=ot: e.tensor_tensor(
                out=ot[:], in0=ps_o[:, 0:4 * NT], in1=rdb[:], op=ALU.mult), [t_rd, t_last_o] + do)
            acc_free = [t_o]
            t_st = p.dma("act", f"o{ko}", oT_dst(j, g), ot[:, :].rearrange("p (r t) -> p r t", r=4), [t_o])
            ob.done(ko, t_st)
            out_toks.append(t_st)
        qT.done(kq, t_last_o)
        sc_free = [t_mk, t_pm]
        u0 += S
    return out_toks[-2:]
TWO_PI = 2.0 * math.pi


def _sincos(p, out_t, x_t, tmp_t, deps, cos, ki_t, kf_t):
    off = (math.pi / 2) if cos else 0.0
    V = lambda f, d: p.op("dve", f, d)
    t1 = V(lambda e: e.tensor_scalar(out=tmp_t, in0=x_t, scalar1=off, scalar2=1.0 / TWO_PI, op0=ALU.add, op1=ALU.mult), deps)
    t2 = V(lambda e: e.tensor_copy(out=ki_t, in_=tmp_t), [t1])
    t3 = V(lambda e: e.tensor_copy(out=kf_t, in_=ki_t), [t2])
    t4 = V(lambda e: e.tensor_scalar(out=tmp_t, in0=x_t, scalar1=off, scalar2=None, op0=ALU.add), [t3])
    t5 = V(lambda e: e.scalar_tensor_tensor(out=tmp_t, in0=kf_t, scalar=-TWO_PI, in1=tmp_t, op0=ALU.mult, op1=ALU.add), [t4])
    t6 = V(lambda e: e.tensor_scalar(out=kf_t, in0=tmp_t, scalar1=math.pi, scalar2=-TWO_PI, op0=ALU.is_gt, op1=ALU.mult), [t5])
    t7 = V(lambda e: e.tensor_tensor(out=tmp_t, in0=tmp_t, in1=kf_t, op=ALU.add), [t6])
    t8 = V(lambda e: e.tensor_scalar(out=kf_t, in0=tmp_t, scalar1=-math.pi, scalar2=TWO_PI, op0=ALU.is_lt, op1=ALU.mult), [t7])
    t9 = V(lambda e: e.tensor_tensor(out=tmp_t, in0=tmp_t, in1=kf_t, op=ALU.add), [t8])
    return p.op("act", lambda e: e.activation(out=out_t, in_=tmp_t, func=AF.Sin), [t9])


def emit_scan(p, seqs, I, gU_rows, uidx_d, pubY, HF_d, ident_d):
    V = lambda f, deps=(): p.op("dve", f, list(deps))
    cst = p.sb([128, 4], F32); tvb = p.sb([128, 129], F32); tri = p.sb([128, 128], BF16)
    identf = p.sb([128, 128], F32)
    lmb = p.sb([128, 1024], F32); anb = p.sb([128, 1024], F32); dtb = p.sb([128, 1024], F32)
    lmT = p.sb([128, 16], F32); anT = p.sb([128, 16], F32); dtT = p.sb([128, 16], F32)
    BDA = p.sb([128, 2, 1024], BF16); BDB = p.sb([128, 2, 1024], BF16)
    CcP = p.sb([128, 2, 1024], BF16); CcPf = p.sb([128, 2, 1024], F32)
    Ccc = p.sb([128, 256], F32); dv = p.sb([128, 2], F32)
    Pa = p.sb([128, 16, 128], F32); Pb = p.sb([128, 16, 128], F32)
    Qa = p.sb([128, 16, 129], F32); Qb = p.sb([128, 16, 129], F32)
    Qa16 = p.sb([128, 16, 129], BF16); Qb16 = p.sb([128, 16, 129], BF16)
    q1 = p.sb([128, 16 * 129], F32); q2 = p.sb([128, 16 * 129], F32); q3 = p.sb([128, 16 * 129], F32)
    q4 = p.sb([128, 16 * 129], F32); qi_ = p.sb([128, 16 * 129], I32); kfq = p.sb([128, 16 * 129], F32)
    mask = p.sb([128, 8, 64], F32)
    s1 = q1[:, 0:1024]; s2 = q2[:, 0:1024]; s3 = q3[:, 0:1024]; s4 = q4[:, 0:1024]; si = qi_[:, 0:1024]; sk = kfq[:, 0:1024]
    U = p.sb([128, 2, 4, 1152], F32)
    uidx = p.sb([128, 8], I32)

    d_c = p.dma("sp", "c0", cst[:], I["cst"])
    d_tv = p.dma("sp", "c1", tvb[:], I["tvec"].partition_broadcast(128))
    d_tri = p.dma("pool", "c2", tri[:], I["tri"])
    d_id = p.dma("sp", "c3", identf[:], ident_d)
    d_lm = p.dma("sp", "c4", lmb[:], I["lre_row"].partition_broadcast(128))
    d_an = p.dma("sp", "c5", anb[:], I["lim_row"].partition_broadcast(128))
    d_dt = p.dma("sp", "c6", dtb[:], I["ldt_row"].partition_broadcast(128))
    d_lmT = p.dma("sp", "c7", lmT[:], I["lreT2"])
    d_anT = p.dma("sp", "c8", anT[:], I["limT2"])
    d_dtT = p.dma("sp", "c9", dtT[:], I["ldtT2"])
    d_ccp = p.dma("sp", "c10", CcPf[:], I["CcP"].rearrange("c p n -> p c n"))
    d_ccc = p.dma("sp", "c11", Ccc[:], I["Ccc"])
    d_dv = p.dma("sp", "c12", dv[:], I["dvec"])
    d_mk = p.dma("sp", "c13", mask[:], I["mask"])
    d_ui = p.dma("sp", "c14", uidx[:], uidx_d)
    u_toks = []
    for ck in range(2):
        for r in range(4):
            col = ck * 4 + r
            u_toks.append(p.lane_op("pool", f"ug{col}", lambda e, ck=ck, r=r, col=col: e.indirect_dma_start(
                out=U[:, ck, r, :], out_offset=None, in_=gU_rows,
                in_offset=bass.IndirectOffsetOnAxis(ap=uidx[:, col:col + 1], axis=0)), [d_ui]))
    def disc(lm, an, dt, deps):
        a = p.op("act", lambda e: e.activation(out=dt, in_=dt, func=AF.Exp), deps)
        b = V(lambda e: e.tensor_scalar(out=lm, in0=lm, scalar1=-1e-4, scalar2=None, op0=ALU.min), deps)
        c = V(lambda e: e.tensor_tensor(out=lm, in0=lm, in1=dt, op=ALU.mult), [a, b])
        d = V(lambda e: e.tensor_tensor(out=an, in0=an, in1=dt, op=ALU.mult), [c])
        return d
    t_row = disc(lmb[:], anb[:], dtb[:], [d_lm, d_an, d_dt])
    t_T = disc(lmT[:], anT[:], dtT[:], [d_lmT, d_anT, d_dtT])
    t_ccp = V(lambda e: e.tensor_scalar(out=CcP[:], in0=CcPf[:], scalar1=cst[:, 3:4], scalar2=None, op0=ALU.mult), [d_ccp, d_c])
    t_ccc = V(lambda e: e.tensor_scalar(out=Ccc[:], in0=Ccc[:], scalar1=cst[:, 3:4], scalar2=None, op0=ALU.mult), [d_ccc, d_c])
    G = [p.sb([128, 64], F32) for _ in range(16)]
    gi_i = p.sb([128, 64], I32)
    t_bd = []
    for ck in range(2):
        lre, lim, ldt, bre, bim, mag, cc, ss, ar1, aim, den, cre, cim, t1_, t2_, kf_ = [g[:] for g in G]
        dd = [p.dma("sp", f"g{n_}", t_, I[nm][ck], t_bd) for n_, (t_, nm) in enumerate(
            [(lre, "lre_gi"), (lim, "lim_gi"), (ldt, "ldt_gi"), (bre, "bre_gi"), (bim, "bim_gi")])]
        a = p.op("act", lambda e: e.activation(out=ldt, in_=ldt, func=AF.Exp), dd)
        b = V(lambda e: e.tensor_scalar(out=lre, in0=lre, scalar1=-1e-4, scalar2=None, op0=ALU.min), dd)
        c1 = V(lambda e: e.tensor_tensor(out=t1_, in0=lre, in1=ldt, op=ALU.mult), [a, b])
        c2 = V(lambda e: e.tensor_tensor(out=t2_, in0=lim, in1=ldt, op=ALU.mult), [c1])
        m = p.op("act", lambda e: e.activation(out=mag, in_=t1_, func=AF.Exp), [c1])
        ts = _sincos(p, ss, t2_, den, [c2, m], False, gi_i[:], kf_)
        tc = _sincos(p, cc, t2_, den, [ts], True, gi_i[:], kf_)
        x1 = V(lambda e: e.tensor_tensor(out=ar1, in0=mag, in1=cc, op=ALU.mult), [tc])
        x1 = V(lambda e: e.tensor_scalar(out=ar1, in0=ar1, scalar1=-1.0, scalar2=None, op0=ALU.add), [x1])
        x2 = V(lambda e: e.tensor_tensor(out=aim, in0=mag, in1=ss, op=ALU.mult), [x1])
        y1 = V(lambda e: e.tensor_tensor(out=den, in0=lre, in1=lre, op=ALU.mult), [x2])
        y2 = V(lambda e: e.tensor_tensor(out=t1_, in0=lim, in1=lim, op=ALU.mult), [y1])
        y3 = V(lambda e: e.tensor_tensor(out=den, in0=den, in1=t1_, op=ALU.add), [y2])
        y4 = V(lambda e: e.reciprocal(out=den, in_=den), [y3])
        z1 = V(lambda e: e.tensor_tensor(out=cre, in0=ar1, in1=lre, op=ALU.mult), [y4])
        z2 = V(lambda e: e.tensor_tensor(out=t1_, in0=aim, in1=lim, op=ALU.mult), [z1])
        z3 = V(lambda e: e.tensor_tensor(out=cre, in0=cre, in1=t1_, op=ALU.add), [z2])
        z4 = V(lambda e: e.tensor_tensor(out=cre, in0=cre, in1=den, op=ALU.mult), [z3])
        w1 = V(lambda e: e.tensor_tensor(out=cim, in0=aim, in1=lre, op=ALU.mult), [z4])
        w2 = V(lambda e: e.tensor_tensor(out=t1_, in0=ar1, in1=lim, op=ALU.mult), [w1])
        w3 = V(lambda e: e.tensor_tensor(out=cim, in0=cim, in1=t1_, op=ALU.subtract), [w2])
        w4 = V(lambda e: e.tensor_tensor(out=cim, in0=cim, in1=den, op=ALU.mult), [w3])
        q_1 = V(lambda e: e.tensor_tensor(out=t1_, in0=cre, in1=bre, op=ALU.mult), [w4])
        q_2 = V(lambda e: e.tensor_tensor(out=t2_, in0=cim, in1=bim, op=ALU.mult), [q_1])
        q_3 = V(lambda e: e.tensor_tensor(out=t1_, in0=t1_, in1=t2_, op=ALU.subtract), [q_2])
        q_4 = V(lambda e: e.tensor_tensor(out=t2_, in0=cre, in1=bim, op=ALU.mult), [q_3])
        q_5 = V(lambda e: e.tensor_tensor(out=mag, in0=cim, in1=bre, op=ALU.mult), [q_4])
        q_6 = V(lambda e: e.tensor_tensor(out=t2_, in0=t2_, in1=mag, op=ALU.add), [q_5])
        bcv = lambda t: t.unsqueeze(1).to_broadcast([128, 8, 64])
        bdv = lambda T_, ck=ck: T_[:, ck, :].rearrange("p (g n) -> p g n", n=128)
        r1 = V(lambda e, bdv=bdv: e.tensor_tensor(out=bdv(BDA)[:, :, 0:64], in0=mask[:], in1=bcv(t1_), op=ALU.mult), [q_6, d_mk])
        r2 = V(lambda e, bdv=bdv: e.tensor_tensor(out=bdv(BDA)[:, :, 64:128], in0=mask[:], in1=bcv(t2_), op=ALU.mult), [r1])
        r3 = V(lambda e, bdv=bdv: e.tensor_tensor(out=bdv(BDB)[:, :, 0:64], in0=mask[:], in1=bcv(t2_), op=ALU.mult), [r2])
        r4 = V(lambda e, bdv=bdv: e.tensor_tensor(out=bdv(BDB)[:, :, 64:128], in0=mask[:], in1=bcv(t1_), op=ALU.mult), [r3])
        t_bd = [r4]
    a1 = V(lambda e: e.tensor_scalar(out=s1, in0=lmb[:], scalar1=cst[:, 1:2], scalar2=None, op0=ALU.mult), [t_row, d_c])
    a2 = p.op("act", lambda e: e.activation(out=s1, in_=s1, func=AF.Exp), [a1])
    a3 = V(lambda e: e.tensor_scalar(out=s2, in0=anb[:], scalar1=cst[:, 0:1], scalar2=None, op0=ALU.mult), [t_row, d_c])
    a4 = _sincos(p, s3, s2, s4, [a3], True, si, sk)
    a5 = V(lambda e: e.tensor_tensor(out=s3, in0=s3, in1=s1, op=ALU.mult), [a4, a2])
    s1v = lambda t: t.rearrange("p (g n) -> p g n", n=64)
    a6 = V(lambda e: e.tensor_copy(out=Pa[:, :, 0:64], in_=s1v(s3)), [a5])
    a7 = V(lambda e: e.tensor_copy(out=Pa[:, :, 64:128], in_=s1v(s3)), [a6])
    a8 = _sincos(p, s3, s2, s4, [a7], False, si, sk)
    a9 = V(lambda e: e.tensor_tensor(out=s3, in0=s3, in1=s1, op=ALU.mult), [a8])
    a10 = V(lambda e: e.tensor_copy(out=Pb[:, :, 0:64], in_=s1v(s3)), [a9])
    a11 = V(lambda e: e.tensor_scalar(out=Pb[:, :, 64:128], in0=s1v(s3), scalar1=-1.0, scalar2=None, op0=ALU.mult), [a10])
    qv = lambda t: t[:, :].rearrange("p (g t) -> p g t", t=129)
    b1 = V(lambda e: e.tensor_tensor(out=qv(q1), in0=tvb[:, :].unsqueeze(1).to_broadcast([128, 16, 129]),
                                     in1=lmT[:, :].unsqueeze(2).to_broadcast([128, 16, 129]), op=ALU.mult), [d_tv, t_T, a11])
    b2 = p.op("act", lambda e: e.activation(out=q1[:], in_=q1[:], func=AF.Exp), [b1])
    b3 = V(lambda e: e.tensor_tensor(out=qv(q2), in0=tvb[:, :].unsqueeze(1).to_broadcast([128, 16, 129]),
                                     in1=anT[:, :].unsqueeze(2).to_broadcast([128, 16, 129]), op=ALU.mult), [d_tv, t_T])
    b4 = _sincos(p, q3[:], q2[:], q4[:], [b3], True, qi_[:], kfq[:])
    b5 = V(lambda e: e.tensor_tensor(out=Qa[:, :, :], in0=qv(q3), in1=qv(q1), op=ALU.mult), [b4, b2])
    b6 = _sincos(p, q3[:], q2[:], q4[:], [b5], False, qi_[:], kfq[:])
    b7 = V(lambda e: e.tensor_tensor(out=q3[:], in0=q3[:], in1=q1[:], op=ALU.mult), [b6])
    b8 = V(lambda e: e.tensor_scalar(out=Qb[:, :, :], in0=qv(q3), scalar1=cst[:, 2:3], scalar2=None, op0=ALU.mult), [b7, d_c])
    b9 = V(lambda e: e.tensor_copy(out=Qa16[:], in_=Qa[:]), [b5])
    b10 = V(lambda e: e.tensor_copy(out=Qb16[:], in_=Qb[:]), [b8])
    tabs = [a7, a11, b9, b10, t_ccp, t_ccc, d_tri, d_dv, d_id] + t_bd + u_toks

    ub = Ring([p.sb([128, 128], BF16) for _ in range(3)])
    banks = Ring([p.ps([128, 512], F32) for _ in range(6)])
    ps_y = p.ps([128, 512], F32)
    ps_t = p.ps([128, 512], F32)
    t1b = Ring([p.sb([128, 512], F32) for _ in range(2)]); t2b = Ring([p.sb([128, 512], F32) for _ in range(2)])
    Vb = Ring([p.sb([128, 512], BF16) for _ in range(2)])
    x1b = Ring([p.sb([128, 4, 128], F32) for _ in range(2)]); x2b = Ring([p.sb([128, 4, 128], F32) for _ in range(2)])
    Xb = Ring([p.sb([128, 8, 128], BF16) for _ in range(2)])
    PadA = Ring([p.sb([128, 8, 128], BF16) for _ in range(2)]); PadB = Ring([p.sb([128, 8, 128], BF16) for _ in range(2)])
    yo = Ring([p.sb([128, 128], F32) for _ in range(3)])
    yr = Ring([p.sb([128, 128], F32) for _ in range(3)])
    EA = p.sb([128, 16], F32); EB = p.sb([128, 16], F32); e1t = p.sb([128, 16], F32)
    HA = [p.sb([128, 16], F32) for _ in range(2)]; HB = [p.sb([128, 16], F32) for _ in range(2)]
    n1 = p.sb([128, 16], F32); n2 = p.sb([128, 16], F32)
    t_z = [V(lambda e, pad=pad: e.memset(pad[:], 0.0)) for pad in PadA.bufs + PadB.bufs]
    hcur = 0
    t_Hread = []; t_E_read = []; outs = []; y_free = []; tr_free = []
    for si_, (blocks, init) in enumerate(seqs):
        if init is not None:
            ta = p.dma("sp", "h0a", HA[hcur][:], I["H0A"][init], t_Hread)
            tb_ = p.dma("sp", "h0b", HB[hcur][:], I["H0B"][init], t_Hread)
        else:
            ta = V(lambda e, h=HA[hcur]: e.memset(h[:], 0.0), t_Hread)
            tb_ = V(lambda e, h=HB[hcur]: e.memset(h[:], 0.0), t_Hread)
        t_H = [ta, tb_]
        for (rk, col0, L, yrow0) in blocks:
            e_toks = []; pad_readers = []
            for ck in range(2):
                uft = U[:, ck, rk, col0:col0 + L]
                kb_, ubt, dub = ub.next()
                t_ub = p.op("act", lambda e, ubt=ubt, uft=uft: e.copy(out=ubt[:, 0:L], in_=uft), tabs + dub)
                kx, Xt, dX = Xb.next()
                x_toks = []
                for hc in range(2):
                    gg0 = ck * 8 + hc * 4
                    rows = slice(hc * 64, (hc + 1) * 64)
                    cols = slice(hc * 512, (hc + 1) * 512)
                    kA, bA, dA = banks.next()
                    mA = p.op("pe", lambda e, bA=bA, ubt=ubt, rows=rows, cols=cols, ck=ck: e.matmul(
                        bA[0:L, :], lhsT=ubt[rows, 0:L], rhs=BDA[rows, ck, cols], start=True, stop=True), [t_ub] + dA)
                    kB, bB, dB = banks.next()
                    mB = p.op("pe", lambda e, bB=bB, ubt=ubt, rows=rows, cols=cols, ck=ck: e.matmul(
                        bB[0:L, :], lhsT=ubt[rows, 0:L], rhs=BDB[rows, ck, cols], start=True, stop=True), [t_ub] + dB)
                    k1, t1t, d1_ = t1b.next(); k2, t2t, d2_ = t2b.next(); kv, Vt, dV = Vb.next()
                    pv = lambda t, gg0=gg0: t[0:L, gg0:gg0 + 4, :]
                    v3 = lambda t: t[0:L, :].rearrange("p (g n) -> p g n", n=128)
                    o1 = V(lambda e, t1t=t1t, bA=bA, pv=pv, v3=v3: e.tensor_tensor(out=v3(t1t), in0=v3(bA), in1=pv(Pa), op=ALU.mult), [mA] + d1_)
                    o2 = V(lambda e, t2t=t2t, bB=bB, pv=pv, v3=v3: e.tensor_tensor(out=v3(t2t), in0=v3(bB), in1=pv(Pb), op=ALU.mult), [mB] + d2_)
                    banks.done(kA, o1); banks.done(kB, o2)
                    o3 = V(lambda e, Vt=Vt, t1t=t1t, t2t=t2t: e.tensor_tensor(out=Vt[0:L, :], in0=t1t[0:L, :], in1=t2t[0:L, :], op=ALU.add), [o1, o2] + dV)
                    t1b.done(k1, o3); t2b.done(k2, o3)
                    kcA, cA, dcA = banks.next(); kcB, cB, dcB = banks.next()
                    mc = None
                    for g in range(4):
                        mc = p.op("pe", lambda e, cA=cA, Vt=Vt, g=g: e.matmul(
                            cA[:, g * 128:g * 128 + L], lhsT=Vt[0:L, g * 128:(g + 1) * 128], rhs=tri[0:L, 0:L],
                            start=True, stop=True), ([o3] + dcA + dcB) if g == 0 else [], inc=False)
                        mc = p.op("pe", lambda e, cB=cB, Vt=Vt, g=g: e.matmul(
                            cB[0:64, g * 128:g * 128 + L], lhsT=Vt[0:L, g * 128 + 64:(g + 1) * 128], rhs=tri[0:L, 0:L],
                            start=True, stop=True), [], inc=False)
                        mc = p.op("pe", lambda e, cB=cB, Vt=Vt, g=g: e.matmul(
                            cB[64:128, g * 128:g * 128 + L], lhsT=Vt[0:L, g * 128:g * 128 + 64], rhs=tri[0:L, 0:L],
                            start=True, stop=True), [], inc=(g == 3))
                    Vb.done(kv, mc)
                    c3 = lambda t: t[:, :].rearrange("p (g n) -> p g n", n=128)[:, :, 0:L]
                    qv_ = lambda t, gg0=gg0: t[:, gg0:gg0 + 4, 0:L]
                    kx1, x1t, dx1 = x1b.next(); kx2, x2t, dx2 = x2b.next()
                    r1 = V(lambda e, x1t=x1t, cA=cA, c3=c3, qv_=qv_: e.tensor_tensor(out=x1t[:, :, 0:L], in0=c3(cA), in1=qv_(Qa), op=ALU.mult), [mc] + dx1)
                    r2 = V(lambda e, x2t=x2t, cB=cB, c3=c3, qv_=qv_: e.tensor_tensor(out=x2t[:, :, 0:L], in0=c3(cB), in1=qv_(Qb), op=ALU.mult), [mc] + dx2)
                    r3 = V(lambda e, Xt=Xt, x1t=x1t, x2t=x2t, hc=hc: e.tensor_tensor(
                        out=Xt[:, hc * 4:(hc + 1) * 4, 0:L], in0=x1t[:, :, 0:L], in1=x2t[:, :, 0:L], op=ALU.add), [r1, r2] + (dX if hc == 0 else []))
                    r4 = V(lambda e, x1t=x1t, x2t=x2t, gg0=gg0: e.tensor_tensor(
                        out=EA[:, gg0:gg0 + 4], in0=x1t[:, :, L - 1], in1=x2t[:, :, L - 1], op=ALU.add), [r1, r2] + t_E_read)
                    r5 = V(lambda e, cB=cB, gg0=gg0: e.tensor_tensor(
                        out=e1t[:, gg0:gg0 + 4], in0=cB[:, :].rearrange("p (g n) -> p g n", n=128)[:, :, L - 1],
                        in1=Qa[:, gg0:gg0 + 4, L - 1], op=ALU.mult), [mc] + t_E_read)
                    r6 = V(lambda e, cA=cA, gg0=gg0: e.tensor_tensor(
                        out=EB[:, gg0:gg0 + 4], in0=cA[:, :].rearrange("p (g n) -> p g n", n=128)[:, :, L - 1],
                        in1=Qb[:, gg0:gg0 + 4, L - 1], op=ALU.mult), [mc] + t_E_read)
                    r7 = V(lambda e, gg0=gg0: e.tensor_tensor(
                        out=EB[:, gg0:gg0 + 4], in0=e1t[:, gg0:gg0 + 4], in1=EB[:, gg0:gg0 + 4], op=ALU.subtract), [r5, r6])
                    banks.done(kcA, r1, r6); banks.done(kcB, r2, r5)
                    x1b.done(kx1, r3, r4); x2b.done(kx2, r3, r4)
                    x_toks.append(r3)
                    e_toks += [r4, r7]
                ub.done(kb_, mB)
                kpa, pa, dpa = PadA.next(); kpb, pbt, dpb = PadB.next()
                tp = None
                for g in range(8):
                    Gx = ck * 8 + g
                    tp = V(lambda e, pa=pa, g=g, Gx=Gx, h=HA[hcur]: e.tensor_scalar(
                        out=pa[:, g, g * 16:(g + 1) * 16], in0=Ccc[:, Gx * 16:(Gx + 1) * 16], scalar1=h[:, Gx:Gx + 1],
                        scalar2=None, op0=ALU.mult), (t_H + [t_ccc] + dpa + t_z) if g == 0 else [])
                    tp = V(lambda e, pbt=pbt, g=g, Gx=Gx, h=HB[hcur]: e.tensor_scalar(
                        out=pbt[:, g, g * 16:(g + 1) * 16], in0=Ccc[:, Gx * 16:(Gx + 1) * 16], scalar1=h[:, Gx:Gx + 1],
                        scalar2=None, op0=ALU.mult), dpb if g == 0 else [])
                my = None
                for g in range(8):
                    Gx = ck * 8 + g
                    my = p.op("pe", lambda e, Xt=Xt, g=g, ck=ck: e.matmul(
                        ps_y[:, 0:L], lhsT=CcP[:, ck, g * 128:(g + 1) * 128], rhs=Xt[:, g, 0:L], start=(g == 0), stop=False),
                        (x_toks + [tp] + y_free) if g == 0 else [], inc=False)
                    my = p.op("pe", lambda e, pa=pa, g=g, Gx=Gx: e.matmul(
                        ps_y[:, 0:L], lhsT=pa[:, g, :], rhs=Qa16[:, Gx, 1:L + 1], start=False, stop=False), [], inc=False)
                    my = p.op("pe", lambda e, pbt=pbt, g=g, Gx=Gx: e.matmul(
                        ps_y[:, 0:L], lhsT=pbt[:, g, :], rhs=Qb16[:, Gx, 1:L + 1], start=False, stop=(g == 7)), [], inc=(g == 7))
                Xb.done(kx, my); PadA.done(kpa, my); PadB.done(kpb, my)
                pad_readers.append(tp)
                ko, yot, dyo = yo.next()
                ty = V(lambda e, yot=yot, uft=uft, ck=ck: e.scalar_tensor_tensor(
                    out=yot[:, 0:L], in0=uft, scalar=dv[:, ck:ck + 1], in1=ps_y[:, 0:L], op0=ALU.mult, op1=ALU.add), [my] + dyo)
                y_free = [ty]
                ttr = p.op("pe", lambda e, yot=yot: e.transpose(out=ps_t[0:L, 0:128], in_=yot[:, 0:L], identity=identf[:]),
                           [ty] + tr_free)
                kyr, yrt, dyr = yr.next()
                tcp = p.op("act", lambda e, yrt=yrt: e.copy(out=yrt[0:L, :], in_=ps_t[0:L, 0:128]), [ttr] + dyr)
                tr_free = [tcp]
                yo.done(ko, ttr)
                tst = p.dma("act", f"y{kyr}", pubY(yrow0, ck, L), yrt[0:L, :], [tcp])
                yr.done(kyr, tst)
                outs.append(tst)
            hn = 1 - hcur
            QaL = Qa[:, :, L]; QbL = Qb[:, :, L]
            dep0 = t_H + e_toks + pad_readers + t_Hread
            u1 = V(lambda e, h=HA[hcur], QaL=QaL: e.tensor_tensor(out=n1[:], in0=h[:], in1=QaL, op=ALU.mult), dep0)
            u2 = V(lambda e, h=HB[hcur], QbL=QbL: e.tensor_tensor(out=n2[:], in0=h[:], in1=QbL, op=ALU.mult), [u1])
            u3 = V(lambda e: e.tensor_tensor(out=n1[:], in0=n1[:], in1=n2[:], op=ALU.add), [u2])
            u4 = V(lambda e, h=HA[hn]: e.tensor_tensor(out=h[:], in0=n1[:], in1=EA[:], op=ALU.add), [u3])
            u5 = V(lambda e, h=HB[hcur], QaL=QaL: e.tensor_tensor(out=n1[:], in0=h[:], in1=QaL, op=ALU.mult), [u4])
            u6 = V(lambda e, h=HA[hcur], QbL=QbL: e.tensor_tensor(out=n2[:], in0=h[:], in1=QbL, op=ALU.mult), [u5])
            u7 = V(lambda e: e.tensor_tensor(out=n1[:], in0=n1[:], in1=n2[:], op=ALU.subtract), [u6])
            u8 = V(lambda e, h=HB[hn]: e.tensor_tensor(out=h[:], in0=n1[:], in1=EB[:], op=ALU.add), [u7])
            t_E_read = [u8]; t_Hread = [u8]; t_H = [u4, u8]
            hcur = hn
        tf = p.dma("sp", "hf", HF_d[si_], HA[hcur][:], t_H)
        outs.append(tf)
        t_Hread = t_Hread + [tf]
    return outs[-8:]
R_ = 1152
RT_ = 9


def _zz_block(k, j):
    m = j // 2
    return 8 * m + k if j % 2 == 0 else 8 * m + 7 - k


def _zz_owner(S):
    m, x = S // 8, S % 8
    return (x, 2 * m) if x <= 3 else (7 - x, 2 * m + 1)


P_SMAX = [8 * (j // 2) + 4 if j % 2 == 0 else 8 * (j // 2) + 8 for j in range(8)]


def emit_rmsout(p, x, nw, y, eps=1e-6):
    D = 2048
    nwb = p.sb([128, D], F32)
    t_nw = p.dma("sp", "nw", nwb[:], nw.partition_broadcast(128))
    xb = Ring([p.sb([128, D], F32) for _ in range(2)])
    tf = Ring([p.sb([128, D], F32) for _ in range(2)])
    st = Ring([p.sb([128, 2], F32) for _ in range(2)])
    fin = []
    for r in range(RT_):
        rows = slice(r * 128, (r + 1) * 128)
        kx, xt, dx = xb.next()
        t_x = p.dma("sp", f"x{kx}", xt[:], x[rows, :], dx)
        kt, tt, dt_ = tf.next()
        ks, s_, ds = st.next()
        t_sq = p.op("act", lambda e, tt=tt, xt=xt, s_=s_: e.activation(out=tt[:], in_=xt[:], func=AF.Square, accum_out=s_[:, 0:1]), [t_x] + dt_ + ds)
        t1 = p.op("dve", lambda e, s_=s_: e.tensor_scalar(out=s_[:, 1:2], in0=s_[:, 0:1], scalar1=1.0 / D, scalar2=eps, op0=ALU.mult, op1=ALU.add), [t_sq])
        t2 = p.op("act", lambda e, s_=s_: e.activation(out=s_[:, 1:2], in_=s_[:, 1:2], func=AF.Sqrt), [t1])
        t3 = p.op("dve", lambda e, s_=s_: e.reciprocal(out=s_[:, 1:2], in_=s_[:, 1:2]), [t2])
        t4 = p.op("dve", lambda e, tt=tt, xt=xt, s_=s_: e.scalar_tensor_tensor(out=tt[:], in0=xt[:], scalar=s_[:, 1:2], in1=nwb[:], op0=ALU.mult, op1=ALU.mult), [t3, t_nw])
        xb.done(kx, t4); st.done(ks, t4)
        t5 = p.dma("act", f"o{kt}", y[rows, :], tt[:], [t4])
        tf.done(kt, t5)
        fin.append(t5)
    return fin[-2:]


def emit_gather(p, ck_d, cv_d, ci_d, pt_d, Kp, Vp, ip):
    pt = p.sb([128, 1], I32)
    idx = p.sb([128, 8], I32)
    bufs = Ring([p.sb([128, 8192], F32) for _ in range(3)])
    t_pt = p.dma("sp", "pt", pt[:], pt_d)
    t_i = None
    for e_ in range(8):
        t_i = p.op("dve", lambda e, e_=e_: e.tensor_scalar(
            out=idx[:, e_:e_ + 1], in0=pt[:], scalar1=8, scalar2=e_, op0=ALU.mult, op1=ALU.add), [t_pt])
    outs = []
    jobs = [(ci_d, pt, 0, ip)] + [(ck_d, idx, e_, Kp[:, e_, :]) for e_ in range(8)] + \
           [(cv_d, idx, e_, Vp[:, e_, :]) for e_ in range(8)]
    for n, (src, it, col, dst) in enumerate(jobs):
        kb, bt, db = bufs.next()
        tok = p.lane_op("pool", f"g{kb}", lambda e, bt=bt, src=src, it=it, col=col: e.indirect_dma_start(
            out=bt[:], out_offset=None, in_=src, in_offset=bass.IndirectOffsetOnAxis(ap=it[:, col:col + 1], axis=0)),
            [t_i, t_pt] + db)
        t_o = p.dma("sp", f"go{kb}", dst, bt[:], [tok])
        bufs.done(kb, t_o)
        outs.append(t_o)
    return outs[-3:]


class PromptSrc:
    def __init__(self, p, qT_s, qiT_s, wT_s, gKT, gV, gKi, idxK_d, idxI_d):
        self.p = p
        self.qT_s, self.qiT_s, self.wT_s, self.gKT, self.gV, self.gKi = qT_s, qiT_s, wT_s, gKT, gV, gKi
        self.idxK = p.sb([128, 4 * 144], I32)
        self.idxI = p.sb([128, 144], I32)
        self.t_ik = p.dma("sp", "ik", self.idxK[:], idxK_d)
        self.t_ii = p.dma("sp", "ii", self.idxI[:], idxI_d)
        self.u0 = [sum(P_SMAX[:j]) for j in range(8)]
        self.n = 0

    def load_q(self, j, t, deps):
        return self.p.dma("pool", "lq", t[:, :].rearrange("p (h t) -> p h t", h=16),
                          self.qT_s[:, j * 128:(j + 1) * 128].rearrange("(h d) t -> d h t", d=128), deps)

    def load_qi(self, j, t, deps):
        return self.p.dma("pool", "lqi", t[:, :].rearrange("p (h t) -> p h t", h=16),
                          self.qiT_s[:, j * 128:(j + 1) * 128].rearrange("(h d) t -> d h t", d=64), deps)

    def load_w(self, j, t, deps):
        return self.p.dma("sp", "lw", t[:, :].rearrange("p (h t) -> p h t", h=16),
                          self.wT_s[:, j * 128:(j + 1) * 128].partition_broadcast(128), deps)

    def _ind(self, t, table, idx_ap, deps, dep2):
        self.n += 1
        return self.p.lane_op("pool", f"in{self.n % 6}", lambda e: e.indirect_dma_start(
            out=t[:, :], out_offset=None, in_=table, in_offset=bass.IndirectOffsetOnAxis(ap=idx_ap, axis=0)),
            list(deps) + [dep2])

    def load_ki(self, j, dl, t, deps):
        u = self.u0[j] + dl
        return self._ind(t, self.gKi, self.idxI[:, u:u + 1], deps, self.t_ii)

    def load_k(self, j, g, dl, t, deps):
        u = self.u0[j] + dl
        return self._ind(t, self.gKT, self.idxK[:, g * 144 + u:g * 144 + u + 1], deps, self.t_ik)

    def load_v(self, j, g, dl, t, deps):
        u = self.u0[j] + dl
        return self._ind(t, self.gV, self.idxK[:, g * 144 + u:g * 144 + u + 1], deps, self.t_ik)


class SampleSrc:
    def __init__(self, p, qT_s, qiT_s, wT_s, kT_s, kiT_s, z, KTs, kiTs, Vp):
        self.p = p
        self.qT_s, self.qiT_s, self.wT_s, self.kT_s, self.kiT_s, self.z = qT_s, qiT_s, wT_s, kT_s, kiT_s, z
        self.KTs, self.kiTs, self.Vp = KTs, kiTs, Vp
        self.c = slice(1024, 1028)

    def load_q(self, j, t, deps):
        return self.p.dma("pool", "lq", t[:, :].rearrange("p (h t) -> p h t", h=16),
                          self.qT_s[:, self.c].rearrange("(h d) t -> d h t", d=128), deps)

    def load_qi(self, j, t, deps):
        return self.p.dma("pool", "lqi", t[:, :].rearrange("p (h t) -> p h t", h=16),
                          self.qiT_s[:, self.c].rearrange("(h d) t -> d h t", d=64), deps)

    def load_w(self, j, t, deps):
        return self.p.dma("sp", "lw", t[:, :].rearrange("p (h t) -> p h t", h=16),
                          self.wT_s[:, self.c].partition_broadcast(128), deps)

    def _new(self, t, dst, src, deps):
        z_ = self.p.op("dve", lambda e: e.memset(t[:, :], 0.0), deps)
        return self.p.dma("pool", "ln", dst, src, [z_])

    def load_ki(self, j, dl, t, deps):
        if dl == 0:
            return self._new(t, t[0:64, 0:4], self.kiT_s[0:64, self.c], deps)
        pg = 128 - dl
        return self.p.dma("pool", "lki", t[0:64, :], self.kiTs[:, pg * 128:(pg + 1) * 128], deps)

    def load_k(self, j, g, dl, t, deps):
        if dl == 0:
            return self._new(t, t[:, 0:4], self.kT_s[g * 128:(g + 1) * 128, self.c], deps)
        pg = 128 - dl
        return self.p.dma("pool", "lk", t[:, :], self.KTs[g * 128:(g + 1) * 128, pg * 128:(pg + 1) * 128], deps)

    def load_v(self, j, g, dl, t, deps):
        if dl == 0:
            return self._new(t, t[0:4, :], self.z[1024:1028, 2560 + g * 128:2560 + (g + 1) * 128], deps)
        pg = 128 - dl
        return self.p.dma("pool", "lv", t[:, :], self.Vp[pg * 128:(pg + 1) * 128, g * 128:(g + 1) * 128], deps)


def build_fused():
    nc = bass.Bass("TRN2", target_bir_lowering=False)
    EI = lambda n, s, dt=F32: nc.dram_tensor(n, list(s), dt, kind="ExternalInput").ap()
    EO = lambda n, s, dt=F32: nc.dram_tensor(n, list(s), dt, kind="ExternalOutput").ap()
    IT = lambda n, s, dt=F32: nc.dram_tensor(n, list(s), dt)
    xrows = EI("xrows", [R_, 2048]); prows = EI("prows", [4, R_, 256])
    norm_w = EI("norm_w", [4, 1, 2048]); ple_nw = EI("ple_nw", [4, 1, 2048]); fnw = EI("fnw", [1, 2048])
    a_win = EI("a_win", [2, 2048, 6224]); a_wout = EI("a_wout", [2, 2048, 2048])
    s_win = EI("s_win", [2, 2048, 4096]); s_wglu = EI("s_wglu", [2, 2048, 2048]); s_bglu = EI("s_bglu", [2, 1, 2048])
    s_wout = EI("s_wout", [2, 2048, 2048]); p_wg = EI("p_wg", [4, 2048, 2048]); p_wp = EI("p_wp", [4, 256, 2048])
    ident = EI("ident", [128, 128])
    ck = [EI(f"ck{l}", [10240, 8192]) for l in range(2)]; cv = [EI(f"cv{l}", [10240, 8192]) for l in range(2)]
    ci = [EI(f"ci{l}", [1280, 8192]) for l in range(2)]
    pt = EI("pt", [128, 1], I32)
    vtp = EI("vtp", [1, 144]); Dp = EI("Dp", [2, 128, 2048]); C0p = EI("C0p", [128, 128])
    vts = EI("vts", [1, 129]); Ds = EI("Ds", [2, 128, 64]); C0s = EI("C0s", [128, 4]); cbv = EI("cbv", [1, 16])
    idxK = EI("idxK", [128, 576], I32); idxI = EI("idxI", [128, 144], I32)
    SP = {}
    for nm, shp in [("lre_row", [1, 1024]), ("lim_row", [1, 1024]), ("ldt_row", [1, 1024]),
                    ("lreT2", [128, 16]), ("limT2", [128, 16]), ("ldtT2", [128, 16]),
                    ("lre_gi", [2, 128, 64]), ("lim_gi", [2, 128, 64]), ("ldt_gi", [2, 128, 64]),
                    ("bre_gi", [2, 128, 64]), ("bim_gi", [2, 128, 64]),
                    ("CcP", [2, 128, 1024]), ("Ccc", [128, 256]), ("dvec", [128, 2]),
                    ("H0A", [4, 128, 16]), ("H0B", [4, 128, 16])]:
        SP[nm] = EI("sp_" + nm, [2, 2] + shp)
    tri = EI("tri", [128, 128]); cst = EI("cst", [128, 4]); tvec = EI("tvec", [1, 129]); mask = EI("mask", [128, 8, 64])
    uidx = EI("uidx", [2, 128, 8], I32); yidx = EI("yidx", [128, 36], I32)
    yout = EO("yout", [R_, 2048]); kvk = EO("kvk", [2, R_, 1088])
    HFp = EO("HFp", [2, 2, 1, 128, 16]); HFs = EO("HFs", [2, 2, 4, 128, 16])

    hA = IT("hA", [R_, 2048]).ap(); hB = IT("hB", [R_, 2048]).ap()
    z = IT("z", [R_, 6224]).ap()
    qT_s = IT("qT_s", [2048, R_]).ap(); kT_s = IT("kT_s", [512, R_]).ap(); qiT_s = IT("qiT_s", [1024, R_]).ap()
    kiT_s = IT("kiT_s", [128, R_]).ap(); wT_s = IT("wT_s", [16, R_]).ap()
    pubKT = IT("pubKT", [4 * 8 * 128, 128]); pubV = IT("pubV", [4 * 8 * 128, 128]); pubKi = IT("pubKi", [8 * 128, 128])
    gKT = IT("gKT", [4 * 4 * 8 * 128, 128]); gV = IT("gV", [4 * 4 * 8 * 128, 128]); gKi = IT("gKi", [4 * 8 * 128, 128])
    oT_s = IT("oT_s", [2048, R_]).ap(); o_s = IT("o_s", [R_, 2048]).ap(); g_s = IT("g_s", [R_, 2048]).ap()
    Kp = IT("Kp", [16384, 512]).ap(); Vp = IT("Vp", [16384, 512]).ap(); ip = IT("ip", [16384, 64]).ap()
    KTs = IT("KTs", [512, 16384]).ap(); kiTs = IT("kiTs", [64, 16384]).ap()
    uT_s = IT("uT_s", [2048, R_]); gU = IT("gU", [4 * 2048, R_])
    pubY = IT("pubY", [4608, 512]); gY = IT("gY", [4 * 4608, 512])
    y_s = IT("y_s", [R_, 2048]).ap(); y3 = IT("y3", [R_, 2048]).ap()
    RG = [[0, 1, 2, 3], [4, 5, 6, 7]]

    def allgather(p, src, dst, deps, CR=None):
        rows = src.ap().shape[0]
        CR = CR or rows
        prev = list(deps)
        for c_ in range(rows // CR):
            cc = p.lane_op("pool", "cc", lambda e, c_=c_: e.collective_compute(
                "AllGather", ALU.bypass, replica_groups=RG, ins=[src.ap()[c_ * CR:(c_ + 1) * CR, :].opt()],
                outs=[dst.ap()[c_ * 4 * CR:(c_ + 1) * 4 * CR, :].opt()]), prev, incv=1)
            prev = [cc]
        return prev[0]

    with ExitStack() as es:
        p = Prog(nc, es)
        h, h1 = None, None
        cur = xrows
        bufs = [hA, hB]
        for i in range(4):
            l = i // 2
            hn1 = bufs[0] if cur is not bufs[0] else bufs[1]
            if i % 2 == 0:
                with p.stage():
                    emit_linear(p, RT_, 2048, 6224, "rms", "none", cur, a_win[l], z, ident, nw=norm_w[i])
                for (src, dst, C) in [(z[:, 0:2048], qT_s, 2048), (z[:, 2048:2560], kT_s, 512),
                                      (z[:, 5120:6144], qiT_s, 1024), (z[:, 6144:6208], kiT_s, 64),
                                      (z[:, 6208:6224], wT_s, 16)]:
                    with p.stage():
                        emit_transpose(p, src, dst, R_, C, ident)
                with p.stage():
                    d0 = p.dma("sp", "k0", kvk[l][:, 0:1024], z[:, 2048:3072])
                    d1 = p.dma("sp", "k1", kvk[l][:, 1024:1088], z[:, 6144:6208])
                    pk = pubKT.ap().rearrange("(g j d) s -> g j d s", g=4, j=8)
                    d2 = [p.dma("act", f"k2{g}", pk[g], kT_s[g * 128:(g + 1) * 128, 0:1024].rearrange("d (j s) -> j d s", s=128))
                          for g in range(4)]
                    pv = pubV.ap().rearrange("(g j s) d -> g j s d", g=4, j=8)
                    d3 = [p.dma("act", f"k3{g}", pv[g], z[0:1024, 2560 + g * 128:2560 + (g + 1) * 128].rearrange("(j s) d -> j s d", s=128))
                          for g in range(4)]
                    d4 = p.dma("sp", "k4", pubKi.ap().rearrange("(j d) s -> j d s", j=8),
                               kiT_s[:, 0:1024].rearrange("d (j s) -> j d s", s=128))
                    c1 = allgather(p, pubKT, gKT, d2, 2048)
                    c2_ = allgather(p, pubV, gV, d3 + [c1], 2048)
                    c3 = allgather(p, pubKi, gKi, [d4, c2_])
                    p.op("sp", None, [d0, d1, c3], inc=False)
                with p.stage():
                    src = PromptSrc(p, qT_s, qiT_s, wT_s, gKT.ap(), gV.ap(), gKi.ap(), idxK, idxI)
                    emit_attn(p, 8, 128, P_SMAX, src, vtp, Dp, cbv, C0p,
                              lambda j, g: oT_s[g * 512:(g + 1) * 512, j * 128:(j + 1) * 128].rearrange("(r d) t -> d r t", d=128))
                with p.stage():
                    emit_gather(p, ck[l], cv[l], ci[l], pt, Kp.rearrange("(pg e s) c -> pg e (s c)", pg=128, e=8),
                                Vp.rearrange("(pg e s) c -> pg e (s c)", pg=128, e=8), ip.rearrange("(pg s) c -> pg (s c)", pg=128))
                with p.stage():
                    emit_transpose(p, Kp, KTs, 16384, 512, ident)
                with p.stage():
                    emit_transpose(p, ip, kiTs, 16384, 64, ident)
                with p.stage():
                    src = SampleSrc(p, qT_s, qiT_s, wT_s, kT_s, kiT_s, z, KTs, kiTs, Vp)
                    emit_attn(p, 1, 4, [129], src, vts, Ds, cbv, C0s,
                              lambda j, g: oT_s[g * 512:(g + 1) * 512, 1024:1028].rearrange("(r d) t -> d r t", d=128))
                with p.stage():
                    emit_transpose(p, oT_s, o_s, 2048, R_, ident)
                with p.stage():
                    emit_linear(p, RT_, 2048, 2048, "gate", "res", o_s, a_wout[l], hn1, ident, x2=z[:, 3072:5120], e1=cur)
            else:
                with p.stage():
                    emit_linear(p, RT_, 2048, 4096, "rms", "none", cur, s_win[l], z[:, 0:4096], ident, nw=norm_w[i])
                with p.stage():
                    emit_transpose(p, z[:, 0:2048], uT_s.ap(), R_, 2048, ident)
                with p.stage():
                    c1 = allgather(p, uT_s, gU, [], 128)
                    p.op("sp", None, [c1], inc=False)
                for c2 in range(2):
                    I = {k_: v_[l, c2] for k_, v_ in SP.items()}
                    I.update({"tri": tri, "cst": cst, "tvec": tvec, "mask": mask})
                    pY = lambda yrow0, ck_, L, c2=c2: pubY.ap()[yrow0:yrow0 + L, c2 * 256 + ck_ * 128:c2 * 256 + (ck_ + 1) * 128]
                    pblocks = []
                    for S in range(32):
                        r, jj = _zz_owner(S)
                        pblocks.append((r, jj * 128, 128, S * 128))
                    with p.stage():
                        emit_scan(p, [(pblocks, None)], I, gU.ap(), uidx[c2], pY, HFp[l, c2], ident)
                    with p.stage():
                        emit_scan(p, [([(i_, 1024, 4, 4096 + 128 * i_)], i_) for i_ in range(4)], I, gU.ap(), uidx[c2], pY,
                                  HFs[l, c2], ident)
                with p.stage():
                    c1 = allgather(p, pubY, gY, [], 512)
                    yix = p.sb([128, 36], I32)
                    t_yi = p.dma("sp", "yi", yix[:], yidx)
                    yb = Ring([p.sb([128, 512], F32) for _ in range(4)])
                    fin = []
                    for jj in range(9):
                        for r in range(4):
                            kb, bt, db = yb.next()
                            col = jj * 4 + r
                            tg = p.lane_op("pool", f"yg{kb}", lambda e, bt=bt, col=col: e.indirect_dma_start(
                                out=bt[:], out_offset=None, in_=gY.ap(),
                                in_offset=bass.IndirectOffsetOnAxis(ap=yix[:, col:col + 1], axis=0)), [c1, t_yi] + db)
                            to = p.dma("sp", f"yo{kb}", y_s[jj * 128:(jj + 1) * 128, r * 512:(r + 1) * 512], bt[:], [tg])
                            yb.done(kb, to)
                            fin.append(to)
                    p.op("sp", None, fin[-4:], inc=False)
                with p.stage():
                    emit_linear(p, RT_, 2048, 2048, "gelu", "glu", y_s, s_wglu[l], y3, ident, e1=y_s, bv=s_bglu[l])
                with p.stage():
                    emit_linear(p, RT_, 2048, 2048, "gate", "res", y3, s_wout[l], hn1, ident, x2=z[:, 2048:4096], e1=cur)
            with p.stage():
                emit_linear(p, RT_, 2048, 2048, "rms", "sigmoid", hn1, p_wg[i], g_s, ident, nw=ple_nw[i])
            hn2 = bufs[0] if hn1 is not bufs[0] else bufs[1]
            with p.stage():
                emit_linear(p, RT_, 256, 2048, "plain", "gres", prows[i], p_wp[i], hn2, ident, e1=hn1, e2=g_s)
            cur = hn2
        with p.stage():
            fin = emit_rmsout(p, cur, fnw, yout)
            p.op("sp", None, fin, inc=False)
        with p.stage():
            for e_ in ("pe", "act", "dve", "pool", "sp"):
                p.op(e_, None, [], inc=False)
    return nc


def _t5_bucket_np(n):
    n = np.asarray(n, dtype=np.int32)
    nf = np.maximum(n, 1).astype(np.float32)
    large = 16 + (np.log(nf / np.float32(16)) / np.float32(math.log(128 / 16)) * np.float32(16)).astype(np.int32)
    large = np.minimum(large, 31)
    return np.where(n < 16, n, large)


def _bias_tiles(rel_bias, NT):
    s_l = np.arange(128)[:, None]
    t_l = np.arange(NT)[None, :]
    d0 = t_l - s_l
    b0 = rel_bias[_t5_bucket_np(np.maximum(d0, 0))]
    b0 = np.where((d0 >= 0)[:, :, None], b0, np.float32(NEG))
    b1 = rel_bias[_t5_bucket_np(128 + d0)]
    D = np.stack([b0, b1]).transpose(0, 1, 3, 2).reshape(2, 128, 16 * NT)
    C0 = np.where(d0 >= 0, np.float32(0), np.float32(NEG)).astype(np.float32)
    return np.ascontiguousarray(D, dtype=np.float32), C0


def _scan_inputs(m, b, k, T, pi, ssm_lambda_re, ssm_lambda_im, ssm_log_dt, ssm_b_re, ssm_b_im, ssm_c_re, ssm_c_im, ssm_d, state_ssm_re, state_ssm_im):
    sp = {nm: [] for nm in ("lre_row", "lim_row", "ldt_row", "lreT2", "limT2", "ldtT2", "lre_gi", "lim_gi", "ldt_gi",
                            "bre_gi", "bim_gi", "CcP", "Ccc", "dvec", "H0A", "H0B")}
    for l in range(2):
        for c2 in range(2):
            g0 = 32 * k + 16 * c2
            gs = slice(g0, g0 + 16)
            lre, lim, ldt = ssm_lambda_re[l][gs], ssm_lambda_im[l][gs], ssm_log_dt[l][gs]
            sp["lre_row"].append(lre.reshape(1, 1024)); sp["lim_row"].append(lim.reshape(1, 1024))
            sp["ldt_row"].append(np.broadcast_to(ldt[:, None], (16, 64)).reshape(1, 1024))
            sp["lreT2"].append(np.concatenate([lre.T, lre.T], 0)); sp["limT2"].append(np.concatenate([lim.T, lim.T], 0))
            sp["ldtT2"].append(np.broadcast_to(ldt[None, :], (128, 16)))
            rep = lambda a: np.stack([np.repeat(a[ck * 8:(ck + 1) * 8], 16, axis=0) for ck in range(2)])
            sp["lre_gi"].append(rep(lre)); sp["lim_gi"].append(rep(lim))
            sp["ldt_gi"].append(rep(np.broadcast_to(ldt[:, None], (16, 64))))
            tb = lambda a: np.stack([a[ck * 8:(ck + 1) * 8].transpose(0, 2, 1).reshape(128, 64) for ck in range(2)])
            sp["bre_gi"].append(tb(ssm_b_re[l][gs])); sp["bim_gi"].append(tb(ssm_b_im[l][gs]))
            CcP = np.zeros((2, 128, 8, 128), np.float32)
            Ccc = np.zeros((128, 16, 16), np.float32)
            for G in range(16):
                ck_, g = G // 8, G % 8
                cc = np.concatenate([ssm_c_re[l][g0 + G].T, ssm_c_im[l][g0 + G].T], axis=0)
                CcP[ck_, :, g, g * 16:(g + 1) * 16] = cc
                Ccc[:, G, :] = cc
            sp["CcP"].append(CcP.reshape(2, 128, 1024)); sp["Ccc"].append(Ccc.reshape(128, 256))
            sp["dvec"].append(ssm_d[l][512 * k + 256 * c2:512 * k + 256 * c2 + 256].reshape(2, 128).T)
            hre = state_ssm_re[l][4 * b:4 * b + 4, gs].transpose(0, 2, 1)
            him = state_ssm_im[l][4 * b:4 * b + 4, gs].transpose(0, 2, 1)
            sp["H0A"].append(np.concatenate([hre, him], 1)); sp["H0B"].append(np.concatenate([him, hre], 1))
    for nm, lst in sp.items():
        a = np.stack([np.ascontiguousarray(x_, dtype=np.float32) for x_ in lst])
        m["sp_" + nm] = np.ascontiguousarray(a.reshape((2, 2) + a.shape[1:]))
    uidx = np.zeros((2, 128, 8), np.int32)
    for c2 in range(2):
        for ck_ in range(2):
            for r in range(4):
                uidx[c2, :, ck_ * 4 + r] = (4 * k + 2 * c2 + ck_) * 512 + r * 128 + pi
    m["uidx"] = uidx
    yidx = np.zeros((128, 36), np.int32)
    for jj in range(9):
        for r in range(4):
            if jj < 8:
                yidx[:, jj * 4 + r] = (T[jj] // 4) * 2048 + r * 512 + (T[jj] % 4) * 128 + pi
            else:
                yidx[:, jj * 4 + r] = 8 * 2048 + r * 512 + 128 * k + pi
    m["yidx"] = yidx


_IDENT = np.eye(128, dtype=np.float32)
_TRI = np.triu(np.ones((128, 128), np.float32))
_CST = np.stack([np.arange(128), -np.arange(128), np.where(np.arange(128) < 64, -1.0, 1.0),
                 np.where(np.arange(128) < 64, 1.0, -1.0)], axis=1).astype(np.float32)
_TVEC = np.arange(129, dtype=np.float32).reshape(1, 129)
_MASK = (np.arange(128)[:, None, None] // 16 == np.arange(8)[None, :, None]).astype(np.float32) * np.ones((1, 1, 64), np.float32)
_NC = {}


def kernel(x_prompt, x_sample, cache_k, cache_v, cache_kidx, state_ssm_re, state_ssm_im, page_table,
           p_prompt, p_sample, norm_w, final_norm_w, rel_bias, attn_w_in, attn_w_out, ssm_w_in,
           ssm_lambda_re, ssm_lambda_im, ssm_log_dt, ssm_b_re, ssm_b_im, ssm_c_re, ssm_c_im, ssm_d,
           ssm_w_glu, ssm_b_glu, ssm_w_out, ple_norm_w, ple_w_gate, ple_w_proj):
    f32 = lambda a: np.ascontiguousarray(np.asarray(a), dtype=np.float32)
    (x_prompt, x_sample, cache_k, cache_v, cache_kidx, state_ssm_re, state_ssm_im, p_prompt, p_sample, norm_w,
     final_norm_w, rel_bias, attn_w_in, attn_w_out, ssm_w_in, ssm_lambda_re, ssm_lambda_im, ssm_log_dt, ssm_b_re,
     ssm_b_im, ssm_c_re, ssm_c_im, ssm_d, ssm_w_glu, ssm_b_glu, ssm_w_out, ple_norm_w, ple_w_gate, ple_w_proj) = [
        f32(a) for a in (x_prompt, x_sample, cache_k, cache_v, cache_kidx, state_ssm_re, state_ssm_im, p_prompt,
                         p_sample, norm_w, final_norm_w, rel_bias, attn_w_in, attn_w_out, ssm_w_in, ssm_lambda_re,
                         ssm_lambda_im, ssm_log_dt, ssm_b_re, ssm_b_im, ssm_c_re, ssm_c_im, ssm_d, ssm_w_glu,
                         ssm_b_glu, ssm_w_out, ple_norm_w, ple_w_gate, ple_w_proj)]
    page_table = np.asarray(page_table).astype(np.int32)
    if "nc" not in _NC:
        _NC["nc"] = build_fused()
    nc = _NC["nc"]
    Dp, C0p = _bias_tiles(rel_bias, 128)
    Ds, C0s = _bias_tiles(rel_bias, 4)
    shared = {
        "norm_w": norm_w.reshape(4, 1, 2048), "ple_nw": ple_norm_w.reshape(4, 1, 2048), "fnw": final_norm_w.reshape(1, 2048),
        "a_win": attn_w_in, "a_wout": attn_w_out, "s_win": ssm_w_in, "s_wglu": ssm_w_glu,
        "s_bglu": ssm_b_glu.reshape(2, 1, 2048), "s_wout": ssm_w_out, "p_wg": ple_w_gate, "p_wp": ple_w_proj,
        "ident": _IDENT, "Dp": Dp, "C0p": C0p, "Ds": Ds, "C0s": C0s, "vts": np.zeros((1, 129), np.float32),
        "cbv": np.ascontiguousarray(rel_bias[31].reshape(1, 16)), "tri": _TRI, "cst": _CST, "tvec": _TVEC,
        "mask": np.ascontiguousarray(_MASK),
    }
    for l in range(2):
        shared[f"ck{l}"] = cache_k[l].reshape(10240, 8192)
        shared[f"cv{l}"] = cache_v[l].reshape(10240, 8192)
        shared[f"ci{l}"] = cache_kidx[l].reshape(1280, 8192)
    pi = np.arange(128, dtype=np.int32)
    in_maps = []
    for c in range(NCORES):
        b, k = c // 4, c % 4
        T = [_zz_block(k, j) for j in range(8)]
        rows = np.concatenate([np.arange(t * 128, (t + 1) * 128) for t in T])
        m = dict(shared)
        xr = np.zeros((R_, 2048), np.float32)
        xr[0:1024] = x_prompt[b][rows]
        xr[1024:1028] = x_sample[c]
        pr = np.zeros((4, R_, 256), np.float32)
        pr[:, 0:1024] = p_prompt[:, b][:, rows]
        pr[:, 1024:1028] = p_sample[:, c]
        m["xrows"] = xr
        m["prows"] = pr
        m["pt"] = np.ascontiguousarray(page_table[c].reshape(128, 1))
        m["vtp"] = np.concatenate([np.where(np.arange(P_SMAX[j]) <= T[j], 0.0, NEG) for j in range(8)]).reshape(1, 144).astype(np.float32)
        idxK = np.zeros((128, 4, 144), np.int32)
        idxI = np.zeros((128, 144), np.int32)
        u = 0
        for j in range(8):
            for dl in range(P_SMAX[j]):
                r, jj = _zz_owner(max(T[j] - dl, 0))
                for g in range(4):
                    idxK[:, g, u] = (g // 2) * 8192 + r * 2048 + (g % 2) * 1024 + jj * 128 + pi
                idxI[:, u] = r * 1024 + jj * 128 + pi
                u += 1
        m["idxK"] = idxK.reshape(128, 576)
        m["idxI"] = idxI
        _scan_inputs(m, b, k, T, pi, ssm_lambda_re, ssm_lambda_im, ssm_log_dt, ssm_b_re, ssm_b_im, ssm_c_re, ssm_c_im, ssm_d, state_ssm_re, state_ssm_im)
        in_maps.append(m)
    res = run_bass_kernel_spmd(nc, in_maps, core_ids=list(range(NCORES)))
    y_p = np.zeros((2, 4096, 2048), np.float32); y_s = np.zeros((8, 4, 2048), np.float32)
    k_p = np.zeros((2, 2, 4096, 4, 128), np.float32); v_p = np.zeros_like(k_p); ki_p = np.zeros((2, 2, 4096, 64), np.float32)
    k_s = np.zeros((2, 8, 4, 4, 128), np.float32); v_s = np.zeros_like(k_s); ki_s = np.zeros((2, 8, 4, 64), np.float32)
    hr_p = np.zeros((2, 2, 128, 64), np.float32); hi_p = np.zeros_like(hr_p)
    hr_s = np.zeros((2, 8, 128, 64), np.float32); hi_s = np.zeros_like(hr_s)
    for c in range(NCORES):
        b, k = c // 4, c % 4
        r_ = res.results[c]
        yo, kv = r_["yout"], r_["kvk"]
        for j in range(8):
            t = _zz_block(k, j)
            sl = slice(t * 128, (t + 1) * 128)
            y_p[b, sl] = yo[j * 128:(j + 1) * 128]
            for l in range(2):
                blk = kv[l][j * 128:(j + 1) * 128]
                k_p[l, b, sl] = blk[:, 0:512].reshape(128, 4, 128)
                v_p[l, b, sl] = blk[:, 512:1024].reshape(128, 4, 128)
                ki_p[l, b, sl] = blk[:, 1024:1088]
        y_s[c] = yo[1024:1028]
        for l in range(2):
            blk = kv[l][1024:1028]
            k_s[l, c] = blk[:, 0:512].reshape(4, 4, 128); v_s[l, c] = blk[:, 512:1024].reshape(4, 4, 128)
            ki_s[l, c] = blk[:, 1024:1088]
            for c2 in range(2):
                gs = slice(32 * k + 16 * c2, 32 * k + 16 * c2 + 16)
                H = r_["HFp"][l, c2, 0]
                hr_p[l, b, gs] = H[0:64].T; hi_p[l, b, gs] = H[64:128].T
                for i_ in range(4):
                    H = r_["HFs"][l, c2, i_]
                    hr_s[l, 4 * b + i_, gs] = H[0:64].T; hi_s[l, 4 * b + i_, gs] = H[64:128].T
    return (y_p, y_s, k_p, v_p, ki_p, hr_p, hi_p, k_s, v_s, ki_s, hr_s, hi_s)
```

```python
import math
from contextlib import ExitStack, contextmanager
import numpy as np
import concourse.bass as bass
import concourse.mybir as mybir
from concourse.bass_utils import run_bass_kernel_spmd

F32 = mybir.dt.float32
BF16 = mybir.dt.bfloat16
I32 = mybir.dt.int32
AF = mybir.ActivationFunctionType
ALU = mybir.AluOpType
AX = mybir.AxisListType
NCORES = 8
EPOCH = 30000
PER = EPOCH // 16


class Prog:
    ENG = ("pe", "act", "dve", "pool", "sp")

    def __init__(self, nc, es):
        self.nc = nc
        self.es = es
        self.cnt = {e: 0 for e in self.ENG}
        self.lanes = {}
        self.csem = {e: [] for e in self.ENG}
        self.lsem = {}
        self.lane_inc = {}
        self.waited = {e: {} for e in self.ENG}
        self.nuniq = 0
        self.st = None
        self.ops = None
        self.barrier = []
        self.seen = set()

    @contextmanager
    def stage(self, name=""):
        with ExitStack() as st:
            self.st = st
            self.ops = {e: [] for e in self.ENG}
            self.seen = set()
            self.stage_lane_map = {}
            yield self
            self._emit()
            bar = []
            for e in self.ENG:
                if self.cnt[e] > 0:
                    bar.append(("c", e, self.cnt[e] - 1))
            for ln, (ep, val) in self.lanes.items():
                if val > 0:
                    bar.append(("d", ln, ep, val))
            self.barrier = bar
            self.st = None

    def sb(self, shape, dt, name=None):
        self.nuniq += 1
        return self.st.enter_context(self.nc.sbuf_tensor(name or f"sb{self.nuniq}", list(shape), dt))

    def ps(self, shape, dt=F32, name=None):
        self.nuniq += 1
        return self.st.enter_context(self.nc.psum_tensor(name or f"ps{self.nuniq}", list(shape), dt))

    def _deps(self, eng, deps):
        deps = [d for d in deps if d is not None]
        if eng not in self.seen:
            self.seen.add(eng)
            deps = list(self.barrier) + deps
        return tuple(deps)

    def op(self, eng, fn, deps=(), inc=True):
        deps = self._deps(eng, deps)
        tok = None
        if inc:
            tok = ("c", eng, self.cnt[eng])
            self.cnt[eng] += 1
        self.ops[eng].append((fn, deps, tok, 1))
        return tok

    def dma(self, eng, lane, out, in_, deps=()):
        return self.lane_op(eng, lane, lambda e: e.dma_start(out=out, in_=in_), deps)

    def lane_op(self, eng, lane, fn, deps=(), incv=16):
        deps = self._deps(eng, deps)
        if incv == 16:
            lane = "L%d" % self.stage_lane_map.setdefault(lane, len(self.stage_lane_map))
        ep, val = self.lanes.get(lane, (0, 0))
        if val + incv > EPOCH:
            ep, val = ep + 1, 0
        val += incv
        self.lanes[lane] = (ep, val)
        tok = ("d", lane, ep, val)
        self.ops[eng].append((fn, deps, tok, incv))
        return tok

    def _semval(self, tok):
        if tok[0] == "c":
            _, e, i = tok
            ep = i // EPOCH
            while len(self.csem[e]) <= ep:
                self.csem[e].append(self.es.enter_context(self.nc.semaphore(f"s_{e}{len(self.csem[e])}")))
            return self.csem[e][ep], (i % EPOCH) + 1
        _, ln, ep, val = tok
        lst = self.lsem.setdefault(ln, [])
        while len(lst) <= ep:
            lst.append(self.es.enter_context(self.nc.semaphore(f"l_{ln}_{len(lst)}")))
        return lst[ep], val

    def _emit(self):
        nc = self.nc
        with nc.Block() as block:
            def make(ename):
                def body(eng):
                    waited = self.waited[ename]
                    for fn, deps, tok, incv in self.ops[ename]:
                        for d in deps:
                            s, v = self._semval(d)
                            key = id(s)
                            if waited.get(key, 0) >= v:
                                continue
                            waited[key] = v
                            eng.wait_ge(s, v)
                        if fn is None:
                            continue
                        ins = fn(eng)
                        if tok is not None:
                            s, v = self._semval(tok)
                            ins.then_inc(s, incv if tok[0] == "d" else 1)
                return body
            block.tensor(make("pe"))
            block.scalar(make("act"))
            block.vector(make("dve"))
            block.gpsimd(make("pool"))
            block.sync(make("sp"))


class Ring:
    def __init__(self, bufs):
        self.bufs = bufs
        self.i = 0
        self.readers = [[] for _ in bufs]

    def next(self):
        k = self.i % len(self.bufs)
        self.i += 1
        deps = self.readers[k]
        self.readers[k] = []
        return k, self.bufs[k], deps

    def done(self, k, *toks):
        self.readers[k].extend(t for t in toks if t is not None)
def emit_linear(p, RT, D, N, pro, epi, x, w, y, ident_d, x2=None, nw=None, e1=None, e2=None, bv=None, eps=1e-6):
    R = RT * 128
    KC = D // 128
    NT = (N + 511) // 512
    ident_f = p.sb([128, 128], F32)
    ident = p.sb([128, 128], BF16)
    XT = p.sb([128, KC, R], BF16)
    xbufs = Ring([p.sb([128, D], F32) for _ in range(2)])
    x2bufs = Ring([p.sb([128, D], F32) for _ in range(2)]) if pro == "gate" else None
    tmpf = Ring([p.sb([128, D], F32) for _ in range(2)]) if pro in ("gate", "gelu", "rms") else None
    xpb = Ring([p.sb([128, D], BF16) for _ in range(2)])
    nwb = p.sb([128, D], F32) if pro == "rms" else None
    stat = Ring([p.sb([128, 2], F32) for _ in range(2)]) if pro == "rms" else None
    pT = Ring([p.ps([128, 4, 128], BF16) for _ in range(2)])
    wb = Ring([p.sb([128, KC, 512], BF16) for _ in range(2)])
    acc = Ring([p.ps([128, 512], F32) for _ in range(3)])
    ob = Ring([p.sb([128, 512], F32) for _ in range(3)])
    e1b = Ring([p.sb([128, 512], F32) for _ in range(2)]) if e1 is not None else None
    e2b = Ring([p.sb([128, 512], F32) for _ in range(2)]) if e2 is not None else None
    bb = Ring([p.sb([128, 512], F32) for _ in range(2)]) if bv is not None else None
    tb = Ring([p.sb([128, 512], F32) for _ in range(2)]) if epi in ("gres", "glu") else None
    GC = 2.0 * math.sqrt(2.0 / math.pi)

    t_id = p.dma("sp", "ident", ident_f[:], ident_d)
    t_ident = p.op("dve", lambda e: e.tensor_copy(out=ident[:], in_=ident_f[:]), [t_id])
    t_nw = p.dma("sp", "nw", nwb[:], nw.partition_broadcast(128)) if pro == "rms" else None
    xt_toks = []
    for r in range(RT):
        rows = slice(r * 128, (r + 1) * 128)
        kx, xb, dx = xbufs.next()
        t_x = p.dma("sp", f"x{kx}", xb[:], x[rows, :], dx)
        kp, xp, dxp = xpb.next()
        if pro == "rms":
            kt, tf, dt_ = tmpf.next()
            ks, st, ds = stat.next()
            t_sq = p.op("act", lambda e, tf=tf, xb=xb, st=st: e.activation(
                out=tf[:], in_=xb[:], func=AF.Square, accum_out=st[:, 0:1]), [t_x] + dt_ + ds)
            t_r1 = p.op("dve", lambda e, st=st: e.tensor_scalar(
                out=st[:, 1:2], in0=st[:, 0:1], scalar1=1.0 / D, scalar2=eps, op0=ALU.mult, op1=ALU.add), [t_sq])
            t_r1b = p.op("act", lambda e, st=st: e.activation(out=st[:, 1:2], in_=st[:, 1:2], func=AF.Sqrt), [t_r1])
            t_r2 = p.op("dve", lambda e, st=st: e.reciprocal(out=st[:, 1:2], in_=st[:, 1:2]), [t_r1b])
            t_xp = p.op("dve", lambda e, xp=xp, xb=xb, st=st: e.scalar_tensor_tensor(
                out=xp[:], in0=xb[:], scalar=st[:, 1:2], in1=nwb[:], op0=ALU.mult, op1=ALU.mult),
                [t_r2, t_nw] + dxp)
            tmpf.done(kt, t_sq); stat.done(ks, t_xp); xbufs.done(kx, t_xp)
        elif pro == "gate":
            k2, x2b, d2 = x2bufs.next()
            t_x2 = p.dma("act", f"x2{k2}", x2b[:], x2[rows, :], d2)
            kt, tf, dt_ = tmpf.next()
            t_s = p.op("act", lambda e, tf=tf, x2b=x2b: e.activation(out=tf[:], in_=x2b[:], func=AF.Silu), [t_x2] + dt_)
            t_xp = p.op("dve", lambda e, xp=xp, xb=xb, tf=tf: e.tensor_tensor(
                out=xp[:], in0=xb[:], in1=tf[:], op=ALU.mult), [t_x, t_s] + dxp)
            x2bufs.done(k2, t_s); tmpf.done(kt, t_xp); xbufs.done(kx, t_xp)
        elif pro == "gelu":
            kt, tf, dt_ = tmpf.next()
            t_a = p.op("dve", lambda e, tf=tf, xb=xb: e.tensor_tensor(out=tf[:], in0=xb[:], in1=xb[:], op=ALU.mult), [t_x] + dt_)
            t_b = p.op("dve", lambda e, tf=tf: e.tensor_scalar(
                out=tf[:], in0=tf[:], scalar1=0.044715, scalar2=1.0, op0=ALU.mult, op1=ALU.add), [t_a])
            t_c = p.op("dve", lambda e, tf=tf, xb=xb: e.tensor_tensor(out=tf[:], in0=tf[:], in1=xb[:], op=ALU.mult), [t_b])
            t_d = p.op("act", lambda e, tf=tf: e.activation(out=tf[:], in_=tf[:], func=AF.Sigmoid, scale=GC), [t_c])
            t_xp = p.op("dve", lambda e, xp=xp, xb=xb, tf=tf: e.tensor_tensor(
                out=xp[:], in0=xb[:], in1=tf[:], op=ALU.mult), [t_d] + dxp)
            tmpf.done(kt, t_xp); xbufs.done(kx, t_xp)
        else:
            t_xp = p.op("dve", lambda e, xp=xp, xb=xb: e.tensor_copy(out=xp[:], in_=xb[:]), [t_x] + dxp)
            xbufs.done(kx, t_xp)
        last = []
        for k0 in range(0, KC, 4):
            nk = min(4, KC - k0)
            kq, pt, dq = pT.next()
            tt = None
            for j in range(nk):
                tt = p.op("pe", lambda e, pt=pt, xp=xp, j=j, k0=k0: e.transpose(
                    out=pt[:, j, :], in_=xp[:, (k0 + j) * 128:(k0 + j + 1) * 128], identity=ident[:]),
                    [t_xp, t_ident] + dq, inc=(j == nk - 1))
            if (k0 // 4) % 2 == 0:
                t_cp = p.op("act", lambda e, pt=pt, k0=k0, nk=nk, rows=rows: e.copy(
                    out=XT[:, k0:k0 + nk, rows], in_=pt[:, 0:nk, :]), [tt])
            else:
                t_cp = p.op("dve", lambda e, pt=pt, k0=k0, nk=nk, rows=rows: e.tensor_copy(
                    out=XT[:, k0:k0 + nk, rows], in_=pt[:, 0:nk, :]), [tt])
            pT.done(kq, t_cp)
            last.append(t_cp)
            xpb.done(kp, tt)
        xt_toks.append(last)
    wv = w.rearrange("(k p) n -> p k n", p=128)
    fin = []
    for n in range(NT):
        n0 = n * 512
        ns = min(512, N - n0)
        kw, wt, dw = wb.next()
        t_w = p.dma("pool", f"w{kw}", wt[:, :, 0:ns], wv[:, :, n0:n0 + ns], dw)
        t_bb = None
        if bv is not None:
            kb, bt, db = bb.next()
            t_bb = p.dma("act", f"b{kb}", bt[:, 0:ns], bv[0:1, n0:n0 + ns].partition_broadcast(128), db)
        mm_last = []
        for r in range(RT):
            rows = slice(r * 128, (r + 1) * 128)
            ka, ac, da = acc.next()
            tm = None
            for k in range(KC):
                tm = p.op("pe", lambda e, ac=ac, wt=wt, k=k, rows=rows, ns=ns: e.matmul(
                    ac[:, 0:ns], lhsT=XT[:, k, rows], rhs=wt[:, k, 0:ns], start=(k == 0), stop=(k == KC - 1)),
                    ([t_w] + xt_toks[r] + da) if k == 0 else [], inc=(k == KC - 1))
            mm_last.append(tm)
            ko, ot, do = ob.next()
            if epi == "none":
                t_o = p.op("act", lambda e, ot=ot, ac=ac, ns=ns: e.copy(out=ot[:, 0:ns], in_=ac[:, 0:ns]), [tm] + do)
                acc.done(ka, t_o)
            elif epi == "sigmoid":
                t_o = p.op("act", lambda e, ot=ot, ac=ac, ns=ns: e.activation(
                    out=ot[:, 0:ns], in_=ac[:, 0:ns], func=AF.Sigmoid), [tm] + do)
                acc.done(ka, t_o)
            elif epi == "res":
                k1, et, d1 = e1b.next()
                t_e = p.dma("sp", f"e1{k1}", et[:, 0:ns], e1[rows, n0:n0 + ns], d1)
                t_o = p.op("dve", lambda e, ot=ot, ac=ac, et=et, ns=ns: e.tensor_tensor(
                    out=ot[:, 0:ns], in0=ac[:, 0:ns], in1=et[:, 0:ns], op=ALU.add), [tm, t_e] + do)
                acc.done(ka, t_o); e1b.done(k1, t_o)
            elif epi == "gres":
                k1, et, d1 = e1b.next()
                t_e = p.dma("sp", f"e1{k1}", et[:, 0:ns], e1[rows, n0:n0 + ns], d1)
                k2, et2, d2 = e2b.next()
                t_e2 = p.dma("sp", f"e2{k2}", et2[:, 0:ns], e2[rows, n0:n0 + ns], d2)
                kt, tt_, dtt = tb.next()
                t_m = p.op("dve", lambda e, tt_=tt_, ac=ac, et2=et2, ns=ns: e.tensor_tensor(
                    out=tt_[:, 0:ns], in0=ac[:, 0:ns], in1=et2[:, 0:ns], op=ALU.mult), [tm, t_e2] + dtt)
                t_o = p.op("dve", lambda e, ot=ot, tt_=tt_, et=et, ns=ns: e.tensor_tensor(
                    out=ot[:, 0:ns], in0=tt_[:, 0:ns], in1=et[:, 0:ns], op=ALU.add), [t_m, t_e] + do)
                acc.done(ka, t_m); e1b.done(k1, t_o); e2b.done(k2, t_m); tb.done(kt, t_o)
            elif epi == "glu":
                k1, et, d1 = e1b.next()
                t_e = p.dma("sp", f"e1{k1}", et[:, 0:ns], e1[rows, n0:n0 + ns], d1)
                kt, tt_, dtt = tb.next()
                t_a = p.op("dve", lambda e, tt_=tt_, et=et, ns=ns: e.tensor_tensor(
                    out=tt_[:, 0:ns], in0=et[:, 0:ns], in1=et[:, 0:ns], op=ALU.mult), [t_e] + dtt)
                t_b = p.op("dve", lambda e, tt_=tt_, ns=ns: e.tensor_scalar(
                    out=tt_[:, 0:ns], in0=tt_[:, 0:ns], scalar1=0.044715, scalar2=1.0, op0=ALU.mult, op1=ALU.add), [t_a])
                t_c = p.op("dve", lambda e, tt_=tt_, et=et, ns=ns: e.tensor_tensor(
                    out=tt_[:, 0:ns], in0=tt_[:, 0:ns], in1=et[:, 0:ns], op=ALU.mult), [t_b])
                t_d = p.op("act", lambda e, tt_=tt_, ns=ns: e.activation(
                    out=tt_[:, 0:ns], in_=tt_[:, 0:ns], func=AF.Sigmoid, scale=GC), [t_c])
                t_g = p.op("dve", lambda e, tt_=tt_, et=et, ns=ns: e.tensor_tensor(
                    out=tt_[:, 0:ns], in0=tt_[:, 0:ns], in1=et[:, 0:ns], op=ALU.mult), [t_d])
                t_s1 = p.op("dve", lambda e, ot=ot, ac=ac, bt=bt, ns=ns: e.tensor_tensor(
                    out=ot[:, 0:ns], in0=ac[:, 0:ns], in1=bt[:, 0:ns], op=ALU.add), [tm, t_bb] + do)
                t_s2 = p.op("act", lambda e, ot=ot, ns=ns: e.activation(
                    out=ot[:, 0:ns], in_=ot[:, 0:ns], func=AF.Sigmoid), [t_s1])
                t_o = p.op("dve", lambda e, ot=ot, tt_=tt_, ns=ns: e.tensor_tensor(
                    out=ot[:, 0:ns], in0=ot[:, 0:ns], in1=tt_[:, 0:ns], op=ALU.mult), [t_s2, t_g])
                acc.done(ka, t_s1); e1b.done(k1, t_g); tb.done(kt, t_o)
            else:
                raise ValueError(epi)
            t_st = p.dma("act", f"o{ko}", y[rows, n0:n0 + ns], ot[:, 0:ns], [t_o])
            ob.done(ko, t_st)
            fin.append(t_st)
        wb.done(kw, *mm_last)
        if bv is not None:
            bb.done(kb, *mm_last)
    return fin[-3:]


def emit_transpose(p, src, dst, R, C, ident_d):
    identf = p.sb([128, 128], F32)
    t_id = p.dma("sp", "ident", identf[:], ident_d)
    inb = Ring([p.sb([128, 512], F32) for _ in range(3)])
    pst = Ring([p.ps([128, 512], F32) for _ in range(3)])
    outb = Ring([p.sb([128, 512], F32) for _ in range(3)])
    fin = []
    for r0 in range(0, R, 512):
        nr = min(512, R - r0)
        nrt = nr // 128
        for c0 in range(0, C, 128):
            cw = min(128, C - c0)
            ki, it, di = inb.next()
            t_in = p.dma("sp", f"ti{ki}", it[:, 0:nrt * 128].rearrange("p (q c) -> p q c", c=128)[:, :, 0:cw],
                         src[r0:r0 + nr, c0:c0 + cw].rearrange("(q p) c -> p q c", p=128), di)
            kp, pt, dp = pst.next()
            tt = None
            for q in range(nrt):
                tt = p.op("pe", lambda e, pt=pt, it=it, q=q, cw=cw: e.transpose(
                    out=pt[0:cw, q * 128:(q + 1) * 128], in_=it[:, q * 128:q * 128 + cw], identity=identf[:]),
                    [t_in, t_id] + dp, inc=(q == nrt - 1))
            inb.done(ki, tt)
            ko, ot, do = outb.next()
            t_cp = p.op("act" if (c0 // 128) % 2 == 0 else "dve",
                        (lambda e, ot=ot, pt=pt, cw=cw, nr=nr: e.copy(out=ot[0:cw, 0:nr], in_=pt[0:cw, 0:nr]))
                        if (c0 // 128) % 2 == 0 else
                        (lambda e, ot=ot, pt=pt, cw=cw, nr=nr: e.tensor_copy(out=ot[0:cw, 0:nr], in_=pt[0:cw, 0:nr])),
                        [tt] + do)
            pst.done(kp, t_cp)
            t_st = p.dma("act", f"to{ko}", dst[c0:c0 + cw, r0:r0 + nr], ot[0:cw, 0:nr], [t_cp])
            outb.done(ko, t_st)
            fin.append(t_st)
    return fin[-3:]
NEG = -1.0e30
TOPK = 256
BIS_LO, BIS_HI, BIS_IT = -8192.0, 8192.0, 28


def emit_attn(p, NJ, NT, smax, src, vt_d, D_d, cb_d, C0_d, oT_dst):
    HN = 16 * NT
    HPM = min(16, 512 // NT)
    NMM = 16 // HPM
    SMX = max(smax)
    NU = sum(smax)
    u0s = [sum(smax[:j]) for j in range(NJ)]
    scale = 128.0 ** -0.5
    n4 = 4 * NT
    ones = p.sb([128, 128], BF16)
    vt = p.sb([128, NU], F32)
    Dt = p.sb([128, 2, HN], F32)
    cb = p.sb([128, 16], F32)
    C0 = p.sb([128, NT], F32)
    qT = Ring([p.sb([128, HN], BF16) for _ in range(2)])
    qiT = Ring([p.sb([64, HN], BF16) for _ in range(2)])
    wb = Ring([p.sb([128, HN], F32) for _ in range(2)])
    SC = p.sb([128, SMX, NT], F32)
    CM = p.sb([128, SMX, NT], BF16)
    MKs = [p.sb([128, SMX, NT], BF16) for _ in range(2 if NJ > 1 else 1)]
    kib = Ring([p.sb([128, 128], BF16) for _ in range(3)])
    rb = Ring([p.sb([128, HN], F32) for _ in range(2)])
    ktb = Ring([p.sb([128, 128], BF16) for _ in range(3)])
    vb = Ring([p.sb([128, 128], BF16) for _ in range(3)])
    banks = Ring([p.ps([128, 512], F32) for _ in range(4)])
    ps_cnt = p.ps([128, 512], F32)
    ps_o = p.ps([128, 512], F32)
    ps_d = p.ps([128, 512], F32)
    lo = p.sb([128, NT], F32)
    mid = p.sb([128, NT], F32)
    ge = p.sb([128, NT], F32)
    lgt = Ring([p.sb([128, 4 * NT], F32) for _ in range(2)])
    pb = Ring([p.sb([128, 4 * NT], BF16) for _ in range(2)])
    pmb = Ring([p.sb([128, 4 * NT], BF16) for _ in range(2)])
    rdb = p.sb([128, 4 * NT], F32)
    ob = Ring([p.sb([128, 4 * NT], F32) for _ in range(2)])

    t_ones = p.op("dve", lambda e: e.memset(ones[:], 1.0))
    t_vt = p.dma("sp", "vt", vt[:], vt_d.partition_broadcast(128))
    t_D = p.dma("sp", "D", Dt[:], D_d.rearrange("a p n -> p a n"))
    t_cb = p.dma("sp", "cb", cb[:], cb_d.partition_broadcast(128))
    t_C0 = p.dma("sp", "C0", C0[:], C0_d)
    st = {"acc_free": [], "sc_free": [], "out": []}

    def phase_a(j):
        S = smax[j]
        kq, qt, dq = qT.next()
        t_q = src.load_q(j, qt, dq)
        kqi, qit, dqi = qiT.next()
        t_qi = src.load_qi(j, qit, dqi)
        kw, wt, dw_ = wb.next()
        t_w = src.load_w(j, wt, dw_)
        sc_toks = []
        mm_toks, t_rs = [], []
        for dl in range(S):
            kk, kit, dk = kib.next()
            t_ki = src.load_ki(j, dl, kit, dk)
            kr, rt, dr = rb.next()
            t_rs = []
            mm_toks = []
            for m in range(NMM):
                kb_, bk, dbk = banks.next()
                n_ = HPM * NT
                t_mm = p.op("pe", lambda e, bk=bk, kit=kit, qit=qit, m=m, n_=n_: e.matmul(
                    bk[:, 0:n_], lhsT=kit[0:64, :], rhs=qit[:, m * n_:(m + 1) * n_], start=True, stop=True),
                    [t_ki, t_qi] + dbk)
                t_r = p.op("dve", lambda e, bk=bk, rt=rt, wt=wt, m=m, n_=n_: e.scalar_tensor_tensor(
                    out=rt[:, m * n_:(m + 1) * n_], in0=bk[:, 0:n_], scalar=0.0, in1=wt[:, m * n_:(m + 1) * n_],
                    op0=ALU.max, op1=ALU.mult), [t_mm, t_w] + (dr if m == 0 else []))
                banks.done(kb_, t_r)
                t_rs.append(t_r)
                mm_toks.append(t_mm)
            kib.done(kk, *mm_toks)
            t_red = p.op("dve", lambda e, rt=rt, dl=dl: e.tensor_reduce(
                out=SC[:, dl, :], in_=rt[:, :].rearrange("p (h t) -> p t h", h=16), axis=AX.X, op=ALU.add),
                t_rs + (st["sc_free"] if dl == 0 else []))
            rb.done(kr, t_red)
            u = u0s[j] + dl
            t_v = p.op("dve", lambda e, dl=dl, u=u: e.tensor_scalar(
                out=SC[:, dl, :], in0=SC[:, dl, :], scalar1=vt[:, u:u + 1], scalar2=None, op0=ALU.add), [t_red, t_vt])
            if dl == 0:
                t_v = p.op("dve", lambda e: e.tensor_tensor(
                    out=SC[:, 0, :], in0=SC[:, 0, :], in1=C0[:, :], op=ALU.add), [t_v, t_C0])
            sc_toks.append(t_v)
        qiT.done(kqi, *mm_toks)
        wb.done(kw, *t_rs)
        return kq, qt, t_q, sc_toks

    def gen_bisect(S, sc_toks, res):
        t_m = p.op("dve", lambda e: e.memset(mid[:], 0.5 * (BIS_LO + BIS_HI)))
        t_prev = [t_m]
        t_cm_free = []
        for it in range(BIS_IT):
            h = (BIS_HI - BIS_LO) / 2.0 ** (it + 1)
            t_cmp = p.op("dve", lambda e, S=S: e.tensor_tensor(
                out=CM[:, 0:S, :], in0=SC[:, 0:S, :], in1=mid[:, :].unsqueeze(1).to_broadcast([128, S, NT]),
                op=ALU.is_ge), t_prev + sc_toks + t_cm_free)
            t_c = None
            for dl in range(S):
                t_c = p.op("pe", lambda e, dl=dl, S=S: e.matmul(
                    ps_cnt[:, 0:NT], lhsT=ones[:, :], rhs=CM[:, dl, :], start=(dl == 0), stop=(dl == S - 1)),
                    ([t_cmp, t_ones] + t_prev) if dl == 0 else [], inc=(dl == S - 1))
            t_ge = p.op("dve", lambda e, h=h: e.tensor_scalar(
                out=ge[:], in0=ps_cnt[:, 0:NT], scalar1=float(TOPK) - 0.5, scalar2=h, op0=ALU.is_ge, op1=ALU.mult), [t_c])
            t_md = p.op("dve", lambda e, h=h: e.scalar_tensor_tensor(
                out=mid[:], in0=ge[:], scalar=-0.5 * h, in1=mid[:], op0=ALU.add, op1=ALU.add), [t_ge])
            t_prev = [t_md]
            t_cm_free = [t_c]
            yield None
        hf = (BIS_HI - BIS_LO) / 2.0 ** (BIS_IT + 1)
        res["t_lo"] = p.op("dve", lambda e: e.tensor_scalar(
            out=lo[:], in0=mid[:], scalar1=-hf, scalar2=None, op0=ALU.add), t_prev)

    def gen_c(j, S, kq, qt, t_q, MK, t_mk):
        last = {}
        for g in range(4):
            pend = None
            for dl in range(S):
                kk, ktt, dkt = ktb.next()
                t_kt = src.load_k(j, g, dl, ktt, dkt)
                kv, vtl, dv = vb.next()
                t_vv = src.load_v(j, g, dl, vtl, dv)
                kb_, bk, dbk = banks.next()
                t_s = p.op("pe", lambda e, bk=bk, ktt=ktt, g=g: e.matmul(
                    bk[:, 0:n4], lhsT=ktt[:, 0:128], rhs=qt[:, g * n4:(g + 1) * n4], start=True, stop=True),
                    [t_kt, t_q] + dbk)
                ktb.done(kk, t_s)
                kp_, pt, dp = pb.next()
                if dl >= 2:
                    t_e = None
                    for r in range(4):
                        h = 4 * g + r
                        t_e = p.op("act", lambda e, pt=pt, bk=bk, r=r, h=h: e.activation(
                            out=pt[:, r * NT:(r + 1) * NT], in_=bk[:, r * NT:(r + 1) * NT], func=AF.Exp,
                            bias=cb[:, h:h + 1], scale=scale), [t_s, t_cb] + (dp if r == 0 else []))
                    banks.done(kb_, t_e)
                else:
                    kl, lt, dlg = lgt.next()
                    t_l = p.op("dve", lambda e, lt=lt, bk=bk, dl=dl, g=g: e.scalar_tensor_tensor(
                        out=lt[:, 0:n4], in0=bk[:, 0:n4], scalar=scale, in1=Dt[:, dl, g * n4:(g + 1) * n4],
                        op0=ALU.mult, op1=ALU.add), [t_s, t_D] + dlg)
                    banks.done(kb_, t_l)
                    t_e = p.op("act", lambda e, pt=pt, lt=lt: e.activation(
                        out=pt[:, 0:n4], in_=lt[:, 0:n4], func=AF.Exp), [t_l] + dp)
                    lgt.done(kl, t_e)
                km, pmt, dpm = pmb.next()
                t_pm = p.op("dve", lambda e, pmt=pmt, pt=pt, dl=dl: e.tensor_tensor(
                    out=pmt[:, :].rearrange("p (r t) -> p r t", r=4),
                    in0=pt[:, :].rearrange("p (r t) -> p r t", r=4),
                    in1=MK[:, dl, :].unsqueeze(1).to_broadcast([128, 4, NT]), op=ALU.mult), [t_e, t_mk] + dpm)
                pb.done(kp_, t_pm)
                last["pm"] = t_pm
                if pend is not None:
                    pend()

                def stage2(vtl=vtl, pmt=pmt, kv=kv, km=km, t_vv=t_vv, t_pm=t_pm, first=(dl == 0), lastf=(dl == S - 1)):
                    t_o_ = p.op("pe", lambda e: e.matmul(
                        ps_o[:, 0:n4], lhsT=vtl[:, 0:128], rhs=pmt[:, 0:n4], start=first, stop=lastf),
                        [t_vv, t_pm] + (st["acc_free"] if first else []))
                    t_d_ = p.op("pe", lambda e: e.matmul(
                        ps_d[:, 0:n4], lhsT=ones[:, :], rhs=pmt[:, 0:n4], start=first, stop=lastf), [t_pm])
                    vb.done(kv, t_o_)
                    pmb.done(km, t_d_)
                    last["o"], last["d"] = t_o_, t_d_
                pend = stage2
                yield None
            pend()
            t_rd = p.op("dve", lambda e: e.reciprocal(out=rdb[:], in_=ps_d[:, 0:n4]), [last["d"]] + st["acc_free"])
            ko, ot, do = ob.next()
            t_o = p.op("dve", lambda e, ot=ot: e.tensor_tensor(
                out=ot[:], in0=ps_o[:, 0:n4], in1=rdb[:], op=ALU.mult), [t_rd, last["o"]] + do)
            st["acc_free"] = [t_o]
            t_st = p.dma("act", f"o{ko}", oT_dst(j, g), ot[:, :].rearrange("p (r t) -> p r t", r=4), [t_o])
            ob.done(ko, t_st)
            st["out"].append(t_st)
        qT.done(kq, last["o"])

    order = list(range(NJ))[::-1]
    pend_c = None
    for idx, j in enumerate(order):
        S = smax[j]
        kq, qt, t_q, sc_toks = phase_a(j)
        res = {}
        per = 0
        if pend_c is not None:
            per = -(-(4 * smax[order[idx - 1]]) // BIS_IT)
        for _ in gen_bisect(S, sc_toks, res):
            for _u in range(per):
                if pend_c is not None and next(pend_c, "END") == "END":
                    pend_c = None
        if pend_c is not None:
            for _ in pend_c:
                pass
        MK = MKs[idx % len(MKs)]
        t_mk = p.op("dve", lambda e, S=S, MK=MK: e.tensor_tensor(
            out=MK[:, 0:S, :], in0=SC[:, 0:S, :], in1=lo[:, :].unsqueeze(1).to_broadcast([128, S, NT]),
            op=ALU.is_ge), [res["t_lo"]])
        st["sc_free"] = [t_mk]
        pend_c = gen_c(j, S, kq, qt, t_q, MK, t_mk)
    for _ in pend_c:
        pass
    return st["out"][-2:]
TWO_PI = 2.0 * math.pi


def _sincos(p, out_t, x_t, tmp_t, deps, cos, ki_t, kf_t):
    off = (math.pi / 2) if cos else 0.0
    V = lambda f, d: p.op("dve", f, d)
    t1 = V(lambda e: e.tensor_scalar(out=tmp_t, in0=x_t, scalar1=off, scalar2=1.0 / TWO_PI, op0=ALU.add, op1=ALU.mult), deps)
    t2 = V(lambda e: e.tensor_copy(out=ki_t, in_=tmp_t), [t1])
    t3 = V(lambda e: e.tensor_copy(out=kf_t, in_=ki_t), [t2])
    t4 = V(lambda e: e.tensor_scalar(out=tmp_t, in0=x_t, scalar1=off, scalar2=None, op0=ALU.add), [t3])
    t5 = V(lambda e: e.scalar_tensor_tensor(out=tmp_t, in0=kf_t, scalar=-TWO_PI, in1=tmp_t, op0=ALU.mult, op1=ALU.add), [t4])
    t6 = V(lambda e: e.tensor_scalar(out=kf_t, in0=tmp_t, scalar1=math.pi, scalar2=-TWO_PI, op0=ALU.is_gt, op1=ALU.mult), [t5])
    t7 = V(lambda e: e.tensor_tensor(out=tmp_t, in0=tmp_t, in1=kf_t, op=ALU.add), [t6])
    t8 = V(lambda e: e.tensor_scalar(out=kf_t, in0=tmp_t, scalar1=-math.pi, scalar2=TWO_PI, op0=ALU.is_lt, op1=ALU.mult), [t7])
    t9 = V(lambda e: e.tensor_tensor(out=tmp_t, in0=tmp_t, in1=kf_t, op=ALU.add), [t8])
    return p.op("act", lambda e: e.activation(out=out_t, in_=tmp_t, func=AF.Sin), [t9])


def emit_scan(p, seqs, I, gU_rows, uidx_d, pubY, HF_d, ident_d):
    V = lambda f, deps=(): p.op("dve", f, list(deps))
    cst = p.sb([128, 4], F32); tvb = p.sb([128, 129], F32); tri = p.sb([128, 128], BF16)
    identf = p.sb([128, 128], F32)
    lmb = p.sb([128, 1024], F32); anb = p.sb([128, 1024], F32); dtb = p.sb([128, 1024], F32)
    lmT = p.sb([128, 16], F32); anT = p.sb([128, 16], F32); dtT = p.sb([128, 16], F32)
    BDA = p.sb([128, 2, 1024], BF16); BDB = p.sb([128, 2, 1024], BF16)
    CcP = p.sb([128, 2, 1024], BF16); CcPf = p.sb([128, 2, 1024], F32)
    Ccc = p.sb([128, 256], F32); dv = p.sb([128, 2], F32)
    Pa = p.sb([128, 16, 128], F32); Pb = p.sb([128, 16, 128], F32)
    Qa = p.sb([128, 16, 129], F32); Qb = p.sb([128, 16, 129], F32)
    Qa16 = p.sb([128, 16, 129], BF16); Qb16 = p.sb([128, 16, 129], BF16)
    q1 = p.sb([128, 16 * 129], F32); q2 = p.sb([128, 16 * 129], F32); q3 = p.sb([128, 16 * 129], F32)
    q4 = p.sb([128, 16 * 129], F32); qi_ = p.sb([128, 16 * 129], I32); kfq = p.sb([128, 16 * 129], F32)
    mask = p.sb([128, 8, 64], F32)
    s1 = q1[:, 0:1024]; s2 = q2[:, 0:1024]; s3 = q3[:, 0:1024]; s4 = q4[:, 0:1024]; si = qi_[:, 0:1024]; sk = kfq[:, 0:1024]
    U = p.sb([128, 2, 4, 1152], F32)
    uidx = p.sb([128, 8], I32)

    d_c = p.dma("sp", "c0", cst[:], I["cst"])
    d_tv = p.dma("sp", "c1", tvb[:], I["tvec"].partition_broadcast(128))
    d_tri = p.dma("pool", "c2", tri[:], I["tri"])
    d_id = p.dma("sp", "c3", identf[:], ident_d)
    d_lm = p.dma("sp", "c4", lmb[:], I["lre_row"].partition_broadcast(128))
    d_an = p.dma("sp", "c5", anb[:], I["lim_row"].partition_broadcast(128))
    d_dt = p.dma("sp", "c6", dtb[:], I["ldt_row"].partition_broadcast(128))
    d_lmT = p.dma("sp", "c7", lmT[:], I["lreT2"])
    d_anT = p.dma("sp", "c8", anT[:], I["limT2"])
    d_dtT = p.dma("sp", "c9", dtT[:], I["ldtT2"])
    d_ccp = p.dma("sp", "c10", CcPf[:], I["CcP"].rearrange("c p n -> p c n"))
    d_ccc = p.dma("sp", "c11", Ccc[:], I["Ccc"])
    d_dv = p.dma("sp", "c12", dv[:], I["dvec"])
    d_mk = p.dma("sp", "c13", mask[:], I["mask"])
    d_ui = p.dma("sp", "c14", uidx[:], uidx_d)
    u_toks = []
    for ck in range(2):
        for r in range(4):
            col = ck * 4 + r
            u_toks.append(p.lane_op("pool", f"ug{col}", lambda e, ck=ck, r=r, col=col: e.indirect_dma_start(
                out=U[:, ck, r, :], out_offset=None, in_=gU_rows,
                in_offset=bass.IndirectOffsetOnAxis(ap=uidx[:, col:col + 1], axis=0)), [d_ui]))
    def disc(lm, an, dt, deps):
        a = p.op("act", lambda e: e.activation(out=dt, in_=dt, func=AF.Exp), deps)
        b = V(lambda e: e.tensor_scalar(out=lm, in0=lm, scalar1=-1e-4, scalar2=None, op0=ALU.min), deps)
        c = V(lambda e: e.tensor_tensor(out=lm, in0=lm, in1=dt, op=ALU.mult), [a, b])
        d = V(lambda e: e.tensor_tensor(out=an, in0=an, in1=dt, op=ALU.mult), [c])
        return d
    t_row = disc(lmb[:], anb[:], dtb[:], [d_lm, d_an, d_dt])
    t_T = disc(lmT[:], anT[:], dtT[:], [d_lmT, d_anT, d_dtT])
    t_ccp = V(lambda e: e.tensor_scalar(out=CcP[:], in0=CcPf[:], scalar1=cst[:, 3:4], scalar2=None, op0=ALU.mult), [d_ccp, d_c])
    t_ccc = V(lambda e: e.tensor_scalar(out=Ccc[:], in0=Ccc[:], scalar1=cst[:, 3:4], scalar2=None, op0=ALU.mult), [d_ccc, d_c])
    G = [p.sb([128, 64], F32) for _ in range(16)]
    gi_i = p.sb([128, 64], I32)
    t_bd = []
    for ck in range(2):
        lre, lim, ldt, bre, bim, mag, cc, ss, ar1, aim, den, cre, cim, t1_, t2_, kf_ = [g[:] for g in G]
        dd = [p.dma("sp", f"g{n_}", t_, I[nm][ck], t_bd) for n_, (t_, nm) in enumerate(
            [(lre, "lre_gi"), (lim, "lim_gi"), (ldt, "ldt_gi"), (bre, "bre_gi"), (bim, "bim_gi")])]
        a = p.op("act", lambda e: e.activation(out=ldt, in_=ldt, func=AF.Exp), dd)
        b = V(lambda e: e.tensor_scalar(out=lre, in0=lre, scalar1=-1e-4, scalar2=None, op0=ALU.min), dd)
        c1 = V(lambda e: e.tensor_tensor(out=t1_, in0=lre, in1=ldt, op=ALU.mult), [a, b])
        c2 = V(lambda e: e.tensor_tensor(out=t2_, in0=lim, in1=ldt, op=ALU.mult), [c1])
        m = p.op("act", lambda e: e.activation(out=mag, in_=t1_, func=AF.Exp), [c1])
        ts = _sincos(p, ss, t2_, den, [c2, m], False, gi_i[:], kf_)
        tc = _sincos(p, cc, t2_, den, [ts], True, gi_i[:], kf_)
        x1 = V(lambda e: e.tensor_tensor(out=ar1, in0=mag, in1=cc, op=ALU.mult), [tc])
        x1 = V(lambda e: e.tensor_scalar(out=ar1, in0=ar1, scalar1=-1.0, scalar2=None, op0=ALU.add), [x1])
        x2 = V(lambda e: e.tensor_tensor(out=aim, in0=mag, in1=ss, op=ALU.mult), [x1])
        y1 = V(lambda e: e.tensor_tensor(out=den, in0=lre, in1=lre, op=ALU.mult), [x2])
        y2 = V(lambda e: e.tensor_tensor(out=t1_, in0=lim, in1=lim, op=ALU.mult), [y1])
        y3 = V(lambda e: e.tensor_tensor(out=den, in0=den, in1=t1_, op=ALU.add), [y2])
        y4 = V(lambda e: e.reciprocal(out=den, in_=den), [y3])
        z1 = V(lambda e: e.tensor_tensor(out=cre, in0=ar1, in1=lre, op=ALU.mult), [y4])
        z2 = V(lambda e: e.tensor_tensor(out=t1_, in0=aim, in1=lim, op=ALU.mult), [z1])
        z3 = V(lambda e: e.tensor_tensor(out=cre, in0=cre, in1=t1_, op=ALU.add), [z2])
        z4 = V(lambda e: e.tensor_tensor(out=cre, in0=cre, in1=den, op=ALU.mult), [z3])
        w1 = V(lambda e: e.tensor_tensor(out=cim, in0=aim, in1=lre, op=ALU.mult), [z4])
        w2 = V(lambda e: e.tensor_tensor(out=t1_, in0=ar1, in1=lim, op=ALU.mult), [w1])
        w3 = V(lambda e: e.tensor_tensor(out=cim, in0=cim, in1=t1_, op=ALU.subtract), [w2])
        w4 = V(lambda e: e.tensor_tensor(out=cim, in0=cim, in1=den, op=ALU.mult), [w3])
        q_1 = V(lambda e: e.tensor_tensor(out=t1_, in0=cre, in1=bre, op=ALU.mult), [w4])
        q_2 = V(lambda e: e.tensor_tensor(out=t2_, in0=cim, in1=bim, op=ALU.mult), [q_1])
        q_3 = V(lambda e: e.tensor_tensor(out=t1_, in0=t1_, in1=t2_, op=ALU.subtract), [q_2])
        q_4 = V(lambda e: e.tensor_tensor(out=t2_, in0=cre, in1=bim, op=ALU.mult), [q_3])
        q_5 = V(lambda e: e.tensor_tensor(out=mag, in0=cim, in1=bre, op=ALU.mult), [q_4])
        q_6 = V(lambda e: e.tensor_tensor(out=t2_, in0=t2_, in1=mag, op=ALU.add), [q_5])
        bcv = lambda t: t.unsqueeze(1).to_broadcast([128, 8, 64])
        bdv = lambda T_, ck=ck: T_[:, ck, :].rearrange("p (g n) -> p g n", n=128)
        r1 = V(lambda e, bdv=bdv: e.tensor_tensor(out=bdv(BDA)[:, :, 0:64], in0=mask[:], in1=bcv(t1_), op=ALU.mult), [q_6, d_mk])
        r2 = V(lambda e, bdv=bdv: e.tensor_tensor(out=bdv(BDA)[:, :, 64:128], in0=mask[:], in1=bcv(t2_), op=ALU.mult), [r1])
        r3 = V(lambda e, bdv=bdv: e.tensor_tensor(out=bdv(BDB)[:, :, 0:64], in0=mask[:], in1=bcv(t2_), op=ALU.mult), [r2])
        r4 = V(lambda e, bdv=bdv: e.tensor_tensor(out=bdv(BDB)[:, :, 64:128], in0=mask[:], in1=bcv(t1_), op=ALU.mult), [r3])
        t_bd = [r4]
    a1 = V(lambda e: e.tensor_scalar(out=s1, in0=lmb[:], scalar1=cst[:, 1:2], scalar2=None, op0=ALU.mult), [t_row, d_c])
    a2 = p.op("act", lambda e: e.activation(out=s1, in_=s1, func=AF.Exp), [a1])
    a3 = V(lambda e: e.tensor_scalar(out=s2, in0=anb[:], scalar1=cst[:, 0:1], scalar2=None, op0=ALU.mult), [t_row, d_c])
    a4 = _sincos(p, s3, s2, s4, [a3], True, si, sk)
    a5 = V(lambda e: e.tensor_tensor(out=s3, in0=s3, in1=s1, op=ALU.mult), [a4, a2])
    s1v = lambda t: t.rearrange("p (g n) -> p g n", n=64)
    a6 = V(lambda e: e.tensor_copy(out=Pa[:, :, 0:64], in_=s1v(s3)), [a5])
    a7 = V(lambda e: e.tensor_copy(out=Pa[:, :, 64:128], in_=s1v(s3)), [a6])
    a8 = _sincos(p, s3, s2, s4, [a7], False, si, sk)
    a9 = V(lambda e: e.tensor_tensor(out=s3, in0=s3, in1=s1, op=ALU.mult), [a8])
    a10 = V(lambda e: e.tensor_copy(out=Pb[:, :, 0:64], in_=s1v(s3)), [a9])
    a11 = V(lambda e: e.tensor_scalar(out=Pb[:, :, 64:128], in0=s1v(s3), scalar1=-1.0, scalar2=None, op0=ALU.mult), [a10])
    qv = lambda t: t[:, :].rearrange("p (g t) -> p g t", t=129)
    b1 = V(lambda e: e.tensor_tensor(out=qv(q1), in0=tvb[:, :].unsqueeze(1).to_broadcast([128, 16, 129]),
                                     in1=lmT[:, :].unsqueeze(2).to_broadcast([128, 16, 129]), op=ALU.mult), [d_tv, t_T, a11])
    b2 = p.op("act", lambda e: e.activation(out=q1[:], in_=q1[:], func=AF.Exp), [b1])
    b3 = V(lambda e: e.tensor_tensor(out=qv(q2), in0=tvb[:, :].unsqueeze(1).to_broadcast([128, 16, 129]),
                                     in1=anT[:, :].unsqueeze(2).to_broadcast([128, 16, 129]), op=ALU.mult), [d_tv, t_T])
    b4 = _sincos(p, q3[:], q2[:], q4[:], [b3], True, qi_[:], kfq[:])
    b5 = V(lambda e: e.tensor_tensor(out=Qa[:, :, :], in0=qv(q3), in1=qv(q1), op=ALU.mult), [b4, b2])
    b6 = _sincos(p, q3[:], q2[:], q4[:], [b5], False, qi_[:], kfq[:])
    b7 = V(lambda e: e.tensor_tensor(out=q3[:], in0=q3[:], in1=q1[:], op=ALU.mult), [b6])
    b8 = V(lambda e: e.tensor_scalar(out=Qb[:, :, :], in0=qv(q3), scalar1=cst[:, 2:3], scalar2=None, op0=ALU.mult), [b7, d_c])
    b9 = V(lambda e: e.tensor_copy(out=Qa16[:], in_=Qa[:]), [b5])
    b10 = V(lambda e: e.tensor_copy(out=Qb16[:], in_=Qb[:]), [b8])
    tabs = [a7, a11, b9, b10, t_ccp, t_ccc, d_tri, d_dv, d_id] + t_bd + u_toks

    ub = Ring([p.sb([128, 128], BF16) for _ in range(3)])
    banks = Ring([p.ps([128, 512], F32) for _ in range(6)])
    ps_y = p.ps([128, 512], F32)
    ps_t = p.ps([128, 512], F32)
    t1b = Ring([p.sb([128, 512], F32) for _ in range(2)]); t2b = Ring([p.sb([128, 512], F32) for _ in range(2)])
    Vb = Ring([p.sb([128, 512], BF16) for _ in range(2)])
    x1b = Ring([p.sb([128, 4, 128], F32) for _ in range(2)]); x2b = Ring([p.sb([128, 4, 128], F32) for _ in range(2)])
    Xb = Ring([p.sb([128, 8, 128], BF16) for _ in range(2)])
    PadA = Ring([p.sb([128, 8, 128], BF16) for _ in range(2)]); PadB = Ring([p.sb([128, 8, 128], BF16) for _ in range(2)])
    yo = Ring([p.sb([128, 128], F32) for _ in range(3)])
    yr = Ring([p.sb([128, 128], F32) for _ in range(3)])
    EA = p.sb([128, 16], F32); EB = p.sb([128, 16], F32); e1t = p.sb([128, 16], F32)
    HA = [p.sb([128, 16], F32) for _ in range(2)]; HB = [p.sb([128, 16], F32) for _ in range(2)]
    n1 = p.sb([128, 16], F32); n2 = p.sb([128, 16], F32)
    t_z = [V(lambda e, pad=pad: e.memset(pad[:], 0.0)) for pad in PadA.bufs + PadB.bufs]
    hcur = 0
    t_Hread = []; t_E_read = []; outs = []; y_free = []; tr_free = []
    for si_, (blocks, init) in enumerate(seqs):
        if init is not None:
            ta = p.dma("sp", "h0a", HA[hcur][:], I["H0A"][init], t_Hread)
            tb_ = p.dma("sp", "h0b", HB[hcur][:], I["H0B"][init], t_Hread)
        else:
            ta = V(lambda e, h=HA[hcur]: e.memset(h[:], 0.0), t_Hread)
            tb_ = V(lambda e, h=HB[hcur]: e.memset(h[:], 0.0), t_Hread)
        t_H = [ta, tb_]
        for (rk, col0, L, yrow0) in blocks:
            e_toks = []; pad_readers = []
            for ck in range(2):
                uft = U[:, ck, rk, col0:col0 + L]
                kb_, ubt, dub = ub.next()
                t_ub = p.op("act", lambda e, ubt=ubt, uft=uft: e.copy(out=ubt[:, 0:L], in_=uft), tabs + dub)
                kx, Xt, dX = Xb.next()
                x_toks = []
                for hc in range(2):
                    gg0 = ck * 8 + hc * 4
                    rows = slice(hc * 64, (hc + 1) * 64)
                    cols = slice(hc * 512, (hc + 1) * 512)
                    kA, bA, dA = banks.next()
                    mA = p.op("pe", lambda e, bA=bA, ubt=ubt, rows=rows, cols=cols, ck=ck: e.matmul(
                        bA[0:L, :], lhsT=ubt[rows, 0:L], rhs=BDA[rows, ck, cols], start=True, stop=True), [t_ub] + dA)
                    kB, bB, dB = banks.next()
                    mB = p.op("pe", lambda e, bB=bB, ubt=ubt, rows=rows, cols=cols, ck=ck: e.matmul(
                        bB[0:L, :], lhsT=ubt[rows, 0:L], rhs=BDB[rows, ck, cols], start=True, stop=True), [t_ub] + dB)
                    k1, t1t, d1_ = t1b.next(); k2, t2t, d2_ = t2b.next(); kv, Vt, dV = Vb.next()
                    pv = lambda t, gg0=gg0: t[0:L, gg0:gg0 + 4, :]
                    v3 = lambda t: t[0:L, :].rearrange("p (g n) -> p g n", n=128)
                    o1 = V(lambda e, t1t=t1t, bA=bA, pv=pv, v3=v3: e.tensor_tensor(out=v3(t1t), in0=v3(bA), in1=pv(Pa), op=ALU.mult), [mA] + d1_)
                    o2 = V(lambda e, t2t=t2t, bB=bB, pv=pv, v3=v3: e.tensor_tensor(out=v3(t2t), in0=v3(bB), in1=pv(Pb), op=ALU.mult), [mB] + d2_)
                    banks.done(kA, o1); banks.done(kB, o2)
                    o3 = V(lambda e, Vt=Vt, t1t=t1t, t2t=t2t: e.tensor_tensor(out=Vt[0:L, :], in0=t1t[0:L, :], in1=t2t[0:L, :], op=ALU.add), [o1, o2] + dV)
                    t1b.done(k1, o3); t2b.done(k2, o3)
                    kcA, cA, dcA = banks.next(); kcB, cB, dcB = banks.next()
                    mc = None
                    for g in range(4):
                        mc = p.op("pe", lambda e, cA=cA, Vt=Vt, g=g: e.matmul(
                            cA[:, g * 128:g * 128 + L], lhsT=Vt[0:L, g * 128:(g + 1) * 128], rhs=tri[0:L, 0:L],
                            start=True, stop=True), ([o3] + dcA + dcB) if g == 0 else [], inc=False)
                        mc = p.op("pe", lambda e, cB=cB, Vt=Vt, g=g: e.matmul(
                            cB[0:64, g * 128:g * 128 + L], lhsT=Vt[0:L, g * 128 + 64:(g + 1) * 128], rhs=tri[0:L, 0:L],
                            start=True, stop=True), [], inc=False)
                        mc = p.op("pe", lambda e, cB=cB, Vt=Vt, g=g: e.matmul(
                            cB[64:128, g * 128:g * 128 + L], lhsT=Vt[0:L, g * 128:g * 128 + 64], rhs=tri[0:L, 0:L],
                            start=True, stop=True), [], inc=(g == 3))
                    Vb.done(kv, mc)
                    c3 = lambda t: t[:, :].rearrange("p (g n) -> p g n", n=128)[:, :, 0:L]
                    qv_ = lambda t, gg0=gg0: t[:, gg0:gg0 + 4, 0:L]
                    kx1, x1t, dx1 = x1b.next(); kx2, x2t, dx2 = x2b.next()
                    r1 = V(lambda e, x1t=x1t, cA=cA, c3=c3, qv_=qv_: e.tensor_tensor(out=x1t[:, :, 0:L], in0=c3(cA), in1=qv_(Qa), op=ALU.mult), [mc] + dx1)
                    r2 = V(lambda e, x2t=x2t, cB=cB, c3=c3, qv_=qv_: e.tensor_tensor(out=x2t[:, :, 0:L], in0=c3(cB), in1=qv_(Qb), op=ALU.mult), [mc] + dx2)
                    r3 = V(lambda e, Xt=Xt, x1t=x1t, x2t=x2t, hc=hc: e.tensor_tensor(
                        out=Xt[:, hc * 4:(hc + 1) * 4, 0:L], in0=x1t[:, :, 0:L], in1=x2t[:, :, 0:L], op=ALU.add), [r1, r2] + (dX if hc == 0 else []))
                    r4 = V(lambda e, x1t=x1t, x2t=x2t, gg0=gg0: e.tensor_tensor(
                        out=EA[:, gg0:gg0 + 4], in0=x1t[:, :, L - 1], in1=x2t[:, :, L - 1], op=ALU.add), [r1, r2] + t_E_read)
                    r5 = V(lambda e, cB=cB, gg0=gg0: e.tensor_tensor(
                        out=e1t[:, gg0:gg0 + 4], in0=cB[:, :].rearrange("p (g n) -> p g n", n=128)[:, :, L - 1],
                        in1=Qa[:, gg0:gg0 + 4, L - 1], op=ALU.mult), [mc] + t_E_read)
                    r6 = V(lambda e, cA=cA, gg0=gg0: e.tensor_tensor(
                        out=EB[:, gg0:gg0 + 4], in0=cA[:, :].rearrange("p (g n) -> p g n", n=128)[:, :, L - 1],
                        in1=Qb[:, gg0:gg0 + 4, L - 1], op=ALU.mult), [mc] + t_E_read)
                    r7 = V(lambda e, gg0=gg0: e.tensor_tensor(
                        out=EB[:, gg0:gg0 + 4], in0=e1t[:, gg0:gg0 + 4], in1=EB[:, gg0:gg0 + 4], op=ALU.subtract), [r5, r6])
                    banks.done(kcA, r1, r6); banks.done(kcB, r2, r5)
                    x1b.done(kx1, r3, r4); x2b.done(kx2, r3, r4)
                    x_toks.append(r3)
                    e_toks += [r4, r7]
                ub.done(kb_, mB)
                kpa, pa, dpa = PadA.next(); kpb, pbt, dpb = PadB.next()
                tp = None
                for g in range(8):
                    Gx = ck * 8 + g
                    tp = V(lambda e, pa=pa, g=g, Gx=Gx, h=HA[hcur]: e.tensor_scalar(
                        out=pa[:, g, g * 16:(g + 1) * 16], in0=Ccc[:, Gx * 16:(Gx + 1) * 16], scalar1=h[:, Gx:Gx + 1],
                        scalar2=None, op0=ALU.mult), (t_H + [t_ccc] + dpa + t_z) if g == 0 else [])
                    tp = V(lambda e, pbt=pbt, g=g, Gx=Gx, h=HB[hcur]: e.tensor_scalar(
                        out=pbt[:, g, g * 16:(g + 1) * 16], in0=Ccc[:, Gx * 16:(Gx + 1) * 16], scalar1=h[:, Gx:Gx + 1],
                        scalar2=None, op0=ALU.mult), dpb if g == 0 else [])
                my = None
                for g in range(8):
                    Gx = ck * 8 + g
                    my = p.op("pe", lambda e, Xt=Xt, g=g, ck=ck: e.matmul(
                        ps_y[:, 0:L], lhsT=CcP[:, ck, g * 128:(g + 1) * 128], rhs=Xt[:, g, 0:L], start=(g == 0), stop=False),
                        (x_toks + [tp] + y_free) if g == 0 else [], inc=False)
                    my = p.op("pe", lambda e, pa=pa, g=g, Gx=Gx: e.matmul(
                        ps_y[:, 0:L], lhsT=pa[:, g, :], rhs=Qa16[:, Gx, 1:L + 1], start=False, stop=False), [], inc=False)
                    my = p.op("pe", lambda e, pbt=pbt, g=g, Gx=Gx: e.matmul(
                        ps_y[:, 0:L], lhsT=pbt[:, g, :], rhs=Qb16[:, Gx, 1:L + 1], start=False, stop=(g == 7)), [], inc=(g == 7))
                Xb.done(kx, my); PadA.done(kpa, my); PadB.done(kpb, my)
                pad_readers.append(tp)
                ko, yot, dyo = yo.next()
                ty = V(lambda e, yot=yot, uft=uft, ck=ck: e.scalar_tensor_tensor(
                    out=yot[:, 0:L], in0=uft, scalar=dv[:, ck:ck + 1], in1=ps_y[:, 0:L], op0=ALU.mult, op1=ALU.add), [my] + dyo)
                y_free = [ty]
                ttr = p.op("pe", lambda e, yot=yot: e.transpose(out=ps_t[0:L, 0:128], in_=yot[:, 0:L], identity=identf[:]),
                           [ty] + tr_free)
                kyr, yrt, dyr = yr.next()
                tcp = p.op("act", lambda e, yrt=yrt: e.copy(out=yrt[0:L, :], in_=ps_t[0:L, 0:128]), [ttr] + dyr)
                tr_free = [tcp]
                yo.done(ko, ttr)
                tst = p.dma("act", f"y{kyr}", pubY(yrow0, ck, L), yrt[0:L, :], [tcp])
                yr.done(kyr, tst)
                outs.append(tst)
            hn = 1 - hcur
            QaL = Qa[:, :, L]; QbL = Qb[:, :, L]
            dep0 = t_H + e_toks + pad_readers + t_Hread
            u1 = V(lambda e, h=HA[hcur], QaL=QaL: e.tensor_tensor(out=n1[:], in0=h[:], in1=QaL, op=ALU.mult), dep0)
            u2 = V(lambda e, h=HB[hcur], QbL=QbL: e.tensor_tensor(out=n2[:], in0=h[:], in1=QbL, op=ALU.mult), [u1])
            u3 = V(lambda e: e.tensor_tensor(out=n1[:], in0=n1[:], in1=n2[:], op=ALU.add), [u2])
            u4 = V(lambda e, h=HA[hn]: e.tensor_tensor(out=h[:], in0=n1[:], in1=EA[:], op=ALU.add), [u3])
            u5 = V(lambda e, h=HB[hcur], QaL=QaL: e.tensor_tensor(out=n1[:], in0=h[:], in1=QaL, op=ALU.mult), [u4])
            u6 = V(lambda e, h=HA[hcur], QbL=QbL: e.tensor_tensor(out=n2[:], in0=h[:], in1=QbL, op=ALU.mult), [u5])
            u7 = V(lambda e: e.tensor_tensor(out=n1[:], in0=n1[:], in1=n2[:], op=ALU.subtract), [u6])
            u8 = V(lambda e, h=HB[hn]: e.tensor_tensor(out=h[:], in0=n1[:], in1=EB[:], op=ALU.add), [u7])
            t_E_read = [u8]; t_Hread = [u8]; t_H = [u4, u8]
            hcur = hn
        tf = p.dma("sp", "hf", HF_d[si_], HA[hcur][:], t_H)
        outs.append(tf)
        t_Hread = t_Hread + [tf]
    return outs[-8:]
R_ = 1152
RT_ = 9


def _zz_block(k, j):
    m = j // 2
    return 8 * m + k if j % 2 == 0 else 8 * m + 7 - k


def _zz_owner(S):
    m, x = S // 8, S % 8
    return (x, 2 * m) if x <= 3 else (7 - x, 2 * m + 1)


P_SMAX = [8 * (j // 2) + 4 if j % 2 == 0 else 8 * (j // 2) + 8 for j in range(8)]


def emit_rmsout(p, x, nw, y, eps=1e-6):
    D = 2048
    nwb = p.sb([128, D], F32)
    t_nw = p.dma("sp", "nw", nwb[:], nw.partition_broadcast(128))
    xb = Ring([p.sb([128, D], F32) for _ in range(2)])
    tf = Ring([p.sb([128, D], F32) for _ in range(2)])
    st = Ring([p.sb([128, 2], F32) for _ in range(2)])
    fin = []
    for r in range(RT_):
        rows = slice(r * 128, (r + 1) * 128)
        kx, xt, dx = xb.next()
        t_x = p.dma("sp", f"x{kx}", xt[:], x[rows, :], dx)
        kt, tt, dt_ = tf.next()
        ks, s_, ds = st.next()
        t_sq = p.op("act", lambda e, tt=tt, xt=xt, s_=s_: e.activation(out=tt[:], in_=xt[:], func=AF.Square, accum_out=s_[:, 0:1]), [t_x] + dt_ + ds)
        t1 = p.op("dve", lambda e, s_=s_: e.tensor_scalar(out=s_[:, 1:2], in0=s_[:, 0:1], scalar1=1.0 / D, scalar2=eps, op0=ALU.mult, op1=ALU.add), [t_sq])
        t2 = p.op("act", lambda e, s_=s_: e.activation(out=s_[:, 1:2], in_=s_[:, 1:2], func=AF.Sqrt), [t1])
        t3 = p.op("dve", lambda e, s_=s_: e.reciprocal(out=s_[:, 1:2], in_=s_[:, 1:2]), [t2])
        t4 = p.op("dve", lambda e, tt=tt, xt=xt, s_=s_: e.scalar_tensor_tensor(out=tt[:], in0=xt[:], scalar=s_[:, 1:2], in1=nwb[:], op0=ALU.mult, op1=ALU.mult), [t3, t_nw])
        xb.done(kx, t4); st.done(ks, t4)
        t5 = p.dma("act", f"o{kt}", y[rows, :], tt[:], [t4])
        tf.done(kt, t5)
        fin.append(t5)
    return fin[-2:]


def emit_gather(p, ck_d, cv_d, ci_d, pt_d, Kp, Vp, ip):
    pt = p.sb([128, 1], I32)
    idx = p.sb([128, 8], I32)
    bufs = Ring([p.sb([128, 8192], F32) for _ in range(3)])
    t_pt = p.dma("sp", "pt", pt[:], pt_d)
    t_i = None
    for e_ in range(8):
        t_i = p.op("dve", lambda e, e_=e_: e.tensor_scalar(
            out=idx[:, e_:e_ + 1], in0=pt[:], scalar1=8, scalar2=e_, op0=ALU.mult, op1=ALU.add), [t_pt])
    outs = []
    jobs = [(ci_d, pt, 0, ip)] + [(ck_d, idx, e_, Kp[:, e_, :]) for e_ in range(8)] + \
           [(cv_d, idx, e_, Vp[:, e_, :]) for e_ in range(8)]
    for n, (src, it, col, dst) in enumerate(jobs):
        kb, bt, db = bufs.next()
        tok = p.lane_op("pool", f"g{kb}", lambda e, bt=bt, src=src, it=it, col=col: e.indirect_dma_start(
            out=bt[:], out_offset=None, in_=src, in_offset=bass.IndirectOffsetOnAxis(ap=it[:, col:col + 1], axis=0)),
            [t_i, t_pt] + db)
        t_o = p.dma("sp", f"go{kb}", dst, bt[:], [tok])
        bufs.done(kb, t_o)
        outs.append(t_o)
    return outs[-3:]


class PromptSrc:
    def __init__(self, p, qT_s, qiT_s, wT_s, gKT, gV, gKi, idxK_d, idxI_d):
        self.p = p
        self.qT_s, self.qiT_s, self.wT_s, self.gKT, self.gV, self.gKi = qT_s, qiT_s, wT_s, gKT, gV, gKi
        self.idxK = p.sb([128, 4 * 144], I32)
        self.idxI = p.sb([128, 144], I32)
        self.t_ik = p.dma("sp", "ik", self.idxK[:], idxK_d)
        self.t_ii = p.dma("sp", "ii", self.idxI[:], idxI_d)
        self.u0 = [sum(P_SMAX[:j]) for j in range(8)]
        self.n = 0

    def load_q(self, j, t, deps):
        return self.p.dma("pool", "lq", t[:, :].rearrange("p (h t) -> p h t", h=16),
                          self.qT_s[:, j * 128:(j + 1) * 128].rearrange("(h d) t -> d h t", d=128), deps)

    def load_qi(self, j, t, deps):
        return self.p.dma("pool", "lqi", t[:, :].rearrange("p (h t) -> p h t", h=16),
                          self.qiT_s[:, j * 128:(j + 1) * 128].rearrange("(h d) t -> d h t", d=64), deps)

    def load_w(self, j, t, deps):
        return self.p.dma("sp", "lw", t[:, :].rearrange("p (h t) -> p h t", h=16),
                          self.wT_s[:, j * 128:(j + 1) * 128].partition_broadcast(128), deps)

    def _ind(self, t, table, idx_ap, deps, dep2):
        self.n += 1
        return self.p.lane_op("pool", f"in{self.n % 6}", lambda e: e.indirect_dma_start(
            out=t[:, :], out_offset=None, in_=table, in_offset=bass.IndirectOffsetOnAxis(ap=idx_ap, axis=0)),
            list(deps) + [dep2])

    def load_ki(self, j, dl, t, deps):
        u = self.u0[j] + dl
        return self._ind(t, self.gKi, self.idxI[:, u:u + 1], deps, self.t_ii)

    def load_k(self, j, g, dl, t, deps):
        u = self.u0[j] + dl
        return self._ind(t, self.gKT, self.idxK[:, g * 144 + u:g * 144 + u + 1], deps, self.t_ik)

    def load_v(self, j, g, dl, t, deps):
        u = self.u0[j] + dl
        return self._ind(t, self.gV, self.idxK[:, g * 144 + u:g * 144 + u + 1], deps, self.t_ik)


class SampleSrc:
    def __init__(self, p, qT_s, qiT_s, wT_s, kT_s, kiT_s, z, KTs, kiTs, Vp):
        self.p = p
        self.qT_s, self.qiT_s, self.wT_s, self.kT_s, self.kiT_s, self.z = qT_s, qiT_s, wT_s, kT_s, kiT_s, z
        self.KTs, self.kiTs, self.Vp = KTs, kiTs, Vp
        self.c = slice(1024, 1028)

    def load_q(self, j, t, deps):
        return self.p.dma("pool", "lq", t[:, :].rearrange("p (h t) -> p h t", h=16),
                          self.qT_s[:, self.c].rearrange("(h d) t -> d h t", d=128), deps)

    def load_qi(self, j, t, deps):
        return self.p.dma("pool", "lqi", t[:, :].rearrange("p (h t) -> p h t", h=16),
                          self.qiT_s[:, self.c].rearrange("(h d) t -> d h t", d=64), deps)

    def load_w(self, j, t, deps):
        return self.p.dma("sp", "lw", t[:, :].rearrange("p (h t) -> p h t", h=16),
                          self.wT_s[:, self.c].partition_broadcast(128), deps)

    def _new(self, t, dst, src, deps):
        z_ = self.p.op("dve", lambda e: e.memset(t[:, :], 0.0), deps)
        return self.p.dma("pool", "ln", dst, src, [z_])

    def load_ki(self, j, dl, t, deps):
        if dl == 0:
            return self._new(t, t[0:64, 0:4], self.kiT_s[0:64, self.c], deps)
        pg = 128 - dl
        return self.p.dma("pool", "lki", t[0:64, :], self.kiTs[:, pg * 128:(pg + 1) * 128], deps)

    def load_k(self, j, g, dl, t, deps):
        if dl == 0:
            return self._new(t, t[:, 0:4], self.kT_s[g * 128:(g + 1) * 128, self.c], deps)
        pg = 128 - dl
        return self.p.dma("pool", "lk", t[:, :], self.KTs[g * 128:(g + 1) * 128, pg * 128:(pg + 1) * 128], deps)

    def load_v(self, j, g, dl, t, deps):
        if dl == 0:
            return self._new(t, t[0:4, :], self.z[1024:1028, 2560 + g * 128:2560 + (g + 1) * 128], deps)
        pg = 128 - dl
        return self.p.dma("pool", "lv", t[:, :], self.Vp[pg * 128:(pg + 1) * 128, g * 128:(g + 1) * 128], deps)


def build_fused():
    nc = bass.Bass("TRN2", target_bir_lowering=False)
    EI = lambda n, s, dt=F32: nc.dram_tensor(n, list(s), dt, kind="ExternalInput").ap()
    EO = lambda n, s, dt=F32: nc.dram_tensor(n, list(s), dt, kind="ExternalOutput").ap()
    IT = lambda n, s, dt=F32: nc.dram_tensor(n, list(s), dt)
    xrows = EI("xrows", [R_, 2048]); prows = EI("prows", [4, R_, 256])
    norm_w = EI("norm_w", [4, 1, 2048]); ple_nw = EI("ple_nw", [4, 1, 2048]); fnw = EI("fnw", [1, 2048])
    a_win = EI("a_win", [2, 2048, 6224]); a_wout = EI("a_wout", [2, 2048, 2048])
    s_win = EI("s_win", [2, 2048, 4096]); s_wglu = EI("s_wglu", [2, 2048, 2048]); s_bglu = EI("s_bglu", [2, 1, 2048])
    s_wout = EI("s_wout", [2, 2048, 2048]); p_wg = EI("p_wg", [4, 2048, 2048]); p_wp = EI("p_wp", [4, 256, 2048])
    ident = EI("ident", [128, 128])
    ck = [EI(f"ck{l}", [10240, 8192]) for l in range(2)]; cv = [EI(f"cv{l}", [10240, 8192]) for l in range(2)]
    ci = [EI(f"ci{l}", [1280, 8192]) for l in range(2)]
    pt = EI("pt", [128, 1], I32)
    vtp = EI("vtp", [1, 144]); Dp = EI("Dp", [2, 128, 2048]); C0p = EI("C0p", [128, 128])
    vts = EI("vts", [1, 129]); Ds = EI("Ds", [2, 128, 64]); C0s = EI("C0s", [128, 4]); cbv = EI("cbv", [1, 16])
    idxK = EI("idxK", [128, 576], I32); idxI = EI("idxI", [128, 144], I32)
    SP = {}
    for nm, shp in [("lre_row", [1, 1024]), ("lim_row", [1, 1024]), ("ldt_row", [1, 1024]),
                    ("lreT2", [128, 16]), ("limT2", [128, 16]), ("ldtT2", [128, 16]),
                    ("lre_gi", [2, 128, 64]), ("lim_gi", [2, 128, 64]), ("ldt_gi", [2, 128, 64]),
                    ("bre_gi", [2, 128, 64]), ("bim_gi", [2, 128, 64]),
                    ("CcP", [2, 128, 1024]), ("Ccc", [128, 256]), ("dvec", [128, 2]),
                    ("H0A", [4, 128, 16]), ("H0B", [4, 128, 16])]:
        SP[nm] = EI("sp_" + nm, [2, 2] + shp)
    tri = EI("tri", [128, 128]); cst = EI("cst", [128, 4]); tvec = EI("tvec", [1, 129]); mask = EI("mask", [128, 8, 64])
    uidx = EI("uidx", [2, 128, 8], I32); yidx = EI("yidx", [128, 36], I32)
    yout = EO("yout", [R_, 2048]); kvk = EO("kvk", [2, R_, 1088])
    HFp = EO("HFp", [2, 2, 1, 128, 16]); HFs = EO("HFs", [2, 2, 4, 128, 16])

    hA = IT("hA", [R_, 2048]).ap(); hB = IT("hB", [R_, 2048]).ap()
    z = IT("z", [R_, 6224]).ap()
    qT_s = IT("qT_s", [2048, R_]).ap(); kT_s = IT("kT_s", [512, R_]).ap(); qiT_s = IT("qiT_s", [1024, R_]).ap()
    kiT_s = IT("kiT_s", [128, R_]).ap(); wT_s = IT("wT_s", [16, R_]).ap()
    pubKT = IT("pubKT", [4 * 8 * 128, 128]); pubV = IT("pubV", [4 * 8 * 128, 128]); pubKi = IT("pubKi", [8 * 128, 128])
    gKT = IT("gKT", [4 * 4 * 8 * 128, 128]); gV = IT("gV", [4 * 4 * 8 * 128, 128]); gKi = IT("gKi", [4 * 8 * 128, 128])
    oT_s = IT("oT_s", [2048, R_]).ap(); o_s = IT("o_s", [R_, 2048]).ap(); g_s = IT("g_s", [R_, 2048]).ap()
    Kp = IT("Kp", [16384, 512]).ap(); Vp = IT("Vp", [16384, 512]).ap(); ip = IT("ip", [16384, 64]).ap()
    KTs = IT("KTs", [512, 16384]).ap(); kiTs = IT("kiTs", [64, 16384]).ap()
    uT_s = IT("uT_s", [2048, R_]); gU = IT("gU", [4 * 2048, R_])
    pubY = IT("pubY", [4608, 512]); gY = IT("gY", [4 * 4608, 512])
    y_s = IT("y_s", [R_, 2048]).ap(); y3 = IT("y3", [R_, 2048]).ap()
    RG = [[0, 1, 2, 3], [4, 5, 6, 7]]

    def allgather(p, src, dst, deps, CR=None):
        rows = src.ap().shape[0]
        CR = CR or rows
        prev = list(deps)
        for c_ in range(rows // CR):
            cc = p.lane_op("pool", "cc", lambda e, c_=c_: e.collective_compute(
                "AllGather", ALU.bypass, replica_groups=RG, ins=[src.ap()[c_ * CR:(c_ + 1) * CR, :].opt()],
                outs=[dst.ap()[c_ * 4 * CR:(c_ + 1) * 4 * CR, :].opt()]), prev, incv=1)
            prev = [cc]
        return prev[0]

    with ExitStack() as es:
        p = Prog(nc, es)
        h, h1 = None, None
        cur = xrows
        bufs = [hA, hB]
        for i in range(4):
            l = i // 2
            hn1 = bufs[0] if cur is not bufs[0] else bufs[1]
            if i % 2 == 0:
                with p.stage():
                    emit_linear(p, RT_, 2048, 6224, "rms", "none", cur, a_win[l], z, ident, nw=norm_w[i])
                for (src, dst, C) in [(z[:, 0:2048], qT_s, 2048), (z[:, 2048:2560], kT_s, 512),
                                      (z[:, 5120:6144], qiT_s, 1024), (z[:, 6144:6208], kiT_s, 64),
                                      (z[:, 6208:6224], wT_s, 16)]:
                    with p.stage():
                        emit_transpose(p, src, dst, R_, C, ident)
                with p.stage():
                    d0 = p.dma("sp", "k0", kvk[l][:, 0:1024], z[:, 2048:3072])
                    d1 = p.dma("sp", "k1", kvk[l][:, 1024:1088], z[:, 6144:6208])
                    pk = pubKT.ap().rearrange("(g j d) s -> g j d s", g=4, j=8)
                    d2 = [p.dma("act", f"k2{g}", pk[g], kT_s[g * 128:(g + 1) * 128, 0:1024].rearrange("d (j s) -> j d s", s=128))
                          for g in range(4)]
                    pv = pubV.ap().rearrange("(g j s) d -> g j s d", g=4, j=8)
                    d3 = [p.dma("act", f"k3{g}", pv[g], z[0:1024, 2560 + g * 128:2560 + (g + 1) * 128].rearrange("(j s) d -> j s d", s=128))
                          for g in range(4)]
                    d4 = p.dma("sp", "k4", pubKi.ap().rearrange("(j d) s -> j d s", j=8),
                               kiT_s[:, 0:1024].rearrange("d (j s) -> j d s", s=128))
                    c1 = allgather(p, pubKT, gKT, d2, 2048)
                    c2_ = allgather(p, pubV, gV, d3 + [c1], 2048)
                    c3 = allgather(p, pubKi, gKi, [d4, c2_])
                    p.op("sp", None, [d0, d1, c3], inc=False)
                with p.stage():
                    src = PromptSrc(p, qT_s, qiT_s, wT_s, gKT.ap(), gV.ap(), gKi.ap(), idxK, idxI)
                    emit_attn(p, 8, 128, P_SMAX, src, vtp, Dp, cbv, C0p,
                              lambda j, g: oT_s[g * 512:(g + 1) * 512, j * 128:(j + 1) * 128].rearrange("(r d) t -> d r t", d=128))
                with p.stage():
                    emit_gather(p, ck[l], cv[l], ci[l], pt, Kp.rearrange("(pg e s) c -> pg e (s c)", pg=128, e=8),
                                Vp.rearrange("(pg e s) c -> pg e (s c)", pg=128, e=8), ip.rearrange("(pg s) c -> pg (s c)", pg=128))
                with p.stage():
                    emit_transpose(p, Kp, KTs, 16384, 512, ident)
                with p.stage():
                    emit_transpose(p, ip, kiTs, 16384, 64, ident)
                with p.stage():
                    src = SampleSrc(p, qT_s, qiT_s, wT_s, kT_s, kiT_s, z, KTs, kiTs, Vp)
                    emit_attn(p, 1, 4, [129], src, vts, Ds, cbv, C0s,
                              lambda j, g: oT_s[g * 512:(g + 1) * 512, 1024:1028].rearrange("(r d) t -> d r t", d=128))
                with p.stage():
                    emit_transpose(p, oT_s, o_s, 2048, R_, ident)
                with p.stage():
                    emit_linear(p, RT_, 2048, 2048, "gate", "res", o_s, a_wout[l], hn1, ident, x2=z[:, 3072:5120], e1=cur)
            else:
                with p.stage():
                    emit_linear(p, RT_, 2048, 4096, "rms", "none", cur, s_win[l], z[:, 0:4096], ident, nw=norm_w[i])
                with p.stage():
                    emit_transpose(p, z[:, 0:2048], uT_s.ap(), R_, 2048, ident)
                with p.stage():
                    c1 = allgather(p, uT_s, gU, [], 128)
                    p.op("sp", None, [c1], inc=False)
                for c2 in range(2):
                    I = {k_: v_[l, c2] for k_, v_ in SP.items()}
                    I.update({"tri": tri, "cst": cst, "tvec": tvec, "mask": mask})
                    pY = lambda yrow0, ck_, L, c2=c2: pubY.ap()[yrow0:yrow0 + L, c2 * 256 + ck_ * 128:c2 * 256 + (ck_ + 1) * 128]
                    pblocks = []
                    for S in range(32):
                        r, jj = _zz_owner(S)
                        pblocks.append((r, jj * 128, 128, S * 128))
                    with p.stage():
                        emit_scan(p, [(pblocks, None)], I, gU.ap(), uidx[c2], pY, HFp[l, c2], ident)
                    with p.stage():
                        emit_scan(p, [([(i_, 1024, 4, 4096 + 128 * i_)], i_) for i_ in range(4)], I, gU.ap(), uidx[c2], pY,
                                  HFs[l, c2], ident)
                with p.stage():
                    c1 = allgather(p, pubY, gY, [], 512)
                    yix = p.sb([128, 36], I32)
                    t_yi = p.dma("sp", "yi", yix[:], yidx)
                    yb = Ring([p.sb([128, 512], F32) for _ in range(4)])
                    fin = []
                    for jj in range(9):
                        for r in range(4):
                            kb, bt, db = yb.next()
                            col = jj * 4 + r
                            tg = p.lane_op("pool", f"yg{kb}", lambda e, bt=bt, col=col: e.indirect_dma_start(
                                out=bt[:], out_offset=None, in_=gY.ap(),
                                in_offset=bass.IndirectOffsetOnAxis(ap=yix[:, col:col + 1], axis=0)), [c1, t_yi] + db)
                            to = p.dma("sp", f"yo{kb}", y_s[jj * 128:(jj + 1) * 128, r * 512:(r + 1) * 512], bt[:], [tg])
                            yb.done(kb, to)
                            fin.append(to)
                    p.op("sp", None, fin[-4:], inc=False)
                with p.stage():
                    emit_linear(p, RT_, 2048, 2048, "gelu", "glu", y_s, s_wglu[l], y3, ident, e1=y_s, bv=s_bglu[l])
                with p.stage():
                    emit_linear(p, RT_, 2048, 2048, "gate", "res", y3, s_wout[l], hn1, ident, x2=z[:, 2048:4096], e1=cur)
            with p.stage():
                emit_linear(p, RT_, 2048, 2048, "rms", "sigmoid", hn1, p_wg[i], g_s, ident, nw=ple_nw[i])
            hn2 = bufs[0] if hn1 is not bufs[0] else bufs[1]
            with p.stage():
                emit_linear(p, RT_, 256, 2048, "plain", "gres", prows[i], p_wp[i], hn2, ident, e1=hn1, e2=g_s)
            cur = hn2
        with p.stage():
            fin = emit_rmsout(p, cur, fnw, yout)
            p.op("sp", None, fin, inc=False)
        with p.stage():
            for e_ in ("pe", "act", "dve", "pool", "sp"):
                p.op(e_, None, [], inc=False)
    return nc


def _t5_bucket_np(n):
    n = np.asarray(n, dtype=np.int32)
    nf = np.maximum(n, 1).astype(np.float32)
    large = 16 + (np.log(nf / np.float32(16)) / np.float32(math.log(128 / 16)) * np.float32(16)).astype(np.int32)
    large = np.minimum(large, 31)
    return np.where(n < 16, n, large)


def _bias_tiles(rel_bias, NT):
    s_l = np.arange(128)[:, None]
    t_l = np.arange(NT)[None, :]
    d0 = t_l - s_l
    b0 = rel_bias[_t5_bucket_np(np.maximum(d0, 0))]
    b0 = np.where((d0 >= 0)[:, :, None], b0, np.float32(NEG))
    b1 = rel_bias[_t5_bucket_np(128 + d0)]
    D = np.stack([b0, b1]).transpose(0, 1, 3, 2).reshape(2, 128, 16 * NT)
    C0 = np.where(d0 >= 0, np.float32(0), np.float32(NEG)).astype(np.float32)
    return np.ascontiguousarray(D, dtype=np.float32), C0


def _scan_inputs(m, b, k, T, pi, ssm_lambda_re, ssm_lambda_im, ssm_log_dt, ssm_b_re, ssm_b_im, ssm_c_re, ssm_c_im, ssm_d, state_ssm_re, state_ssm_im):
    sp = {nm: [] for nm in ("lre_row", "lim_row", "ldt_row", "lreT2", "limT2", "ldtT2", "lre_gi", "lim_gi", "ldt_gi",
                            "bre_gi", "bim_gi", "CcP", "Ccc", "dvec", "H0A", "H0B")}
    for l in range(2):
        for c2 in range(2):
            g0 = 32 * k + 16 * c2
            gs = slice(g0, g0 + 16)
            lre, lim, ldt = ssm_lambda_re[l][gs], ssm_lambda_im[l][gs], ssm_log_dt[l][gs]
            sp["lre_row"].append(lre.reshape(1, 1024)); sp["lim_row"].append(lim.reshape(1, 1024))
            sp["ldt_row"].append(np.broadcast_to(ldt[:, None], (16, 64)).reshape(1, 1024))
            sp["lreT2"].append(np.concatenate([lre.T, lre.T], 0)); sp["limT2"].append(np.concatenate([lim.T, lim.T], 0))
            sp["ldtT2"].append(np.broadcast_to(ldt[None, :], (128, 16)))
            rep = lambda a: np.stack([np.repeat(a[ck * 8:(ck + 1) * 8], 16, axis=0) for ck in range(2)])
            sp["lre_gi"].append(rep(lre)); sp["lim_gi"].append(rep(lim))
            sp["ldt_gi"].append(rep(np.broadcast_to(ldt[:, None], (16, 64))))
            tb = lambda a: np.stack([a[ck * 8:(ck + 1) * 8].transpose(0, 2, 1).reshape(128, 64) for ck in range(2)])
            sp["bre_gi"].append(tb(ssm_b_re[l][gs])); sp["bim_gi"].append(tb(ssm_b_im[l][gs]))
            CcP = np.zeros((2, 128, 8, 128), np.float32)
            Ccc = np.zeros((128, 16, 16), np.float32)
            for G in range(16):
                ck_, g = G // 8, G % 8
                cc = np.concatenate([ssm_c_re[l][g0 + G].T, ssm_c_im[l][g0 + G].T], axis=0)
                CcP[ck_, :, g, g * 16:(g + 1) * 16] = cc
                Ccc[:, G, :] = cc
            sp["CcP"].append(CcP.reshape(2, 128, 1024)); sp["Ccc"].append(Ccc.reshape(128, 256))
            sp["dvec"].append(ssm_d[l][512 * k + 256 * c2:512 * k + 256 * c2 + 256].reshape(2, 128).T)
            hre = state_ssm_re[l][4 * b:4 * b + 4, gs].transpose(0, 2, 1)
            him = state_ssm_im[l][4 * b:4 * b + 4, gs].transpose(0, 2, 1)
            sp["H0A"].append(np.concatenate([hre, him], 1)); sp["H0B"].append(np.concatenate([him, hre], 1))
    for nm, lst in sp.items():
        a = np.stack([np.ascontiguousarray(x_, dtype=np.float32) for x_ in lst])
        m["sp_" + nm] = np.ascontiguousarray(a.reshape((2, 2) + a.shape[1:]))
    uidx = np.zeros((2, 128, 8), np.int32)
    for c2 in range(2):
        for ck_ in range(2):
            for r in range(4):
                uidx[c2, :, ck_ * 4 + r] = (4 * k + 2 * c2 + ck_) * 512 + r * 128 + pi
    m["uidx"] = uidx
    yidx = np.zeros((128, 36), np.int32)
    for jj in range(9):
        for r in range(4):
            if jj < 8:
                yidx[:, jj * 4 + r] = (T[jj] // 4) * 2048 + r * 512 + (T[jj] % 4) * 128 + pi
            else:
                yidx[:, jj * 4 + r] = 8 * 2048 + r * 512 + 128 * k + pi
    m["yidx"] = yidx


_IDENT = np.eye(128, dtype=np.float32)
_TRI = np.triu(np.ones((128, 128), np.float32))
_CST = np.stack([np.arange(128), -np.arange(128), np.where(np.arange(128) < 64, -1.0, 1.0),
                 np.where(np.arange(128) < 64, 1.0, -1.0)], axis=1).astype(np.float32)
_TVEC = np.arange(129, dtype=np.float32).reshape(1, 129)
_MASK = (np.arange(128)[:, None, None] // 16 == np.arange(8)[None, :, None]).astype(np.float32) * np.ones((1, 1, 64), np.float32)
_NC = {}


def kernel(x_prompt, x_sample, cache_k, cache_v, cache_kidx, state_ssm_re, state_ssm_im, page_table,
           p_prompt, p_sample, norm_w, final_norm_w, rel_bias, attn_w_in, attn_w_out, ssm_w_in,
           ssm_lambda_re, ssm_lambda_im, ssm_log_dt, ssm_b_re, ssm_b_im, ssm_c_re, ssm_c_im, ssm_d,
           ssm_w_glu, ssm_b_glu, ssm_w_out, ple_norm_w, ple_w_gate, ple_w_proj):
    f32 = lambda a: np.ascontiguousarray(np.asarray(a), dtype=np.float32)
    (x_prompt, x_sample, cache_k, cache_v, cache_kidx, state_ssm_re, state_ssm_im, p_prompt, p_sample, norm_w,
     final_norm_w, rel_bias, attn_w_in, attn_w_out, ssm_w_in, ssm_lambda_re, ssm_lambda_im, ssm_log_dt, ssm_b_re,
     ssm_b_im, ssm_c_re, ssm_c_im, ssm_d, ssm_w_glu, ssm_b_glu, ssm_w_out, ple_norm_w, ple_w_gate, ple_w_proj) = [
        f32(a) for a in (x_prompt, x_sample, cache_k, cache_v, cache_kidx, state_ssm_re, state_ssm_im, p_prompt,
                         p_sample, norm_w, final_norm_w, rel_bias, attn_w_in, attn_w_out, ssm_w_in, ssm_lambda_re,
                         ssm_lambda_im, ssm_log_dt, ssm_b_re, ssm_b_im, ssm_c_re, ssm_c_im, ssm_d, ssm_w_glu,
                         ssm_b_glu, ssm_w_out, ple_norm_w, ple_w_gate, ple_w_proj)]
    page_table = np.asarray(page_table).astype(np.int32)
    if "nc" not in _NC:
        _NC["nc"] = build_fused()
    nc = _NC["nc"]
    Dp, C0p = _bias_tiles(rel_bias, 128)
    Ds, C0s = _bias_tiles(rel_bias, 4)
    shared = {
        "norm_w": norm_w.reshape(4, 1, 2048), "ple_nw": ple_norm_w.reshape(4, 1, 2048), "fnw": final_norm_w.reshape(1, 2048),
        "a_win": attn_w_in, "a_wout": attn_w_out, "s_win": ssm_w_in, "s_wglu": ssm_w_glu,
        "s_bglu": ssm_b_glu.reshape(2, 1, 2048), "s_wout": ssm_w_out, "p_wg": ple_w_gate, "p_wp": ple_w_proj,
        "ident": _IDENT, "Dp": Dp, "C0p": C0p, "Ds": Ds, "C0s": C0s, "vts": np.zeros((1, 129), np.float32),
        "cbv": np.ascontiguousarray(rel_bias[31].reshape(1, 16)), "tri": _TRI, "cst": _CST, "tvec": _TVEC,
        "mask": np.ascontiguousarray(_MASK),
    }
    for l in range(2):
        shared[f"ck{l}"] = cache_k[l].reshape(10240, 8192)
        shared[f"cv{l}"] = cache_v[l].reshape(10240, 8192)
        shared[f"ci{l}"] = cache_kidx[l].reshape(1280, 8192)
    pi = np.arange(128, dtype=np.int32)
    in_maps = []
    for c in range(NCORES):
        b, k = c // 4, c % 4
        T = [_zz_block(k, j) for j in range(8)]
        rows = np.concatenate([np.arange(t * 128, (t + 1) * 128) for t in T])
        m = dict(shared)
        xr = np.zeros((R_, 2048), np.float32)
        xr[0:1024] = x_prompt[b][rows]
        xr[1024:1028] = x_sample[c]
        pr = np.zeros((4, R_, 256), np.float32)
        pr[:, 0:1024] = p_prompt[:, b][:, rows]
        pr[:, 1024:1028] = p_sample[:, c]
        m["xrows"] = xr
        m["prows"] = pr
        m["pt"] = np.ascontiguousarray(page_table[c].reshape(128, 1))
        m["vtp"] = np.concatenate([np.where(np.arange(P_SMAX[j]) <= T[j], 0.0, NEG) for j in range(8)]).reshape(1, 144).astype(np.float32)
        idxK = np.zeros((128, 4, 144), np.int32)
        idxI = np.zeros((128, 144), np.int32)
        u = 0
        for j in range(8):
            for dl in range(P_SMAX[j]):
                r, jj = _zz_owner(max(T[j] - dl, 0))
                for g in range(4):
                    idxK[:, g, u] = (g // 2) * 8192 + r * 2048 + (g % 2) * 1024 + jj * 128 + pi
                idxI[:, u] = r * 1024 + jj * 128 + pi
                u += 1
        m["idxK"] = idxK.reshape(128, 576)
        m["idxI"] = idxI
        _scan_inputs(m, b, k, T, pi, ssm_lambda_re, ssm_lambda_im, ssm_log_dt, ssm_b_re, ssm_b_im, ssm_c_re, ssm_c_im, ssm_d, state_ssm_re, state_ssm_im)
        in_maps.append(m)
    res = run_bass_kernel_spmd(nc, in_maps, core_ids=list(range(NCORES)))
    y_p = np.zeros((2, 4096, 2048), np.float32); y_s = np.zeros((8, 4, 2048), np.float32)
    k_p = np.zeros((2, 2, 4096, 4, 128), np.float32); v_p = np.zeros_like(k_p); ki_p = np.zeros((2, 2, 4096, 64), np.float32)
    k_s = np.zeros((2, 8, 4, 4, 128), np.float32); v_s = np.zeros_like(k_s); ki_s = np.zeros((2, 8, 4, 64), np.float32)
    hr_p = np.zeros((2, 2, 128, 64), np.float32); hi_p = np.zeros_like(hr_p)
    hr_s = np.zeros((2, 8, 128, 64), np.float32); hi_s = np.zeros_like(hr_s)
    for c in range(NCORES):
        b, k = c // 4, c % 4
        r_ = res.results[c]
        yo, kv = r_["yout"], r_["kvk"]
        for j in range(8):
            t = _zz_block(k, j)
            sl = slice(t * 128, (t + 1) * 128)
            y_p[b, sl] = yo[j * 128:(j + 1) * 128]
            for l in range(2):
                blk = kv[l][j * 128:(j + 1) * 128]
                k_p[l, b, sl] = blk[:, 0:512].reshape(128, 4, 128)
                v_p[l, b, sl] = blk[:, 512:1024].reshape(128, 4, 128)
                ki_p[l, b, sl] = blk[:, 1024:1088]
        y_s[c] = yo[1024:1028]
        for l in range(2):
            blk = kv[l][1024:1028]
            k_s[l, c] = blk[:, 0:512].reshape(4, 4, 128); v_s[l, c] = blk[:, 512:1024].reshape(4, 4, 128)
            ki_s[l, c] = blk[:, 1024:1088]
            for c2 in range(2):
                gs = slice(32 * k + 16 * c2, 32 * k + 16 * c2 + 16)
                H = r_["HFp"][l, c2, 0]
                hr_p[l, b, gs] = H[0:64].T; hi_p[l, b, gs] = H[64:128].T
                for i_ in range(4):
                    H = r_["HFs"][l, c2, i_]
                    hr_s[l, 4 * b + i_, gs] = H[0:64].T; hi_s[l, 4 * b + i_, gs] = H[64:128].T
    return (y_p, y_s, k_p, v_p, ki_p, hr_p, hi_p, k_s, v_s, ki_s, hr_s, hi_s)
```

```python
import math
from contextlib import ExitStack, contextmanager
import numpy as np
import concourse.bass as bass
import concourse.mybir as mybir
from concourse.bass_utils import run_bass_kernel_spmd

F32 = mybir.dt.float32
BF16 = mybir.dt.bfloat16
I32 = mybir.dt.int32
AF = mybir.ActivationFunctionType
ALU = mybir.AluOpType
AX = mybir.AxisListType
NCORES = 8
EPOCH = 30000
PER = EPOCH // 16


class Prog:
    ENG = ("pe", "act", "dve", "pool", "sp")

    def __init__(self, nc, es):
        self.nc = nc
        self.es = es
        self.cnt = {e: 0 for e in self.ENG}
        self.lanes = {}
        self.csem = {e: [] for e in self.ENG}
        self.lsem = {}
        self.lane_inc = {}
        self.waited = {e: {} for e in self.ENG}
        self.nuniq = 0
        self.st = None
        self.ops = None
        self.barrier = []
        self.seen = set()

    @contextmanager
    def stage(self, name=""):
        with ExitStack() as st:
            self.st = st
            self.ops = {e: [] for e in self.ENG}
            self.seen = set()
            self.stage_lane_map = {}
            yield self
            self._emit()
            bar = []
            for e in self.ENG:
                if self.cnt[e] > 0:
                    bar.append(("c", e, self.cnt[e] - 1))
            for ln, (ep, val) in self.lanes.items():
                if val > 0:
                    bar.append(("d", ln, ep, val))
            self.barrier = bar
            self.st = None

    def sb(self, shape, dt, name=None):
        self.nuniq += 1
        return self.st.enter_context(self.nc.sbuf_tensor(name or f"sb{self.nuniq}", list(shape), dt))

    def ps(self, shape, dt=F32, name=None):
        self.nuniq += 1
        return self.st.enter_context(self.nc.psum_tensor(name or f"ps{self.nuniq}", list(shape), dt))

    def _deps(self, eng, deps):
        deps = [d for d in deps if d is not None]
        if eng not in self.seen:
            self.seen.add(eng)
            deps = list(self.barrier) + deps
        return tuple(deps)

    def op(self, eng, fn, deps=(), inc=True):
        deps = self._deps(eng, deps)
        tok = None
        if inc:
            tok = ("c", eng, self.cnt[eng])
            self.cnt[eng] += 1
        self.ops[eng].append((fn, deps, tok, 1))
        return tok

    def dma(self, eng, lane, out, in_, deps=()):
        return self.lane_op(eng, lane, lambda e: e.dma_start(out=out, in_=in_), deps)

    def lane_op(self, eng, lane, fn, deps=(), incv=16):
        deps = self._deps(eng, deps)
        if incv == 16:
            lane = "L%d" % self.stage_lane_map.setdefault(lane, len(self.stage_lane_map))
        ep, val = self.lanes.get(lane, (0, 0))
        if val + incv > EPOCH:
            ep, val = ep + 1, 0
        val += incv
        self.lanes[lane] = (ep, val)
        tok = ("d", lane, ep, val)
        self.ops[eng].append((fn, deps, tok, incv))
        return tok

    def _semval(self, tok):
        if tok[0] == "c":
            _, e, i = tok
            ep = i // EPOCH
            while len(self.csem[e]) <= ep:
                self.csem[e].append(self.es.enter_context(self.nc.semaphore(f"s_{e}{len(self.csem[e])}")))
            return self.csem[e][ep], (i % EPOCH) + 1
        _, ln, ep, val = tok
        lst = self.lsem.setdefault(ln, [])
        while len(lst) <= ep:
            lst.append(self.es.enter_context(self.nc.semaphore(f"l_{ln}_{len(lst)}")))
        return lst[ep], val

    def _emit(self):
        nc = self.nc
        with nc.Block() as block:
            def make(ename):
                def body(eng):
                    waited = self.waited[ename]
                    for fn, deps, tok, incv in self.ops[ename]:
                        for d in deps:
                            s, v = self._semval(d)
                            key = id(s)
                            if waited.get(key, 0) >= v:
                                continue
                            waited[key] = v
                            eng.wait_ge(s, v)
                        if fn is None:
                            continue
                        ins = fn(eng)
                        if tok is not None:
                            s, v = self._semval(tok)
                            ins.then_inc(s, incv if tok[0] == "d" else 1)
                return body
            block.tensor(make("pe"))
            block.scalar(make("act"))
            block.vector(make("dve"))
            block.gpsimd(make("pool"))
            block.sync(make("sp"))


class Ring:
    def __init__(self, bufs):
        self.bufs = bufs
        self.i = 0
        self.readers = [[] for _ in bufs]

    def next(self):
        k = self.i % len(self.bufs)
        self.i += 1
        deps = self.readers[k]
        self.readers[k] = []
        return k, self.bufs[k], deps

    def done(self, k, *toks):
        self.readers[k].extend(t for t in toks if t is not None)
def emit_linear(p, RT, D, N, pro, epi, x, w, y, ident_d, x2=None, nw=None, e1=None, e2=None, bv=None, eps=1e-6):
    R = RT * 128
    KC = D // 128
    NT = (N + 511) // 512
    ident_f = p.sb([128, 128], F32)
    ident = p.sb([128, 128], BF16)
    XT = p.sb([128, KC, R], BF16)
    xbufs = Ring([p.sb([128, D], F32) for _ in range(2)])
    x2bufs = Ring([p.sb([128, D], F32) for _ in range(2)]) if pro == "gate" else None
    tmpf = Ring([p.sb([128, D], F32) for _ in range(2)]) if pro in ("gate", "gelu", "rms") else None
    xpb = Ring([p.sb([128, D], BF16) for _ in range(2)])
    nwb = p.sb([128, D], F32) if pro == "rms" else None
    stat = Ring([p.sb([128, 2], F32) for _ in range(2)]) if pro == "rms" else None
    pT = Ring([p.ps([128, 4, 128], BF16) for _ in range(2)])
    wb = Ring([p.sb([128, KC, 512], BF16) for _ in range(2)])
    acc = Ring([p.ps([128, 512], F32) for _ in range(3)])
    ob = Ring([p.sb([128, 512], F32) for _ in range(3)])
    e1b = Ring([p.sb([128, 512], F32) for _ in range(2)]) if e1 is not None else None
    e2b = Ring([p.sb([128, 512], F32) for _ in range(2)]) if e2 is not None else None
    bb = Ring([p.sb([128, 512], F32) for _ in range(2)]) if bv is not None else None
    tb = Ring([p.sb([128, 512], F32) for _ in range(2)]) if epi in ("gres", "glu") else None
    GC = 2.0 * math.sqrt(2.0 / math.pi)

    t_id = p.dma("sp", "ident", ident_f[:], ident_d)
    t_ident = p.op("dve", lambda e: e.tensor_copy(out=ident[:], in_=ident_f[:]), [t_id])
    t_nw = p.dma("sp", "nw", nwb[:], nw.partition_broadcast(128)) if pro == "rms" else None
    xt_toks = []
    for r in range(RT):
        rows = slice(r * 128, (r + 1) * 128)
        kx, xb, dx = xbufs.next()
        t_x = p.dma("sp", f"x{kx}", xb[:], x[rows, :], dx)
        kp, xp, dxp = xpb.next()
        if pro == "rms":
            kt, tf, dt_ = tmpf.next()
            ks, st, ds = stat.next()
            t_sq = p.op("act", lambda e, tf=tf, xb=xb, st=st: e.activation(
                out=tf[:], in_=xb[:], func=AF.Square, accum_out=st[:, 0:1]), [t_x] + dt_ + ds)
            t_r1 = p.op("dve", lambda e, st=st: e.tensor_scalar(
                out=st[:, 1:2], in0=st[:, 0:1], scalar1=1.0 / D, scalar2=eps, op0=ALU.mult, op1=ALU.add), [t_sq])
            t_r1b = p.op("act", lambda e, st=st: e.activation(out=st[:, 1:2], in_=st[:, 1:2], func=AF.Sqrt), [t_r1])
            t_r2 = p.op("dve", lambda e, st=st: e.reciprocal(out=st[:, 1:2], in_=st[:, 1:2]), [t_r1b])
            t_xp = p.op("dve", lambda e, xp=xp, xb=xb, st=st: e.scalar_tensor_tensor(
                out=xp[:], in0=xb[:], scalar=st[:, 1:2], in1=nwb[:], op0=ALU.mult, op1=ALU.mult),
                [t_r2, t_nw] + dxp)
            tmpf.done(kt, t_sq); stat.done(ks, t_xp); xbufs.done(kx, t_xp)
        elif pro == "gate":
            k2, x2b, d2 = x2bufs.next()
            t_x2 = p.dma("act", f"x2{k2}", x2b[:], x2[rows, :], d2)
            kt, tf, dt_ = tmpf.next()
            t_s = p.op("act", lambda e, tf=tf, x2b=x2b: e.activation(out=tf[:], in_=x2b[:], func=AF.Silu), [t_x2] + dt_)
            t_xp = p.op("dve", lambda e, xp=xp, xb=xb, tf=tf: e.tensor_tensor(
                out=xp[:], in0=xb[:], in1=tf[:], op=ALU.mult), [t_x, t_s] + dxp)
            x2bufs.done(k2, t_s); tmpf.done(kt, t_xp); xbufs.done(kx, t_xp)
        elif pro == "gelu":
            kt, tf, dt_ = tmpf.next()
            t_a = p.op("dve", lambda e, tf=tf, xb=xb: e.tensor_tensor(out=tf[:], in0=xb[:], in1=xb[:], op=ALU.mult), [t_x] + dt_)
            t_b = p.op("dve", lambda e, tf=tf: e.tensor_scalar(
                out=tf[:], in0=tf[:], scalar1=0.044715, scalar2=1.0, op0=ALU.mult, op1=ALU.add), [t_a])
            t_c = p.op("dve", lambda e, tf=tf, xb=xb: e.tensor_tensor(out=tf[:], in0=tf[:], in1=xb[:], op=ALU.mult), [t_b])
            t_d = p.op("act", lambda e, tf=tf: e.activation(out=tf[:], in_=tf[:], func=AF.Sigmoid, scale=GC), [t_c])
            t_xp = p.op("dve", lambda e, xp=xp, xb=xb, tf=tf: e.tensor_tensor(
                out=xp[:], in0=xb[:], in1=tf[:], op=ALU.mult), [t_d] + dxp)
            tmpf.done(kt, t_xp); xbufs.done(kx, t_xp)
        else:
            t_xp = p.op("dve", lambda e, xp=xp, xb=xb: e.tensor_copy(out=xp[:], in_=xb[:]), [t_x] + dxp)
            xbufs.done(kx, t_xp)
        last = []
        for k0 in range(0, KC, 4):
            nk = min(4, KC - k0)
            kq, pt, dq = pT.next()
            tt = None
            for j in range(nk):
                tt = p.op("pe", lambda e, pt=pt, xp=xp, j=j, k0=k0: e.transpose(
                    out=pt[:, j, :], in_=xp[:, (k0 + j) * 128:(k0 + j + 1) * 128], identity=ident[:]),
                    [t_xp, t_ident] + dq, inc=(j == nk - 1))
            if (k0 // 4) % 2 == 0:
                t_cp = p.op("act", lambda e, pt=pt, k0=k0, nk=nk, rows=rows: e.copy(
                    out=XT[:, k0:k0 + nk, rows], in_=pt[:, 0:nk, :]), [tt])
            else:
                t_cp = p.op("dve", lambda e, pt=pt, k0=k0, nk=nk, rows=rows: e.tensor_copy(
                    out=XT[:, k0:k0 + nk, rows], in_=pt[:, 0:nk, :]), [tt])
            pT.done(kq, t_cp)
            last.append(t_cp)
            xpb.done(kp, tt)
        xt_toks.append(last)
    wv = w.rearrange("(k p) n -> p k n", p=128)
    fin = []
    for n in range(NT):
        n0 = n * 512
        ns = min(512, N - n0)
        kw, wt, dw = wb.next()
        t_w = p.dma("pool", f"w{kw}", wt[:, :, 0:ns], wv[:, :, n0:n0 + ns], dw)
        t_bb = None
        if bv is not None:
            kb, bt, db = bb.next()
            t_bb = p.dma("act", f"b{kb}", bt[:, 0:ns], bv[0:1, n0:n0 + ns].partition_broadcast(128), db)
        mm_last = []
        for r in range(RT):
            rows = slice(r * 128, (r + 1) * 128)
            ka, ac, da = acc.next()
            tm = None
            for k in range(KC):
                tm = p.op("pe", lambda e, ac=ac, wt=wt, k=k, rows=rows, ns=ns: e.matmul(
                    ac[:, 0:ns], lhsT=XT[:, k, rows], rhs=wt[:, k, 0:ns], start=(k == 0), stop=(k == KC - 1)),
                    ([t_w] + xt_toks[r] + da) if k == 0 else [], inc=(k == KC - 1))
            mm_last.append(tm)
            ko, ot, do = ob.next()
            if epi == "none":
                t_o = p.op("act", lambda e, ot=ot, ac=ac, ns=ns: e.copy(out=ot[:, 0:ns], in_=ac[:, 0:ns]), [tm] + do)
                acc.done(ka, t_o)
            elif epi == "sigmoid":
                t_o = p.op("act", lambda e, ot=ot, ac=ac, ns=ns: e.activation(
                    out=ot[:, 0:ns], in_=ac[:, 0:ns], func=AF.Sigmoid), [tm] + do)
                acc.done(ka, t_o)
            elif epi == "res":
                k1, et, d1 = e1b.next()
                t_e = p.dma("sp", f"e1{k1}", et[:, 0:ns], e1[rows, n0:n0 + ns], d1)
                t_o = p.op("dve", lambda e, ot=ot, ac=ac, et=et, ns=ns: e.tensor_tensor(
                    out=ot[:, 0:ns], in0=ac[:, 0:ns], in1=et[:, 0:ns], op=ALU.add), [tm, t_e] + do)
                acc.done(ka, t_o); e1b.done(k1, t_o)
            elif epi == "gres":
                k1, et, d1 = e1b.next()
                t_e = p.dma("sp", f"e1{k1}", et[:, 0:ns], e1[rows, n0:n0 + ns], d1)
                k2, et2, d2 = e2b.next()
                t_e2 = p.dma("sp", f"e2{k2}", et2[:, 0:ns], e2[rows, n0:n0 + ns], d2)
                kt, tt_, dtt = tb.next()
                t_m = p.op("dve", lambda e, tt_=tt_, ac=ac, et2=et2, ns=ns: e.tensor_tensor(
                    out=tt_[:, 0:ns], in0=ac[:, 0:ns], in1=et2[:, 0:ns], op=ALU.mult), [tm, t_e2] + dtt)
                t_o = p.op("dve", lambda e, ot=ot, tt_=tt_, et=et, ns=ns: e.tensor_tensor(
                    out=ot[:, 0:ns], in0=tt_[:, 0:ns], in1=et[:, 0:ns], op=ALU.add), [t_m, t_e] + do)
                acc.done(ka, t_m); e1b.done(k1, t_o); e2b.done(k2, t_m); tb.done(kt, t_o)
            elif epi == "glu":
                k1, et, d1 = e1b.next()
                t_e = p.dma("sp", f"e1{k1}", et[:, 0:ns], e1[rows, n0:n0 + ns], d1)
                kt, tt_, dtt = tb.next()
                t_a = p.op("dve", lambda e, tt_=tt_, et=et, ns=ns: e.tensor_tensor(
                    out=tt_[:, 0:ns], in0=et[:, 0:ns], in1=et[:, 0:ns], op=ALU.mult), [t_e] + dtt)
                t_b = p.op("dve", lambda e, tt_=tt_, ns=ns: e.tensor_scalar(
                    out=tt_[:, 0:ns], in0=tt_[:, 0:ns], scalar1=0.044715, scalar2=1.0, op0=ALU.mult, op1=ALU.add), [t_a])
                t_c = p.op("dve", lambda e, tt_=tt_, et=et, ns=ns: e.tensor_tensor(
                    out=tt_[:, 0:ns], in0=tt_[:, 0:ns], in1=et[:, 0:ns], op=ALU.mult), [t_b])
                t_d = p.op("act", lambda e, tt_=tt_, ns=ns: e.activation(
                    out=tt_[:, 0:ns], in_=tt_[:, 0:ns], func=AF.Sigmoid, scale=GC), [t_c])
                t_g = p.op("dve", lambda e, tt_=tt_, et=et, ns=ns: e.tensor_tensor(
                    out=tt_[:, 0:ns], in0=tt_[:, 0:ns], in1=et[:, 0:ns], op=ALU.mult), [t_d])
                t_s1 = p.op("dve", lambda e, ot=ot, ac=ac, bt=bt, ns=ns: e.tensor_tensor(
                    out=ot[:, 0:ns], in0=ac[:, 0:ns], in1=bt[:, 0:ns], op=ALU.add), [tm, t_bb] + do)
                t_s2 = p.op("act", lambda e, ot=ot, ns=ns: e.activation(
                    out=ot[:, 0:ns], in_=ot[:, 0:ns], func=AF.Sigmoid), [t_s1])
                t_o = p.op("dve", lambda e, ot=ot, tt_=tt_, ns=ns: e.tensor_tensor(
                    out=ot[:, 0:ns], in0=ot[:, 0:ns], in1=tt_[:, 0:ns], op=ALU.mult), [t_s2, t_g])
                acc.done(ka, t_s1); e1b.done(k1, t_g); tb.done(kt, t_o)
            else:
                raise ValueError(epi)
            t_st = p.dma("act", f"o{ko}", y[rows, n0:n0 + ns], ot[:, 0:ns], [t_o])
            ob.done(ko, t_st)
            fin.append(t_st)
        wb.done(kw, *mm_last)
        if bv is not None:
            bb.done(kb, *mm_last)
    return fin[-3:]


def emit_transpose(p, src, dst, R, C, ident_d):
    identf = p.sb([128, 128], F32)
    t_id = p.dma("sp", "ident", identf[:], ident_d)
    inb = Ring([p.sb([128, 512], F32) for _ in range(3)])
    pst = Ring([p.ps([128, 512], F32) for _ in range(3)])
    outb = Ring([p.sb([128, 512], F32) for _ in range(3)])
    fin = []
    for r0 in range(0, R, 512):
        nr = min(512, R - r0)
        nrt = nr // 128
        for c0 in range(0, C, 128):
            cw = min(128, C - c0)
            ki, it, di = inb.next()
            t_in = p.dma("sp", f"ti{ki}", it[:, 0:nrt * 128].rearrange("p (q c) -> p q c", c=128)[:, :, 0:cw],
                         src[r0:r0 + nr, c0:c0 + cw].rearrange("(q p) c -> p q c", p=128), di)
            kp, pt, dp = pst.next()
            tt = None
            for q in range(nrt):
                tt = p.op("pe", lambda e, pt=pt, it=it, q=q, cw=cw: e.transpose(
                    out=pt[0:cw, q * 128:(q + 1) * 128], in_=it[:, q * 128:q * 128 + cw], identity=identf[:]),
                    [t_in, t_id] + dp, inc=(q == nrt - 1))
            inb.done(ki, tt)
            ko, ot, do = outb.next()
            t_cp = p.op("act" if (c0 // 128) % 2 == 0 else "dve",
                        (lambda e, ot=ot, pt=pt, cw=cw, nr=nr: e.copy(out=ot[0:cw, 0:nr], in_=pt[0:cw, 0:nr]))
                        if (c0 // 128) % 2 == 0 else
                        (lambda e, ot=ot, pt=pt, cw=cw, nr=nr: e.tensor_copy(out=ot[0:cw, 0:nr], in_=pt[0:cw, 0:nr])),
                        [tt] + do)
            pst.done(kp, t_cp)
            t_st = p.dma("act", f"to{ko}", dst[c0:c0 + cw, r0:r0 + nr], ot[0:cw, 0:nr], [t_cp])
            outb.done(ko, t_st)
            fin.append(t_st)
    return fin[-3:]
NEG = -1.0e30
TOPK = 256
BIS_LO, BIS_HI, BIS_IT = -8192.0, 8192.0, 28


def emit_attn(p, NJ, NT, smax, src, vt_d, D_d, cb_d, C0_d, oT_dst):
    HN = 16 * NT
    HPM = min(16, 512 // NT)
    NMM = 16 // HPM
    SMX = max(smax)
    NU = sum(smax)
    u0s = [sum(smax[:j]) for j in range(NJ)]
    scale = 128.0 ** -0.5
    n4 = 4 * NT
    ones = p.sb([128, 128], BF16)
    vt = p.sb([128, NU], F32)
    Dt = p.sb([128, 2, HN], F32)
    cb = p.sb([128, 16], F32)
    C0 = p.sb([128, NT], F32)
    qT = Ring([p.sb([128, HN], BF16) for _ in range(2)])
    qiT = Ring([p.sb([64, HN], BF16) for _ in range(2)])
    wb = Ring([p.sb([128, HN], F32) for _ in range(2)])
    SC = p.sb([128, SMX, NT], F32)
    CM = p.sb([128, SMX, NT], BF16)
    MKs = [p.sb([128, SMX, NT], BF16) for _ in range(2 if NJ > 1 else 1)]
    kib = Ring([p.sb([128, 128], BF16) for _ in range(3)])
    rb = Ring([p.sb([128, HN], F32) for _ in range(2)])
    ktb = Ring([p.sb([128, 128], BF16) for _ in range(3)])
    vb = Ring([p.sb([128, 128], BF16) for _ in range(3)])
    banks = Ring([p.ps([128, 512], F32) for _ in range(4)])
    ps_cnt = p.ps([128, 512], F32)
    ps_o = p.ps([128, 512], F32)
    ps_d = p.ps([128, 512], F32)
    lo = p.sb([128, NT], F32)
    mid = p.sb([128, NT], F32)
    ge = p.sb([128, NT], F32)
    lgt = Ring([p.sb([128, 4 * NT], F32) for _ in range(2)])
    pb = Ring([p.sb([128, 4 * NT], BF16) for _ in range(2)])
    pmb = Ring([p.sb([128, 4 * NT], BF16) for _ in range(2)])
    rdb = p.sb([128, 4 * NT], F32)
    ob = Ring([p.sb([128, 4 * NT], F32) for _ in range(2)])

    t_ones = p.op("dve", lambda e: e.memset(ones[:], 1.0))
    t_vt = p.dma("sp", "vt", vt[:], vt_d.partition_broadcast(128))
    t_D = p.dma("sp", "D", Dt[:], D_d.rearrange("a p n -> p a n"))
    t_cb = p.dma("sp", "cb", cb[:], cb_d.partition_broadcast(128))
    t_C0 = p.dma("sp", "C0", C0[:], C0_d)
    st = {"acc_free": [], "sc_free": [], "out": []}

    def phase_a(j):
        S = smax[j]
        kq, qt, dq = qT.next()
        t_q = src.load_q(j, qt, dq)
        kqi, qit, dqi = qiT.next()
        t_qi = src.load_qi(j, qit, dqi)
        kw, wt, dw_ = wb.next()
        t_w = src.load_w(j, wt, dw_)
        sc_toks = []
        mm_toks, t_rs = [], []
        for dl in range(S):
            kk, kit, dk = kib.next()
            t_ki = src.load_ki(j, dl, kit, dk)
            kr, rt, dr = rb.next()
            t_rs = []
            mm_toks = []
            for m in range(NMM):
                kb_, bk, dbk = banks.next()
                n_ = HPM * NT
                t_mm = p.op("pe", lambda e, bk=bk, kit=kit, qit=qit, m=m, n_=n_: e.matmul(
                    bk[:, 0:n_], lhsT=kit[0:64, :], rhs=qit[:, m * n_:(m + 1) * n_], start=True, stop=True),
                    [t_ki, t_qi] + dbk)
                t_r = p.op("dve", lambda e, bk=bk, rt=rt, wt=wt, m=m, n_=n_: e.scalar_tensor_tensor(
                    out=rt[:, m * n_:(m + 1) * n_], in0=bk[:, 0:n_], scalar=0.0, in1=wt[:, m * n_:(m + 1) * n_],
                    op0=ALU.max, op1=ALU.mult), [t_mm, t_w] + (dr if m == 0 else []))
                banks.done(kb_, t_r)
                t_rs.append(t_r)
                mm_toks.append(t_mm)
            kib.done(kk, *mm_toks)
            t_red = p.op("dve", lambda e, rt=rt, dl=dl: e.tensor_reduce(
                out=SC[:, dl, :], in_=rt[:, :].rearrange("p (h t) -> p t h", h=16), axis=AX.X, op=ALU.add),
                t_rs + (st["sc_free"] if dl == 0 else []))
            rb.done(kr, t_red)
            u = u0s[j] + dl
            t_v = p.op("dve", lambda e, dl=dl, u=u: e.tensor_scalar(
                out=SC[:, dl, :], in0=SC[:, dl, :], scalar1=vt[:, u:u + 1], scalar2=None, op0=ALU.add), [t_red, t_vt])
            if dl == 0:
                t_v = p.op("dve", lambda e: e.tensor_tensor(
                    out=SC[:, 0, :], in0=SC[:, 0, :], in1=C0[:, :], op=ALU.add), [t_v, t_C0])
            sc_toks.append(t_v)
        qiT.done(kqi, *mm_toks)
        wb.done(kw, *t_rs)
        return kq, qt, t_q, sc_toks

    def gen_bisect(S, sc_toks, res):
        t_m = p.op("dve", lambda e: e.memset(mid[:], 0.5 * (BIS_LO + BIS_HI)))
        t_prev = [t_m]
        t_cm_free = []
        for it in range(BIS_IT):
            h = (BIS_HI - BIS_LO) / 2.0 ** (it + 1)
            t_cmp = p.op("dve", lambda e, S=S: e.tensor_tensor(
                out=CM[:, 0:S, :], in0=SC[:, 0:S, :], in1=mid[:, :].unsqueeze(1).to_broadcast([128, S, NT]),
                op=ALU.is_ge), t_prev + sc_toks + t_cm_free)
            t_c = None
            for dl in range(S):
                t_c = p.op("pe", lambda e, dl=dl, S=S: e.matmul(
                    ps_cnt[:, 0:NT], lhsT=ones[:, :], rhs=CM[:, dl, :], start=(dl == 0), stop=(dl == S - 1)),
                    ([t_cmp, t_ones] + t_prev) if dl == 0 else [], inc=(dl == S - 1))
            t_ge = p.op("dve", lambda e, h=h: e.tensor_scalar(
                out=ge[:], in0=ps_cnt[:, 0:NT], scalar1=float(TOPK) - 0.5, scalar2=h, op0=ALU.is_ge, op1=ALU.mult), [t_c])
            t_md = p.op("dve", lambda e, h=h: e.scalar_tensor_tensor(
                out=mid[:], in0=ge[:], scalar=-0.5 * h, in1=mid[:], op0=ALU.add, op1=ALU.add), [t_ge])
            t_prev = [t_md]
            t_cm_free = [t_c]
            yield None
        hf = (BIS_HI - BIS_LO) / 2.0 ** (BIS_IT + 1)
        res["t_lo"] = p.op("dve", lambda e: e.tensor_scalar(
            out=lo[:], in0=mid[:], scalar1=-hf, scalar2=None, op0=ALU.add), t_prev)

    def gen_c(j, S, kq, qt, t_q, MK, t_mk):
        last = {}
        for g in range(4):
            pend = None
            for dl in range(S):
                kk, ktt, dkt = ktb.next()
                t_kt = src.load_k(j, g, dl, ktt, dkt)
                kv, vtl, dv = vb.next()
                t_vv = src.load_v(j, g, dl, vtl, dv)
                kb_, bk, dbk = banks.next()
                t_s = p.op("pe", lambda e, bk=bk, ktt=ktt, g=g: e.matmul(
                    bk[:, 0:n4], lhsT=ktt[:, 0:128], rhs=qt[:, g * n4:(g + 1) * n4], start=True, stop=True),
                    [t_kt, t_q] + dbk)
                ktb.done(kk, t_s)
                kp_, pt, dp = pb.next()
                if dl >= 2:
                    t_e = None
                    for r in range(4):
                        h = 4 * g + r
                        t_e = p.op("act", lambda e, pt=pt, bk=bk, r=r, h=h: e.activation(
                            out=pt[:, r * NT:(r + 1) * NT], in_=bk[:, r * NT:(r + 1) * NT], func=AF.Exp,
                            bias=cb[:, h:h + 1], scale=scale), [t_s, t_cb] + (dp if r == 0 else []))
                    banks.done(kb_, t_e)
                else:
                    kl, lt, dlg = lgt.next()
                    t_l = p.op("dve", lambda e, lt=lt, bk=bk, dl=dl, g=g: e.scalar_tensor_tensor(
                        out=lt[:, 0:n4], in0=bk[:, 0:n4], scalar=scale, in1=Dt[:, dl, g * n4:(g + 1) * n4],
                        op0=ALU.mult, op1=ALU.add), [t_s, t_D] + dlg)
                    banks.done(kb_, t_l)
                    t_e = p.op("act", lambda e, pt=pt, lt=lt: e.activation(
                        out=pt[:, 0:n4], in_=lt[:, 0:n4], func=AF.Exp), [t_l] + dp)
                    lgt.done(kl, t_e)
                km, pmt, dpm = pmb.next()
                t_pm = p.op("dve", lambda e, pmt=pmt, pt=pt, dl=dl: e.tensor_tensor(
                    out=pmt[:, :].rearrange("p (r t) -> p r t", r=4),
                    in0=pt[:, :].rearrange("p (r t) -> p r t", r=4),
                    in1=MK[:, dl, :].unsqueeze(1).to_broadcast([128, 4, NT]), op=ALU.mult), [t_e, t_mk] + dpm)
                pb.done(kp_, t_pm)
                last["pm"] = t_pm
                if pend is not None:
                    pend()

                def stage2(vtl=vtl, pmt=pmt, kv=kv, km=km, t_vv=t_vv, t_pm=t_pm, first=(dl == 0), lastf=(dl == S - 1)):
                    t_o_ = p.op("pe", lambda e: e.matmul(
                        ps_o[:, 0:n4], lhsT=vtl[:, 0:128], rhs=pmt[:, 0:n4], start=first, stop=lastf),
                        [t_vv, t_pm] + (st["acc_free"] if first else []))
                    t_d_ = p.op("pe", lambda e: e.matmul(
                        ps_d[:, 0:n4], lhsT=ones[:, :], rhs=pmt[:, 0:n4], start=first, stop=lastf), [t_pm])
                    vb.done(kv, t_o_)
                    pmb.done(km, t_d_)
                    last["o"], last["d"] = t_o_, t_d_
                pend = stage2
                yield None
            pend()
            t_rd = p.op("dve", lambda e: e.reciprocal(out=rdb[:], in_=ps_d[:, 0:n4]), [last["d"]] + st["acc_free"])
            ko, ot, do = ob.next()
            t_o = p.op("dve", lambda e, ot=ot: e.tensor_tensor(
                out=ot[:], in0=ps_o[:, 0:n4], in1=rdb[:], op=ALU.mult), [t_rd, last["o"]] + do)
            st["acc_free"] = [t_o]
            t_st = p.dma("act", f"o{ko}", oT_dst(j, g), ot[:, :].rearrange("p (r t) -> p r t", r=4), [t_o])
            ob.done(ko, t_st)
            st["out"].append(t_st)
        qT.done(kq, last["o"])

    order = list(range(NJ))[::-1]
    pend_c = None
    for idx, j in enumerate(order):
        S = smax[j]
        kq, qt, t_q, sc_toks = phase_a(j)
        res = {}
        per = 0
        if pend_c is not None:
            per = -(-(4 * smax[order[idx - 1]]) // BIS_IT)
        for _ in gen_bisect(S, sc_toks, res):
            for _u in range(per):
                if pend_c is not None and next(pend_c, "END") == "END":
                    pend_c = None
        if pend_c is not None:
            for _ in pend_c:
                pass
        MK = MKs[idx % len(MKs)]
        t_mk = p.op("dve", lambda e, S=S, MK=MK: e.tensor_tensor(
            out=MK[:, 0:S, :], in0=SC[:, 0:S, :], in1=lo[:, :].unsqueeze(1).to_broadcast([128, S, NT]),
            op=ALU.is_ge), [res["t_lo"]])
        st["sc_free"] = [t_mk]
        pend_c = gen_c(j, S, kq, qt, t_q, MK, t_mk)
    for _ in pend_c:
        pass
    return st["out"][-2:]
TWO_PI = 2.0 * math.pi


def _sincos(p, out_t, x_t, tmp_t, deps, cos, ki_t, kf_t):
    off = (math.pi / 2) if cos else 0.0
    V = lambda f, d: p.op("dve", f, d)
    t1 = V(lambda e: e.tensor_scalar(out=tmp_t, in0=x_t, scalar1=off, scalar2=1.0 / TWO_PI, op0=ALU.add, op1=ALU.mult), deps)
    t2 = V(lambda e: e.tensor_copy(out=ki_t, in_=tmp_t), [t1])
    t3 = V(lambda e: e.tensor_copy(out=kf_t, in_=ki_t), [t2])
    t4 = V(lambda e: e.tensor_scalar(out=tmp_t, in0=x_t, scalar1=off, scalar2=None, op0=ALU.add), [t3])
    t5 = V(lambda e: e.scalar_tensor_tensor(out=tmp_t, in0=kf_t, scalar=-TWO_PI, in1=tmp_t, op0=ALU.mult, op1=ALU.add), [t4])
    t6 = V(lambda e: e.tensor_scalar(out=kf_t, in0=tmp_t, scalar1=math.pi, scalar2=-TWO_PI, op0=ALU.is_gt, op1=ALU.mult), [t5])
    t7 = V(lambda e: e.tensor_tensor(out=tmp_t, in0=tmp_t, in1=kf_t, op=ALU.add), [t6])
    t8 = V(lambda e: e.tensor_scalar(out=kf_t, in0=tmp_t, scalar1=-math.pi, scalar2=TWO_PI, op0=ALU.is_lt, op1=ALU.mult), [t7])
    t9 = V(lambda e: e.tensor_tensor(out=tmp_t, in0=tmp_t, in1=kf_t, op=ALU.add), [t8])
    return p.op("act", lambda e: e.activation(out=out_t, in_=tmp_t, func=AF.Sin), [t9])


def emit_scan(p, seqs, I, gU_rows, uidx_d, pubY, HF_d, ident_d, pre=()):
    V = lambda f, deps=(): p.op("dve", f, list(deps))
    cst = p.sb([128, 4], F32); tvb = p.sb([128, 129], F32); tri = p.sb([128, 128], BF16)
    identf = p.sb([128, 128], F32)
    lmb = p.sb([128, 1024], F32); anb = p.sb([128, 1024], F32); dtb = p.sb([128, 1024], F32)
    lmT = p.sb([128, 16], F32); anT = p.sb([128, 16], F32); dtT = p.sb([128, 16], F32)
    BDA = p.sb([128, 2, 1024], BF16); BDB = p.sb([128, 2, 1024], BF16)
    CcP = p.sb([128, 2, 1024], BF16); CcPf = p.sb([128, 2, 1024], F32)
    Ccc = p.sb([128, 256], F32); dv = p.sb([128, 2], F32)
    Pa = p.sb([128, 16, 128], F32); Pb = p.sb([128, 16, 128], F32)
    Qa = p.sb([128, 16, 129], F32); Qb = p.sb([128, 16, 129], F32)
    Qa16 = p.sb([128, 16, 129], BF16); Qb16 = p.sb([128, 16, 129], BF16)
    q1 = p.sb([128, 16 * 129], F32); q2 = p.sb([128, 16 * 129], F32); q3 = p.sb([128, 16 * 129], F32)
    q4 = p.sb([128, 16 * 129], F32); qi_ = p.sb([128, 16 * 129], I32); kfq = p.sb([128, 16 * 129], F32)
    mask = p.sb([128, 8, 64], F32)
    s1 = q1[:, 0:1024]; s2 = q2[:, 0:1024]; s3 = q3[:, 0:1024]; s4 = q4[:, 0:1024]; si = qi_[:, 0:1024]; sk = kfq[:, 0:1024]
    U = p.sb([128, 2, 4, 1152], F32)
    uidx = p.sb([128, 8], I32)

    d_c = p.dma("sp", "c0", cst[:], I["cst"])
    d_tv = p.dma("sp", "c1", tvb[:], I["tvec"].partition_broadcast(128))
    d_tri = p.dma("pool", "c2", tri[:], I["tri"])
    d_id = p.dma("sp", "c3", identf[:], ident_d)
    d_lm = p.dma("sp", "c4", lmb[:], I["lre_row"].partition_broadcast(128))
    d_an = p.dma("sp", "c5", anb[:], I["lim_row"].partition_broadcast(128))
    d_dt = p.dma("sp", "c6", dtb[:], I["ldt_row"].partition_broadcast(128))
    d_lmT = p.dma("sp", "c7", lmT[:], I["lreT2"])
    d_anT = p.dma("sp", "c8", anT[:], I["limT2"])
    d_dtT = p.dma("sp", "c9", dtT[:], I["ldtT2"])
    d_ccp = p.dma("sp", "c10", CcPf[:], I["CcP"].rearrange("c p n -> p c n"))
    d_ccc = p.dma("sp", "c11", Ccc[:], I["Ccc"])
    d_dv = p.dma("sp", "c12", dv[:], I["dvec"])
    d_mk = p.dma("sp", "c13", mask[:], I["mask"])
    d_ui = p.dma("sp", "c14", uidx[:], uidx_d)
    u_toks = []
    for ck in range(2):
        for r in range(4):
            col = ck * 4 + r
            u_toks.append(p.lane_op("pool", f"ug{col}", lambda e, ck=ck, r=r, col=col: e.indirect_dma_start(
                out=U[:, ck, r, :], out_offset=None, in_=gU_rows,
                in_offset=bass.IndirectOffsetOnAxis(ap=uidx[:, col:col + 1], axis=0)), [d_ui] + list(pre)))
    def disc(lm, an, dt, deps):
        a = p.op("act", lambda e: e.activation(out=dt, in_=dt, func=AF.Exp), deps)
        b = V(lambda e: e.tensor_scalar(out=lm, in0=lm, scalar1=-1e-4, scalar2=None, op0=ALU.min), deps)
        c = V(lambda e: e.tensor_tensor(out=lm, in0=lm, in1=dt, op=ALU.mult), [a, b])
        d = V(lambda e: e.tensor_tensor(out=an, in0=an, in1=dt, op=ALU.mult), [c])
        return d
    t_row = disc(lmb[:], anb[:], dtb[:], [d_lm, d_an, d_dt])
    t_T = disc(lmT[:], anT[:], dtT[:], [d_lmT, d_anT, d_dtT])
    t_ccp = V(lambda e: e.tensor_scalar(out=CcP[:], in0=CcPf[:], scalar1=cst[:, 3:4], scalar2=None, op0=ALU.mult), [d_ccp, d_c])
    t_ccc = V(lambda e: e.tensor_scalar(out=Ccc[:], in0=Ccc[:], scalar1=cst[:, 3:4], scalar2=None, op0=ALU.mult), [d_ccc, d_c])
    G = [p.sb([128, 64], F32) for _ in range(16)]
    gi_i = p.sb([128, 64], I32)
    t_bd = []
    for ck in range(2):
        lre, lim, ldt, bre, bim, mag, cc, ss, ar1, aim, den, cre, cim, t1_, t2_, kf_ = [g[:] for g in G]
        dd = [p.dma("sp", f"g{n_}", t_, I[nm][ck], t_bd) for n_, (t_, nm) in enumerate(
            [(lre, "lre_gi"), (lim, "lim_gi"), (ldt, "ldt_gi"), (bre, "bre_gi"), (bim, "bim_gi")])]
        a = p.op("act", lambda e: e.activation(out=ldt, in_=ldt, func=AF.Exp), dd)
        b = V(lambda e: e.tensor_scalar(out=lre, in0=lre, scalar1=-1e-4, scalar2=None, op0=ALU.min), dd)
        c1 = V(lambda e: e.tensor_tensor(out=t1_, in0=lre, in1=ldt, op=ALU.mult), [a, b])
        c2 = V(lambda e: e.tensor_tensor(out=t2_, in0=lim, in1=ldt, op=ALU.mult), [c1])
        m = p.op("act", lambda e: e.activation(out=mag, in_=t1_, func=AF.Exp), [c1])
        ts = _sincos(p, ss, t2_, den, [c2, m], False, gi_i[:], kf_)
        tc = _sincos(p, cc, t2_, den, [ts], True, gi_i[:], kf_)
        x1 = V(lambda e: e.tensor_tensor(out=ar1, in0=mag, in1=cc, op=ALU.mult), [tc])
        x1 = V(lambda e: e.tensor_scalar(out=ar1, in0=ar1, scalar1=-1.0, scalar2=None, op0=ALU.add), [x1])
        x2 = V(lambda e: e.tensor_tensor(out=aim, in0=mag, in1=ss, op=ALU.mult), [x1])
        y1 = V(lambda e: e.tensor_tensor(out=den, in0=lre, in1=lre, op=ALU.mult), [x2])
        y2 = V(lambda e: e.tensor_tensor(out=t1_, in0=lim, in1=lim, op=ALU.mult), [y1])
        y3 = V(lambda e: e.tensor_tensor(out=den, in0=den, in1=t1_, op=ALU.add), [y2])
        y4 = V(lambda e: e.reciprocal(out=den, in_=den), [y3])
        z1 = V(lambda e: e.tensor_tensor(out=cre, in0=ar1, in1=lre, op=ALU.mult), [y4])
        z2 = V(lambda e: e.tensor_tensor(out=t1_, in0=aim, in1=lim, op=ALU.mult), [z1])
        z3 = V(lambda e: e.tensor_tensor(out=cre, in0=cre, in1=t1_, op=ALU.add), [z2])
        z4 = V(lambda e: e.tensor_tensor(out=cre, in0=cre, in1=den, op=ALU.mult), [z3])
        w1 = V(lambda e: e.tensor_tensor(out=cim, in0=aim, in1=lre, op=ALU.mult), [z4])
        w2 = V(lambda e: e.tensor_tensor(out=t1_, in0=ar1, in1=lim, op=ALU.mult), [w1])
        w3 = V(lambda e: e.tensor_tensor(out=cim, in0=cim, in1=t1_, op=ALU.subtract), [w2])
        w4 = V(lambda e: e.tensor_tensor(out=cim, in0=cim, in1=den, op=ALU.mult), [w3])
        q_1 = V(lambda e: e.tensor_tensor(out=t1_, in0=cre, in1=bre, op=ALU.mult), [w4])
        q_2 = V(lambda e: e.tensor_tensor(out=t2_, in0=cim, in1=bim, op=ALU.mult), [q_1])
        q_3 = V(lambda e: e.tensor_tensor(out=t1_, in0=t1_, in1=t2_, op=ALU.subtract), [q_2])
        q_4 = V(lambda e: e.tensor_tensor(out=t2_, in0=cre, in1=bim, op=ALU.mult), [q_3])
        q_5 = V(lambda e: e.tensor_tensor(out=mag, in0=cim, in1=bre, op=ALU.mult), [q_4])
        q_6 = V(lambda e: e.tensor_tensor(out=t2_, in0=t2_, in1=mag, op=ALU.add), [q_5])
        bcv = lambda t: t.unsqueeze(1).to_broadcast([128, 8, 64])
        bdv = lambda T_, ck=ck: T_[:, ck, :].rearrange("p (g n) -> p g n", n=128)
        r1 = V(lambda e, bdv=bdv: e.tensor_tensor(out=bdv(BDA)[:, :, 0:64], in0=mask[:], in1=bcv(t1_), op=ALU.mult), [q_6, d_mk])
        r2 = V(lambda e, bdv=bdv: e.tensor_tensor(out=bdv(BDA)[:, :, 64:128], in0=mask[:], in1=bcv(t2_), op=ALU.mult), [r1])
        r3 = V(lambda e, bdv=bdv: e.tensor_tensor(out=bdv(BDB)[:, :, 0:64], in0=mask[:], in1=bcv(t2_), op=ALU.mult), [r2])
        r4 = V(lambda e, bdv=bdv: e.tensor_tensor(out=bdv(BDB)[:, :, 64:128], in0=mask[:], in1=bcv(t1_), op=ALU.mult), [r3])
        t_bd = [r4]
    a1 = V(lambda e: e.tensor_scalar(out=s1, in0=lmb[:], scalar1=cst[:, 1:2], scalar2=None, op0=ALU.mult), [t_row, d_c])
    a2 = p.op("act", lambda e: e.activation(out=s1, in_=s1, func=AF.Exp), [a1])
    a3 = V(lambda e: e.tensor_scalar(out=s2, in0=anb[:], scalar1=cst[:, 0:1], scalar2=None, op0=ALU.mult), [t_row, d_c])
    a4 = _sincos(p, s3, s2, s4, [a3], True, si, sk)
    a5 = V(lambda e: e.tensor_tensor(out=s3, in0=s3, in1=s1, op=ALU.mult), [a4, a2])
    s1v = lambda t: t.rearrange("p (g n) -> p g n", n=64)
    a6 = V(lambda e: e.tensor_copy(out=Pa[:, :, 0:64], in_=s1v(s3)), [a5])
    a7 = V(lambda e: e.tensor_copy(out=Pa[:, :, 64:128], in_=s1v(s3)), [a6])
    a8 = _sincos(p, s3, s2, s4, [a7], False, si, sk)
    a9 = V(lambda e: e.tensor_tensor(out=s3, in0=s3, in1=s1, op=ALU.mult), [a8])
    a10 = V(lambda e: e.tensor_copy(out=Pb[:, :, 0:64], in_=s1v(s3)), [a9])
    a11 = V(lambda e: e.tensor_scalar(out=Pb[:, :, 64:128], in0=s1v(s3), scalar1=-1.0, scalar2=None, op0=ALU.mult), [a10])
    qv = lambda t: t[:, :].rearrange("p (g t) -> p g t", t=129)
    b1 = V(lambda e: e.tensor_tensor(out=qv(q1), in0=tvb[:, :].unsqueeze(1).to_broadcast([128, 16, 129]),
                                     in1=lmT[:, :].unsqueeze(2).to_broadcast([128, 16, 129]), op=ALU.mult), [d_tv, t_T, a11])
    b2 = p.op("act", lambda e: e.activation(out=q1[:], in_=q1[:], func=AF.Exp), [b1])
    b3 = V(lambda e: e.tensor_tensor(out=qv(q2), in0=tvb[:, :].unsqueeze(1).to_broadcast([128, 16, 129]),
                                     in1=anT[:, :].unsqueeze(2).to_broadcast([128, 16, 129]), op=ALU.mult), [d_tv, t_T])
    b4 = _sincos(p, q3[:], q2[:], q4[:], [b3], True, qi_[:], kfq[:])
    b5 = V(lambda e: e.tensor_tensor(out=Qa[:, :, :], in0=qv(q3), in1=qv(q1), op=ALU.mult), [b4, b2])
    b6 = _sincos(p, q3[:], q2[:], q4[:], [b5], False, qi_[:], kfq[:])
    b7 = V(lambda e: e.tensor_tensor(out=q3[:], in0=q3[:], in1=q1[:], op=ALU.mult), [b6])
    b8 = V(lambda e: e.tensor_scalar(out=Qb[:, :, :], in0=qv(q3), scalar1=cst[:, 2:3], scalar2=None, op0=ALU.mult), [b7, d_c])
    b9 = V(lambda e: e.tensor_copy(out=Qa16[:], in_=Qa[:]), [b5])
    b10 = V(lambda e: e.tensor_copy(out=Qb16[:], in_=Qb[:]), [b8])
    tabs = [a7, a11, b9, b10, t_ccp, t_ccc, d_tri, d_dv, d_id] + t_bd + u_toks

    ub = Ring([p.sb([128, 128], BF16) for _ in range(3)])
    banks = Ring([p.ps([128, 512], F32) for _ in range(6)])
    ps_y = p.ps([128, 512], F32)
    ps_t = p.ps([128, 512], F32)
    t1b = Ring([p.sb([128, 512], F32) for _ in range(2)]); t2b = Ring([p.sb([128, 512], F32) for _ in range(2)])
    Vb = Ring([p.sb([128, 512], BF16) for _ in range(2)])
    x1b = Ring([p.sb([128, 4, 128], F32) for _ in range(2)]); x2b = Ring([p.sb([128, 4, 128], F32) for _ in range(2)])
    Xb = Ring([p.sb([128, 8, 128], BF16) for _ in range(2)])
    PadA = Ring([p.sb([128, 8, 128], BF16) for _ in range(2)]); PadB = Ring([p.sb([128, 8, 128], BF16) for _ in range(2)])
    yo = Ring([p.sb([128, 128], F32) for _ in range(3)])
    yr = Ring([p.sb([128, 128], F32) for _ in range(3)])
    EA = p.sb([128, 16], F32); EB = p.sb([128, 16], F32); e1t = p.sb([128, 16], F32)
    HA = [p.sb([128, 16], F32) for _ in range(2)]; HB = [p.sb([128, 16], F32) for _ in range(2)]
    n1 = p.sb([128, 16], F32); n2 = p.sb([128, 16], F32)
    t_z = [V(lambda e, pad=pad: e.memset(pad[:], 0.0)) for pad in PadA.bufs + PadB.bufs]
    hcur = 0
    t_Hread = []; t_E_read = []; outs = []; y_free = []; tr_free = []
    for si_, (blocks, init) in enumerate(seqs):
        if init is not None:
            ta = p.dma("sp", "h0a", HA[hcur][:], I["H0A"][init], t_Hread)
            tb_ = p.dma("sp", "h0b", HB[hcur][:], I["H0B"][init], t_Hread)
        else:
            ta = V(lambda e, h=HA[hcur]: e.memset(h[:], 0.0), t_Hread)
            tb_ = V(lambda e, h=HB[hcur]: e.memset(h[:], 0.0), t_Hread)
        t_H = [ta, tb_]
        def do_block(rk, col0, L, yrow0):
            nonlocal t_H, hcur, t_E_read, t_Hread, y_free, tr_free
            e_toks = []; pad_readers = []
            for ck in range(2):
                uft = U[:, ck, rk, col0:col0 + L]
                kb_, ubt, dub = ub.next()
                t_ub = p.op("act", lambda e, ubt=ubt, uft=uft: e.copy(out=ubt[:, 0:L], in_=uft), tabs + dub)
                kx, Xt, dX = Xb.next()
                x_toks = []
                for hc in range(2):
                    gg0 = ck * 8 + hc * 4
                    rows = slice(hc * 64, (hc + 1) * 64)
                    cols = slice(hc * 512, (hc + 1) * 512)
                    kA, bA, dA = banks.next()
                    mA = p.op("pe", lambda e, bA=bA, ubt=ubt, rows=rows, cols=cols, ck=ck: e.matmul(
                        bA[0:L, :], lhsT=ubt[rows, 0:L], rhs=BDA[rows, ck, cols], start=True, stop=True), [t_ub] + dA)
                    kB, bB, dB = banks.next()
                    mB = p.op("pe", lambda e, bB=bB, ubt=ubt, rows=rows, cols=cols, ck=ck: e.matmul(
                        bB[0:L, :], lhsT=ubt[rows, 0:L], rhs=BDB[rows, ck, cols], start=True, stop=True), [t_ub] + dB)
                    k1, t1t, d1_ = t1b.next(); k2, t2t, d2_ = t2b.next(); kv, Vt, dV = Vb.next()
                    pv = lambda t, gg0=gg0: t[0:L, gg0:gg0 + 4, :]
                    v3 = lambda t: t[0:L, :].rearrange("p (g n) -> p g n", n=128)
                    o1 = V(lambda e, t1t=t1t, bA=bA, pv=pv, v3=v3: e.tensor_tensor(out=v3(t1t), in0=v3(bA), in1=pv(Pa), op=ALU.mult), [mA] + d1_)
                    o2 = V(lambda e, t2t=t2t, bB=bB, pv=pv, v3=v3: e.tensor_tensor(out=v3(t2t), in0=v3(bB), in1=pv(Pb), op=ALU.mult), [mB] + d2_)
                    banks.done(kA, o1); banks.done(kB, o2)
                    o3 = V(lambda e, Vt=Vt, t1t=t1t, t2t=t2t: e.tensor_tensor(out=Vt[0:L, :], in0=t1t[0:L, :], in1=t2t[0:L, :], op=ALU.add), [o1, o2] + dV)
                    t1b.done(k1, o3); t2b.done(k2, o3)
                    kcA, cA, dcA = banks.next(); kcB, cB, dcB = banks.next()
                    mc = None
                    for g in range(4):
                        mc = p.op("pe", lambda e, cA=cA, Vt=Vt, g=g: e.matmul(
                            cA[:, g * 128:g * 128 + L], lhsT=Vt[0:L, g * 128:(g + 1) * 128], rhs=tri[0:L, 0:L],
                            start=True, stop=True), ([o3] + dcA + dcB) if g == 0 else [], inc=False)
                        mc = p.op("pe", lambda e, cB=cB, Vt=Vt, g=g: e.matmul(
                            cB[0:64, g * 128:g * 128 + L], lhsT=Vt[0:L, g * 128 + 64:(g + 1) * 128], rhs=tri[0:L, 0:L],
                            start=True, stop=True), [], inc=False)
                        mc = p.op("pe", lambda e, cB=cB, Vt=Vt, g=g: e.matmul(
                            cB[64:128, g * 128:g * 128 + L], lhsT=Vt[0:L, g * 128:g * 128 + 64], rhs=tri[0:L, 0:L],
                            start=True, stop=True), [], inc=(g == 3))
                    Vb.done(kv, mc)
                    c3 = lambda t: t[:, :].rearrange("p (g n) -> p g n", n=128)[:, :, 0:L]
                    qv_ = lambda t, gg0=gg0: t[:, gg0:gg0 + 4, 0:L]
                    kx1, x1t, dx1 = x1b.next(); kx2, x2t, dx2 = x2b.next()
                    r1 = V(lambda e, x1t=x1t, cA=cA, c3=c3, qv_=qv_: e.tensor_tensor(out=x1t[:, :, 0:L], in0=c3(cA), in1=qv_(Qa), op=ALU.mult), [mc] + dx1)
                    r2 = V(lambda e, x2t=x2t, cB=cB, c3=c3, qv_=qv_: e.tensor_tensor(out=x2t[:, :, 0:L], in0=c3(cB), in1=qv_(Qb), op=ALU.mult), [mc] + dx2)
                    r3 = V(lambda e, Xt=Xt, x1t=x1t, x2t=x2t, hc=hc: e.tensor_tensor(
                        out=Xt[:, hc * 4:(hc + 1) * 4, 0:L], in0=x1t[:, :, 0:L], in1=x2t[:, :, 0:L], op=ALU.add), [r1, r2] + (dX if hc == 0 else []))
                    r4 = V(lambda e, x1t=x1t, x2t=x2t, gg0=gg0: e.tensor_tensor(
                        out=EA[:, gg0:gg0 + 4], in0=x1t[:, :, L - 1], in1=x2t[:, :, L - 1], op=ALU.add), [r1, r2] + t_E_read)
                    r5 = V(lambda e, cB=cB, gg0=gg0: e.tensor_tensor(
                        out=e1t[:, gg0:gg0 + 4], in0=cB[:, :].rearrange("p (g n) -> p g n", n=128)[:, :, L - 1],
                        in1=Qa[:, gg0:gg0 + 4, L - 1], op=ALU.mult), [mc] + t_E_read)
                    r6 = V(lambda e, cA=cA, gg0=gg0: e.tensor_tensor(
                        out=EB[:, gg0:gg0 + 4], in0=cA[:, :].rearrange("p (g n) -> p g n", n=128)[:, :, L - 1],
                        in1=Qb[:, gg0:gg0 + 4, L - 1], op=ALU.mult), [mc] + t_E_read)
                    r7 = V(lambda e, gg0=gg0: e.tensor_tensor(
                        out=EB[:, gg0:gg0 + 4], in0=e1t[:, gg0:gg0 + 4], in1=EB[:, gg0:gg0 + 4], op=ALU.subtract), [r5, r6])
                    banks.done(kcA, r1, r6); banks.done(kcB, r2, r5)
                    x1b.done(kx1, r3, r4); x2b.done(kx2, r3, r4)
                    x_toks.append(r3)
                    e_toks += [r4, r7]
                ub.done(kb_, mB)
                kpa, pa, dpa = PadA.next(); kpb, pbt, dpb = PadB.next()
                tp = None
                for g in range(8):
                    Gx = ck * 8 + g
                    tp = V(lambda e, pa=pa, g=g, Gx=Gx, h=HA[hcur]: e.tensor_scalar(
                        out=pa[:, g, g * 16:(g + 1) * 16], in0=Ccc[:, Gx * 16:(Gx + 1) * 16], scalar1=h[:, Gx:Gx + 1],
                        scalar2=None, op0=ALU.mult), (t_H + [t_ccc] + dpa + t_z) if g == 0 else [])
                    tp = V(lambda e, pbt=pbt, g=g, Gx=Gx, h=HB[hcur]: e.tensor_scalar(
                        out=pbt[:, g, g * 16:(g + 1) * 16], in0=Ccc[:, Gx * 16:(Gx + 1) * 16], scalar1=h[:, Gx:Gx + 1],
                        scalar2=None, op0=ALU.mult), dpb if g == 0 else [])
                my = None
                for g in range(8):
                    Gx = ck * 8 + g
                    my = p.op("pe", lambda e, Xt=Xt, g=g, ck=ck: e.matmul(
                        ps_y[:, 0:L], lhsT=CcP[:, ck, g * 128:(g + 1) * 128], rhs=Xt[:, g, 0:L], start=(g == 0), stop=False),
                        (x_toks + [tp] + y_free) if g == 0 else [], inc=False)
                    my = p.op("pe", lambda e, pa=pa, g=g, Gx=Gx: e.matmul(
                        ps_y[:, 0:L], lhsT=pa[:, g, :], rhs=Qa16[:, Gx, 1:L + 1], start=False, stop=False), [], inc=False)
                    my = p.op("pe", lambda e, pbt=pbt, g=g, Gx=Gx: e.matmul(
                        ps_y[:, 0:L], lhsT=pbt[:, g, :], rhs=Qb16[:, Gx, 1:L + 1], start=False, stop=(g == 7)), [], inc=(g == 7))
                Xb.done(kx, my); PadA.done(kpa, my); PadB.done(kpb, my)
                pad_readers.append(tp)
                ko, yot, dyo = yo.next()
                ty = V(lambda e, yot=yot, uft=uft, ck=ck: e.scalar_tensor_tensor(
                    out=yot[:, 0:L], in0=uft, scalar=dv[:, ck:ck + 1], in1=ps_y[:, 0:L], op0=ALU.mult, op1=ALU.add), [my] + dyo)
                y_free = [ty]
                ttr = p.op("pe", lambda e, yot=yot: e.transpose(out=ps_t[0:L, 0:128], in_=yot[:, 0:L], identity=identf[:]),
                           [ty] + tr_free)
                kyr, yrt, dyr = yr.next()
                tcp = p.op("act", lambda e, yrt=yrt: e.copy(out=yrt[0:L, :], in_=ps_t[0:L, 0:128]), [ttr] + dyr)
                tr_free = [tcp]
                yo.done(ko, ttr)
                tst = p.dma("act", f"y{kyr}", pubY(yrow0, ck, L), yrt[0:L, :], [tcp])
                yr.done(kyr, tst)
                outs.append(tst)
            hn = 1 - hcur
            QaL = Qa[:, :, L]; QbL = Qb[:, :, L]
            dep0 = t_H + e_toks + pad_readers + t_Hread
            u1 = V(lambda e, h=HA[hcur], QaL=QaL: e.tensor_tensor(out=n1[:], in0=h[:], in1=QaL, op=ALU.mult), dep0)
            u2 = V(lambda e, h=HB[hcur], QbL=QbL: e.tensor_tensor(out=n2[:], in0=h[:], in1=QbL, op=ALU.mult), [u1])
            u3 = V(lambda e: e.tensor_tensor(out=n1[:], in0=n1[:], in1=n2[:], op=ALU.add), [u2])
            u4 = V(lambda e, h=HA[hn]: e.tensor_tensor(out=h[:], in0=n1[:], in1=EA[:], op=ALU.add), [u3])
            u5 = V(lambda e, h=HB[hcur], QaL=QaL: e.tensor_tensor(out=n1[:], in0=h[:], in1=QaL, op=ALU.mult), [u4])
            u6 = V(lambda e, h=HA[hcur], QbL=QbL: e.tensor_tensor(out=n2[:], in0=h[:], in1=QbL, op=ALU.mult), [u5])
            u7 = V(lambda e: e.tensor_tensor(out=n1[:], in0=n1[:], in1=n2[:], op=ALU.subtract), [u6])
            u8 = V(lambda e, h=HB[hn]: e.tensor_tensor(out=h[:], in0=n1[:], in1=EB[:], op=ALU.add), [u7])
            t_E_read = [u8]; t_Hread = [u8]; t_H = [u4, u8]
            hcur = hn
        for blk_ in blocks:
            do_block(*blk_)
        tf = p.dma("sp", "hf", HF_d[si_], HA[hcur][:], t_H)
        outs.append(tf)
        t_Hread = t_Hread + [tf]
    return outs[-8:]
R_ = 1152
RT_ = 9


def _zz_block(k, j):
    m = j // 2
    return 8 * m + k if j % 2 == 0 else 8 * m + 7 - k


def _zz_owner(S):
    m, x = S // 8, S % 8
    return (x, 2 * m) if x <= 3 else (7 - x, 2 * m + 1)


P_SMAX = [8 * (j // 2) + 4 if j % 2 == 0 else 8 * (j // 2) + 8 for j in range(8)]


def emit_rmsout(p, x, nw, y, eps=1e-6):
    D = 2048
    nwb = p.sb([128, D], F32)
    t_nw = p.dma("sp", "nw", nwb[:], nw.partition_broadcast(128))
    xb = Ring([p.sb([128, D], F32) for _ in range(2)])
    tf = Ring([p.sb([128, D], F32) for _ in range(2)])
    st = Ring([p.sb([128, 2], F32) for _ in range(2)])
    fin = []
    for r in range(RT_):
        rows = slice(r * 128, (r + 1) * 128)
        kx, xt, dx = xb.next()
        t_x = p.dma("sp", f"x{kx}", xt[:], x[rows, :], dx)
        kt, tt, dt_ = tf.next()
        ks, s_, ds = st.next()
        t_sq = p.op("act", lambda e, tt=tt, xt=xt, s_=s_: e.activation(out=tt[:], in_=xt[:], func=AF.Square, accum_out=s_[:, 0:1]), [t_x] + dt_ + ds)
        t1 = p.op("dve", lambda e, s_=s_: e.tensor_scalar(out=s_[:, 1:2], in0=s_[:, 0:1], scalar1=1.0 / D, scalar2=eps, op0=ALU.mult, op1=ALU.add), [t_sq])
        t2 = p.op("act", lambda e, s_=s_: e.activation(out=s_[:, 1:2], in_=s_[:, 1:2], func=AF.Sqrt), [t1])
        t3 = p.op("dve", lambda e, s_=s_: e.reciprocal(out=s_[:, 1:2], in_=s_[:, 1:2]), [t2])
        t4 = p.op("dve", lambda e, tt=tt, xt=xt, s_=s_: e.scalar_tensor_tensor(out=tt[:], in0=xt[:], scalar=s_[:, 1:2], in1=nwb[:], op0=ALU.mult, op1=ALU.mult), [t3, t_nw])
        xb.done(kx, t4); st.done(ks, t4)
        t5 = p.dma("act", f"o{kt}", y[rows, :], tt[:], [t4])
        tf.done(kt, t5)
        fin.append(t5)
    return fin[-2:]


def emit_gather(p, ck_d, cv_d, ci_d, pt_d, Kp, Vp, ip):
    pt = p.sb([128, 1], I32)
    idx = p.sb([128, 8], I32)
    bufs = Ring([p.sb([128, 8192], F32) for _ in range(3)])
    t_pt = p.dma("sp", "pt", pt[:], pt_d)
    t_i = None
    for e_ in range(8):
        t_i = p.op("dve", lambda e, e_=e_: e.tensor_scalar(
            out=idx[:, e_:e_ + 1], in0=pt[:], scalar1=8, scalar2=e_, op0=ALU.mult, op1=ALU.add), [t_pt])
    outs = []
    jobs = [(ci_d, pt, 0, ip)] + [(ck_d, idx, e_, Kp[:, e_, :]) for e_ in range(8)] + \
           [(cv_d, idx, e_, Vp[:, e_, :]) for e_ in range(8)]
    for n, (src, it, col, dst) in enumerate(jobs):
        kb, bt, db = bufs.next()
        tok = p.lane_op("pool", f"g{kb}", lambda e, bt=bt, src=src, it=it, col=col: e.indirect_dma_start(
            out=bt[:], out_offset=None, in_=src, in_offset=bass.IndirectOffsetOnAxis(ap=it[:, col:col + 1], axis=0)),
            [t_i, t_pt] + db)
        t_o = p.dma("sp", f"go{kb}", dst, bt[:], [tok])
        bufs.done(kb, t_o)
        outs.append(t_o)
    return outs[-3:]


class PromptSrc:
    def __init__(self, p, qT_s, qiT_s, wT_s, gKT, gV, gKi, idxK_d, idxI_d):
        self.p = p
        self.qT_s, self.qiT_s, self.wT_s, self.gKT, self.gV, self.gKi = qT_s, qiT_s, wT_s, gKT, gV, gKi
        self.idxK = p.sb([128, 4 * 144], I32)
        self.idxI = p.sb([128, 144], I32)
        self.t_ik = p.dma("sp", "ik", self.idxK[:], idxK_d)
        self.t_ii = p.dma("sp", "ii", self.idxI[:], idxI_d)
        self.u0 = [sum(P_SMAX[:j]) for j in range(8)]
        self.n = 0

    def load_q(self, j, t, deps):
        return self.p.dma("pool", "lq", t[:, :].rearrange("p (h t) -> p h t", h=16),
                          self.qT_s[:, j * 128:(j + 1) * 128].rearrange("(h d) t -> d h t", d=128), deps)

    def load_qi(self, j, t, deps):
        return self.p.dma("pool", "lqi", t[:, :].rearrange("p (h t) -> p h t", h=16),
                          self.qiT_s[:, j * 128:(j + 1) * 128].rearrange("(h d) t -> d h t", d=64), deps)

    def load_w(self, j, t, deps):
        return self.p.dma("sp", "lw", t[:, :].rearrange("p (h t) -> p h t", h=16),
                          self.wT_s[:, j * 128:(j + 1) * 128].partition_broadcast(128), deps)

    def _ind(self, t, table, idx_ap, deps, dep2):
        self.n += 1
        return self.p.lane_op("pool", f"in{self.n % 6}", lambda e: e.indirect_dma_start(
            out=t[:, :], out_offset=None, in_=table, in_offset=bass.IndirectOffsetOnAxis(ap=idx_ap, axis=0)),
            list(deps) + [dep2])

    def load_ki(self, j, dl, t, deps):
        u = self.u0[j] + dl
        return self._ind(t, self.gKi, self.idxI[:, u:u + 1], deps, self.t_ii)

    def load_k(self, j, g, dl, t, deps):
        u = self.u0[j] + dl
        return self._ind(t, self.gKT, self.idxK[:, g * 144 + u:g * 144 + u + 1], deps, self.t_ik)

    def load_v(self, j, g, dl, t, deps):
        u = self.u0[j] + dl
        return self._ind(t, self.gV, self.idxK[:, g * 144 + u:g * 144 + u + 1], deps, self.t_ik)


class SampleSrc:
    def __init__(self, p, qT_s, qiT_s, wT_s, kT_s, kiT_s, z, KTs, kiTs, Vp):
        self.p = p
        self.qT_s, self.qiT_s, self.wT_s, self.kT_s, self.kiT_s, self.z = qT_s, qiT_s, wT_s, kT_s, kiT_s, z
        self.KTs, self.kiTs, self.Vp = KTs, kiTs, Vp
        self.c = slice(1024, 1028)

    def load_q(self, j, t, deps):
        return self.p.dma("pool", "lq", t[:, :].rearrange("p (h t) -> p h t", h=16),
                          self.qT_s[:, self.c].rearrange("(h d) t -> d h t", d=128), deps)

    def load_qi(self, j, t, deps):
        return self.p.dma("pool", "lqi", t[:, :].rearrange("p (h t) -> p h t", h=16),
                          self.qiT_s[:, self.c].rearrange("(h d) t -> d h t", d=64), deps)

    def load_w(self, j, t, deps):
        return self.p.dma("sp", "lw", t[:, :].rearrange("p (h t) -> p h t", h=16),
                          self.wT_s[:, self.c].partition_broadcast(128), deps)

    def _new(self, t, dst, src, deps):
        z_ = self.p.op("dve", lambda e: e.memset(t[:, :], 0.0), deps)
        return self.p.dma("pool", "ln", dst, src, [z_])

    def load_ki(self, j, dl, t, deps):
        if dl == 0:
            return self._new(t, t[0:64, 0:4], self.kiT_s[0:64, self.c], deps)
        pg = 128 - dl
        return self.p.dma("pool", "lki", t[0:64, :], self.kiTs[:, pg * 128:(pg + 1) * 128], deps)

    def load_k(self, j, g, dl, t, deps):
        if dl == 0:
            return self._new(t, t[:, 0:4], self.kT_s[g * 128:(g + 1) * 128, self.c], deps)
        pg = 128 - dl
        return self.p.dma("pool", "lk", t[:, :], self.KTs[g * 128:(g + 1) * 128, pg * 128:(pg + 1) * 128], deps)

    def load_v(self, j, g, dl, t, deps):
        if dl == 0:
            return self._new(t, t[0:4, :], self.z[1024:1028, 2560 + g * 128:2560 + (g + 1) * 128], deps)
        pg = 128 - dl
        return self.p.dma("pool", "lv", t[:, :], self.Vp[pg * 128:(pg + 1) * 128, g * 128:(g + 1) * 128], deps)


def build_fused():
    nc = bass.Bass("TRN2", target_bir_lowering=False)
    EI = lambda n, s, dt=F32: nc.dram_tensor(n, list(s), dt, kind="ExternalInput").ap()
    EO = lambda n, s, dt=F32: nc.dram_tensor(n, list(s), dt, kind="ExternalOutput").ap()
    IT = lambda n, s, dt=F32: nc.dram_tensor(n, list(s), dt)
    xrows = EI("xrows", [R_, 2048]); prows = EI("prows", [4, R_, 256])
    norm_w = EI("norm_w", [4, 1, 2048]); ple_nw = EI("ple_nw", [4, 1, 2048]); fnw = EI("fnw", [1, 2048])
    a_win = EI("a_win", [2, 2048, 6224]); a_wout = EI("a_wout", [2, 2048, 2048])
    s_win = EI("s_win", [2, 2048, 4096]); s_wglu = EI("s_wglu", [2, 2048, 2048]); s_bglu = EI("s_bglu", [2, 1, 2048])
    s_wout = EI("s_wout", [2, 2048, 2048]); p_wg = EI("p_wg", [4, 2048, 2048]); p_wp = EI("p_wp", [4, 256, 2048])
    ident = EI("ident", [128, 128])
    ck = [EI(f"ck{l}", [10240, 8192]) for l in range(2)]; cv = [EI(f"cv{l}", [10240, 8192]) for l in range(2)]
    ci = [EI(f"ci{l}", [1280, 8192]) for l in range(2)]
    pt = EI("pt", [128, 1], I32)
    vtp = EI("vtp", [1, 144]); Dp = EI("Dp", [2, 128, 2048]); C0p = EI("C0p", [128, 128])
    vts = EI("vts", [1, 129]); Ds = EI("Ds", [2, 128, 64]); C0s = EI("C0s", [128, 4]); cbv = EI("cbv", [1, 16])
    idxK = EI("idxK", [128, 576], I32); idxI = EI("idxI", [128, 144], I32)
    SP = {}
    for nm, shp in [("lre_row", [1, 1024]), ("lim_row", [1, 1024]), ("ldt_row", [1, 1024]),
                    ("lreT2", [128, 16]), ("limT2", [128, 16]), ("ldtT2", [128, 16]),
                    ("lre_gi", [2, 128, 64]), ("lim_gi", [2, 128, 64]), ("ldt_gi", [2, 128, 64]),
                    ("bre_gi", [2, 128, 64]), ("bim_gi", [2, 128, 64]),
                    ("CcP", [2, 128, 1024]), ("Ccc", [128, 256]), ("dvec", [128, 2]),
                    ("H0A", [4, 128, 16]), ("H0B", [4, 128, 16])]:
        SP[nm] = EI("sp_" + nm, [2, 2] + shp)
    tri = EI("tri", [128, 128]); cst = EI("cst", [128, 4]); tvec = EI("tvec", [1, 129]); mask = EI("mask", [128, 8, 64])
    uidx = EI("uidx", [2, 128, 8], I32); yidx = EI("yidx", [128, 36], I32)
    yout = EO("yout", [R_, 2048]); kvk = EO("kvk", [2, R_, 1088])
    HFp = EO("HFp", [2, 2, 1, 128, 16]); HFs = EO("HFs", [2, 2, 4, 128, 16])

    hA = IT("hA", [R_, 2048]).ap(); hB = IT("hB", [R_, 2048]).ap()
    z = IT("z", [R_, 6224]).ap()
    qT_s = IT("qT_s", [2048, R_]).ap(); kT_s = IT("kT_s", [512, R_]).ap(); qiT_s = IT("qiT_s", [1024, R_]).ap()
    kiT_s = IT("kiT_s", [128, R_]).ap(); wT_s = IT("wT_s", [16, R_]).ap()
    pubKT = IT("pubKT", [4 * 8 * 128, 128]); pubV = IT("pubV", [4 * 8 * 128, 128]); pubKi = IT("pubKi", [8 * 128, 128])
    gKT = IT("gKT", [4 * 4 * 8 * 128, 128]); gV = IT("gV", [4 * 4 * 8 * 128, 128]); gKi = IT("gKi", [4 * 8 * 128, 128])
    oT_s = IT("oT_s", [2048, R_]).ap(); o_s = IT("o_s", [R_, 2048]).ap(); g_s = IT("g_s", [R_, 2048]).ap()
    Kp = IT("Kp", [16384, 512]).ap(); Vp = IT("Vp", [16384, 512]).ap(); ip = IT("ip", [16384, 64]).ap()
    KTs = IT("KTs", [512, 16384]).ap(); kiTs = IT("kiTs", [64, 16384]).ap()
    uT_s = IT("uT_s", [2048, R_]); gU = IT("gU", [4 * 2048, R_])
    pubY = IT("pubY", [4608, 512]); gY = IT("gY", [4 * 4608, 512])
    y_s = IT("y_s", [R_, 2048]).ap(); y3 = IT("y3", [R_, 2048]).ap()
    RG = [[0, 1, 2, 3], [4, 5, 6, 7]]

    def allgather(p, src, dst, deps, CR=None):
        rows = src.ap().shape[0]
        CR = CR or rows
        prev = list(deps)
        for c_ in range(rows // CR):
            cc = p.lane_op("pool", "cc", lambda e, c_=c_: e.collective_compute(
                "AllGather", ALU.bypass, replica_groups=RG, ins=[src.ap()[c_ * CR:(c_ + 1) * CR, :].opt()],
                outs=[dst.ap()[c_ * 4 * CR:(c_ + 1) * 4 * CR, :].opt()]), prev, incv=1)
            prev = [cc]
        return prev[0]

    with ExitStack() as es:
        p = Prog(nc, es)
        h, h1 = None, None
        cur = xrows
        bufs = [hA, hB]
        for i in range(4):
            l = i // 2
            hn1 = bufs[0] if cur is not bufs[0] else bufs[1]
            if i % 2 == 0:
                with p.stage():
                    emit_linear(p, RT_, 2048, 6224, "rms", "none", cur, a_win[l], z, ident, nw=norm_w[i])
                for (src, dst, C) in [(z[:, 0:2048], qT_s, 2048), (z[:, 2048:2560], kT_s, 512),
                                      (z[:, 5120:6144], qiT_s, 1024), (z[:, 6144:6208], kiT_s, 64),
                                      (z[:, 6208:6224], wT_s, 16)]:
                    with p.stage():
                        emit_transpose(p, src, dst, R_, C, ident)
                with p.stage():
                    d0 = p.dma("sp", "k0", kvk[l][:, 0:1024], z[:, 2048:3072])
                    d1 = p.dma("sp", "k1", kvk[l][:, 1024:1088], z[:, 6144:6208])
                    pk = pubKT.ap().rearrange("(g j d) s -> g j d s", g=4, j=8)
                    d2 = [p.dma("act", f"k2{g}", pk[g], kT_s[g * 128:(g + 1) * 128, 0:1024].rearrange("d (j s) -> j d s", s=128))
                          for g in range(4)]
                    pv = pubV.ap().rearrange("(g j s) d -> g j s d", g=4, j=8)
                    d3 = [p.dma("act", f"k3{g}", pv[g], z[0:1024, 2560 + g * 128:2560 + (g + 1) * 128].rearrange("(j s) d -> j s d", s=128))
                          for g in range(4)]
                    d4 = p.dma("sp", "k4", pubKi.ap().rearrange("(j d) s -> j d s", j=8),
                               kiT_s[:, 0:1024].rearrange("d (j s) -> j d s", s=128))
                    c1 = allgather(p, pubKT, gKT, d2, 2048)
                    c2_ = allgather(p, pubV, gV, d3 + [c1], 2048)
                    c3 = allgather(p, pubKi, gKi, [d4, c2_])
                    p.op("sp", None, [d0, d1, c3], inc=False)
                with p.stage():
                    src = PromptSrc(p, qT_s, qiT_s, wT_s, gKT.ap(), gV.ap(), gKi.ap(), idxK, idxI)
                    emit_attn(p, 8, 128, P_SMAX, src, vtp, Dp, cbv, C0p,
                              lambda j, g: oT_s[g * 512:(g + 1) * 512, j * 128:(j + 1) * 128].rearrange("(r d) t -> d r t", d=128))
                with p.stage():
                    emit_gather(p, ck[l], cv[l], ci[l], pt, Kp.rearrange("(pg e s) c -> pg e (s c)", pg=128, e=8),
                                Vp.rearrange("(pg e s) c -> pg e (s c)", pg=128, e=8), ip.rearrange("(pg s) c -> pg (s c)", pg=128))
                with p.stage():
                    emit_transpose(p, Kp, KTs, 16384, 512, ident)
                with p.stage():
                    emit_transpose(p, ip, kiTs, 16384, 64, ident)
                with p.stage():
                    src = SampleSrc(p, qT_s, qiT_s, wT_s, kT_s, kiT_s, z, KTs, kiTs, Vp)
                    emit_attn(p, 1, 4, [129], src, vts, Ds, cbv, C0s,
                              lambda j, g: oT_s[g * 512:(g + 1) * 512, 1024:1028].rearrange("(r d) t -> d r t", d=128))
                with p.stage():
                    emit_transpose(p, oT_s, o_s, 2048, R_, ident)
                with p.stage():
                    emit_linear(p, RT_, 2048, 2048, "gate", "res", o_s, a_wout[l], hn1, ident, x2=z[:, 3072:5120], e1=cur)
            else:
                with p.stage():
                    emit_linear(p, RT_, 2048, 4096, "rms", "none", cur, s_win[l], z[:, 0:4096], ident, nw=norm_w[i])
                with p.stage():
                    emit_transpose(p, z[:, 0:2048], uT_s.ap(), R_, 2048, ident)
                for c2 in range(2):
                    I = {k_: v_[l, c2] for k_, v_ in SP.items()}
                    I.update({"tri": tri, "cst": cst, "tvec": tvec, "mask": mask})
                    pY = lambda yrow0, ck_, L, c2=c2: pubY.ap()[yrow0:yrow0 + L, c2 * 256 + ck_ * 128:c2 * 256 + (ck_ + 1) * 128]
                    pblocks = []
                    for S in range(32):
                        r, jj = _zz_owner(S)
                        pblocks.append((r, jj * 128, 128, S * 128))
                    with p.stage():
                        pre = [allgather(p, uT_s, gU, [], 128)] if c2 == 0 else []
                        seqs = [(pblocks, None)] + [([(i_, 1024, 4, 4096 + 128 * i_)], i_) for i_ in range(4)]
                        HFl = [HFp[l, c2][0]] + [HFs[l, c2][i_] for i_ in range(4)]
                        emit_scan(p, seqs, I, gU.ap(), uidx[c2], pY, HFl, ident, pre=pre)
                with p.stage():
                    c1 = allgather(p, pubY, gY, [], 512)
                    yix = p.sb([128, 36], I32)
                    t_yi = p.dma("sp", "yi", yix[:], yidx)
                    yb = Ring([p.sb([128, 512], F32) for _ in range(4)])
                    fin = []
                    for jj in range(9):
                        for r in range(4):
                            kb, bt, db = yb.next()
                            col = jj * 4 + r
                            tg = p.lane_op("pool", f"yg{kb}", lambda e, bt=bt, col=col: e.indirect_dma_start(
                                out=bt[:], out_offset=None, in_=gY.ap(),
                                in_offset=bass.IndirectOffsetOnAxis(ap=yix[:, col:col + 1], axis=0)), [c1, t_yi] + db)
                            to = p.dma("sp", f"yo{kb}", y_s[jj * 128:(jj + 1) * 128, r * 512:(r + 1) * 512], bt[:], [tg])
                            yb.done(kb, to)
                            fin.append(to)
                    p.op("sp", None, fin[-4:], inc=False)
                with p.stage():
                    emit_linear(p, RT_, 2048, 2048, "gelu", "glu", y_s, s_wglu[l], y3, ident, e1=y_s, bv=s_bglu[l])
                with p.stage():
                    emit_linear(p, RT_, 2048, 2048, "gate", "res", y3, s_wout[l], hn1, ident, x2=z[:, 2048:4096], e1=cur)
            with p.stage():
                emit_linear(p, RT_, 2048, 2048, "rms", "sigmoid", hn1, p_wg[i], g_s, ident, nw=ple_nw[i])
            hn2 = bufs[0] if hn1 is not bufs[0] else bufs[1]
            with p.stage():
                emit_linear(p, RT_, 256, 2048, "plain", "gres", prows[i], p_wp[i], hn2, ident, e1=hn1, e2=g_s)
            cur = hn2
        with p.stage():
            fin = emit_rmsout(p, cur, fnw, yout)
            p.op("sp", None, fin, inc=False)
        with p.stage():
            for e_ in ("pe", "act", "dve", "pool", "sp"):
                p.op(e_, None, [], inc=False)
    return nc


def _t5_bucket_np(n):
    n = np.asarray(n, dtype=np.int32)
    nf = np.maximum(n, 1).astype(np.float32)
    large = 16 + (np.log(nf / np.float32(16)) / np.float32(math.log(128 / 16)) * np.float32(16)).astype(np.int32)
    large = np.minimum(large, 31)
    return np.where(n < 16, n, large)


def _bias_tiles(rel_bias, NT):
    s_l = np.arange(128)[:, None]
    t_l = np.arange(NT)[None, :]
    d0 = t_l - s_l
    b0 = rel_bias[_t5_bucket_np(np.maximum(d0, 0))]
    b0 = np.where((d0 >= 0)[:, :, None], b0, np.float32(NEG))
    b1 = rel_bias[_t5_bucket_np(128 + d0)]
    D = np.stack([b0, b1]).transpose(0, 1, 3, 2).reshape(2, 128, 16 * NT)
    C0 = np.where(d0 >= 0, np.float32(0), np.float32(NEG)).astype(np.float32)
    return np.ascontiguousarray(D, dtype=np.float32), C0


def _scan_inputs(m, b, k, T, pi, ssm_lambda_re, ssm_lambda_im, ssm_log_dt, ssm_b_re, ssm_b_im, ssm_c_re, ssm_c_im, ssm_d, state_ssm_re, state_ssm_im):
    sp = {nm: [] for nm in ("lre_row", "lim_row", "ldt_row", "lreT2", "limT2", "ldtT2", "lre_gi", "lim_gi", "ldt_gi",
                            "bre_gi", "bim_gi", "CcP", "Ccc", "dvec", "H0A", "H0B")}
    for l in range(2):
        for c2 in range(2):
            g0 = 32 * k + 16 * c2
            gs = slice(g0, g0 + 16)
            lre, lim, ldt = ssm_lambda_re[l][gs], ssm_lambda_im[l][gs], ssm_log_dt[l][gs]
            sp["lre_row"].append(lre.reshape(1, 1024)); sp["lim_row"].append(lim.reshape(1, 1024))
            sp["ldt_row"].append(np.broadcast_to(ldt[:, None], (16, 64)).reshape(1, 1024))
            sp["lreT2"].append(np.concatenate([lre.T, lre.T], 0)); sp["limT2"].append(np.concatenate([lim.T, lim.T], 0))
            sp["ldtT2"].append(np.broadcast_to(ldt[None, :], (128, 16)))
            rep = lambda a: np.stack([np.repeat(a[ck * 8:(ck + 1) * 8], 16, axis=0) for ck in range(2)])
            sp["lre_gi"].append(rep(lre)); sp["lim_gi"].append(rep(lim))
            sp["ldt_gi"].append(rep(np.broadcast_to(ldt[:, None], (16, 64))))
            tb = lambda a: np.stack([a[ck * 8:(ck + 1) * 8].transpose(0, 2, 1).reshape(128, 64) for ck in range(2)])
            sp["bre_gi"].append(tb(ssm_b_re[l][gs])); sp["bim_gi"].append(tb(ssm_b_im[l][gs]))
            CcP = np.zeros((2, 128, 8, 128), np.float32)
            Ccc = np.zeros((128, 16, 16), np.float32)
            for G in range(16):
                ck_, g = G // 8, G % 8
                cc = np.concatenate([ssm_c_re[l][g0 + G].T, ssm_c_im[l][g0 + G].T], axis=0)
                CcP[ck_, :, g, g * 16:(g + 1) * 16] = cc
                Ccc[:, G, :] = cc
            sp["CcP"].append(CcP.reshape(2, 128, 1024)); sp["Ccc"].append(Ccc.reshape(128, 256))
            sp["dvec"].append(ssm_d[l][512 * k + 256 * c2:512 * k + 256 * c2 + 256].reshape(2, 128).T)
            hre = state_ssm_re[l][4 * b:4 * b + 4, gs].transpose(0, 2, 1)
            him = state_ssm_im[l][4 * b:4 * b + 4, gs].transpose(0, 2, 1)
            sp["H0A"].append(np.concatenate([hre, him], 1)); sp["H0B"].append(np.concatenate([him, hre], 1))
    for nm, lst in sp.items():
        a = np.stack([np.ascontiguousarray(x_, dtype=np.float32) for x_ in lst])
        m["sp_" + nm] = np.ascontiguousarray(a.reshape((2, 2) + a.shape[1:]))
    uidx = np.zeros((2, 128, 8), np.int32)
    for c2 in range(2):
        for ck_ in range(2):
            for r in range(4):
                uidx[c2, :, ck_ * 4 + r] = (4 * k + 2 * c2 + ck_) * 512 + r * 128 + pi
    m["uidx"] = uidx
    yidx = np.zeros((128, 36), np.int32)
    for jj in range(9):
        for r in range(4):
            if jj < 8:
                yidx[:, jj * 4 + r] = (T[jj] // 4) * 2048 + r * 512 + (T[jj] % 4) * 128 + pi
            else:
                yidx[:, jj * 4 + r] = 8 * 2048 + r * 512 + 128 * k + pi
    m["yidx"] = yidx


_IDENT = np.eye(128, dtype=np.float32)
_TRI = np.triu(np.ones((128, 128), np.float32))
_CST = np.stack([np.arange(128), -np.arange(128), np.where(np.arange(128) < 64, -1.0, 1.0),
                 np.where(np.arange(128) < 64, 1.0, -1.0)], axis=1).astype(np.float32)
_TVEC = np.arange(129, dtype=np.float32).reshape(1, 129)
_MASK = (np.arange(128)[:, None, None] // 16 == np.arange(8)[None, :, None]).astype(np.float32) * np.ones((1, 1, 64), np.float32)
_NC = {}


def kernel(x_prompt, x_sample, cache_k, cache_v, cache_kidx, state_ssm_re, state_ssm_im, page_table,
           p_prompt, p_sample, norm_w, final_norm_w, rel_bias, attn_w_in, attn_w_out, ssm_w_in,
           ssm_lambda_re, ssm_lambda_im, ssm_log_dt, ssm_b_re, ssm_b_im, ssm_c_re, ssm_c_im, ssm_d,
           ssm_w_glu, ssm_b_glu, ssm_w_out, ple_norm_w, ple_w_gate, ple_w_proj):
    f32 = lambda a: np.ascontiguousarray(np.asarray(a), dtype=np.float32)
    (x_prompt, x_sample, cache_k, cache_v, cache_kidx, state_ssm_re, state_ssm_im, p_prompt, p_sample, norm_w,
     final_norm_w, rel_bias, attn_w_in, attn_w_out, ssm_w_in, ssm_lambda_re, ssm_lambda_im, ssm_log_dt, ssm_b_re,
     ssm_b_im, ssm_c_re, ssm_c_im, ssm_d, ssm_w_glu, ssm_b_glu, ssm_w_out, ple_norm_w, ple_w_gate, ple_w_proj) = [
        f32(a) for a in (x_prompt, x_sample, cache_k, cache_v, cache_kidx, state_ssm_re, state_ssm_im, p_prompt,
                         p_sample, norm_w, final_norm_w, rel_bias, attn_w_in, attn_w_out, ssm_w_in, ssm_lambda_re,
                         ssm_lambda_im, ssm_log_dt, ssm_b_re, ssm_b_im, ssm_c_re, ssm_c_im, ssm_d, ssm_w_glu,
                         ssm_b_glu, ssm_w_out, ple_norm_w, ple_w_gate, ple_w_proj)]
    page_table = np.asarray(page_table).astype(np.int32)
    if "nc" not in _NC:
        _NC["nc"] = build_fused()
    nc = _NC["nc"]
    Dp, C0p = _bias_tiles(rel_bias, 128)
    Ds, C0s = _bias_tiles(rel_bias, 4)
    shared = {
        "norm_w": norm_w.reshape(4, 1, 2048), "ple_nw": ple_norm_w.reshape(4, 1, 2048), "fnw": final_norm_w.reshape(1, 2048),
        "a_win": attn_w_in, "a_wout": attn_w_out, "s_win": ssm_w_in, "s_wglu": ssm_w_glu,
        "s_bglu": ssm_b_glu.reshape(2, 1, 2048), "s_wout": ssm_w_out, "p_wg": ple_w_gate, "p_wp": ple_w_proj,
        "ident": _IDENT, "Dp": Dp, "C0p": C0p, "Ds": Ds, "C0s": C0s, "vts": np.zeros((1, 129), np.float32),
        "cbv": np.ascontiguousarray(rel_bias[31].reshape(1, 16)), "tri": _TRI, "cst": _CST, "tvec": _TVEC,
        "mask": np.ascontiguousarray(_MASK),
    }
    for l in range(2):
        shared[f"ck{l}"] = cache_k[l].reshape(10240, 8192)
        shared[f"cv{l}"] = cache_v[l].reshape(10240, 8192)
        shared[f"ci{l}"] = cache_kidx[l].reshape(1280, 8192)
    pi = np.arange(128, dtype=np.int32)
    in_maps = []
    for c in range(NCORES):
        b, k = c // 4, c % 4
        T = [_zz_block(k, j) for j in range(8)]
        rows = np.concatenate([np.arange(t * 128, (t + 1) * 128) for t in T])
        m = dict(shared)
        xr = np.zeros((R_, 2048), np.float32)
        xr[0:1024] = x_prompt[b][rows]
        xr[1024:1028] = x_sample[c]
        pr = np.zeros((4, R_, 256), np.float32)
        pr[:, 0:1024] = p_prompt[:, b][:, rows]
        pr[:, 1024:1028] = p_sample[:, c]
        m["xrows"] = xr
        m["prows"] = pr
        m["pt"] = np.ascontiguousarray(page_table[c].reshape(128, 1))
        m["vtp"] = np.concatenate([np.where(np.arange(P_SMAX[j]) <= T[j], 0.0, NEG) for j in range(8)]).reshape(1, 144).astype(np.float32)
        idxK = np.zeros((128, 4, 144), np.int32)
        idxI = np.zeros((128, 144), np.int32)
        u = 0
        for j in range(8):
            for dl in range(P_SMAX[j]):
                r, jj = _zz_owner(max(T[j] - dl, 0))
                for g in range(4):
                    idxK[:, g, u] = (g // 2) * 8192 + r * 2048 + (g % 2) * 1024 + jj * 128 + pi
                idxI[:, u] = r * 1024 + jj * 128 + pi
                u += 1
        m["idxK"] = idxK.reshape(128, 576)
        m["idxI"] = idxI
        _scan_inputs(m, b, k, T, pi, ssm_lambda_re, ssm_lambda_im, ssm_log_dt, ssm_b_re, ssm_b_im, ssm_c_re, ssm_c_im, ssm_d, state_ssm_re, state_ssm_im)
        in_maps.append(m)
    res = run_bass_kernel_spmd(nc, in_maps, core_ids=list(range(NCORES)))
    y_p = np.zeros((2, 4096, 2048), np.float32); y_s = np.zeros((8, 4, 2048), np.float32)
    k_p = np.zeros((2, 2, 4096, 4, 128), np.float32); v_p = np.zeros_like(k_p); ki_p = np.zeros((2, 2, 4096, 64), np.float32)
    k_s = np.zeros((2, 8, 4, 4, 128), np.float32); v_s = np.zeros_like(k_s); ki_s = np.zeros((2, 8, 4, 64), np.float32)
    hr_p = np.zeros((2, 2, 128, 64), np.float32); hi_p = np.zeros_like(hr_p)
    hr_s = np.zeros((2, 8, 128, 64), np.float32); hi_s = np.zeros_like(hr_s)
    for c in range(NCORES):
        b, k = c // 4, c % 4
        r_ = res.results[c]
        yo, kv = r_["yout"], r_["kvk"]
        for j in range(8):
            t = _zz_block(k, j)
            sl = slice(t * 128, (t + 1) * 128)
            y_p[b, sl] = yo[j * 128:(j + 1) * 128]
            for l in range(2):
                blk = kv[l][j * 128:(j + 1) * 128]
                k_p[l, b, sl] = blk[:, 0:512].reshape(128, 4, 128)
                v_p[l, b, sl] = blk[:, 512:1024].reshape(128, 4, 128)
                ki_p[l, b, sl] = blk[:, 1024:1088]
        y_s[c] = yo[1024:1028]
        for l in range(2):
            blk = kv[l][1024:1028]
            k_s[l, c] = blk[:, 0:512].reshape(4, 4, 128); v_s[l, c] = blk[:, 512:1024].reshape(4, 4, 128)
            ki_s[l, c] = blk[:, 1024:1088]
            for c2 in range(2):
                gs = slice(32 * k + 16 * c2, 32 * k + 16 * c2 + 16)
                H = r_["HFp"][l, c2, 0]
                hr_p[l, b, gs] = H[0:64].T; hi_p[l, b, gs] = H[64:128].T
                for i_ in range(4):
                    H = r_["HFs"][l, c2, i_]
                    hr_s[l, 4 * b + i_, gs] = H[0:64].T; hi_s[l, 4 * b + i_, gs] = H[64:128].T
    return (y_p, y_s, k_p, v_p, ki_p, hr_p, hi_p, k_s, v_s, ki_s, hr_s, hi_s)
```

```python
import math
from contextlib import ExitStack, contextmanager
import numpy as np
import concourse.bass as bass
import concourse.mybir as mybir
from concourse.bass_utils import run_bass_kernel_spmd

F32 = mybir.dt.float32
BF16 = mybir.dt.bfloat16
I32 = mybir.dt.int32
AF = mybir.ActivationFunctionType
ALU = mybir.AluOpType
AX = mybir.AxisListType
NCORES = 8
EPOCH = 30000
PER = EPOCH // 16


class Prog:
    ENG = ("pe", "act", "dve", "pool", "sp")

    def __init__(self, nc, es):
        self.nc = nc
        self.es = es
        self.cnt = {e: 0 for e in self.ENG}
        self.lanes = {}
        self.csem = {e: [] for e in self.ENG}
        self.lsem = {}
        self.lane_inc = {}
        self.waited = {e: {} for e in self.ENG}
        self.nuniq = 0
        self.st = None
        self.ops = None
        self.barrier = []
        self.seen = set()

    @contextmanager
    def stage(self, name=""):
        with ExitStack() as st:
            self.st = st
            self.ops = {e: [] for e in self.ENG}
            self.seen = set()
            self.stage_lane_map = {}
            yield self
            self._emit()
            bar = []
            for e in self.ENG:
                if self.cnt[e] > 0:
                    bar.append(("c", e, self.cnt[e] - 1))
            for ln, (ep, val) in self.lanes.items():
                if val > 0:
                    bar.append(("d", ln, ep, val))
            self.barrier = bar
            self.st = None

    def sb(self, shape, dt, name=None):
        self.nuniq += 1
        return self.st.enter_context(self.nc.sbuf_tensor(name or f"sb{self.nuniq}", list(shape), dt))

    def ps(self, shape, dt=F32, name=None):
        self.nuniq += 1
        return self.st.enter_context(self.nc.psum_tensor(name or f"ps{self.nuniq}", list(shape), dt))

    def _deps(self, eng, deps):
        deps = [d for d in deps if d is not None]
        if eng not in self.seen:
            self.seen.add(eng)
            deps = list(self.barrier) + deps
        return tuple(deps)

    def op(self, eng, fn, deps=(), inc=True):
        deps = self._deps(eng, deps)
        tok = None
        if inc:
            tok = ("c", eng, self.cnt[eng])
            self.cnt[eng] += 1
        self.ops[eng].append((fn, deps, tok, 1))
        return tok

    def dma(self, eng, lane, out, in_, deps=()):
        return self.lane_op(eng, lane, lambda e: e.dma_start(out=out, in_=in_), deps)

    def lane_op(self, eng, lane, fn, deps=(), incv=16):
        deps = self._deps(eng, deps)
        if incv == 16:
            lane = "L%d" % self.stage_lane_map.setdefault(lane, len(self.stage_lane_map))
        ep, val = self.lanes.get(lane, (0, 0))
        if val + incv > EPOCH:
            ep, val = ep + 1, 0
        val += incv
        self.lanes[lane] = (ep, val)
        tok = ("d", lane, ep, val)
        self.ops[eng].append((fn, deps, tok, incv))
        return tok

    def _semval(self, tok):
        if tok[0] == "c":
            _, e, i = tok
            ep = i // EPOCH
            while len(self.csem[e]) <= ep:
                self.csem[e].append(self.es.enter_context(self.nc.semaphore(f"s_{e}{len(self.csem[e])}")))
            return self.csem[e][ep], (i % EPOCH) + 1
        _, ln, ep, val = tok
        lst = self.lsem.setdefault(ln, [])
        while len(lst) <= ep:
            lst.append(self.es.enter_context(self.nc.semaphore(f"l_{ln}_{len(lst)}")))
        return lst[ep], val

    def _emit(self):
        nc = self.nc
        with nc.Block() as block:
            def make(ename):
                def body(eng):
                    waited = self.waited[ename]
                    for fn, deps, tok, incv in self.ops[ename]:
                        for d in deps:
                            s, v = self._semval(d)
                            key = id(s)
                            if waited.get(key, 0) >= v:
                                continue
                            waited[key] = v
                            eng.wait_ge(s, v)
                        if fn is None:
                            continue
                        ins = fn(eng)
                        if tok is not None:
                            s, v = self._semval(tok)
                            ins.then_inc(s, incv if tok[0] == "d" else 1)
                return body
            block.tensor(make("pe"))
            block.scalar(make("act"))
            block.vector(make("dve"))
            block.gpsimd(make("pool"))
            block.sync(make("sp"))


class Ring:
    def __init__(self, bufs):
        self.bufs = bufs
        self.i = 0
        self.readers = [[] for _ in bufs]

    def next(self):
        k = self.i % len(self.bufs)
        self.i += 1
        deps = self.readers[k]
        self.readers[k] = []
        return k, self.bufs[k], deps

    def done(self, k, *toks):
        self.readers[k].extend(t for t in toks if t is not None)
def emit_linear(p, RT, D, N, pro, epi, x, w, y, ident_d, x2=None, nw=None, e1=None, e2=None, bv=None, eps=1e-6):
    R = RT * 128
    KC = D // 128
    NT = (N + 511) // 512
    ident_f = p.sb([128, 128], F32)
    ident = p.sb([128, 128], BF16)
    XT = p.sb([128, KC, R], BF16)
    xbufs = Ring([p.sb([128, D], F32) for _ in range(2)])
    x2bufs = Ring([p.sb([128, D], F32) for _ in range(2)]) if pro == "gate" else None
    tmpf = Ring([p.sb([128, D], F32) for _ in range(2)]) if pro in ("gate", "gelu", "rms") else None
    xpb = Ring([p.sb([128, D], BF16) for _ in range(2)])
    nwb = p.sb([128, D], F32) if pro == "rms" else None
    stat = Ring([p.sb([128, 2], F32) for _ in range(2)]) if pro == "rms" else None
    pT = Ring([p.ps([128, 4, 128], BF16) for _ in range(2)])
    wb = Ring([p.sb([128, KC, 512], BF16) for _ in range(2)])
    acc = Ring([p.ps([128, 512], F32) for _ in range(3)])
    ob = Ring([p.sb([128, 512], F32) for _ in range(3)])
    e1b = Ring([p.sb([128, 512], F32) for _ in range(2)]) if e1 is not None else None
    e2b = Ring([p.sb([128, 512], F32) for _ in range(2)]) if e2 is not None else None
    bb = Ring([p.sb([128, 512], F32) for _ in range(2)]) if bv is not None else None
    tb = Ring([p.sb([128, 512], F32) for _ in range(2)]) if epi in ("gres", "glu") else None
    GC = 2.0 * math.sqrt(2.0 / math.pi)

    t_id = p.dma("sp", "ident", ident_f[:], ident_d)
    t_ident = p.op("dve", lambda e: e.tensor_copy(out=ident[:], in_=ident_f[:]), [t_id])
    t_nw = p.dma("sp", "nw", nwb[:], nw.partition_broadcast(128)) if pro == "rms" else None
    xt_toks = []
    for r in range(RT):
        rows = slice(r * 128, (r + 1) * 128)
        kx, xb, dx = xbufs.next()
        t_x = p.dma("sp", f"x{kx}", xb[:], x[rows, :], dx)
        kp, xp, dxp = xpb.next()
        if pro == "rms":
            kt, tf, dt_ = tmpf.next()
            ks, st, ds = stat.next()
            t_sq = p.op("act", lambda e, tf=tf, xb=xb, st=st: e.activation(
                out=tf[:], in_=xb[:], func=AF.Square, accum_out=st[:, 0:1]), [t_x] + dt_ + ds)
            t_r1 = p.op("dve", lambda e, st=st: e.tensor_scalar(
                out=st[:, 1:2], in0=st[:, 0:1], scalar1=1.0 / D, scalar2=eps, op0=ALU.mult, op1=ALU.add), [t_sq])
            t_r1b = p.op("act", lambda e, st=st: e.activation(out=st[:, 1:2], in_=st[:, 1:2], func=AF.Sqrt), [t_r1])
            t_r2 = p.op("dve", lambda e, st=st: e.reciprocal(out=st[:, 1:2], in_=st[:, 1:2]), [t_r1b])
            t_xp = p.op("dve", lambda e, xp=xp, xb=xb, st=st: e.scalar_tensor_tensor(
                out=xp[:], in0=xb[:], scalar=st[:, 1:2], in1=nwb[:], op0=ALU.mult, op1=ALU.mult),
                [t_r2, t_nw] + dxp)
            tmpf.done(kt, t_sq); stat.done(ks, t_xp); xbufs.done(kx, t_xp)
        elif pro == "gate":
            k2, x2b, d2 = x2bufs.next()
            t_x2 = p.dma("act", f"x2{k2}", x2b[:], x2[rows, :], d2)
            kt, tf, dt_ = tmpf.next()
            t_s = p.op("act", lambda e, tf=tf, x2b=x2b: e.activation(out=tf[:], in_=x2b[:], func=AF.Silu), [t_x2] + dt_)
            t_xp = p.op("dve", lambda e, xp=xp, xb=xb, tf=tf: e.tensor_tensor(
                out=xp[:], in0=xb[:], in1=tf[:], op=ALU.mult), [t_x, t_s] + dxp)
            x2bufs.done(k2, t_s); tmpf.done(kt, t_xp); xbufs.done(kx, t_xp)
        elif pro == "gelu":
            kt, tf, dt_ = tmpf.next()
            t_a = p.op("dve", lambda e, tf=tf, xb=xb: e.tensor_tensor(out=tf[:], in0=xb[:], in1=xb[:], op=ALU.mult), [t_x] + dt_)
            t_b = p.op("dve", lambda e, tf=tf: e.tensor_scalar(
                out=tf[:], in0=tf[:], scalar1=0.044715, scalar2=1.0, op0=ALU.mult, op1=ALU.add), [t_a])
            t_c = p.op("dve", lambda e, tf=tf, xb=xb: e.tensor_tensor(out=tf[:], in0=tf[:], in1=xb[:], op=ALU.mult), [t_b])
            t_d = p.op("act", lambda e, tf=tf: e.activation(out=tf[:], in_=tf[:], func=AF.Sigmoid, scale=GC), [t_c])
            t_xp = p.op("dve", lambda e, xp=xp, xb=xb, tf=tf: e.tensor_tensor(
                out=xp[:], in0=xb[:], in1=tf[:], op=ALU.mult), [t_d] + dxp)
            tmpf.done(kt, t_xp); xbufs.done(kx, t_xp)
        else:
            t_xp = p.op("dve", lambda e, xp=xp, xb=xb: e.tensor_copy(out=xp[:], in_=xb[:]), [t_x] + dxp)
            xbufs.done(kx, t_xp)
        last = []
        for k0 in range(0, KC, 4):
            nk = min(4, KC - k0)
            kq, pt, dq = pT.next()
            tt = None
            for j in range(nk):
                tt = p.op("pe", lambda e, pt=pt, xp=xp, j=j, k0=k0: e.transpose(
                    out=pt[:, j, :], in_=xp[:, (k0 + j) * 128:(k0 + j + 1) * 128], identity=ident[:]),
                    [t_xp, t_ident] + dq, inc=(j == nk - 1))
            if (k0 // 4) % 2 == 0:
                t_cp = p.op("act", lambda e, pt=pt, k0=k0, nk=nk, rows=rows: e.copy(
                    out=XT[:, k0:k0 + nk, rows], in_=pt[:, 0:nk, :]), [tt])
            else:
                t_cp = p.op("dve", lambda e, pt=pt, k0=k0, nk=nk, rows=rows: e.tensor_copy(
                    out=XT[:, k0:k0 + nk, rows], in_=pt[:, 0:nk, :]), [tt])
            pT.done(kq, t_cp)
            last.append(t_cp)
            xpb.done(kp, tt)
        xt_toks.append(last)
    wv = w.rearrange("(k p) n -> p k n", p=128)
    fin = []
    for n in range(NT):
        n0 = n * 512
        ns = min(512, N - n0)
        kw, wt, dw = wb.next()
        t_w = p.dma("pool", f"w{kw}", wt[:, :, 0:ns], wv[:, :, n0:n0 + ns], dw)
        t_bb = None
        if bv is not None:
            kb, bt, db = bb.next()
            t_bb = p.dma("act", f"b{kb}", bt[:, 0:ns], bv[0:1, n0:n0 + ns].partition_broadcast(128), db)
        mm_last = []
        for r in range(RT):
            rows = slice(r * 128, (r + 1) * 128)
            ka, ac, da = acc.next()
            tm = None
            for k in range(KC):
                tm = p.op("pe", lambda e, ac=ac, wt=wt, k=k, rows=rows, ns=ns: e.matmul(
                    ac[:, 0:ns], lhsT=XT[:, k, rows], rhs=wt[:, k, 0:ns], start=(k == 0), stop=(k == KC - 1)),
                    ([t_w] + xt_toks[r] + da) if k == 0 else [], inc=(k == KC - 1))
            mm_last.append(tm)
            ko, ot, do = ob.next()
            if epi == "none":
                t_o = p.op("act", lambda e, ot=ot, ac=ac, ns=ns: e.copy(out=ot[:, 0:ns], in_=ac[:, 0:ns]), [tm] + do)
                acc.done(ka, t_o)
            elif epi == "sigmoid":
                t_o = p.op("act", lambda e, ot=ot, ac=ac, ns=ns: e.activation(
                    out=ot[:, 0:ns], in_=ac[:, 0:ns], func=AF.Sigmoid), [tm] + do)
                acc.done(ka, t_o)
            elif epi == "res":
                k1, et, d1 = e1b.next()
                t_e = p.dma("sp", f"e1{k1}", et[:, 0:ns], e1[rows, n0:n0 + ns], d1)
                t_o = p.op("dve", lambda e, ot=ot, ac=ac, et=et, ns=ns: e.tensor_tensor(
                    out=ot[:, 0:ns], in0=ac[:, 0:ns], in1=et[:, 0:ns], op=ALU.add), [tm, t_e] + do)
                acc.done(ka, t_o); e1b.done(k1, t_o)
            elif epi == "gres":
                k1, et, d1 = e1b.next()
                t_e = p.dma("sp", f"e1{k1}", et[:, 0:ns], e1[rows, n0:n0 + ns], d1)
                k2, et2, d2 = e2b.next()
                t_e2 = p.dma("sp", f"e2{k2}", et2[:, 0:ns], e2[rows, n0:n0 + ns], d2)
                kt, tt_, dtt = tb.next()
                t_m = p.op("dve", lambda e, tt_=tt_, ac=ac, et2=et2, ns=ns: e.tensor_tensor(
                    out=tt_[:, 0:ns], in0=ac[:, 0:ns], in1=et2[:, 0:ns], op=ALU.mult), [tm, t_e2] + dtt)
                t_o = p.op("dve", lambda e, ot=ot, tt_=tt_, et=et, ns=ns: e.tensor_tensor(
                    out=ot[:, 0:ns], in0=tt_[:, 0:ns], in1=et[:, 0:ns], op=ALU.add), [t_m, t_e] + do)
                acc.done(ka, t_m); e1b.done(k1, t_o); e2b.done(k2, t_m); tb.done(kt, t_o)
            elif epi == "glu":
                k1, et, d1 = e1b.next()
                t_e = p.dma("sp", f"e1{k1}", et[:, 0:ns], e1[rows, n0:n0 + ns], d1)
                kt, tt_, dtt = tb.next()
                t_a = p.op("dve", lambda e, tt_=tt_, et=et, ns=ns: e.tensor_tensor(
                    out=tt_[:, 0:ns], in0=et[:, 0:ns], in1=et[:, 0:ns], op=ALU.mult), [t_e] + dtt)
                t_b = p.op("dve", lambda e, tt_=tt_, ns=ns: e.tensor_scalar(
                    out=tt_[:, 0:ns], in0=tt_[:, 0:ns], scalar1=0.044715, scalar2=1.0, op0=ALU.mult, op1=ALU.add), [t_a])
                t_c = p.op("dve", lambda e, tt_=tt_, et=et, ns=ns: e.tensor_tensor(
                    out=tt_[:, 0:ns], in0=tt_[:, 0:ns], in1=et[:, 0:ns], op=ALU.mult), [t_b])
                t_d = p.op("act", lambda e, tt_=tt_, ns=ns: e.activation(
                    out=tt_[:, 0:ns], in_=tt_[:, 0:ns], func=AF.Sigmoid, scale=GC), [t_c])
                t_g = p.op("dve", lambda e, tt_=tt_, et=et, ns=ns: e.tensor_tensor(
                    out=tt_[:, 0:ns], in0=tt_[:, 0:ns], in1=et[:, 0:ns], op=ALU.mult), [t_d])
                t_s1 = p.op("dve", lambda e, ot=ot, ac=ac, bt=bt, ns=ns: e.tensor_tensor(
                    out=ot[:, 0:ns], in0=ac[:, 0:ns], in1=bt[:, 0:ns], op=ALU.add), [tm, t_bb] + do)
                t_s2 = p.op("act", lambda e, ot=ot, ns=ns: e.activation(
                    out=ot[:, 0:ns], in_=ot[:, 0:ns], func=AF.Sigmoid), [t_s1])
                t_o = p.op("dve", lambda e, ot=ot, tt_=tt_, ns=ns: e.tensor_tensor(
                    out=ot[:, 0:ns], in0=ot[:, 0:ns], in1=tt_[:, 0:ns], op=ALU.mult), [t_s2, t_g])
                acc.done(ka, t_s1); e1b.done(k1, t_g); tb.done(kt, t_o)
            else:
                raise ValueError(epi)
            t_st = p.dma("act", f"o{ko}", y[rows, n0:n0 + ns], ot[:, 0:ns], [t_o])
            ob.done(ko, t_st)
            fin.append(t_st)
        wb.done(kw, *mm_last)
        if bv is not None:
            bb.done(kb, *mm_last)
    return fin[-3:]


def emit_transpose(p, src, dst, R, C, ident_d):
    identf = p.sb([128, 128], F32)
    t_id = p.dma("sp", "ident", identf[:], ident_d)
    inb = Ring([p.sb([128, 512], F32) for _ in range(3)])
    pst = Ring([p.ps([128, 512], F32) for _ in range(3)])
    outb = Ring([p.sb([128, 512], F32) for _ in range(3)])
    fin = []
    for r0 in range(0, R, 512):
        nr = min(512, R - r0)
        nrt = nr // 128
        for c0 in range(0, C, 128):
            cw = min(128, C - c0)
            ki, it, di = inb.next()
            t_in = p.dma("sp", f"ti{ki}", it[:, 0:nrt * 128].rearrange("p (q c) -> p q c", c=128)[:, :, 0:cw],
                         src[r0:r0 + nr, c0:c0 + cw].rearrange("(q p) c -> p q c", p=128), di)
            kp, pt, dp = pst.next()
            tt = None
            for q in range(nrt):
                tt = p.op("pe", lambda e, pt=pt, it=it, q=q, cw=cw: e.transpose(
                    out=pt[0:cw, q * 128:(q + 1) * 128], in_=it[:, q * 128:q * 128 + cw], identity=identf[:]),
                    [t_in, t_id] + dp, inc=(q == nrt - 1))
            inb.done(ki, tt)
            ko, ot, do = outb.next()
            t_cp = p.op("act" if (c0 // 128) % 2 == 0 else "dve",
                        (lambda e, ot=ot, pt=pt, cw=cw, nr=nr: e.copy(out=ot[0:cw, 0:nr], in_=pt[0:cw, 0:nr]))
                        if (c0 // 128) % 2 == 0 else
                        (lambda e, ot=ot, pt=pt, cw=cw, nr=nr: e.tensor_copy(out=ot[0:cw, 0:nr], in_=pt[0:cw, 0:nr])),
                        [tt] + do)
            pst.done(kp, t_cp)
            t_st = p.dma("act", f"to{ko}", dst[c0:c0 + cw, r0:r0 + nr], ot[0:cw, 0:nr], [t_cp])
            outb.done(ko, t_st)
            fin.append(t_st)
    return fin[-3:]
NEG = -1.0e30
TOPK = 256
BIS_LO, BIS_HI, BIS_IT = -8192.0, 8192.0, 28


def emit_attn(p, NJ, NT, smax, src, vt_d, D_d, cb_d, C0_d, oT_dst):
    HN = 16 * NT
    HPM = min(16, 512 // NT)
    NMM = 16 // HPM
    SMX = max(smax)
    NU = sum(smax)
    u0s = [sum(smax[:j]) for j in range(NJ)]
    scale = 128.0 ** -0.5
    n4 = 4 * NT
    ones = p.sb([128, 128], BF16)
    vt = p.sb([128, NU], F32)
    Dt = p.sb([128, 2, HN], F32)
    cb = p.sb([128, 16], F32)
    C0 = p.sb([128, NT], F32)
    qT = Ring([p.sb([128, HN], BF16) for _ in range(2)])
    qiT = Ring([p.sb([64, HN], BF16) for _ in range(2)])
    wb = Ring([p.sb([128, HN], F32) for _ in range(2)])
    SC = p.sb([128, SMX, NT], F32)
    CM = p.sb([128, SMX, NT], BF16)
    MKs = [p.sb([128, SMX, NT], BF16) for _ in range(2 if NJ > 1 else 1)]
    kib = Ring([p.sb([128, 128], BF16) for _ in range(3)])
    rb = Ring([p.sb([128, HN], F32) for _ in range(2)])
    ktb = Ring([p.sb([128, 128], BF16) for _ in range(3)])
    vb = Ring([p.sb([128, 128], BF16) for _ in range(3)])
    banks = Ring([p.ps([128, 512], F32) for _ in range(4)])
    ps_cnt = p.ps([128, 512], F32)
    ps_o = p.ps([128, 512], F32)
    ps_d = p.ps([128, 512], F32)
    lo = p.sb([128, NT], F32)
    mid = p.sb([128, NT], F32)
    ge = p.sb([128, NT], F32)
    lgt = Ring([p.sb([128, 4 * NT], F32) for _ in range(2)])
    pb = Ring([p.sb([128, 4 * NT], BF16) for _ in range(2)])
    pmb = Ring([p.sb([128, 4 * NT], BF16) for _ in range(2)])
    rdb = p.sb([128, 4 * NT], F32)
    ob = Ring([p.sb([128, 4 * NT], F32) for _ in range(2)])

    t_ones = p.op("dve", lambda e: e.memset(ones[:], 1.0))
    t_vt = p.dma("sp", "vt", vt[:], vt_d.partition_broadcast(128))
    t_D = p.dma("sp", "D", Dt[:], D_d.rearrange("a p n -> p a n"))
    t_cb = p.dma("sp", "cb", cb[:], cb_d.partition_broadcast(128))
    t_C0 = p.dma("sp", "C0", C0[:], C0_d)
    st = {"acc_free": [], "sc_free": [], "out": []}

    def phase_a(j):
        S = smax[j]
        kq, qt, dq = qT.next()
        t_q = src.load_q(j, qt, dq)
        kqi, qit, dqi = qiT.next()
        t_qi = src.load_qi(j, qit, dqi)
        kw, wt, dw_ = wb.next()
        t_w = src.load_w(j, wt, dw_)
        sc_toks = []
        mm_toks, t_rs = [], []
        for dl in range(S):
            kk, kit, dk = kib.next()
            t_ki = src.load_ki(j, dl, kit, dk)
            kr, rt, dr = rb.next()
            t_rs = []
            mm_toks = []
            for m in range(NMM):
                kb_, bk, dbk = banks.next()
                n_ = HPM * NT
                t_mm = p.op("pe", lambda e, bk=bk, kit=kit, qit=qit, m=m, n_=n_: e.matmul(
                    bk[:, 0:n_], lhsT=kit[0:64, :], rhs=qit[:, m * n_:(m + 1) * n_], start=True, stop=True),
                    [t_ki, t_qi] + dbk)
                t_r = p.op("dve", lambda e, bk=bk, rt=rt, wt=wt, m=m, n_=n_: e.scalar_tensor_tensor(
                    out=rt[:, m * n_:(m + 1) * n_], in0=bk[:, 0:n_], scalar=0.0, in1=wt[:, m * n_:(m + 1) * n_],
                    op0=ALU.max, op1=ALU.mult), [t_mm, t_w] + (dr if m == 0 else []))
                banks.done(kb_, t_r)
                t_rs.append(t_r)
                mm_toks.append(t_mm)
            kib.done(kk, *mm_toks)
            t_red = p.op("dve", lambda e, rt=rt, dl=dl: e.tensor_reduce(
                out=SC[:, dl, :], in_=rt[:, :].rearrange("p (h t) -> p t h", h=16), axis=AX.X, op=ALU.add),
                t_rs + (st["sc_free"] if dl == 0 else []))
            rb.done(kr, t_red)
            u = u0s[j] + dl
            t_v = p.op("dve", lambda e, dl=dl, u=u: e.tensor_scalar(
                out=SC[:, dl, :], in0=SC[:, dl, :], scalar1=vt[:, u:u + 1], scalar2=None, op0=ALU.add), [t_red, t_vt])
            if dl == 0:
                t_v = p.op("dve", lambda e: e.tensor_tensor(
                    out=SC[:, 0, :], in0=SC[:, 0, :], in1=C0[:, :], op=ALU.add), [t_v, t_C0])
            sc_toks.append(t_v)
        qiT.done(kqi, *mm_toks)
        wb.done(kw, *t_rs)
        return kq, qt, t_q, sc_toks

    def gen_bisect(S, sc_toks, res):
        t_m = p.op("dve", lambda e: e.memset(mid[:], 0.5 * (BIS_LO + BIS_HI)))
        t_prev = [t_m]
        t_cm_free = []
        for it in range(BIS_IT):
            h = (BIS_HI - BIS_LO) / 2.0 ** (it + 1)
            t_cmp = p.op("dve", lambda e, S=S: e.tensor_tensor(
                out=CM[:, 0:S, :], in0=SC[:, 0:S, :], in1=mid[:, :].unsqueeze(1).to_broadcast([128, S, NT]),
                op=ALU.is_ge), t_prev + sc_toks + t_cm_free)
            t_c = None
            for dl in range(S):
                t_c = p.op("pe", lambda e, dl=dl, S=S: e.matmul(
                    ps_cnt[:, 0:NT], lhsT=ones[:, :], rhs=CM[:, dl, :], start=(dl == 0), stop=(dl == S - 1)),
                    ([t_cmp, t_ones] + t_prev) if dl == 0 else [], inc=(dl == S - 1))
            t_ge = p.op("dve", lambda e, h=h: e.tensor_scalar(
                out=ge[:], in0=ps_cnt[:, 0:NT], scalar1=float(TOPK) - 0.5, scalar2=h, op0=ALU.is_ge, op1=ALU.mult), [t_c])
            t_md = p.op("dve", lambda e, h=h: e.scalar_tensor_tensor(
                out=mid[:], in0=ge[:], scalar=-0.5 * h, in1=mid[:], op0=ALU.add, op1=ALU.add), [t_ge])
            t_prev = [t_md]
            t_cm_free = [t_c]
            yield None
        hf = (BIS_HI - BIS_LO) / 2.0 ** (BIS_IT + 1)
        res["t_lo"] = p.op("dve", lambda e: e.tensor_scalar(
            out=lo[:], in0=mid[:], scalar1=-hf, scalar2=None, op0=ALU.add), t_prev)

    def gen_c(j, S, kq, qt, t_q, MK, t_mk):
        last = {}
        for g in range(4):
            pend = None
            for dl in range(S):
                kk, ktt, dkt = ktb.next()
                t_kt = src.load_k(j, g, dl, ktt, dkt)
                kv, vtl, dv = vb.next()
                t_vv = src.load_v(j, g, dl, vtl, dv)
                kb_, bk, dbk = banks.next()
                t_s = p.op("pe", lambda e, bk=bk, ktt=ktt, g=g: e.matmul(
                    bk[:, 0:n4], lhsT=ktt[:, 0:128], rhs=qt[:, g * n4:(g + 1) * n4], start=True, stop=True),
                    [t_kt, t_q] + dbk)
                ktb.done(kk, t_s)
                kp_, pt, dp = pb.next()
                if dl >= 2:
                    t_e = None
                    for r in range(4):
                        h = 4 * g + r
                        t_e = p.op("act", lambda e, pt=pt, bk=bk, r=r, h=h: e.activation(
                            out=pt[:, r * NT:(r + 1) * NT], in_=bk[:, r * NT:(r + 1) * NT], func=AF.Exp,
                            bias=cb[:, h:h + 1], scale=scale), [t_s, t_cb] + (dp if r == 0 else []))
                    banks.done(kb_, t_e)
                else:
                    kl, lt, dlg = lgt.next()
                    t_l = p.op("dve", lambda e, lt=lt, bk=bk, dl=dl, g=g: e.scalar_tensor_tensor(
                        out=lt[:, 0:n4], in0=bk[:, 0:n4], scalar=scale, in1=Dt[:, dl, g * n4:(g + 1) * n4],
                        op0=ALU.mult, op1=ALU.add), [t_s, t_D] + dlg)
                    banks.done(kb_, t_l)
                    t_e = p.op("act", lambda e, pt=pt, lt=lt: e.activation(
                        out=pt[:, 0:n4], in_=lt[:, 0:n4], func=AF.Exp), [t_l] + dp)
                    lgt.done(kl, t_e)
                km, pmt, dpm = pmb.next()
                t_pm = p.op("dve", lambda e, pmt=pmt, pt=pt, dl=dl: e.tensor_tensor(
                    out=pmt[:, :].rearrange("p (r t) -> p r t", r=4),
                    in0=pt[:, :].rearrange("p (r t) -> p r t", r=4),
                    in1=MK[:, dl, :].unsqueeze(1).to_broadcast([128, 4, NT]), op=ALU.mult), [t_e, t_mk] + dpm)
                pb.done(kp_, t_pm)
                last["pm"] = t_pm
                if pend is not None:
                    pend()

                def stage2(vtl=vtl, pmt=pmt, kv=kv, km=km, t_vv=t_vv, t_pm=t_pm, first=(dl == 0), lastf=(dl == S - 1)):
                    t_o_ = p.op("pe", lambda e: e.matmul(
                        ps_o[:, 0:n4], lhsT=vtl[:, 0:128], rhs=pmt[:, 0:n4], start=first, stop=lastf),
                        [t_vv, t_pm] + (st["acc_free"] if first else []))
                    t_d_ = p.op("pe", lambda e: e.matmul(
                        ps_d[:, 0:n4], lhsT=ones[:, :], rhs=pmt[:, 0:n4], start=first, stop=lastf), [t_pm])
                    vb.done(kv, t_o_)
                    pmb.done(km, t_d_)
                    last["o"], last["d"] = t_o_, t_d_
                pend = stage2
                yield None
            pend()
            t_rd = p.op("dve", lambda e: e.reciprocal(out=rdb[:], in_=ps_d[:, 0:n4]), [last["d"]] + st["acc_free"])
            ko, ot, do = ob.next()
            t_o = p.op("dve", lambda e, ot=ot: e.tensor_tensor(
                out=ot[:], in0=ps_o[:, 0:n4], in1=rdb[:], op=ALU.mult), [t_rd, last["o"]] + do)
            st["acc_free"] = [t_o]
            t_st = p.dma("act", f"o{ko}", oT_dst(j, g), ot[:, :].rearrange("p (r t) -> p r t", r=4), [t_o])
            ob.done(ko, t_st)
            st["out"].append(t_st)
        qT.done(kq, last["o"])

    order = list(range(NJ))[::-1]
    pend_c = None
    for idx, j in enumerate(order):
        S = smax[j]
        kq, qt, t_q, sc_toks = phase_a(j)
        res = {}
        per = 0
        if pend_c is not None:
            per = -(-(4 * smax[order[idx - 1]]) // BIS_IT)
        for _ in gen_bisect(S, sc_toks, res):
            for _u in range(per):
                if pend_c is not None and next(pend_c, "END") == "END":
                    pend_c = None
        if pend_c is not None:
            for _ in pend_c:
                pass
        MK = MKs[idx % len(MKs)]
        t_mk = p.op("dve", lambda e, S=S, MK=MK: e.tensor_tensor(
            out=MK[:, 0:S, :], in0=SC[:, 0:S, :], in1=lo[:, :].unsqueeze(1).to_broadcast([128, S, NT]),
            op=ALU.is_ge), [res["t_lo"]])
        st["sc_free"] = [t_mk]
        pend_c = gen_c(j, S, kq, qt, t_q, MK, t_mk)
    for _ in pend_c:
        pass
    return st["out"][-2:]
TWO_PI = 2.0 * math.pi


def _sincos(p, out_t, x_t, tmp_t, deps, cos, ki_t, kf_t):
    off = (math.pi / 2) if cos else 0.0
    V = lambda f, d: p.op("dve", f, d)
    t1 = V(lambda e: e.tensor_scalar(out=tmp_t, in0=x_t, scalar1=off, scalar2=1.0 / TWO_PI, op0=ALU.add, op1=ALU.mult), deps)
    t2 = V(lambda e: e.tensor_copy(out=ki_t, in_=tmp_t), [t1])
    t3 = V(lambda e: e.tensor_copy(out=kf_t, in_=ki_t), [t2])
    t4 = V(lambda e: e.tensor_scalar(out=tmp_t, in0=x_t, scalar1=off, scalar2=None, op0=ALU.add), [t3])
    t5 = V(lambda e: e.scalar_tensor_tensor(out=tmp_t, in0=kf_t, scalar=-TWO_PI, in1=tmp_t, op0=ALU.mult, op1=ALU.add), [t4])
    t6 = V(lambda e: e.tensor_scalar(out=kf_t, in0=tmp_t, scalar1=math.pi, scalar2=-TWO_PI, op0=ALU.is_gt, op1=ALU.mult), [t5])
    t7 = V(lambda e: e.tensor_tensor(out=tmp_t, in0=tmp_t, in1=kf_t, op=ALU.add), [t6])
    t8 = V(lambda e: e.tensor_scalar(out=kf_t, in0=tmp_t, scalar1=-math.pi, scalar2=TWO_PI, op0=ALU.is_lt, op1=ALU.mult), [t7])
    t9 = V(lambda e: e.tensor_tensor(out=tmp_t, in0=tmp_t, in1=kf_t, op=ALU.add), [t8])
    return p.op("act", lambda e: e.activation(out=out_t, in_=tmp_t, func=AF.Sin), [t9])


def emit_scan(p, seqs, I, gU_rows, uidx_d, pubY, HF_d, ident_d, pre=()):
    V = lambda f, deps=(): p.op("dve", f, list(deps))
    cst = p.sb([128, 4], F32); tvb = p.sb([128, 129], F32); tri = p.sb([128, 128], BF16)
    identf = p.sb([128, 128], F32)
    lmb = p.sb([128, 1024], F32); anb = p.sb([128, 1024], F32); dtb = p.sb([128, 1024], F32)
    lmT = p.sb([128, 16], F32); anT = p.sb([128, 16], F32); dtT = p.sb([128, 16], F32)
    BDA = p.sb([128, 2, 1024], BF16); BDB = p.sb([128, 2, 1024], BF16)
    CcP = p.sb([128, 2, 1024], BF16); CcPf = p.sb([128, 2, 1024], F32)
    Ccc = p.sb([128, 256], F32); dv = p.sb([128, 2], F32)
    Pa = p.sb([128, 16, 128], F32); Pb = p.sb([128, 16, 128], F32)
    Qa = p.sb([128, 16, 129], F32); Qb = p.sb([128, 16, 129], F32)
    Qa16 = p.sb([128, 16, 129], BF16); Qb16 = p.sb([128, 16, 129], BF16)
    q1 = p.sb([128, 16 * 129], F32); q2 = p.sb([128, 16 * 129], F32); q3 = p.sb([128, 16 * 129], F32)
    q4 = p.sb([128, 16 * 129], F32); qi_ = p.sb([128, 16 * 129], I32); kfq = p.sb([128, 16 * 129], F32)
    mask = p.sb([128, 8, 64], F32)
    s1 = q1[:, 0:1024]; s2 = q2[:, 0:1024]; s3 = q3[:, 0:1024]; s4 = q4[:, 0:1024]; si = qi_[:, 0:1024]; sk = kfq[:, 0:1024]
    U = p.sb([128, 2, 4, 1152], F32)
    uidx = p.sb([128, 8], I32)

    d_c = p.dma("sp", "c0", cst[:], I["cst"])
    d_tv = p.dma("sp", "c1", tvb[:], I["tvec"].partition_broadcast(128))
    d_tri = p.dma("pool", "c2", tri[:], I["tri"])
    d_id = p.dma("sp", "c3", identf[:], ident_d)
    d_lm = p.dma("sp", "c4", lmb[:], I["lre_row"].partition_broadcast(128))
    d_an = p.dma("sp", "c5", anb[:], I["lim_row"].partition_broadcast(128))
    d_dt = p.dma("sp", "c6", dtb[:], I["ldt_row"].partition_broadcast(128))
    d_lmT = p.dma("sp", "c7", lmT[:], I["lreT2"])
    d_anT = p.dma("sp", "c8", anT[:], I["limT2"])
    d_dtT = p.dma("sp", "c9", dtT[:], I["ldtT2"])
    d_ccp = p.dma("sp", "c10", CcPf[:], I["CcP"].rearrange("c p n -> p c n"))
    d_ccc = p.dma("sp", "c11", Ccc[:], I["Ccc"])
    d_dv = p.dma("sp", "c12", dv[:], I["dvec"])
    d_mk = p.dma("sp", "c13", mask[:], I["mask"])
    d_ui = p.dma("sp", "c14", uidx[:], uidx_d)
    u_toks = []
    for ck in range(2):
        for r in range(4):
            col = ck * 4 + r
            u_toks.append(p.lane_op("pool", f"ug{col}", lambda e, ck=ck, r=r, col=col: e.indirect_dma_start(
                out=U[:, ck, r, :], out_offset=None, in_=gU_rows,
                in_offset=bass.IndirectOffsetOnAxis(ap=uidx[:, col:col + 1], axis=0)), [d_ui] + list(pre)))
    def disc(lm, an, dt, deps):
        a = p.op("act", lambda e: e.activation(out=dt, in_=dt, func=AF.Exp), deps)
        b = V(lambda e: e.tensor_scalar(out=lm, in0=lm, scalar1=-1e-4, scalar2=None, op0=ALU.min), deps)
        c = V(lambda e: e.tensor_tensor(out=lm, in0=lm, in1=dt, op=ALU.mult), [a, b])
        d = V(lambda e: e.tensor_tensor(out=an, in0=an, in1=dt, op=ALU.mult), [c])
        return d
    t_row = disc(lmb[:], anb[:], dtb[:], [d_lm, d_an, d_dt])
    t_T = disc(lmT[:], anT[:], dtT[:], [d_lmT, d_anT, d_dtT])
    t_ccp = V(lambda e: e.tensor_scalar(out=CcP[:], in0=CcPf[:], scalar1=cst[:, 3:4], scalar2=None, op0=ALU.mult), [d_ccp, d_c])
    t_ccc = V(lambda e: e.tensor_scalar(out=Ccc[:], in0=Ccc[:], scalar1=cst[:, 3:4], scalar2=None, op0=ALU.mult), [d_ccc, d_c])
    G = [p.sb([128, 64], F32) for _ in range(16)]
    gi_i = p.sb([128, 64], I32)
    t_bd = []
    for ck in range(2):
        lre, lim, ldt, bre, bim, mag, cc, ss, ar1, aim, den, cre, cim, t1_, t2_, kf_ = [g[:] for g in G]
        dd = [p.dma("sp", f"g{n_}", t_, I[nm][ck], t_bd) for n_, (t_, nm) in enumerate(
            [(lre, "lre_gi"), (lim, "lim_gi"), (ldt, "ldt_gi"), (bre, "bre_gi"), (bim, "bim_gi")])]
        a = p.op("act", lambda e: e.activation(out=ldt, in_=ldt, func=AF.Exp), dd)
        b = V(lambda e: e.tensor_scalar(out=lre, in0=lre, scalar1=-1e-4, scalar2=None, op0=ALU.min), dd)
        c1 = V(lambda e: e.tensor_tensor(out=t1_, in0=lre, in1=ldt, op=ALU.mult), [a, b])
        c2 = V(lambda e: e.tensor_tensor(out=t2_, in0=lim, in1=ldt, op=ALU.mult), [c1])
        m = p.op("act", lambda e: e.activation(out=mag, in_=t1_, func=AF.Exp), [c1])
        ts = _sincos(p, ss, t2_, den, [c2, m], False, gi_i[:], kf_)
        tc = _sincos(p, cc, t2_, den, [ts], True, gi_i[:], kf_)
        x1 = V(lambda e: e.tensor_tensor(out=ar1, in0=mag, in1=cc, op=ALU.mult), [tc])
        x1 = V(lambda e: e.tensor_scalar(out=ar1, in0=ar1, scalar1=-1.0, scalar2=None, op0=ALU.add), [x1])
        x2 = V(lambda e: e.tensor_tensor(out=aim, in0=mag, in1=ss, op=ALU.mult), [x1])
        y1 = V(lambda e: e.tensor_tensor(out=den, in0=lre, in1=lre, op=ALU.mult), [x2])
        y2 = V(lambda e: e.tensor_tensor(out=t1_, in0=lim, in1=lim, op=ALU.mult), [y1])
        y3 = V(lambda e: e.tensor_tensor(out=den, in0=den, in1=t1_, op=ALU.add), [y2])
        y4 = V(lambda e: e.reciprocal(out=den, in_=den), [y3])
        z1 = V(lambda e: e.tensor_tensor(out=cre, in0=ar1, in1=lre, op=ALU.mult), [y4])
        z2 = V(lambda e: e.tensor_tensor(out=t1_, in0=aim, in1=lim, op=ALU.mult), [z1])
        z3 = V(lambda e: e.tensor_tensor(out=cre, in0=cre, in1=t1_, op=ALU.add), [z2])
        z4 = V(lambda e: e.tensor_tensor(out=cre, in0=cre, in1=den, op=ALU.mult), [z3])
        w1 = V(lambda e: e.tensor_tensor(out=cim, in0=aim, in1=lre, op=ALU.mult), [z4])
        w2 = V(lambda e: e.tensor_tensor(out=t1_, in0=ar1, in1=lim, op=ALU.mult), [w1])
        w3 = V(lambda e: e.tensor_tensor(out=cim, in0=cim, in1=t1_, op=ALU.subtract), [w2])
        w4 = V(lambda e: e.tensor_tensor(out=cim, in0=cim, in1=den, op=ALU.mult), [w3])
        q_1 = V(lambda e: e.tensor_tensor(out=t1_, in0=cre, in1=bre, op=ALU.mult), [w4])
        q_2 = V(lambda e: e.tensor_tensor(out=t2_, in0=cim, in1=bim, op=ALU.mult), [q_1])
        q_3 = V(lambda e: e.tensor_tensor(out=t1_, in0=t1_, in1=t2_, op=ALU.subtract), [q_2])
        q_4 = V(lambda e: e.tensor_tensor(out=t2_, in0=cre, in1=bim, op=ALU.mult), [q_3])
        q_5 = V(lambda e: e.tensor_tensor(out=mag, in0=cim, in1=bre, op=ALU.mult), [q_4])
        q_6 = V(lambda e: e.tensor_tensor(out=t2_, in0=t2_, in1=mag, op=ALU.add), [q_5])
        bcv = lambda t: t.unsqueeze(1).to_broadcast([128, 8, 64])
        bdv = lambda T_, ck=ck: T_[:, ck, :].rearrange("p (g n) -> p g n", n=128)
        r1 = V(lambda e, bdv=bdv: e.tensor_tensor(out=bdv(BDA)[:, :, 0:64], in0=mask[:], in1=bcv(t1_), op=ALU.mult), [q_6, d_mk])
        r2 = V(lambda e, bdv=bdv: e.tensor_tensor(out=bdv(BDA)[:, :, 64:128], in0=mask[:], in1=bcv(t2_), op=ALU.mult), [r1])
        r3 = V(lambda e, bdv=bdv: e.tensor_tensor(out=bdv(BDB)[:, :, 0:64], in0=mask[:], in1=bcv(t2_), op=ALU.mult), [r2])
        r4 = V(lambda e, bdv=bdv: e.tensor_tensor(out=bdv(BDB)[:, :, 64:128], in0=mask[:], in1=bcv(t1_), op=ALU.mult), [r3])
        t_bd = [r4]
    a1 = V(lambda e: e.tensor_scalar(out=s1, in0=lmb[:], scalar1=cst[:, 1:2], scalar2=None, op0=ALU.mult), [t_row, d_c])
    a2 = p.op("act", lambda e: e.activation(out=s1, in_=s1, func=AF.Exp), [a1])
    a3 = V(lambda e: e.tensor_scalar(out=s2, in0=anb[:], scalar1=cst[:, 0:1], scalar2=None, op0=ALU.mult), [t_row, d_c])
    a4 = _sincos(p, s3, s2, s4, [a3], True, si, sk)
    a5 = V(lambda e: e.tensor_tensor(out=s3, in0=s3, in1=s1, op=ALU.mult), [a4, a2])
    s1v = lambda t: t.rearrange("p (g n) -> p g n", n=64)
    a6 = V(lambda e: e.tensor_copy(out=Pa[:, :, 0:64], in_=s1v(s3)), [a5])
    a7 = V(lambda e: e.tensor_copy(out=Pa[:, :, 64:128], in_=s1v(s3)), [a6])
    a8 = _sincos(p, s3, s2, s4, [a7], False, si, sk)
    a9 = V(lambda e: e.tensor_tensor(out=s3, in0=s3, in1=s1, op=ALU.mult), [a8])
    a10 = V(lambda e: e.tensor_copy(out=Pb[:, :, 0:64], in_=s1v(s3)), [a9])
    a11 = V(lambda e: e.tensor_scalar(out=Pb[:, :, 64:128], in0=s1v(s3), scalar1=-1.0, scalar2=None, op0=ALU.mult), [a10])
    qv = lambda t: t[:, :].rearrange("p (g t) -> p g t", t=129)
    b1 = V(lambda e: e.tensor_tensor(out=qv(q1), in0=tvb[:, :].unsqueeze(1).to_broadcast([128, 16, 129]),
                                     in1=lmT[:, :].unsqueeze(2).to_broadcast([128, 16, 129]), op=ALU.mult), [d_tv, t_T, a11])
    b2 = p.op("act", lambda e: e.activation(out=q1[:], in_=q1[:], func=AF.Exp), [b1])
    b3 = V(lambda e: e.tensor_tensor(out=qv(q2), in0=tvb[:, :].unsqueeze(1).to_broadcast([128, 16, 129]),
                                     in1=anT[:, :].unsqueeze(2).to_broadcast([128, 16, 129]), op=ALU.mult), [d_tv, t_T])
    b4 = _sincos(p, q3[:], q2[:], q4[:], [b3], True, qi_[:], kfq[:])
    b5 = V(lambda e: e.tensor_tensor(out=Qa[:, :, :], in0=qv(q3), in1=qv(q1), op=ALU.mult), [b4, b2])
    b6 = _sincos(p, q3[:], q2[:], q4[:], [b5], False, qi_[:], kfq[:])
    b7 = V(lambda e: e.tensor_tensor(out=q3[:], in0=q3[:], in1=q1[:], op=ALU.mult), [b6])
    b8 = V(lambda e: e.tensor_scalar(out=Qb[:, :, :], in0=qv(q3), scalar1=cst[:, 2:3], scalar2=None, op0=ALU.mult), [b7, d_c])
    b9 = V(lambda e: e.tensor_copy(out=Qa16[:], in_=Qa[:]), [b5])
    b10 = V(lambda e: e.tensor_copy(out=Qb16[:], in_=Qb[:]), [b8])
    tabs = [a7, a11, b9, b10, t_ccp, t_ccc, d_tri, d_dv, d_id] + t_bd + u_toks

    ub = Ring([p.sb([128, 128], BF16) for _ in range(3)])
    banks = Ring([p.ps([128, 512], F32) for _ in range(6)])
    ps_y = p.ps([128, 512], F32)
    ps_t = p.ps([128, 512], F32)
    t1b = Ring([p.sb([128, 512], F32) for _ in range(2)]); t2b = Ring([p.sb([128, 512], F32) for _ in range(2)])
    Vb = Ring([p.sb([128, 512], BF16) for _ in range(2)])
    x1b = Ring([p.sb([128, 4, 128], F32) for _ in range(2)]); x2b = Ring([p.sb([128, 4, 128], F32) for _ in range(2)])
    Xb = Ring([p.sb([128, 8, 128], BF16) for _ in range(2)])
    PadA = Ring([p.sb([128, 8, 128], BF16) for _ in range(2)]); PadB = Ring([p.sb([128, 8, 128], BF16) for _ in range(2)])
    yo = Ring([p.sb([128, 128], F32) for _ in range(3)])
    yr = Ring([p.sb([128, 128], F32) for _ in range(3)])
    EA = p.sb([128, 16], F32); EB = p.sb([128, 16], F32); e1t = p.sb([128, 16], F32)
    HA = [p.sb([128, 16], F32) for _ in range(2)]; HB = [p.sb([128, 16], F32) for _ in range(2)]
    n1 = p.sb([128, 16], F32); n2 = p.sb([128, 16], F32)
    t_z = [V(lambda e, pad=pad: e.memset(pad[:], 0.0)) for pad in PadA.bufs + PadB.bufs]
    hcur = 0
    t_Hread = []; t_E_read = []; outs = []; y_free = []; tr_free = []
    for si_, (blocks, init) in enumerate(seqs):
        if init is not None:
            ta = p.dma("sp", "h0a", HA[hcur][:], I["H0A"][init], t_Hread)
            tb_ = p.dma("sp", "h0b", HB[hcur][:], I["H0B"][init], t_Hread)
        else:
            ta = V(lambda e, h=HA[hcur]: e.memset(h[:], 0.0), t_Hread)
            tb_ = V(lambda e, h=HB[hcur]: e.memset(h[:], 0.0), t_Hread)
        t_H = [ta, tb_]
        def do_block(rk, col0, L, yrow0):
            nonlocal t_H, hcur, t_E_read, t_Hread, y_free, tr_free
            e_toks = []; pad_readers = []
            for ck in range(2):
                uft = U[:, ck, rk, col0:col0 + L]
                kb_, ubt, dub = ub.next()
                t_ub = p.op("act", lambda e, ubt=ubt, uft=uft: e.copy(out=ubt[:, 0:L], in_=uft), tabs + dub)
                kx, Xt, dX = Xb.next()
                x_toks = []
                pend_s2 = []
                for hc in range(2):
                    gg0 = ck * 8 + hc * 4
                    rows = slice(hc * 64, (hc + 1) * 64)
                    cols = slice(hc * 512, (hc + 1) * 512)
                    kA, bA, dA = banks.next()
                    mA = p.op("pe", lambda e, bA=bA, ubt=ubt, rows=rows, cols=cols, ck=ck: e.matmul(
                        bA[0:L, :], lhsT=ubt[rows, 0:L], rhs=BDA[rows, ck, cols], start=True, stop=True), [t_ub] + dA)
                    kB, bB, dB = banks.next()
                    mB = p.op("pe", lambda e, bB=bB, ubt=ubt, rows=rows, cols=cols, ck=ck: e.matmul(
                        bB[0:L, :], lhsT=ubt[rows, 0:L], rhs=BDB[rows, ck, cols], start=True, stop=True), [t_ub] + dB)
                    k1, t1t, d1_ = t1b.next(); k2, t2t, d2_ = t2b.next(); kv, Vt, dV = Vb.next()
                    pv = lambda t, gg0=gg0: t[0:L, gg0:gg0 + 4, :]
                    v3 = lambda t: t[0:L, :].rearrange("p (g n) -> p g n", n=128)
                    o1 = V(lambda e, t1t=t1t, bA=bA, pv=pv, v3=v3: e.tensor_tensor(out=v3(t1t), in0=v3(bA), in1=pv(Pa), op=ALU.mult), [mA] + d1_)
                    o2 = V(lambda e, t2t=t2t, bB=bB, pv=pv, v3=v3: e.tensor_tensor(out=v3(t2t), in0=v3(bB), in1=pv(Pb), op=ALU.mult), [mB] + d2_)
                    banks.done(kA, o1); banks.done(kB, o2)
                    o3 = V(lambda e, Vt=Vt, t1t=t1t, t2t=t2t: e.tensor_tensor(out=Vt[0:L, :], in0=t1t[0:L, :], in1=t2t[0:L, :], op=ALU.add), [o1, o2] + dV)
                    t1b.done(k1, o3); t2b.done(k2, o3)
                    def s2(gg0=gg0, hc=hc, Vt=Vt, kv=kv, o3=o3):
                        kcA, cA, dcA = banks.next(); kcB, cB, dcB = banks.next()
                        mc = None
                        for g in range(4):
                            mc = p.op("pe", lambda e, cA=cA, Vt=Vt, g=g: e.matmul(
                                cA[:, g * 128:g * 128 + L], lhsT=Vt[0:L, g * 128:(g + 1) * 128], rhs=tri[0:L, 0:L],
                                start=True, stop=True), ([o3] + dcA + dcB) if g == 0 else [], inc=False)
                            mc = p.op("pe", lambda e, cB=cB, Vt=Vt, g=g: e.matmul(
                                cB[0:64, g * 128:g * 128 + L], lhsT=Vt[0:L, g * 128 + 64:(g + 1) * 128], rhs=tri[0:L, 0:L],
                                start=True, stop=True), [], inc=False)
                            mc = p.op("pe", lambda e, cB=cB, Vt=Vt, g=g: e.matmul(
                                cB[64:128, g * 128:g * 128 + L], lhsT=Vt[0:L, g * 128:g * 128 + 64], rhs=tri[0:L, 0:L],
                                start=True, stop=True), [], inc=(g == 3))
                        Vb.done(kv, mc)
                        c3 = lambda t: t[:, :].rearrange("p (g n) -> p g n", n=128)[:, :, 0:L]
                        qv_ = lambda t, gg0=gg0: t[:, gg0:gg0 + 4, 0:L]
                        kx1, x1t, dx1 = x1b.next(); kx2, x2t, dx2 = x2b.next()
                        r1 = V(lambda e, x1t=x1t, cA=cA, c3=c3, qv_=qv_: e.tensor_tensor(out=x1t[:, :, 0:L], in0=c3(cA), in1=qv_(Qa), op=ALU.mult), [mc] + dx1)
                        r2 = V(lambda e, x2t=x2t, cB=cB, c3=c3, qv_=qv_: e.tensor_tensor(out=x2t[:, :, 0:L], in0=c3(cB), in1=qv_(Qb), op=ALU.mult), [mc] + dx2)
                        r3 = V(lambda e, Xt=Xt, x1t=x1t, x2t=x2t, hc=hc: e.tensor_tensor(
                            out=Xt[:, hc * 4:(hc + 1) * 4, 0:L], in0=x1t[:, :, 0:L], in1=x2t[:, :, 0:L], op=ALU.add), [r1, r2] + (dX if hc == 0 else []))
                        r4 = V(lambda e, x1t=x1t, x2t=x2t, gg0=gg0: e.tensor_tensor(
                            out=EA[:, gg0:gg0 + 4], in0=x1t[:, :, L - 1], in1=x2t[:, :, L - 1], op=ALU.add), [r1, r2] + t_E_read)
                        r5 = V(lambda e, cB=cB, gg0=gg0: e.tensor_tensor(
                            out=e1t[:, gg0:gg0 + 4], in0=cB[:, :].rearrange("p (g n) -> p g n", n=128)[:, :, L - 1],
                            in1=Qa[:, gg0:gg0 + 4, L - 1], op=ALU.mult), [mc] + t_E_read)
                        r6 = V(lambda e, cA=cA, gg0=gg0: e.tensor_tensor(
                            out=EB[:, gg0:gg0 + 4], in0=cA[:, :].rearrange("p (g n) -> p g n", n=128)[:, :, L - 1],
                            in1=Qb[:, gg0:gg0 + 4, L - 1], op=ALU.mult), [mc] + t_E_read)
                        r7 = V(lambda e, gg0=gg0: e.tensor_tensor(
                            out=EB[:, gg0:gg0 + 4], in0=e1t[:, gg0:gg0 + 4], in1=EB[:, gg0:gg0 + 4], op=ALU.subtract), [r5, r6])
                        banks.done(kcA, r1, r6); banks.done(kcB, r2, r5)
                        x1b.done(kx1, r3, r4); x2b.done(kx2, r3, r4)
                        x_toks.append(r3)
                        e_toks.extend([r4, r7])
                    pend_s2.append(s2)
                for f_ in pend_s2:
                    f_()
                ub.done(kb_, mB)
                kpa, pa, dpa = PadA.next(); kpb, pbt, dpb = PadB.next()
                tp = None
                for g in range(8):
                    Gx = ck * 8 + g
                    tp = V(lambda e, pa=pa, g=g, Gx=Gx, h=HA[hcur]: e.tensor_scalar(
                        out=pa[:, g, g * 16:(g + 1) * 16], in0=Ccc[:, Gx * 16:(Gx + 1) * 16], scalar1=h[:, Gx:Gx + 1],
                        scalar2=None, op0=ALU.mult), (t_H + [t_ccc] + dpa + t_z) if g == 0 else [])
                    tp = V(lambda e, pbt=pbt, g=g, Gx=Gx, h=HB[hcur]: e.tensor_scalar(
                        out=pbt[:, g, g * 16:(g + 1) * 16], in0=Ccc[:, Gx * 16:(Gx + 1) * 16], scalar1=h[:, Gx:Gx + 1],
                        scalar2=None, op0=ALU.mult), dpb if g == 0 else [])
                my = None
                for g in range(8):
                    Gx = ck * 8 + g
                    my = p.op("pe", lambda e, Xt=Xt, g=g, ck=ck: e.matmul(
                        ps_y[:, 0:L], lhsT=CcP[:, ck, g * 128:(g + 1) * 128], rhs=Xt[:, g, 0:L], start=(g == 0), stop=False),
                        (x_toks + [tp] + y_free) if g == 0 else [], inc=False)
                    my = p.op("pe", lambda e, pa=pa, g=g, Gx=Gx: e.matmul(
                        ps_y[:, 0:L], lhsT=pa[:, g, :], rhs=Qa16[:, Gx, 1:L + 1], start=False, stop=False), [], inc=False)
                    my = p.op("pe", lambda e, pbt=pbt, g=g, Gx=Gx: e.matmul(
                        ps_y[:, 0:L], lhsT=pbt[:, g, :], rhs=Qb16[:, Gx, 1:L + 1], start=False, stop=(g == 7)), [], inc=(g == 7))
                Xb.done(kx, my); PadA.done(kpa, my); PadB.done(kpb, my)
                pad_readers.append(tp)
                ko, yot, dyo = yo.next()
                ty = V(lambda e, yot=yot, uft=uft, ck=ck: e.scalar_tensor_tensor(
                    out=yot[:, 0:L], in0=uft, scalar=dv[:, ck:ck + 1], in1=ps_y[:, 0:L], op0=ALU.mult, op1=ALU.add), [my] + dyo)
                y_free = [ty]
                ttr = p.op("pe", lambda e, yot=yot: e.transpose(out=ps_t[0:L, 0:128], in_=yot[:, 0:L], identity=identf[:]),
                           [ty] + tr_free)
                kyr, yrt, dyr = yr.next()
                tcp = p.op("act", lambda e, yrt=yrt: e.copy(out=yrt[0:L, :], in_=ps_t[0:L, 0:128]), [ttr] + dyr)
                tr_free = [tcp]
                yo.done(ko, ttr)
                tst = p.dma("act", f"y{kyr}", pubY(yrow0, ck, L), yrt[0:L, :], [tcp])
                yr.done(kyr, tst)
                outs.append(tst)
            hn = 1 - hcur
            QaL = Qa[:, :, L]; QbL = Qb[:, :, L]
            dep0 = t_H + e_toks + pad_readers + t_Hread
            u1 = V(lambda e, h=HA[hcur], QaL=QaL: e.tensor_tensor(out=n1[:], in0=h[:], in1=QaL, op=ALU.mult), dep0)
            u2 = V(lambda e, h=HB[hcur], QbL=QbL: e.tensor_tensor(out=n2[:], in0=h[:], in1=QbL, op=ALU.mult), [u1])
            u3 = V(lambda e: e.tensor_tensor(out=n1[:], in0=n1[:], in1=n2[:], op=ALU.add), [u2])
            u4 = V(lambda e, h=HA[hn]: e.tensor_tensor(out=h[:], in0=n1[:], in1=EA[:], op=ALU.add), [u3])
            u5 = V(lambda e, h=HB[hcur], QaL=QaL: e.tensor_tensor(out=n1[:], in0=h[:], in1=QaL, op=ALU.mult), [u4])
            u6 = V(lambda e, h=HA[hcur], QbL=QbL: e.tensor_tensor(out=n2[:], in0=h[:], in1=QbL, op=ALU.mult), [u5])
            u7 = V(lambda e: e.tensor_tensor(out=n1[:], in0=n1[:], in1=n2[:], op=ALU.subtract), [u6])
            u8 = V(lambda e, h=HB[hn]: e.tensor_tensor(out=h[:], in0=n1[:], in1=EB[:], op=ALU.add), [u7])
            t_E_read = [u8]; t_Hread = [u8]; t_H = [u4, u8]
            hcur = hn
        for blk_ in blocks:
            do_block(*blk_)
        tf = p.dma("sp", "hf", HF_d[si_], HA[hcur][:], t_H)
        outs.append(tf)
        t_Hread = t_Hread + [tf]
    return outs[-8:]
R_ = 1152
RT_ = 9


def _zz_block(k, j):
    m = j // 2
    return 8 * m + k if j % 2 == 0 else 8 * m + 7 - k


def _zz_owner(S):
    m, x = S // 8, S % 8
    return (x, 2 * m) if x <= 3 else (7 - x, 2 * m + 1)


P_SMAX = [8 * (j // 2) + 4 if j % 2 == 0 else 8 * (j // 2) + 8 for j in range(8)]


def emit_rmsout(p, x, nw, y, eps=1e-6):
    D = 2048
    nwb = p.sb([128, D], F32)
    t_nw = p.dma("sp", "nw", nwb[:], nw.partition_broadcast(128))
    xb = Ring([p.sb([128, D], F32) for _ in range(2)])
    tf = Ring([p.sb([128, D], F32) for _ in range(2)])
    st = Ring([p.sb([128, 2], F32) for _ in range(2)])
    fin = []
    for r in range(RT_):
        rows = slice(r * 128, (r + 1) * 128)
        kx, xt, dx = xb.next()
        t_x = p.dma("sp", f"x{kx}", xt[:], x[rows, :], dx)
        kt, tt, dt_ = tf.next()
        ks, s_, ds = st.next()
        t_sq = p.op("act", lambda e, tt=tt, xt=xt, s_=s_: e.activation(out=tt[:], in_=xt[:], func=AF.Square, accum_out=s_[:, 0:1]), [t_x] + dt_ + ds)
        t1 = p.op("dve", lambda e, s_=s_: e.tensor_scalar(out=s_[:, 1:2], in0=s_[:, 0:1], scalar1=1.0 / D, scalar2=eps, op0=ALU.mult, op1=ALU.add), [t_sq])
        t2 = p.op("act", lambda e, s_=s_: e.activation(out=s_[:, 1:2], in_=s_[:, 1:2], func=AF.Sqrt), [t1])
        t3 = p.op("dve", lambda e, s_=s_: e.reciprocal(out=s_[:, 1:2], in_=s_[:, 1:2]), [t2])
        t4 = p.op("dve", lambda e, tt=tt, xt=xt, s_=s_: e.scalar_tensor_tensor(out=tt[:], in0=xt[:], scalar=s_[:, 1:2], in1=nwb[:], op0=ALU.mult, op1=ALU.mult), [t3, t_nw])
        xb.done(kx, t4); st.done(ks, t4)
        t5 = p.dma("act", f"o{kt}", y[rows, :], tt[:], [t4])
        tf.done(kt, t5)
        fin.append(t5)
    return fin[-2:]


def emit_gather(p, ck_d, cv_d, ci_d, pt_d, Kp, Vp, ip):
    pt = p.sb([128, 1], I32)
    idx = p.sb([128, 8], I32)
    bufs = Ring([p.sb([128, 8192], F32) for _ in range(3)])
    t_pt = p.dma("sp", "pt", pt[:], pt_d)
    t_i = None
    for e_ in range(8):
        t_i = p.op("dve", lambda e, e_=e_: e.tensor_scalar(
            out=idx[:, e_:e_ + 1], in0=pt[:], scalar1=8, scalar2=e_, op0=ALU.mult, op1=ALU.add), [t_pt])
    outs = []
    jobs = [(ci_d, pt, 0, ip)] + [(ck_d, idx, e_, Kp[:, e_, :]) for e_ in range(8)] + \
           [(cv_d, idx, e_, Vp[:, e_, :]) for e_ in range(8)]
    for n, (src, it, col, dst) in enumerate(jobs):
        kb, bt, db = bufs.next()
        tok = p.lane_op("pool", f"g{kb}", lambda e, bt=bt, src=src, it=it, col=col: e.indirect_dma_start(
            out=bt[:], out_offset=None, in_=src, in_offset=bass.IndirectOffsetOnAxis(ap=it[:, col:col + 1], axis=0)),
            [t_i, t_pt] + db)
        t_o = p.dma("sp", f"go{kb}", dst, bt[:], [tok])
        bufs.done(kb, t_o)
        outs.append(t_o)
    return outs[-3:]


class PromptSrc:
    def __init__(self, p, qT_s, qiT_s, wT_s, gKT, gV, gKi, idxK_d, idxI_d):
        self.p = p
        self.qT_s, self.qiT_s, self.wT_s, self.gKT, self.gV, self.gKi = qT_s, qiT_s, wT_s, gKT, gV, gKi
        self.idxK = p.sb([128, 4 * 144], I32)
        self.idxI = p.sb([128, 144], I32)
        self.t_ik = p.dma("sp", "ik", self.idxK[:], idxK_d)
        self.t_ii = p.dma("sp", "ii", self.idxI[:], idxI_d)
        self.u0 = [sum(P_SMAX[:j]) for j in range(8)]
        self.n = 0

    def load_q(self, j, t, deps):
        return self.p.dma("pool", "lq", t[:, :].rearrange("p (h t) -> p h t", h=16),
                          self.qT_s[:, j * 128:(j + 1) * 128].rearrange("(h d) t -> d h t", d=128), deps)

    def load_qi(self, j, t, deps):
        return self.p.dma("pool", "lqi", t[:, :].rearrange("p (h t) -> p h t", h=16),
                          self.qiT_s[:, j * 128:(j + 1) * 128].rearrange("(h d) t -> d h t", d=64), deps)

    def load_w(self, j, t, deps):
        return self.p.dma("sp", "lw", t[:, :].rearrange("p (h t) -> p h t", h=16),
                          self.wT_s[:, j * 128:(j + 1) * 128].partition_broadcast(128), deps)

    def _ind(self, t, table, idx_ap, deps, dep2):
        self.n += 1
        return self.p.lane_op("pool", f"in{self.n % 6}", lambda e: e.indirect_dma_start(
            out=t[:, :], out_offset=None, in_=table, in_offset=bass.IndirectOffsetOnAxis(ap=idx_ap, axis=0)),
            list(deps) + [dep2])

    def load_ki(self, j, dl, t, deps):
        u = self.u0[j] + dl
        return self._ind(t, self.gKi, self.idxI[:, u:u + 1], deps, self.t_ii)

    def load_k(self, j, g, dl, t, deps):
        u = self.u0[j] + dl
        return self._ind(t, self.gKT, self.idxK[:, g * 144 + u:g * 144 + u + 1], deps, self.t_ik)

    def load_v(self, j, g, dl, t, deps):
        u = self.u0[j] + dl
        return self._ind(t, self.gV, self.idxK[:, g * 144 + u:g * 144 + u + 1], deps, self.t_ik)


class SampleSrc:
    def __init__(self, p, qT_s, qiT_s, wT_s, kT_s, kiT_s, z, KTs, kiTs, Vp):
        self.p = p
        self.qT_s, self.qiT_s, self.wT_s, self.kT_s, self.kiT_s, self.z = qT_s, qiT_s, wT_s, kT_s, kiT_s, z
        self.KTs, self.kiTs, self.Vp = KTs, kiTs, Vp
        self.c = slice(1024, 1028)

    def load_q(self, j, t, deps):
        return self.p.dma("pool", "lq", t[:, :].rearrange("p (h t) -> p h t", h=16),
                          self.qT_s[:, self.c].rearrange("(h d) t -> d h t", d=128), deps)

    def load_qi(self, j, t, deps):
        return self.p.dma("pool", "lqi", t[:, :].rearrange("p (h t) -> p h t", h=16),
                          self.qiT_s[:, self.c].rearrange("(h d) t -> d h t", d=64), deps)

    def load_w(self, j, t, deps):
        return self.p.dma("sp", "lw", t[:, :].rearrange("p (h t) -> p h t", h=16),
                          self.wT_s[:, self.c].partition_broadcast(128), deps)

    def _new(self, t, dst, src, deps):
        z_ = self.p.op("dve", lambda e: e.memset(t[:, :], 0.0), deps)
        return self.p.dma("pool", "ln", dst, src, [z_])

    def load_ki(self, j, dl, t, deps):
        if dl == 0:
            return self._new(t, t[0:64, 0:4], self.kiT_s[0:64, self.c], deps)
        pg = 128 - dl
        return self.p.dma("pool", "lki", t[0:64, :], self.kiTs[:, pg * 128:(pg + 1) * 128], deps)

    def load_k(self, j, g, dl, t, deps):
        if dl == 0:
            return self._new(t, t[:, 0:4], self.kT_s[g * 128:(g + 1) * 128, self.c], deps)
        pg = 128 - dl
        return self.p.dma("pool", "lk", t[:, :], self.KTs[g * 128:(g + 1) * 128, pg * 128:(pg + 1) * 128], deps)

    def load_v(self, j, g, dl, t, deps):
        if dl == 0:
            return self._new(t, t[0:4, :], self.z[1024:1028, 2560 + g * 128:2560 + (g + 1) * 128], deps)
        pg = 128 - dl
        return self.p.dma("pool", "lv", t[:, :], self.Vp[pg * 128:(pg + 1) * 128, g * 128:(g + 1) * 128], deps)


def build_fused():
    nc = bass.Bass("TRN2", target_bir_lowering=False)
    EI = lambda n, s, dt=F32: nc.dram_tensor(n, list(s), dt, kind="ExternalInput").ap()
    EO = lambda n, s, dt=F32: nc.dram_tensor(n, list(s), dt, kind="ExternalOutput").ap()
    IT = lambda n, s, dt=F32: nc.dram_tensor(n, list(s), dt)
    xrows = EI("xrows", [R_, 2048]); prows = EI("prows", [4, R_, 256])
    norm_w = EI("norm_w", [4, 1, 2048]); ple_nw = EI("ple_nw", [4, 1, 2048]); fnw = EI("fnw", [1, 2048])
    a_win = EI("a_win", [2, 2048, 6224]); a_wout = EI("a_wout", [2, 2048, 2048])
    s_win = EI("s_win", [2, 2048, 4096]); s_wglu = EI("s_wglu", [2, 2048, 2048]); s_bglu = EI("s_bglu", [2, 1, 2048])
    s_wout = EI("s_wout", [2, 2048, 2048]); p_wg = EI("p_wg", [4, 2048, 2048]); p_wp = EI("p_wp", [4, 256, 2048])
    ident = EI("ident", [128, 128])
    ck = [EI(f"ck{l}", [10240, 8192]) for l in range(2)]; cv = [EI(f"cv{l}", [10240, 8192]) for l in range(2)]
    ci = [EI(f"ci{l}", [1280, 8192]) for l in range(2)]
    pt = EI("pt", [128, 1], I32)
    vtp = EI("vtp", [1, 144]); Dp = EI("Dp", [2, 128, 2048]); C0p = EI("C0p", [128, 128])
    vts = EI("vts", [1, 129]); Ds = EI("Ds", [2, 128, 64]); C0s = EI("C0s", [128, 4]); cbv = EI("cbv", [1, 16])
    idxK = EI("idxK", [128, 576], I32); idxI = EI("idxI", [128, 144], I32)
    SP = {}
    for nm, shp in [("lre_row", [1, 1024]), ("lim_row", [1, 1024]), ("ldt_row", [1, 1024]),
                    ("lreT2", [128, 16]), ("limT2", [128, 16]), ("ldtT2", [128, 16]),
                    ("lre_gi", [2, 128, 64]), ("lim_gi", [2, 128, 64]), ("ldt_gi", [2, 128, 64]),
                    ("bre_gi", [2, 128, 64]), ("bim_gi", [2, 128, 64]),
                    ("CcP", [2, 128, 1024]), ("Ccc", [128, 256]), ("dvec", [128, 2]),
                    ("H0A", [4, 128, 16]), ("H0B", [4, 128, 16])]:
        SP[nm] = EI("sp_" + nm, [2, 2] + shp)
    tri = EI("tri", [128, 128]); cst = EI("cst", [128, 4]); tvec = EI("tvec", [1, 129]); mask = EI("mask", [128, 8, 64])
    uidx = EI("uidx", [2, 128, 8], I32); yidx = EI("yidx", [128, 36], I32)
    yout = EO("yout", [R_, 2048]); kvk = EO("kvk", [2, R_, 1088])
    HFp = EO("HFp", [2, 2, 1, 128, 16]); HFs = EO("HFs", [2, 2, 4, 128, 16])

    hA = IT("hA", [R_, 2048]).ap(); hB = IT("hB", [R_, 2048]).ap()
    z = IT("z", [R_, 6224]).ap()
    qT_s = IT("qT_s", [2048, R_]).ap(); kT_s = IT("kT_s", [512, R_]).ap(); qiT_s = IT("qiT_s", [1024, R_]).ap()
    kiT_s = IT("kiT_s", [128, R_]).ap(); wT_s = IT("wT_s", [16, R_]).ap()
    pubKT = IT("pubKT", [4 * 8 * 128, 128]); pubV = IT("pubV", [4 * 8 * 128, 128]); pubKi = IT("pubKi", [8 * 128, 128])
    gKT = IT("gKT", [4 * 4 * 8 * 128, 128]); gV = IT("gV", [4 * 4 * 8 * 128, 128]); gKi = IT("gKi", [4 * 8 * 128, 128])
    oT_s = IT("oT_s", [2048, R_]).ap(); o_s = IT("o_s", [R_, 2048]).ap(); g_s = IT("g_s", [R_, 2048]).ap()
    Kp = IT("Kp", [16384, 512]).ap(); Vp = IT("Vp", [16384, 512]).ap(); ip = IT("ip", [16384, 64]).ap()
    KTs = IT("KTs", [512, 16384]).ap(); kiTs = IT("kiTs", [64, 16384]).ap()
    uT_s = IT("uT_s", [2048, R_]); gU = IT("gU", [4 * 2048, R_])
    pubY = IT("pubY", [4608, 512]); gY = IT("gY", [4 * 4608, 512])
    y_s = IT("y_s", [R_, 2048]).ap(); y3 = IT("y3", [R_, 2048]).ap()
    RG = [[0, 1, 2, 3], [4, 5, 6, 7]]

    def allgather(p, src, dst, deps, CR=None):
        rows = src.ap().shape[0]
        CR = CR or rows
        prev = list(deps)
        for c_ in range(rows // CR):
            cc = p.lane_op("pool", "cc", lambda e, c_=c_: e.collective_compute(
                "AllGather", ALU.bypass, replica_groups=RG, ins=[src.ap()[c_ * CR:(c_ + 1) * CR, :].opt()],
                outs=[dst.ap()[c_ * 4 * CR:(c_ + 1) * 4 * CR, :].opt()]), prev, incv=1)
            prev = [cc]
        return prev[0]

    with ExitStack() as es:
        p = Prog(nc, es)
        h, h1 = None, None
        cur = xrows
        bufs = [hA, hB]
        for i in range(4):
            l = i // 2
            hn1 = bufs[0] if cur is not bufs[0] else bufs[1]
            if i % 2 == 0:
                with p.stage():
                    emit_linear(p, RT_, 2048, 6224, "rms", "none", cur, a_win[l], z, ident, nw=norm_w[i])
                for (src, dst, C) in [(z[:, 0:2048], qT_s, 2048), (z[:, 2048:2560], kT_s, 512),
                                      (z[:, 5120:6144], qiT_s, 1024), (z[:, 6144:6208], kiT_s, 64),
                                      (z[:, 6208:6224], wT_s, 16)]:
                    with p.stage():
                        emit_transpose(p, src, dst, R_, C, ident)
                with p.stage():
                    d0 = p.dma("sp", "k0", kvk[l][:, 0:1024], z[:, 2048:3072])
                    d1 = p.dma("sp", "k1", kvk[l][:, 1024:1088], z[:, 6144:6208])
                    pk = pubKT.ap().rearrange("(g j d) s -> g j d s", g=4, j=8)
                    d2 = [p.dma("act", f"k2{g}", pk[g], kT_s[g * 128:(g + 1) * 128, 0:1024].rearrange("d (j s) -> j d s", s=128))
                          for g in range(4)]
                    pv = pubV.ap().rearrange("(g j s) d -> g j s d", g=4, j=8)
                    d3 = [p.dma("act", f"k3{g}", pv[g], z[0:1024, 2560 + g * 128:2560 + (g + 1) * 128].rearrange("(j s) d -> j s d", s=128))
                          for g in range(4)]
                    d4 = p.dma("sp", "k4", pubKi.ap().rearrange("(j d) s -> j d s", j=8),
                               kiT_s[:, 0:1024].rearrange("d (j s) -> j d s", s=128))
                    c1 = allgather(p, pubKT, gKT, d2, 2048)
                    c2_ = allgather(p, pubV, gV, d3 + [c1], 2048)
                    c3 = allgather(p, pubKi, gKi, [d4, c2_])
                    p.op("sp", None, [d0, d1, c3], inc=False)
                with p.stage():
                    src = PromptSrc(p, qT_s, qiT_s, wT_s, gKT.ap(), gV.ap(), gKi.ap(), idxK, idxI)
                    emit_attn(p, 8, 128, P_SMAX, src, vtp, Dp, cbv, C0p,
                              lambda j, g: oT_s[g * 512:(g + 1) * 512, j * 128:(j + 1) * 128].rearrange("(r d) t -> d r t", d=128))
                with p.stage():
                    emit_gather(p, ck[l], cv[l], ci[l], pt, Kp.rearrange("(pg e s) c -> pg e (s c)", pg=128, e=8),
                                Vp.rearrange("(pg e s) c -> pg e (s c)", pg=128, e=8), ip.rearrange("(pg s) c -> pg (s c)", pg=128))
                with p.stage():
                    emit_transpose(p, Kp, KTs, 16384, 512, ident)
                with p.stage():
                    emit_transpose(p, ip, kiTs, 16384, 64, ident)
                with p.stage():
                    src = SampleSrc(p, qT_s, qiT_s, wT_s, kT_s, kiT_s, z, KTs, kiTs, Vp)
                    emit_attn(p, 1, 4, [129], src, vts, Ds, cbv, C0s,
                              lambda j, g: oT_s[g * 512:(g + 1) * 512, 1024:1028].rearrange("(r d) t -> d r t", d=128))
                with p.stage():
                    emit_transpose(p, oT_s, o_s, 2048, R_, ident)
                with p.stage():
                    emit_linear(p, RT_, 2048, 2048, "gate", "res", o_s, a_wout[l], hn1, ident, x2=z[:, 3072:5120], e1=cur)
            else:
                with p.stage():
                    emit_linear(p, RT_, 2048, 4096, "rms", "none", cur, s_win[l], z[:, 0:4096], ident, nw=norm_w[i])
                with p.stage():
                    emit_transpose(p, z[:, 0:2048], uT_s.ap(), R_, 2048, ident)
                for c2 in range(2):
                    I = {k_: v_[l, c2] for k_, v_ in SP.items()}
                    I.update({"tri": tri, "cst": cst, "tvec": tvec, "mask": mask})
                    pY = lambda yrow0, ck_, L, c2=c2: pubY.ap()[yrow0:yrow0 + L, c2 * 256 + ck_ * 128:c2 * 256 + (ck_ + 1) * 128]
                    pblocks = []
                    for S in range(32):
                        r, jj = _zz_owner(S)
                        pblocks.append((r, jj * 128, 128, S * 128))
                    with p.stage():
                        pre = [allgather(p, uT_s, gU, [], 128)] if c2 == 0 else []
                        seqs = [(pblocks, None)] + [([(i_, 1024, 4, 4096 + 128 * i_)], i_) for i_ in range(4)]
                        HFl = [HFp[l, c2][0]] + [HFs[l, c2][i_] for i_ in range(4)]
                        emit_scan(p, seqs, I, gU.ap(), uidx[c2], pY, HFl, ident, pre=pre)
                with p.stage():
                    c1 = allgather(p, pubY, gY, [], 512)
                    yix = p.sb([128, 36], I32)
                    t_yi = p.dma("sp", "yi", yix[:], yidx)
                    yb = Ring([p.sb([128, 512], F32) for _ in range(4)])
                    fin = []
                    for jj in range(9):
                        for r in range(4):
                            kb, bt, db = yb.next()
                            col = jj * 4 + r
                            tg = p.lane_op("pool", f"yg{kb}", lambda e, bt=bt, col=col: e.indirect_dma_start(
                                out=bt[:], out_offset=None, in_=gY.ap(),
                                in_offset=bass.IndirectOffsetOnAxis(ap=yix[:, col:col + 1], axis=0)), [c1, t_yi] + db)
                            to = p.dma("sp", f"yo{kb}", y_s[jj * 128:(jj + 1) * 128, r * 512:(r + 1) * 512], bt[:], [tg])
                            yb.done(kb, to)
                            fin.append(to)
                    p.op("sp", None, fin[-4:], inc=False)
                with p.stage():
                    emit_linear(p, RT_, 2048, 2048, "gelu", "glu", y_s, s_wglu[l], y3, ident, e1=y_s, bv=s_bglu[l])
                with p.stage():
                    emit_linear(p, RT_, 2048, 2048, "gate", "res", y3, s_wout[l], hn1, ident, x2=z[:, 2048:4096], e1=cur)
            with p.stage():
                emit_linear(p, RT_, 2048, 2048, "rms", "sigmoid", hn1, p_wg[i], g_s, ident, nw=ple_nw[i])
            hn2 = bufs[0] if hn1 is not bufs[0] else bufs[1]
            with p.stage():
                emit_linear(p, RT_, 256, 2048, "plain", "gres", prows[i], p_wp[i], hn2, ident, e1=hn1, e2=g_s)
            cur = hn2
        with p.stage():
            fin = emit_rmsout(p, cur, fnw, yout)
            p.op("sp", None, fin, inc=False)
        with p.stage():
            for e_ in ("pe", "act", "dve", "pool", "sp"):
                p.op(e_, None, [], inc=False)
    return nc


def _t5_bucket_np(n):
    n = np.asarray(n, dtype=np.int32)
    nf = np.maximum(n, 1).astype(np.float32)
    large = 16 + (np.log(nf / np.float32(16)) / np.float32(math.log(128 / 16)) * np.float32(16)).astype(np.int32)
    large = np.minimum(large, 31)
    return np.where(n < 16, n, large)


def _bias_tiles(rel_bias, NT):
    s_l = np.arange(128)[:, None]
    t_l = np.arange(NT)[None, :]
    d0 = t_l - s_l
    b0 = rel_bias[_t5_bucket_np(np.maximum(d0, 0))]
    b0 = np.where((d0 >= 0)[:, :, None], b0, np.float32(NEG))
    b1 = rel_bias[_t5_bucket_np(128 + d0)]
    D = np.stack([b0, b1]).transpose(0, 1, 3, 2).reshape(2, 128, 16 * NT)
    C0 = np.where(d0 >= 0, np.float32(0), np.float32(NEG)).astype(np.float32)
    return np.ascontiguousarray(D, dtype=np.float32), C0


def _scan_inputs(m, b, k, T, pi, ssm_lambda_re, ssm_lambda_im, ssm_log_dt, ssm_b_re, ssm_b_im, ssm_c_re, ssm_c_im, ssm_d, state_ssm_re, state_ssm_im):
    sp = {nm: [] for nm in ("lre_row", "lim_row", "ldt_row", "lreT2", "limT2", "ldtT2", "lre_gi", "lim_gi", "ldt_gi",
                            "bre_gi", "bim_gi", "CcP", "Ccc", "dvec", "H0A", "H0B")}
    for l in range(2):
        for c2 in range(2):
            g0 = 32 * k + 16 * c2
            gs = slice(g0, g0 + 16)
            lre, lim, ldt = ssm_lambda_re[l][gs], ssm_lambda_im[l][gs], ssm_log_dt[l][gs]
            sp["lre_row"].append(lre.reshape(1, 1024)); sp["lim_row"].append(lim.reshape(1, 1024))
            sp["ldt_row"].append(np.broadcast_to(ldt[:, None], (16, 64)).reshape(1, 1024))
            sp["lreT2"].append(np.concatenate([lre.T, lre.T], 0)); sp["limT2"].append(np.concatenate([lim.T, lim.T], 0))
            sp["ldtT2"].append(np.broadcast_to(ldt[None, :], (128, 16)))
            rep = lambda a: np.stack([np.repeat(a[ck * 8:(ck + 1) * 8], 16, axis=0) for ck in range(2)])
            sp["lre_gi"].append(rep(lre)); sp["lim_gi"].append(rep(lim))
            sp["ldt_gi"].append(rep(np.broadcast_to(ldt[:, None], (16, 64))))
            tb = lambda a: np.stack([a[ck * 8:(ck + 1) * 8].transpose(0, 2, 1).reshape(128, 64) for ck in range(2)])
            sp["bre_gi"].append(tb(ssm_b_re[l][gs])); sp["bim_gi"].append(tb(ssm_b_im[l][gs]))
            CcP = np.zeros((2, 128, 8, 128), np.float32)
            Ccc = np.zeros((128, 16, 16), np.float32)
            for G in range(16):
                ck_, g = G // 8, G % 8
                cc = np.concatenate([ssm_c_re[l][g0 + G].T, ssm_c_im[l][g0 + G].T], axis=0)
                CcP[ck_, :, g, g * 16:(g + 1) * 16] = cc
                Ccc[:, G, :] = cc
            sp["CcP"].append(CcP.reshape(2, 128, 1024)); sp["Ccc"].append(Ccc.reshape(128, 256))
            sp["dvec"].append(ssm_d[l][512 * k + 256 * c2:512 * k + 256 * c2 + 256].reshape(2, 128).T)
            hre = state_ssm_re[l][4 * b:4 * b + 4, gs].transpose(0, 2, 1)
            him = state_ssm_im[l][4 * b:4 * b + 4, gs].transpose(0, 2, 1)
            sp["H0A"].append(np.concatenate([hre, him], 1)); sp["H0B"].append(np.concatenate([him, hre], 1))
    for nm, lst in sp.items():
        a = np.stack([np.ascontiguousarray(x_, dtype=np.float32) for x_ in lst])
        m["sp_" + nm] = np.ascontiguousarray(a.reshape((2, 2) + a.shape[1:]))
    uidx = np.zeros((2, 128, 8), np.int32)
    for c2 in range(2):
        for ck_ in range(2):
            for r in range(4):
                uidx[c2, :, ck_ * 4 + r] = (4 * k + 2 * c2 + ck_) * 512 + r * 128 + pi
    m["uidx"] = uidx
    yidx = np.zeros((128, 36), np.int32)
    for jj in range(9):
        for r in range(4):
            if jj < 8:
                yidx[:, jj * 4 + r] = (T[jj] // 4) * 2048 + r * 512 + (T[jj] % 4) * 128 + pi
            else:
                yidx[:, jj * 4 + r] = 8 * 2048 + r * 512 + 128 * k + pi
    m["yidx"] = yidx


_IDENT = np.eye(128, dtype=np.float32)
_TRI = np.triu(np.ones((128, 128), np.float32))
_CST = np.stack([np.arange(128), -np.arange(128), np.where(np.arange(128) < 64, -1.0, 1.0),
                 np.where(np.arange(128) < 64, 1.0, -1.0)], axis=1).astype(np.float32)
_TVEC = np.arange(129, dtype=np.float32).reshape(1, 129)
_MASK = (np.arange(128)[:, None, None] // 16 == np.arange(8)[None, :, None]).astype(np.float32) * np.ones((1, 1, 64), np.float32)
_NC = {}


def kernel(x_prompt, x_sample, cache_k, cache_v, cache_kidx, state_ssm_re, state_ssm_im, page_table,
           p_prompt, p_sample, norm_w, final_norm_w, rel_bias, attn_w_in, attn_w_out, ssm_w_in,
           ssm_lambda_re, ssm_lambda_im, ssm_log_dt, ssm_b_re, ssm_b_im, ssm_c_re, ssm_c_im, ssm_d,
           ssm_w_glu, ssm_b_glu, ssm_w_out, ple_norm_w, ple_w_gate, ple_w_proj):
    f32 = lambda a: np.ascontiguousarray(np.asarray(a), dtype=np.float32)
    (x_prompt, x_sample, cache_k, cache_v, cache_kidx, state_ssm_re, state_ssm_im, p_prompt, p_sample, norm_w,
     final_norm_w, rel_bias, attn_w_in, attn_w_out, ssm_w_in, ssm_lambda_re, ssm_lambda_im, ssm_log_dt, ssm_b_re,
     ssm_b_im, ssm_c_re, ssm_c_im, ssm_d, ssm_w_glu, ssm_b_glu, ssm_w_out, ple_norm_w, ple_w_gate, ple_w_proj) = [
        f32(a) for a in (x_prompt, x_sample, cache_k, cache_v, cache_kidx, state_ssm_re, state_ssm_im, p_prompt,
                         p_sample, norm_w, final_norm_w, rel_bias, attn_w_in, attn_w_out, ssm_w_in, ssm_lambda_re,
                         ssm_lambda_im, ssm_log_dt, ssm_b_re, ssm_b_im, ssm_c_re, ssm_c_im, ssm_d, ssm_w_glu,
                         ssm_b_glu, ssm_w_out, ple_norm_w, ple_w_gate, ple_w_proj)]
    page_table = np.asarray(page_table).astype(np.int32)
    if "nc" not in _NC:
        _NC["nc"] = build_fused()
    nc = _NC["nc"]
    Dp, C0p = _bias_tiles(rel_bias, 128)
    Ds, C0s = _bias_tiles(rel_bias, 4)
    shared = {
        "norm_w": norm_w.reshape(4, 1, 2048), "ple_nw": ple_norm_w.reshape(4, 1, 2048), "fnw": final_norm_w.reshape(1, 2048),
        "a_win": attn_w_in, "a_wout": attn_w_out, "s_win": ssm_w_in, "s_wglu": ssm_w_glu,
        "s_bglu": ssm_b_glu.reshape(2, 1, 2048), "s_wout": ssm_w_out, "p_wg": ple_w_gate, "p_wp": ple_w_proj,
        "ident": _IDENT, "Dp": Dp, "C0p": C0p, "Ds": Ds, "C0s": C0s, "vts": np.zeros((1, 129), np.float32),
        "cbv": np.ascontiguousarray(rel_bias[31].reshape(1, 16)), "tri": _TRI, "cst": _CST, "tvec": _TVEC,
        "mask": np.ascontiguousarray(_MASK),
    }
    for l in range(2):
        shared[f"ck{l}"] = cache_k[l].reshape(10240, 8192)
        shared[f"cv{l}"] = cache_v[l].reshape(10240, 8192)
        shared[f"ci{l}"] = cache_kidx[l].reshape(1280, 8192)
    pi = np.arange(128, dtype=np.int32)
    in_maps = []
    for c in range(NCORES):
        b, k = c // 4, c % 4
        T = [_zz_block(k, j) for j in range(8)]
        rows = np.concatenate([np.arange(t * 128, (t + 1) * 128) for t in T])
        m = dict(shared)
        xr = np.zeros((R_, 2048), np.float32)
        xr[0:1024] = x_prompt[b][rows]
        xr[1024:1028] = x_sample[c]
        pr = np.zeros((4, R_, 256), np.float32)
        pr[:, 0:1024] = p_prompt[:, b][:, rows]
        pr[:, 1024:1028] = p_sample[:, c]
        m["xrows"] = xr
        m["prows"] = pr
        m["pt"] = np.ascontiguousarray(page_table[c].reshape(128, 1))
        m["vtp"] = np.concatenate([np.where(np.arange(P_SMAX[j]) <= T[j], 0.0, NEG) for j in range(8)]).reshape(1, 144).astype(np.float32)
        idxK = np.zeros((128, 4, 144), np.int32)
        idxI = np.zeros((128, 144), np.int32)
        u = 0
        for j in range(8):
            for dl in range(P_SMAX[j]):
                r, jj = _zz_owner(max(T[j] - dl, 0))
                for g in range(4):
                    idxK[:, g, u] = (g // 2) * 8192 + r * 2048 + (g % 2) * 1024 + jj * 128 + pi
                idxI[:, u] = r * 1024 + jj * 128 + pi
                u += 1
        m["idxK"] = idxK.reshape(128, 576)
        m["idxI"] = idxI
        _scan_inputs(m, b, k, T, pi, ssm_lambda_re, ssm_lambda_im, ssm_log_dt, ssm_b_re, ssm_b_im, ssm_c_re, ssm_c_im, ssm_d, state_ssm_re, state_ssm_im)
        in_maps.append(m)
    res = run_bass_kernel_spmd(nc, in_maps, core_ids=list(range(NCORES)))
    y_p = np.zeros((2, 4096, 2048), np.float32); y_s = np.zeros((8, 4, 2048), np.float32)
    k_p = np.zeros((2, 2, 4096, 4, 128), np.float32); v_p = np.zeros_like(k_p); ki_p = np.zeros((2, 2, 4096, 64), np.float32)
    k_s = np.zeros((2, 8, 4, 4, 128), np.float32); v_s = np.zeros_like(k_s); ki_s = np.zeros((2, 8, 4, 64), np.float32)
    hr_p = np.zeros((2, 2, 128, 64), np.float32); hi_p = np.zeros_like(hr_p)
    hr_s = np.zeros((2, 8, 128, 64), np.float32); hi_s = np.zeros_like(hr_s)
    for c in range(NCORES):
        b, k = c // 4, c % 4
        r_ = res.results[c]
        yo, kv = r_["yout"], r_["kvk"]
        for j in range(8):
            t = _zz_block(k, j)
            sl = slice(t * 128, (t + 1) * 128)
            y_p[b, sl] = yo[j * 128:(j + 1) * 128]
            for l in range(2):
                blk = kv[l][j * 128:(j + 1) * 128]
                k_p[l, b, sl] = blk[:, 0:512].reshape(128, 4, 128)
                v_p[l, b, sl] = blk[:, 512:1024].reshape(128, 4, 128)
                ki_p[l, b, sl] = blk[:, 1024:1088]
        y_s[c] = yo[1024:1028]
        for l in range(2):
            blk = kv[l][1024:1028]
            k_s[l, c] = blk[:, 0:512].reshape(4, 4, 128); v_s[l, c] = blk[:, 512:1024].reshape(4, 4, 128)
            ki_s[l, c] = blk[:, 1024:1088]
            for c2 in range(2):
                gs = slice(32 * k + 16 * c2, 32 * k + 16 * c2 + 16)
                H = r_["HFp"][l, c2, 0]
                hr_p[l, b, gs] = H[0:64].T; hi_p[l, b, gs] = H[64:128].T
                for i_ in range(4):
                    H = r_["HFs"][l, c2, i_]
                    hr_s[l, 4 * b + i_, gs] = H[0:64].T; hi_s[l, 4 * b + i_, gs] = H[64:128].T
    return (y_p, y_s, k_p, v_p, ki_p, hr_p, hi_p, k_s, v_s, ki_s, hr_s, hi_s)
```

```python
import math
from contextlib import ExitStack, contextmanager
import numpy as np
import concourse.bass as bass
import concourse.mybir as mybir
from concourse.bass_utils import run_bass_kernel_spmd

F32 = mybir.dt.float32
BF16 = mybir.dt.bfloat16
I32 = mybir.dt.int32
AF = mybir.ActivationFunctionType
ALU = mybir.AluOpType
AX = mybir.AxisListType
NCORES = 8
EPOCH = 30000
PER = EPOCH // 16


class Prog:
    ENG = ("pe", "act", "dve", "pool", "sp")

    def __init__(self, nc, es):
        self.nc = nc
        self.es = es
        self.cnt = {e: 0 for e in self.ENG}
        self.lanes = {}
        self.csem = {e: [] for e in self.ENG}
        self.lsem = {}
        self.lane_inc = {}
        self.waited = {e: {} for e in self.ENG}
        self.nuniq = 0
        self.st = None
        self.ops = None
        self.barrier = []
        self.seen = set()

    @contextmanager
    def stage(self, name=""):
        with ExitStack() as st:
            self.st = st
            self.ops = {e: [] for e in self.ENG}
            self.seen = set()
            self.stage_lane_map = {}
            yield self
            self._emit()
            bar = []
            for e in self.ENG:
                if self.cnt[e] > 0:
                    bar.append(("c", e, self.cnt[e] - 1))
            for ln, (ep, val) in self.lanes.items():
                if val > 0:
                    bar.append(("d", ln, ep, val))
            self.barrier = bar
            self.st = None

    def sb(self, shape, dt, name=None):
        self.nuniq += 1
        return self.st.enter_context(self.nc.sbuf_tensor(name or f"sb{self.nuniq}", list(shape), dt))

    def ps(self, shape, dt=F32, name=None):
        self.nuniq += 1
        return self.st.enter_context(self.nc.psum_tensor(name or f"ps{self.nuniq}", list(shape), dt))

    def _deps(self, eng, deps):
        deps = [d for d in deps if d is not None]
        if eng not in self.seen:
            self.seen.add(eng)
            deps = list(self.barrier) + deps
        return tuple(deps)

    def op(self, eng, fn, deps=(), inc=True):
        deps = self._deps(eng, deps)
        tok = None
        if inc:
            tok = ("c", eng, self.cnt[eng])
            self.cnt[eng] += 1
        self.ops[eng].append((fn, deps, tok, 1))
        return tok

    def dma(self, eng, lane, out, in_, deps=()):
        return self.lane_op(eng, lane, lambda e: e.dma_start(out=out, in_=in_), deps)

    def lane_op(self, eng, lane, fn, deps=(), incv=16):
        deps = self._deps(eng, deps)
        if incv == 16:
            lane = "L%d" % self.stage_lane_map.setdefault(lane, len(self.stage_lane_map))
        ep, val = self.lanes.get(lane, (0, 0))
        if val + incv > EPOCH:
            ep, val = ep + 1, 0
        val += incv
        self.lanes[lane] = (ep, val)
        tok = ("d", lane, ep, val)
        self.ops[eng].append((fn, deps, tok, incv))
        return tok

    def _semval(self, tok):
        if tok[0] == "c":
            _, e, i = tok
            ep = i // EPOCH
            while len(self.csem[e]) <= ep:
                self.csem[e].append(self.es.enter_context(self.nc.semaphore(f"s_{e}{len(self.csem[e])}")))
            return self.csem[e][ep], (i % EPOCH) + 1
        _, ln, ep, val = tok
        lst = self.lsem.setdefault(ln, [])
        while len(lst) <= ep:
            lst.append(self.es.enter_context(self.nc.semaphore(f"l_{ln}_{len(lst)}")))
        return lst[ep], val

    def _emit(self):
        nc = self.nc
        with nc.Block() as block:
            def make(ename):
                def body(eng):
                    waited = self.waited[ename]
                    for fn, deps, tok, incv in self.ops[ename]:
                        for d in deps:
                            s, v = self._semval(d)
                            key = id(s)
                            if waited.get(key, 0) >= v:
                                continue
                            waited[key] = v
                            eng.wait_ge(s, v)
                        if fn is None:
                            continue
                        ins = fn(eng)
                        if tok is not None:
                            s, v = self._semval(tok)
                            ins.then_inc(s, incv if tok[0] == "d" else 1)
                return body
            block.tensor(make("pe"))
            block.scalar(make("act"))
            block.vector(make("dve"))
            block.gpsimd(make("pool"))
            block.sync(make("sp"))


class Ring:
    def __init__(self, bufs):
        self.bufs = bufs
        self.i = 0
        self.readers = [[] for _ in bufs]

    def next(self):
        k = self.i % len(self.bufs)
        self.i += 1
        deps = self.readers[k]
        self.readers[k] = []
        return k, self.bufs[k], deps

    def done(self, k, *toks):
        self.readers[k].extend(t for t in toks if t is not None)
def emit_linear(p, RT, D, N, pro, epi, x, w, y, ident_d, x2=None, nw=None, e1=None, e2=None, bv=None, eps=1e-6):
    R = RT * 128
    KC = D // 128
    NT = (N + 511) // 512
    ident_f = p.sb([128, 128], F32)
    ident = p.sb([128, 128], BF16)
    XT = p.sb([128, KC, R], BF16)
    xbufs = Ring([p.sb([128, D], F32) for _ in range(2)])
    x2bufs = Ring([p.sb([128, D], F32) for _ in range(2)]) if pro == "gate" else None
    tmpf = Ring([p.sb([128, D], F32) for _ in range(2)]) if pro in ("gate", "gelu", "rms") else None
    xpb = Ring([p.sb([128, D], BF16) for _ in range(2)])
    nwb = p.sb([128, D], F32) if pro == "rms" else None
    stat = Ring([p.sb([128, 2], F32) for _ in range(2)]) if pro == "rms" else None
    pT = Ring([p.ps([128, 4, 128], BF16) for _ in range(2)])
    wb = Ring([p.sb([128, KC, 512], BF16) for _ in range(2)])
    acc = Ring([p.ps([128, 512], F32) for _ in range(3)])
    ob = Ring([p.sb([128, 512], F32) for _ in range(3)])
    e1b = Ring([p.sb([128, 512], F32) for _ in range(2)]) if e1 is not None else None
    e2b = Ring([p.sb([128, 512], F32) for _ in range(2)]) if e2 is not None else None
    bb = Ring([p.sb([128, 512], F32) for _ in range(2)]) if bv is not None else None
    tb = Ring([p.sb([128, 512], F32) for _ in range(2)]) if epi in ("gres", "glu") else None
    GC = 2.0 * math.sqrt(2.0 / math.pi)

    t_id = p.dma("sp", "ident", ident_f[:], ident_d)
    t_ident = p.op("dve", lambda e: e.tensor_copy(out=ident[:], in_=ident_f[:]), [t_id])
    t_nw = p.dma("sp", "nw", nwb[:], nw.partition_broadcast(128)) if pro == "rms" else None
    xt_toks = []
    for r in range(RT):
        rows = slice(r * 128, (r + 1) * 128)
        kx, xb, dx = xbufs.next()
        t_x = p.dma("sp", f"x{kx}", xb[:], x[rows, :], dx)
        kp, xp, dxp = xpb.next()
        if pro == "rms":
            kt, tf, dt_ = tmpf.next()
            ks, st, ds = stat.next()
            t_sq = p.op("act", lambda e, tf=tf, xb=xb, st=st: e.activation(
                out=tf[:], in_=xb[:], func=AF.Square, accum_out=st[:, 0:1]), [t_x] + dt_ + ds)
            t_r1 = p.op("dve", lambda e, st=st: e.tensor_scalar(
                out=st[:, 1:2], in0=st[:, 0:1], scalar1=1.0 / D, scalar2=eps, op0=ALU.mult, op1=ALU.add), [t_sq])
            t_r1b = p.op("act", lambda e, st=st: e.activation(out=st[:, 1:2], in_=st[:, 1:2], func=AF.Sqrt), [t_r1])
            t_r2 = p.op("dve", lambda e, st=st: e.reciprocal(out=st[:, 1:2], in_=st[:, 1:2]), [t_r1b])
            t_xp = p.op("dve", lambda e, xp=xp, xb=xb, st=st: e.scalar_tensor_tensor(
                out=xp[:], in0=xb[:], scalar=st[:, 1:2], in1=nwb[:], op0=ALU.mult, op1=ALU.mult),
                [t_r2, t_nw] + dxp)
            tmpf.done(kt, t_sq); stat.done(ks, t_xp); xbufs.done(kx, t_xp)
        elif pro == "gate":
            k2, x2b, d2 = x2bufs.next()
            t_x2 = p.dma("act", f"x2{k2}", x2b[:], x2[rows, :], d2)
            kt, tf, dt_ = tmpf.next()
            t_s = p.op("act", lambda e, tf=tf, x2b=x2b: e.activation(out=tf[:], in_=x2b[:], func=AF.Silu), [t_x2] + dt_)
            t_xp = p.op("dve", lambda e, xp=xp, xb=xb, tf=tf: e.tensor_tensor(
                out=xp[:], in0=xb[:], in1=tf[:], op=ALU.mult), [t_x, t_s] + dxp)
            x2bufs.done(k2, t_s); tmpf.done(kt, t_xp); xbufs.done(kx, t_xp)
        elif pro == "gelu":
            kt, tf, dt_ = tmpf.next()
            t_a = p.op("dve", lambda e, tf=tf, xb=xb: e.tensor_tensor(out=tf[:], in0=xb[:], in1=xb[:], op=ALU.mult), [t_x] + dt_)
            t_b = p.op("dve", lambda e, tf=tf: e.tensor_scalar(
                out=tf[:], in0=tf[:], scalar1=0.044715, scalar2=1.0, op0=ALU.mult, op1=ALU.add), [t_a])
            t_c = p.op("dve", lambda e, tf=tf, xb=xb: e.tensor_tensor(out=tf[:], in0=tf[:], in1=xb[:], op=ALU.mult), [t_b])
            t_d = p.op("act", lambda e, tf=tf: e.activation(out=tf[:], in_=tf[:], func=AF.Sigmoid, scale=GC), [t_c])
            t_xp = p.op("dve", lambda e, xp=xp, xb=xb, tf=tf: e.tensor_tensor(
                out=xp[:], in0=xb[:], in1=tf[:], op=ALU.mult), [t_d] + dxp)
            tmpf.done(kt, t_xp); xbufs.done(kx, t_xp)
        else:
            t_xp = p.op("dve", lambda e, xp=xp, xb=xb: e.tensor_copy(out=xp[:], in_=xb[:]), [t_x] + dxp)
            xbufs.done(kx, t_xp)
        last = []
        for k0 in range(0, KC, 4):
            nk = min(4, KC - k0)
            kq, pt, dq = pT.next()
            tt = None
            for j in range(nk):
                tt = p.op("pe", lambda e, pt=pt, xp=xp, j=j, k0=k0: e.transpose(
                    out=pt[:, j, :], in_=xp[:, (k0 + j) * 128:(k0 + j + 1) * 128], identity=ident[:]),
                    [t_xp, t_ident] + dq, inc=(j == nk - 1))
            if (k0 // 4) % 2 == 0:
                t_cp = p.op("act", lambda e, pt=pt, k0=k0, nk=nk, rows=rows: e.copy(
                    out=XT[:, k0:k0 + nk, rows], in_=pt[:, 0:nk, :]), [tt])
            else:
                t_cp = p.op("dve", lambda e, pt=pt, k0=k0, nk=nk, rows=rows: e.tensor_copy(
                    out=XT[:, k0:k0 + nk, rows], in_=pt[:, 0:nk, :]), [tt])
            pT.done(kq, t_cp)
            last.append(t_cp)
            xpb.done(kp, tt)
        xt_toks.append(last)
    wv = w.rearrange("(k p) n -> p k n", p=128)
    fin = []
    for n in range(NT):
        n0 = n * 512
        ns = min(512, N - n0)
        kw, wt, dw = wb.next()
        t_w = p.dma("pool", f"w{kw}", wt[:, :, 0:ns], wv[:, :, n0:n0 + ns], dw)
        t_bb = None
        if bv is not None:
            kb, bt, db = bb.next()
            t_bb = p.dma("act", f"b{kb}", bt[:, 0:ns], bv[0:1, n0:n0 + ns].partition_broadcast(128), db)
        mm_last = []
        for r in range(RT):
            rows = slice(r * 128, (r + 1) * 128)
            ka, ac, da = acc.next()
            tm = None
            for k in range(KC):
                tm = p.op("pe", lambda e, ac=ac, wt=wt, k=k, rows=rows, ns=ns: e.matmul(
                    ac[:, 0:ns], lhsT=XT[:, k, rows], rhs=wt[:, k, 0:ns], start=(k == 0), stop=(k == KC - 1)),
                    ([t_w] + xt_toks[r] + da) if k == 0 else [], inc=(k == KC - 1))
            mm_last.append(tm)
            ko, ot, do = ob.next()
            if epi == "none":
                t_o = p.op("act", lambda e, ot=ot, ac=ac, ns=ns: e.copy(out=ot[:, 0:ns], in_=ac[:, 0:ns]), [tm] + do)
                acc.done(ka, t_o)
            elif epi == "sigmoid":
                t_o = p.op("act", lambda e, ot=ot, ac=ac, ns=ns: e.activation(
                    out=ot[:, 0:ns], in_=ac[:, 0:ns], func=AF.Sigmoid), [tm] + do)
                acc.done(ka, t_o)
            elif epi == "res":
                k1, et, d1 = e1b.next()
                t_e = p.dma("sp", f"e1{k1}", et[:, 0:ns], e1[rows, n0:n0 + ns], d1)
                t_o = p.op("dve", lambda e, ot=ot, ac=ac, et=et, ns=ns: e.tensor_tensor(
                    out=ot[:, 0:ns], in0=ac[:, 0:ns], in1=et[:, 0:ns], op=ALU.add), [tm, t_e] + do)
                acc.done(ka, t_o); e1b.done(k1, t_o)
            elif epi == "gres":
                k1, et, d1 = e1b.next()
                t_e = p.dma("sp", f"e1{k1}", et[:, 0:ns], e1[rows, n0:n0 + ns], d1)
                k2, et2, d2 = e2b.next()
                t_e2 = p.dma("sp", f"e2{k2}", et2[:, 0:ns], e2[rows, n0:n0 + ns], d2)
                kt, tt_, dtt = tb.next()
                t_m = p.op("dve", lambda e, tt_=tt_, ac=ac, et2=et2, ns=ns: e.tensor_tensor(
                    out=tt_[:, 0:ns], in0=ac[:, 0:ns], in1=et2[:, 0:ns], op=ALU.mult), [tm, t_e2] + dtt)
                t_o = p.op("dve", lambda e, ot=ot, tt_=tt_, et=et, ns=ns: e.tensor_tensor(
                    out=ot[:, 0:ns], in0=tt_[:, 0:ns], in1=et[:, 0:ns], op=ALU.add), [t_m, t_e] + do)
                acc.done(ka, t_m); e1b.done(k1, t_o); e2b.done(k2, t_m); tb.done(kt, t_o)
            elif epi == "glu":
                k1, et, d1 = e1b.next()
                t_e = p.dma("sp", f"e1{k1}", et[:, 0:ns], e1[rows, n0:n0 + ns], d1)
                kt, tt_, dtt = tb.next()
                t_a = p.op("dve", lambda e, tt_=tt_, et=et, ns=ns: e.tensor_tensor(
                    out=tt_[:, 0:ns], in0=et[:, 0:ns], in1=et[:, 0:ns], op=ALU.mult), [t_e] + dtt)
                t_b = p.op("dve", lambda e, tt_=tt_, ns=ns: e.tensor_scalar(
                    out=tt_[:, 0:ns], in0=tt_[:, 0:ns], scalar1=0.044715, scalar2=1.0, op0=ALU.mult, op1=ALU.add), [t_a])
                t_c = p.op("dve", lambda e, tt_=tt_, et=et, ns=ns: e.tensor_tensor(
                    out=tt_[:, 0:ns], in0=tt_[:, 0:ns], in1=et[:, 0:ns], op=ALU.mult), [t_b])
                t_d = p.op("act", lambda e, tt_=tt_, ns=ns: e.activation(
                    out=tt_[:, 0:ns], in_=tt_[:, 0:ns], func=AF.Sigmoid, scale=GC), [t_c])
                t_g = p.op("dve", lambda e, tt_=tt_, et=et, ns=ns: e.tensor_tensor(
                    out=tt_[:, 0:ns], in0=tt_[:, 0:ns], in1=et[:, 0:ns], op=ALU.mult), [t_d])
                t_s1 = p.op("dve", lambda e, ot=ot, ac=ac, bt=bt, ns=ns: e.tensor_tensor(
                    out=ot[:, 0:ns], in0=ac[:, 0:ns], in1=bt[:, 0:ns], op=ALU.add), [tm, t_bb] + do)
                t_s2 = p.op("act", lambda e, ot=ot, ns=ns: e.activation(
                    out=ot[:, 0:ns], in_=ot[:, 0:ns], func=AF.Sigmoid), [t_s1])
                t_o = p.op("dve", lambda e, ot=ot, tt_=tt_, ns=ns: e.tensor_tensor(
                    out=ot[:, 0:ns], in0=ot[:, 0:ns], in1=tt_[:, 0:ns], op=ALU.mult), [t_s2, t_g])
                acc.done(ka, t_s1); e1b.done(k1, t_g); tb.done(kt, t_o)
            else:
                raise ValueError(epi)
            t_st = p.dma("act", f"o{ko}", y[rows, n0:n0 + ns], ot[:, 0:ns], [t_o])
            ob.done(ko, t_st)
            fin.append(t_st)
        wb.done(kw, *mm_last)
        if bv is not None:
            bb.done(kb, *mm_last)
    return fin[-3:]


def emit_transpose(p, src, dst, R, C, ident_d):
    identf = p.sb([128, 128], F32)
    t_id = p.dma("sp", "ident", identf[:], ident_d)
    inb = Ring([p.sb([128, 512], F32) for _ in range(3)])
    pst = Ring([p.ps([128, 512], F32) for _ in range(3)])
    outb = Ring([p.sb([128, 512], F32) for _ in range(3)])
    fin = []
    for r0 in range(0, R, 512):
        nr = min(512, R - r0)
        nrt = nr // 128
        for c0 in range(0, C, 128):
            cw = min(128, C - c0)
            ki, it, di = inb.next()
            t_in = p.dma("sp", f"ti{ki}", it[:, 0:nrt * 128].rearrange("p (q c) -> p q c", c=128)[:, :, 0:cw],
                         src[r0:r0 + nr, c0:c0 + cw].rearrange("(q p) c -> p q c", p=128), di)
            kp, pt, dp = pst.next()
            tt = None
            for q in range(nrt):
                tt = p.op("pe", lambda e, pt=pt, it=it, q=q, cw=cw: e.transpose(
                    out=pt[0:cw, q * 128:(q + 1) * 128], in_=it[:, q * 128:q * 128 + cw], identity=identf[:]),
                    [t_in, t_id] + dp, inc=(q == nrt - 1))
            inb.done(ki, tt)
            ko, ot, do = outb.next()
            t_cp = p.op("act" if (c0 // 128) % 2 == 0 else "dve",
                        (lambda e, ot=ot, pt=pt, cw=cw, nr=nr: e.copy(out=ot[0:cw, 0:nr], in_=pt[0:cw, 0:nr]))
                        if (c0 // 128) % 2 == 0 else
                        (lambda e, ot=ot, pt=pt, cw=cw, nr=nr: e.tensor_copy(out=ot[0:cw, 0:nr], in_=pt[0:cw, 0:nr])),
                        [tt] + do)
            pst.done(kp, t_cp)
            t_st = p.dma("act", f"to{ko}", dst[c0:c0 + cw, r0:r0 + nr], ot[0:cw, 0:nr], [t_cp])
            outb.done(ko, t_st)
            fin.append(t_st)
    return fin[-3:]
NEG = -1.0e30
TOPK = 256
BIS_LO, BIS_HI, BIS_IT = -8192.0, 8192.0, 28


def emit_attn(p, NJ, NT, smax, src, vt_d, D_d, cb_d, C0_d, oT_dst):
    HN = 16 * NT
    HPM = min(16, 512 // NT)
    NMM = 16 // HPM
    SMX = max(smax)
    NU = sum(smax)
    u0s = [sum(smax[:j]) for j in range(NJ)]
    scale = 128.0 ** -0.5
    n4 = 4 * NT
    ones = p.sb([128, 128], BF16)
    vt = p.sb([128, NU], F32)
    Dt = p.sb([128, 2, HN], F32)
    cb = p.sb([128, 16], F32)
    C0 = p.sb([128, NT], F32)
    qT = Ring([p.sb([128, HN], BF16) for _ in range(2)])
    qiT = Ring([p.sb([64, HN], BF16) for _ in range(2)])
    wb = Ring([p.sb([128, HN], F32) for _ in range(2)])
    SC = p.sb([128, SMX, NT], F32)
    CM = p.sb([128, SMX, NT], BF16)
    MKs = [p.sb([128, SMX, NT], BF16) for _ in range(2 if NJ > 1 else 1)]
    kib = Ring([p.sb([128, 128], BF16) for _ in range(3)])
    rb = Ring([p.sb([128, HN], F32) for _ in range(2)])
    ktb = Ring([p.sb([128, 128], BF16) for _ in range(3)])
    vb = Ring([p.sb([128, 128], BF16) for _ in range(3)])
    banks = Ring([p.ps([128, 512], F32) for _ in range(4)])
    ps_cnt = p.ps([128, 512], F32)
    ps_o = p.ps([128, 512], F32)
    ps_d = p.ps([128, 512], F32)
    lo = p.sb([128, NT], F32)
    mid = p.sb([128, NT], F32)
    ge = p.sb([128, NT], F32)
    cntb = p.sb([128, NT], F32)
    GRP = 128 // NT
    lgt = Ring([p.sb([128, 4 * NT], F32) for _ in range(2)])
    pb = Ring([p.sb([128, 4 * NT], BF16) for _ in range(2)])
    pmb = Ring([p.sb([128, 4 * NT], BF16) for _ in range(2)])
    rdb = p.sb([128, 4 * NT], F32)
    ob = Ring([p.sb([128, 4 * NT], F32) for _ in range(2)])

    t_ones = p.op("dve", lambda e: e.memset(ones[:], 1.0))
    t_vt = p.dma("sp", "vt", vt[:], vt_d.partition_broadcast(128))
    t_D = p.dma("sp", "D", Dt[:], D_d.rearrange("a p n -> p a n"))
    t_cb = p.dma("sp", "cb", cb[:], cb_d.partition_broadcast(128))
    t_C0 = p.dma("sp", "C0", C0[:], C0_d)
    st = {"acc_free": [], "sc_free": [], "out": []}

    def phase_a(j):
        S = smax[j]
        kq, qt, dq = qT.next()
        t_q = src.load_q(j, qt, dq)
        kqi, qit, dqi = qiT.next()
        t_qi = src.load_qi(j, qit, dqi)
        kw, wt, dw_ = wb.next()
        t_w = src.load_w(j, wt, dw_)
        sc_toks = []
        mm_toks, t_rs = [], []
        for dl in range(S):
            kk, kit, dk = kib.next()
            t_ki = src.load_ki(j, dl, kit, dk)
            kr, rt, dr = rb.next()
            t_rs = []
            mm_toks = []
            for m in range(NMM):
                kb_, bk, dbk = banks.next()
                n_ = HPM * NT
                t_mm = p.op("pe", lambda e, bk=bk, kit=kit, qit=qit, m=m, n_=n_: e.matmul(
                    bk[:, 0:n_], lhsT=kit[0:64, :], rhs=qit[:, m * n_:(m + 1) * n_], start=True, stop=True),
                    [t_ki, t_qi] + dbk)
                t_r = p.op("dve", lambda e, bk=bk, rt=rt, wt=wt, m=m, n_=n_: e.scalar_tensor_tensor(
                    out=rt[:, m * n_:(m + 1) * n_], in0=bk[:, 0:n_], scalar=0.0, in1=wt[:, m * n_:(m + 1) * n_],
                    op0=ALU.max, op1=ALU.mult), [t_mm, t_w] + (dr if m == 0 else []))
                banks.done(kb_, t_r)
                t_rs.append(t_r)
                mm_toks.append(t_mm)
            kib.done(kk, *mm_toks)
            t_red = p.op("dve", lambda e, rt=rt, dl=dl: e.tensor_reduce(
                out=SC[:, dl, :], in_=rt[:, :].rearrange("p (h t) -> p t h", h=16), axis=AX.X, op=ALU.add),
                t_rs + (st["sc_free"] if dl == 0 else []))
            rb.done(kr, t_red)
            u = u0s[j] + dl
            t_v = p.op("dve", lambda e, dl=dl, u=u: e.tensor_scalar(
                out=SC[:, dl, :], in0=SC[:, dl, :], scalar1=vt[:, u:u + 1], scalar2=None, op0=ALU.add), [t_red, t_vt])
            if dl == 0:
                t_v = p.op("dve", lambda e: e.tensor_tensor(
                    out=SC[:, 0, :], in0=SC[:, 0, :], in1=C0[:, :], op=ALU.add), [t_v, t_C0])
            sc_toks.append(t_v)
        qiT.done(kqi, *mm_toks)
        wb.done(kw, *t_rs)
        return kq, qt, t_q, sc_toks

    def gen_bisect(S, sc_toks, res):
        t_m = p.op("dve", lambda e: e.memset(mid[:], 0.5 * (BIS_LO + BIS_HI)))
        t_prev = [t_m]
        t_cm_free = []
        for it in range(BIS_IT):
            h = (BIS_HI - BIS_LO) / 2.0 ** (it + 1)
            t_cmp = p.op("dve", lambda e, S=S: e.tensor_tensor(
                out=CM[:, 0:S, :], in0=SC[:, 0:S, :], in1=mid[:, :].unsqueeze(1).to_broadcast([128, S, NT]),
                op=ALU.is_ge), t_prev + sc_toks + t_cm_free)
            t_c = None
            if GRP > 1 and S >= GRP:
                for c0 in range(0, S, GRP):
                    n = min(GRP, S - c0)
                    t_c = p.op("pe", lambda e, c0=c0, n=n, S=S: e.matmul(
                        ps_cnt[:, 0:n * NT], lhsT=ones[:, :], rhs=CM[:, c0:c0 + n, :].rearrange("p s t -> p (s t)"),
                        start=(c0 == 0), stop=(c0 + n == S)),
                        ([t_cmp, t_ones] + t_prev) if c0 == 0 else [], inc=(c0 + n == S))
                t_c = p.op("dve", lambda e: e.tensor_reduce(
                    out=cntb[:], in_=ps_cnt[:, 0:GRP * NT].rearrange("p (s t) -> p t s", t=NT), axis=AX.X, op=ALU.add), [t_c])
                cnt_src = cntb[:]
            else:
                for dl in range(S):
                    t_c = p.op("pe", lambda e, dl=dl, S=S: e.matmul(
                        ps_cnt[:, 0:NT], lhsT=ones[:, :], rhs=CM[:, dl, :], start=(dl == 0), stop=(dl == S - 1)),
                        ([t_cmp, t_ones] + t_prev) if dl == 0 else [], inc=(dl == S - 1))
                cnt_src = ps_cnt[:, 0:NT]
            t_ge = p.op("dve", lambda e, h=h, cnt_src=cnt_src: e.tensor_scalar(
                out=ge[:], in0=cnt_src, scalar1=float(TOPK) - 0.5, scalar2=h, op0=ALU.is_ge, op1=ALU.mult), [t_c])
            t_md = p.op("dve", lambda e, h=h: e.scalar_tensor_tensor(
                out=mid[:], in0=ge[:], scalar=-0.5 * h, in1=mid[:], op0=ALU.add, op1=ALU.add), [t_ge])
            t_prev = [t_md]
            t_cm_free = [t_c]
            yield None
        hf = (BIS_HI - BIS_LO) / 2.0 ** (BIS_IT + 1)
        res["t_lo"] = p.op("dve", lambda e: e.tensor_scalar(
            out=lo[:], in0=mid[:], scalar1=-hf, scalar2=None, op0=ALU.add), t_prev)

    def gen_c(j, S, kq, qt, t_q, MK, t_mk):
        last = {}
        for g in range(4):
            pend = None
            for dl in range(S):
                kk, ktt, dkt = ktb.next()
                t_kt = src.load_k(j, g, dl, ktt, dkt)
                kv, vtl, dv = vb.next()
                t_vv = src.load_v(j, g, dl, vtl, dv)
                kb_, bk, dbk = banks.next()
                t_s = p.op("pe", lambda e, bk=bk, ktt=ktt, g=g: e.matmul(
                    bk[:, 0:n4], lhsT=ktt[:, 0:128], rhs=qt[:, g * n4:(g + 1) * n4], start=True, stop=True),
                    [t_kt, t_q] + dbk)
                ktb.done(kk, t_s)
                kp_, pt, dp = pb.next()
                if dl >= 2:
                    t_e = None
                    for r in range(4):
                        h = 4 * g + r
                        t_e = p.op("act", lambda e, pt=pt, bk=bk, r=r, h=h: e.activation(
                            out=pt[:, r * NT:(r + 1) * NT], in_=bk[:, r * NT:(r + 1) * NT], func=AF.Exp,
                            bias=cb[:, h:h + 1], scale=scale), [t_s, t_cb] + (dp if r == 0 else []))
                    banks.done(kb_, t_e)
                else:
                    kl, lt, dlg = lgt.next()
                    t_l = p.op("dve", lambda e, lt=lt, bk=bk, dl=dl, g=g: e.scalar_tensor_tensor(
                        out=lt[:, 0:n4], in0=bk[:, 0:n4], scalar=scale, in1=Dt[:, dl, g * n4:(g + 1) * n4],
                        op0=ALU.mult, op1=ALU.add), [t_s, t_D] + dlg)
                    banks.done(kb_, t_l)
                    t_e = p.op("act", lambda e, pt=pt, lt=lt: e.activation(
                        out=pt[:, 0:n4], in_=lt[:, 0:n4], func=AF.Exp), [t_l] + dp)
                    lgt.done(kl, t_e)
                km, pmt, dpm = pmb.next()
                t_pm = p.op("dve", lambda e, pmt=pmt, pt=pt, dl=dl: e.tensor_tensor(
                    out=pmt[:, :].rearrange("p (r t) -> p r t", r=4),
                    in0=pt[:, :].rearrange("p (r t) -> p r t", r=4),
                    in1=MK[:, dl, :].unsqueeze(1).to_broadcast([128, 4, NT]), op=ALU.mult), [t_e, t_mk] + dpm)
                pb.done(kp_, t_pm)
                last["pm"] = t_pm
                if pend is not None:
                    pend()

                def stage2(vtl=vtl, pmt=pmt, kv=kv, km=km, t_vv=t_vv, t_pm=t_pm, first=(dl == 0), lastf=(dl == S - 1)):
                    t_o_ = p.op("pe", lambda e: e.matmul(
                        ps_o[:, 0:n4], lhsT=vtl[:, 0:128], rhs=pmt[:, 0:n4], start=first, stop=lastf),
                        [t_vv, t_pm] + (st["acc_free"] if first else []))
                    t_d_ = p.op("pe", lambda e: e.matmul(
                        ps_d[:, 0:n4], lhsT=ones[:, :], rhs=pmt[:, 0:n4], start=first, stop=lastf), [t_pm])
                    vb.done(kv, t_o_)
                    pmb.done(km, t_d_)
                    last["o"], last["d"] = t_o_, t_d_
                pend = stage2
                yield None
            pend()
            t_rd = p.op("dve", lambda e: e.reciprocal(out=rdb[:], in_=ps_d[:, 0:n4]), [last["d"]] + st["acc_free"])
            ko, ot, do = ob.next()
            t_o = p.op("dve", lambda e, ot=ot: e.tensor_tensor(
                out=ot[:], in0=ps_o[:, 0:n4], in1=rdb[:], op=ALU.mult), [t_rd, last["o"]] + do)
            st["acc_free"] = [t_o]
            t_st = p.dma("act", f"o{ko}", oT_dst(j, g), ot[:, :].rearrange("p (r t) -> p r t", r=4), [t_o])
            ob.done(ko, t_st)
            st["out"].append(t_st)
        qT.done(kq, last["o"])

    order = list(range(NJ))[::-1]
    pend_c = None
    for idx, j in enumerate(order):
        S = smax[j]
        kq, qt, t_q, sc_toks = phase_a(j)
        res = {}
        per = 0
        if pend_c is not None:
            per = -(-(4 * smax[order[idx - 1]]) // BIS_IT)
        for _ in gen_bisect(S, sc_toks, res):
            for _u in range(per):
                if pend_c is not None and next(pend_c, "END") == "END":
                    pend_c = None
        if pend_c is not None:
            for _ in pend_c:
                pass
        MK = MKs[idx % len(MKs)]
        t_mk = p.op("dve", lambda e, S=S, MK=MK: e.tensor_tensor(
            out=MK[:, 0:S, :], in0=SC[:, 0:S, :], in1=lo[:, :].unsqueeze(1).to_broadcast([128, S, NT]),
            op=ALU.is_ge), [res["t_lo"]])
        st["sc_free"] = [t_mk]
        pend_c = gen_c(j, S, kq, qt, t_q, MK, t_mk)
    for _ in pend_c:
        pass
    return st["out"][-2:]
TWO_PI = 2.0 * math.pi


def _sincos(p, out_t, x_t, tmp_t, deps, cos, ki_t, kf_t):
    off = (math.pi / 2) if cos else 0.0
    V = lambda f, d: p.op("dve", f, d)
    t1 = V(lambda e: e.tensor_scalar(out=tmp_t, in0=x_t, scalar1=off, scalar2=1.0 / TWO_PI, op0=ALU.add, op1=ALU.mult), deps)
    t2 = V(lambda e: e.tensor_copy(out=ki_t, in_=tmp_t), [t1])
    t3 = V(lambda e: e.tensor_copy(out=kf_t, in_=ki_t), [t2])
    t4 = V(lambda e: e.tensor_scalar(out=tmp_t, in0=x_t, scalar1=off, scalar2=None, op0=ALU.add), [t3])
    t5 = V(lambda e: e.scalar_tensor_tensor(out=tmp_t, in0=kf_t, scalar=-TWO_PI, in1=tmp_t, op0=ALU.mult, op1=ALU.add), [t4])
    t6 = V(lambda e: e.tensor_scalar(out=kf_t, in0=tmp_t, scalar1=math.pi, scalar2=-TWO_PI, op0=ALU.is_gt, op1=ALU.mult), [t5])
    t7 = V(lambda e: e.tensor_tensor(out=tmp_t, in0=tmp_t, in1=kf_t, op=ALU.add), [t6])
    t8 = V(lambda e: e.tensor_scalar(out=kf_t, in0=tmp_t, scalar1=-math.pi, scalar2=TWO_PI, op0=ALU.is_lt, op1=ALU.mult), [t7])
    t9 = V(lambda e: e.tensor_tensor(out=tmp_t, in0=tmp_t, in1=kf_t, op=ALU.add), [t8])
    return p.op("act", lambda e: e.activation(out=out_t, in_=tmp_t, func=AF.Sin), [t9])


def emit_scan(p, seqs, I, gU_rows, uidx_d, pubY, HF_d, ident_d, pre=()):
    V = lambda f, deps=(): p.op("dve", f, list(deps))
    cst = p.sb([128, 4], F32); tvb = p.sb([128, 129], F32); tri = p.sb([128, 128], BF16)
    identf = p.sb([128, 128], F32)
    lmb = p.sb([128, 1024], F32); anb = p.sb([128, 1024], F32); dtb = p.sb([128, 1024], F32)
    lmT = p.sb([128, 16], F32); anT = p.sb([128, 16], F32); dtT = p.sb([128, 16], F32)
    BDA = p.sb([128, 2, 1024], BF16); BDB = p.sb([128, 2, 1024], BF16)
    CcP = p.sb([128, 2, 1024], BF16); CcPf = p.sb([128, 2, 1024], F32)
    Ccc = p.sb([128, 256], F32); dv = p.sb([128, 2], F32)
    Pa = p.sb([128, 16, 128], F32); Pb = p.sb([128, 16, 128], F32)
    Qa = p.sb([128, 16, 129], F32); Qb = p.sb([128, 16, 129], F32)
    Qa16 = p.sb([128, 16, 129], BF16); Qb16 = p.sb([128, 16, 129], BF16)
    q1 = p.sb([128, 16 * 129], F32); q2 = p.sb([128, 16 * 129], F32); q3 = p.sb([128, 16 * 129], F32)
    q4 = p.sb([128, 16 * 129], F32); qi_ = p.sb([128, 16 * 129], I32); kfq = p.sb([128, 16 * 129], F32)
    mask = p.sb([128, 8, 64], F32)
    s1 = q1[:, 0:1024]; s2 = q2[:, 0:1024]; s3 = q3[:, 0:1024]; s4 = q4[:, 0:1024]; si = qi_[:, 0:1024]; sk = kfq[:, 0:1024]
    U = p.sb([128, 2, 4, 1152], F32)
    uidx = p.sb([128, 8], I32)

    d_c = p.dma("sp", "c0", cst[:], I["cst"])
    d_tv = p.dma("sp", "c1", tvb[:], I["tvec"].partition_broadcast(128))
    d_tri = p.dma("pool", "c2", tri[:], I["tri"])
    d_id = p.dma("sp", "c3", identf[:], ident_d)
    d_lm = p.dma("sp", "c4", lmb[:], I["lre_row"].partition_broadcast(128))
    d_an = p.dma("sp", "c5", anb[:], I["lim_row"].partition_broadcast(128))
    d_dt = p.dma("sp", "c6", dtb[:], I["ldt_row"].partition_broadcast(128))
    d_lmT = p.dma("sp", "c7", lmT[:], I["lreT2"])
    d_anT = p.dma("sp", "c8", anT[:], I["limT2"])
    d_dtT = p.dma("sp", "c9", dtT[:], I["ldtT2"])
    d_ccp = p.dma("sp", "c10", CcPf[:], I["CcP"].rearrange("c p n -> p c n"))
    d_ccc = p.dma("sp", "c11", Ccc[:], I["Ccc"])
    d_dv = p.dma("sp", "c12", dv[:], I["dvec"])
    d_mk = p.dma("sp", "c13", mask[:], I["mask"])
    d_ui = p.dma("sp", "c14", uidx[:], uidx_d)
    u_toks = []
    for ck in range(2):
        for r in range(4):
            col = ck * 4 + r
            u_toks.append(p.lane_op("pool", f"ug{col}", lambda e, ck=ck, r=r, col=col: e.indirect_dma_start(
                out=U[:, ck, r, :], out_offset=None, in_=gU_rows,
                in_offset=bass.IndirectOffsetOnAxis(ap=uidx[:, col:col + 1], axis=0)), [d_ui] + list(pre)))
    def disc(lm, an, dt, deps):
        a = p.op("act", lambda e: e.activation(out=dt, in_=dt, func=AF.Exp), deps)
        b = V(lambda e: e.tensor_scalar(out=lm, in0=lm, scalar1=-1e-4, scalar2=None, op0=ALU.min), deps)
        c = V(lambda e: e.tensor_tensor(out=lm, in0=lm, in1=dt, op=ALU.mult), [a, b])
        d = V(lambda e: e.tensor_tensor(out=an, in0=an, in1=dt, op=ALU.mult), [c])
        return d
    t_row = disc(lmb[:], anb[:], dtb[:], [d_lm, d_an, d_dt])
    t_T = disc(lmT[:], anT[:], dtT[:], [d_lmT, d_anT, d_dtT])
    t_ccp = V(lambda e: e.tensor_scalar(out=CcP[:], in0=CcPf[:], scalar1=cst[:, 3:4], scalar2=None, op0=ALU.mult), [d_ccp, d_c])
    t_ccc = V(lambda e: e.tensor_scalar(out=Ccc[:], in0=Ccc[:], scalar1=cst[:, 3:4], scalar2=None, op0=ALU.mult), [d_ccc, d_c])
    G = [p.sb([128, 64], F32) for _ in range(16)]
    gi_i = p.sb([128, 64], I32)
    t_bd = []
    for ck in range(2):
        lre, lim, ldt, bre, bim, mag, cc, ss, ar1, aim, den, cre, cim, t1_, t2_, kf_ = [g[:] for g in G]
        dd = [p.dma("sp", f"g{n_}", t_, I[nm][ck], t_bd) for n_, (t_, nm) in enumerate(
            [(lre, "lre_gi"), (lim, "lim_gi"), (ldt, "ldt_gi"), (bre, "bre_gi"), (bim, "bim_gi")])]
        a = p.op("act", lambda e: e.activation(out=ldt, in_=ldt, func=AF.Exp), dd)
        b = V(lambda e: e.tensor_scalar(out=lre, in0=lre, scalar1=-1e-4, scalar2=None, op0=ALU.min), dd)
        c1 = V(lambda e: e.tensor_tensor(out=t1_, in0=lre, in1=ldt, op=ALU.mult), [a, b])
        c2 = V(lambda e: e.tensor_tensor(out=t2_, in0=lim, in1=ldt, op=ALU.mult), [c1])
        m = p.op("act", lambda e: e.activation(out=mag, in_=t1_, func=AF.Exp), [c1])
        ts = _sincos(p, ss, t2_, den, [c2, m], False, gi_i[:], kf_)
        tc = _sincos(p, cc, t2_, den, [ts], True, gi_i[:], kf_)
        x1 = V(lambda e: e.tensor_tensor(out=ar1, in0=mag, in1=cc, op=ALU.mult), [tc])
        x1 = V(lambda e: e.tensor_scalar(out=ar1, in0=ar1, scalar1=-1.0, scalar2=None, op0=ALU.add), [x1])
        x2 = V(lambda e: e.tensor_tensor(out=aim, in0=mag, in1=ss, op=ALU.mult), [x1])
        y1 = V(lambda e: e.tensor_tensor(out=den, in0=lre, in1=lre, op=ALU.mult), [x2])
        y2 = V(lambda e: e.tensor_tensor(out=t1_, in0=lim, in1=lim, op=ALU.mult), [y1])
        y3 = V(lambda e: e.tensor_tensor(out=den, in0=den, in1=t1_, op=ALU.add), [y2])
        y4 = V(lambda e: e.reciprocal(out=den, in_=den), [y3])
        z1 = V(lambda e: e.tensor_tensor(out=cre, in0=ar1, in1=lre, op=ALU.mult), [y4])
        z2 = V(lambda e: e.tensor_tensor(out=t1_, in0=aim, in1=lim, op=ALU.mult), [z1])
        z3 = V(lambda e: e.tensor_tensor(out=cre, in0=cre, in1=t1_, op=ALU.add), [z2])
        z4 = V(lambda e: e.tensor_tensor(out=cre, in0=cre, in1=den, op=ALU.mult), [z3])
        w1 = V(lambda e: e.tensor_tensor(out=cim, in0=aim, in1=lre, op=ALU.mult), [z4])
        w2 = V(lambda e: e.tensor_tensor(out=t1_, in0=ar1, in1=lim, op=ALU.mult), [w1])
        w3 = V(lambda e: e.tensor_tensor(out=cim, in0=cim, in1=t1_, op=ALU.subtract), [w2])
        w4 = V(lambda e: e.tensor_tensor(out=cim, in0=cim, in1=den, op=ALU.mult), [w3])
        q_1 = V(lambda e: e.tensor_tensor(out=t1_, in0=cre, in1=bre, op=ALU.mult), [w4])
        q_2 = V(lambda e: e.tensor_tensor(out=t2_, in0=cim, in1=bim, op=ALU.mult), [q_1])
        q_3 = V(lambda e: e.tensor_tensor(out=t1_, in0=t1_, in1=t2_, op=ALU.subtract), [q_2])
        q_4 = V(lambda e: e.tensor_tensor(out=t2_, in0=cre, in1=bim, op=ALU.mult), [q_3])
        q_5 = V(lambda e: e.tensor_tensor(out=mag, in0=cim, in1=bre, op=ALU.mult), [q_4])
        q_6 = V(lambda e: e.tensor_tensor(out=t2_, in0=t2_, in1=mag, op=ALU.add), [q_5])
        bcv = lambda t: t.unsqueeze(1).to_broadcast([128, 8, 64])
        bdv = lambda T_, ck=ck: T_[:, ck, :].rearrange("p (g n) -> p g n", n=128)
        r1 = V(lambda e, bdv=bdv: e.tensor_tensor(out=bdv(BDA)[:, :, 0:64], in0=mask[:], in1=bcv(t1_), op=ALU.mult), [q_6, d_mk])
        r2 = V(lambda e, bdv=bdv: e.tensor_tensor(out=bdv(BDA)[:, :, 64:128], in0=mask[:], in1=bcv(t2_), op=ALU.mult), [r1])
        r3 = V(lambda e, bdv=bdv: e.tensor_tensor(out=bdv(BDB)[:, :, 0:64], in0=mask[:], in1=bcv(t2_), op=ALU.mult), [r2])
        r4 = V(lambda e, bdv=bdv: e.tensor_tensor(out=bdv(BDB)[:, :, 64:128], in0=mask[:], in1=bcv(t1_), op=ALU.mult), [r3])
        t_bd = [r4]
    a1 = V(lambda e: e.tensor_scalar(out=s1, in0=lmb[:], scalar1=cst[:, 1:2], scalar2=None, op0=ALU.mult), [t_row, d_c])
    a2 = p.op("act", lambda e: e.activation(out=s1, in_=s1, func=AF.Exp), [a1])
    a3 = V(lambda e: e.tensor_scalar(out=s2, in0=anb[:], scalar1=cst[:, 0:1], scalar2=None, op0=ALU.mult), [t_row, d_c])
    a4 = _sincos(p, s3, s2, s4, [a3], True, si, sk)
    a5 = V(lambda e: e.tensor_tensor(out=s3, in0=s3, in1=s1, op=ALU.mult), [a4, a2])
    s1v = lambda t: t.rearrange("p (g n) -> p g n", n=64)
    a6 = V(lambda e: e.tensor_copy(out=Pa[:, :, 0:64], in_=s1v(s3)), [a5])
    a7 = V(lambda e: e.tensor_copy(out=Pa[:, :, 64:128], in_=s1v(s3)), [a6])
    a8 = _sincos(p, s3, s2, s4, [a7], False, si, sk)
    a9 = V(lambda e: e.tensor_tensor(out=s3, in0=s3, in1=s1, op=ALU.mult), [a8])
    a10 = V(lambda e: e.tensor_copy(out=Pb[:, :, 0:64], in_=s1v(s3)), [a9])
    a11 = V(lambda e: e.tensor_scalar(out=Pb[:, :, 64:128], in0=s1v(s3), scalar1=-1.0, scalar2=None, op0=ALU.mult), [a10])
    qv = lambda t: t[:, :].rearrange("p (g t) -> p g t", t=129)
    b1 = V(lambda e: e.tensor_tensor(out=qv(q1), in0=tvb[:, :].unsqueeze(1).to_broadcast([128, 16, 129]),
                                     in1=lmT[:, :].unsqueeze(2).to_broadcast([128, 16, 129]), op=ALU.mult), [d_tv, t_T, a11])
    b2 = p.op("act", lambda e: e.activation(out=q1[:], in_=q1[:], func=AF.Exp), [b1])
    b3 = V(lambda e: e.tensor_tensor(out=qv(q2), in0=tvb[:, :].unsqueeze(1).to_broadcast([128, 16, 129]),
                                     in1=anT[:, :].unsqueeze(2).to_broadcast([128, 16, 129]), op=ALU.mult), [d_tv, t_T])
    b4 = _sincos(p, q3[:], q2[:], q4[:], [b3], True, qi_[:], kfq[:])
    b5 = V(lambda e: e.tensor_tensor(out=Qa[:, :, :], in0=qv(q3), in1=qv(q1), op=ALU.mult), [b4, b2])
    b6 = _sincos(p, q3[:], q2[:], q4[:], [b5], False, qi_[:], kfq[:])
    b7 = V(lambda e: e.tensor_tensor(out=q3[:], in0=q3[:], in1=q1[:], op=ALU.mult), [b6])
    b8 = V(lambda e: e.tensor_scalar(out=Qb[:, :, :], in0=qv(q3), scalar1=cst[:, 2:3], scalar2=None, op0=ALU.mult), [b7, d_c])
    b9 = V(lambda e: e.tensor_copy(out=Qa16[:], in_=Qa[:]), [b5])
    b10 = V(lambda e: e.tensor_copy(out=Qb16[:], in_=Qb[:]), [b8])
    tabs = [a7, a11, b9, b10, t_ccp, t_ccc, d_tri, d_dv, d_id] + t_bd + u_toks

    ub = Ring([p.sb([128, 128], BF16) for _ in range(3)])
    banks = Ring([p.ps([128, 512], F32) for _ in range(6)])
    ps_y = p.ps([128, 512], F32)
    ps_t = p.ps([128, 512], F32)
    t1b = Ring([p.sb([128, 512], F32) for _ in range(2)]); t2b = Ring([p.sb([128, 512], F32) for _ in range(2)])
    Vb = Ring([p.sb([128, 512], BF16) for _ in range(2)])
    x1b = Ring([p.sb([128, 4, 128], F32) for _ in range(2)]); x2b = Ring([p.sb([128, 4, 128], F32) for _ in range(2)])
    Xb = Ring([p.sb([128, 8, 128], BF16) for _ in range(2)])
    PadA = Ring([p.sb([128, 8, 128], BF16) for _ in range(2)]); PadB = Ring([p.sb([128, 8, 128], BF16) for _ in range(2)])
    yo = Ring([p.sb([128, 128], F32) for _ in range(3)])
    yr = Ring([p.sb([128, 128], F32) for _ in range(3)])
    EA = p.sb([128, 16], F32); EB = p.sb([128, 16], F32); e1t = p.sb([128, 16], F32)
    HA = [p.sb([128, 16], F32) for _ in range(2)]; HB = [p.sb([128, 16], F32) for _ in range(2)]
    n1 = p.sb([128, 16], F32); n2 = p.sb([128, 16], F32)
    t_z = [V(lambda e, pad=pad: e.memset(pad[:], 0.0)) for pad in PadA.bufs + PadB.bufs]
    hcur = 0
    t_Hread = []; t_E_read = []; outs = []; y_free = []; tr_free = []
    for si_, (blocks, init) in enumerate(seqs):
        if init is not None:
            ta = p.dma("sp", "h0a", HA[hcur][:], I["H0A"][init], t_Hread)
            tb_ = p.dma("sp", "h0b", HB[hcur][:], I["H0B"][init], t_Hread)
        else:
            ta = V(lambda e, h=HA[hcur]: e.memset(h[:], 0.0), t_Hread)
            tb_ = V(lambda e, h=HB[hcur]: e.memset(h[:], 0.0), t_Hread)
        t_H = [ta, tb_]
        def do_block(rk, col0, L, yrow0):
            nonlocal t_H, hcur, t_E_read, t_Hread, y_free, tr_free
            e_toks = []; pad_readers = []
            for ck in range(2):
                uft = U[:, ck, rk, col0:col0 + L]
                kb_, ubt, dub = ub.next()
                t_ub = p.op("act", lambda e, ubt=ubt, uft=uft: e.copy(out=ubt[:, 0:L], in_=uft), tabs + dub)
                kx, Xt, dX = Xb.next()
                x_toks = []
                pend_s2 = []
                for hc in range(2):
                    gg0 = ck * 8 + hc * 4
                    rows = slice(hc * 64, (hc + 1) * 64)
                    cols = slice(hc * 512, (hc + 1) * 512)
                    kA, bA, dA = banks.next()
                    mA = p.op("pe", lambda e, bA=bA, ubt=ubt, rows=rows, cols=cols, ck=ck: e.matmul(
                        bA[0:L, :], lhsT=ubt[rows, 0:L], rhs=BDA[rows, ck, cols], start=True, stop=True), [t_ub] + dA)
                    kB, bB, dB = banks.next()
                    mB = p.op("pe", lambda e, bB=bB, ubt=ubt, rows=rows, cols=cols, ck=ck: e.matmul(
                        bB[0:L, :], lhsT=ubt[rows, 0:L], rhs=BDB[rows, ck, cols], start=True, stop=True), [t_ub] + dB)
                    k1, t1t, d1_ = t1b.next(); k2, t2t, d2_ = t2b.next(); kv, Vt, dV = Vb.next()
                    pv = lambda t, gg0=gg0: t[0:L, gg0:gg0 + 4, :]
                    v3 = lambda t: t[0:L, :].rearrange("p (g n) -> p g n", n=128)
                    o1 = V(lambda e, t1t=t1t, bA=bA, pv=pv, v3=v3: e.tensor_tensor(out=v3(t1t), in0=v3(bA), in1=pv(Pa), op=ALU.mult), [mA] + d1_)
                    o2 = V(lambda e, t2t=t2t, bB=bB, pv=pv, v3=v3: e.tensor_tensor(out=v3(t2t), in0=v3(bB), in1=pv(Pb), op=ALU.mult), [mB] + d2_)
                    banks.done(kA, o1); banks.done(kB, o2)
                    o3 = V(lambda e, Vt=Vt, t1t=t1t, t2t=t2t: e.tensor_tensor(out=Vt[0:L, :], in0=t1t[0:L, :], in1=t2t[0:L, :], op=ALU.add), [o1, o2] + dV)
                    t1b.done(k1, o3); t2b.done(k2, o3)
                    def s2(gg0=gg0, hc=hc, Vt=Vt, kv=kv, o3=o3):
                        kcA, cA, dcA = banks.next(); kcB, cB, dcB = banks.next()
                        mc = None
                        for g in range(4):
                            mc = p.op("pe", lambda e, cA=cA, Vt=Vt, g=g: e.matmul(
                                cA[:, g * 128:g * 128 + L], lhsT=Vt[0:L, g * 128:(g + 1) * 128], rhs=tri[0:L, 0:L],
                                start=True, stop=True), ([o3] + dcA + dcB) if g == 0 else [], inc=False)
                            mc = p.op("pe", lambda e, cB=cB, Vt=Vt, g=g: e.matmul(
                                cB[0:64, g * 128:g * 128 + L], lhsT=Vt[0:L, g * 128 + 64:(g + 1) * 128], rhs=tri[0:L, 0:L],
                                start=True, stop=True), [], inc=False)
                            mc = p.op("pe", lambda e, cB=cB, Vt=Vt, g=g: e.matmul(
                                cB[64:128, g * 128:g * 128 + L], lhsT=Vt[0:L, g * 128:g * 128 + 64], rhs=tri[0:L, 0:L],
                                start=True, stop=True), [], inc=(g == 3))
                        Vb.done(kv, mc)
                        c3 = lambda t: t[:, :].rearrange("p (g n) -> p g n", n=128)[:, :, 0:L]
                        qv_ = lambda t, gg0=gg0: t[:, gg0:gg0 + 4, 0:L]
                        kx1, x1t, dx1 = x1b.next(); kx2, x2t, dx2 = x2b.next()
                        r1 = V(lambda e, x1t=x1t, cA=cA, c3=c3, qv_=qv_: e.tensor_tensor(out=x1t[:, :, 0:L], in0=c3(cA), in1=qv_(Qa), op=ALU.mult), [mc] + dx1)
                        r2 = V(lambda e, x2t=x2t, cB=cB, c3=c3, qv_=qv_: e.tensor_tensor(out=x2t[:, :, 0:L], in0=c3(cB), in1=qv_(Qb), op=ALU.mult), [mc] + dx2)
                        r3 = V(lambda e, Xt=Xt, x1t=x1t, x2t=x2t, hc=hc: e.tensor_tensor(
                            out=Xt[:, hc * 4:(hc + 1) * 4, 0:L], in0=x1t[:, :, 0:L], in1=x2t[:, :, 0:L], op=ALU.add), [r1, r2] + (dX if hc == 0 else []))
                        r4 = V(lambda e, x1t=x1t, x2t=x2t, gg0=gg0: e.tensor_tensor(
                            out=EA[:, gg0:gg0 + 4], in0=x1t[:, :, L - 1], in1=x2t[:, :, L - 1], op=ALU.add), [r1, r2] + t_E_read)
                        r5 = V(lambda e, cB=cB, gg0=gg0: e.tensor_tensor(
                            out=e1t[:, gg0:gg0 + 4], in0=cB[:, :].rearrange("p (g n) -> p g n", n=128)[:, :, L - 1],
                            in1=Qa[:, gg0:gg0 + 4, L - 1], op=ALU.mult), [mc] + t_E_read)
                        r6 = V(lambda e, cA=cA, gg0=gg0: e.tensor_tensor(
                            out=EB[:, gg0:gg0 + 4], in0=cA[:, :].rearrange("p (g n) -> p g n", n=128)[:, :, L - 1],
                            in1=Qb[:, gg0:gg0 + 4, L - 1], op=ALU.mult), [mc] + t_E_read)
                        r7 = V(lambda e, gg0=gg0: e.tensor_tensor(
                            out=EB[:, gg0:gg0 + 4], in0=e1t[:, gg0:gg0 + 4], in1=EB[:, gg0:gg0 + 4], op=ALU.subtract), [r5, r6])
                        banks.done(kcA, r1, r6); banks.done(kcB, r2, r5)
                        x1b.done(kx1, r3, r4); x2b.done(kx2, r3, r4)
                        x_toks.append(r3)
                        e_toks.extend([r4, r7])
                    pend_s2.append(s2)
                for f_ in pend_s2:
                    f_()
                ub.done(kb_, mB)
                kpa, pa, dpa = PadA.next(); kpb, pbt, dpb = PadB.next()
                tp = None
                for g in range(8):
                    Gx = ck * 8 + g
                    tp = V(lambda e, pa=pa, g=g, Gx=Gx, h=HA[hcur]: e.tensor_scalar(
                        out=pa[:, g, g * 16:(g + 1) * 16], in0=Ccc[:, Gx * 16:(Gx + 1) * 16], scalar1=h[:, Gx:Gx + 1],
                        scalar2=None, op0=ALU.mult), (t_H + [t_ccc] + dpa + t_z) if g == 0 else [])
                    tp = V(lambda e, pbt=pbt, g=g, Gx=Gx, h=HB[hcur]: e.tensor_scalar(
                        out=pbt[:, g, g * 16:(g + 1) * 16], in0=Ccc[:, Gx * 16:(Gx + 1) * 16], scalar1=h[:, Gx:Gx + 1],
                        scalar2=None, op0=ALU.mult), dpb if g == 0 else [])
                my = None
                for g in range(8):
                    Gx = ck * 8 + g
                    my = p.op("pe", lambda e, Xt=Xt, g=g, ck=ck: e.matmul(
                        ps_y[:, 0:L], lhsT=CcP[:, ck, g * 128:(g + 1) * 128], rhs=Xt[:, g, 0:L], start=(g == 0), stop=False),
                        (x_toks + [tp] + y_free) if g == 0 else [], inc=False)
                    my = p.op("pe", lambda e, pa=pa, g=g, Gx=Gx: e.matmul(
                        ps_y[:, 0:L], lhsT=pa[:, g, :], rhs=Qa16[:, Gx, 1:L + 1], start=False, stop=False), [], inc=False)
                    my = p.op("pe", lambda e, pbt=pbt, g=g, Gx=Gx: e.matmul(
                        ps_y[:, 0:L], lhsT=pbt[:, g, :], rhs=Qb16[:, Gx, 1:L + 1], start=False, stop=(g == 7)), [], inc=(g == 7))
                Xb.done(kx, my); PadA.done(kpa, my); PadB.done(kpb, my)
                pad_readers.append(tp)
                ko, yot, dyo = yo.next()
                ty = V(lambda e, yot=yot, uft=uft, ck=ck: e.scalar_tensor_tensor(
                    out=yot[:, 0:L], in0=uft, scalar=dv[:, ck:ck + 1], in1=ps_y[:, 0:L], op0=ALU.mult, op1=ALU.add), [my] + dyo)
                y_free = [ty]
                ttr = p.op("pe", lambda e, yot=yot: e.transpose(out=ps_t[0:L, 0:128], in_=yot[:, 0:L], identity=identf[:]),
                           [ty] + tr_free)
                kyr, yrt, dyr = yr.next()
                tcp = p.op("act", lambda e, yrt=yrt: e.copy(out=yrt[0:L, :], in_=ps_t[0:L, 0:128]), [ttr] + dyr)
                tr_free = [tcp]
                yo.done(ko, ttr)
                tst = p.dma("act", f"y{kyr}", pubY(yrow0, ck, L), yrt[0:L, :], [tcp])
                yr.done(kyr, tst)
                outs.append(tst)
            hn = 1 - hcur
            QaL = Qa[:, :, L]; QbL = Qb[:, :, L]
            dep0 = t_H + e_toks + pad_readers + t_Hread
            u1 = V(lambda e, h=HA[hcur], QaL=QaL: e.tensor_tensor(out=n1[:], in0=h[:], in1=QaL, op=ALU.mult), dep0)
            u2 = V(lambda e, h=HB[hcur], QbL=QbL: e.tensor_tensor(out=n2[:], in0=h[:], in1=QbL, op=ALU.mult), [u1])
            u3 = V(lambda e: e.tensor_tensor(out=n1[:], in0=n1[:], in1=n2[:], op=ALU.add), [u2])
            u4 = V(lambda e, h=HA[hn]: e.tensor_tensor(out=h[:], in0=n1[:], in1=EA[:], op=ALU.add), [u3])
            u5 = V(lambda e, h=HB[hcur], QaL=QaL: e.tensor_tensor(out=n1[:], in0=h[:], in1=QaL, op=ALU.mult), [u4])
            u6 = V(lambda e, h=HA[hcur], QbL=QbL: e.tensor_tensor(out=n2[:], in0=h[:], in1=QbL, op=ALU.mult), [u5])
            u7 = V(lambda e: e.tensor_tensor(out=n1[:], in0=n1[:], in1=n2[:], op=ALU.subtract), [u6])
            u8 = V(lambda e, h=HB[hn]: e.tensor_tensor(out=h[:], in0=n1[:], in1=EB[:], op=ALU.add), [u7])
            t_E_read = [u8]; t_Hread = [u8]; t_H = [u4, u8]
            hcur = hn
        for blk_ in blocks:
            do_block(*blk_)
        tf = p.dma("sp", "hf", HF_d[si_], HA[hcur][:], t_H)
        outs.append(tf)
        t_Hread = t_Hread + [tf]
    return outs[-8:]
R_ = 1152
RT_ = 9


def _zz_block(k, j):
    m = j // 2
    return 8 * m + k if j % 2 == 0 else 8 * m + 7 - k


def _zz_owner(S):
    m, x = S // 8, S % 8
    return (x, 2 * m) if x <= 3 else (7 - x, 2 * m + 1)


P_SMAX = [8 * (j // 2) + 4 if j % 2 == 0 else 8 * (j // 2) + 8 for j in range(8)]


def emit_rmsout(p, x, nw, y, eps=1e-6):
    D = 2048
    nwb = p.sb([128, D], F32)
    t_nw = p.dma("sp", "nw", nwb[:], nw.partition_broadcast(128))
    xb = Ring([p.sb([128, D], F32) for _ in range(2)])
    tf = Ring([p.sb([128, D], F32) for _ in range(2)])
    st = Ring([p.sb([128, 2], F32) for _ in range(2)])
    fin = []
    for r in range(RT_):
        rows = slice(r * 128, (r + 1) * 128)
        kx, xt, dx = xb.next()
        t_x = p.dma("sp", f"x{kx}", xt[:], x[rows, :], dx)
        kt, tt, dt_ = tf.next()
        ks, s_, ds = st.next()
        t_sq = p.op("act", lambda e, tt=tt, xt=xt, s_=s_: e.activation(out=tt[:], in_=xt[:], func=AF.Square, accum_out=s_[:, 0:1]), [t_x] + dt_ + ds)
        t1 = p.op("dve", lambda e, s_=s_: e.tensor_scalar(out=s_[:, 1:2], in0=s_[:, 0:1], scalar1=1.0 / D, scalar2=eps, op0=ALU.mult, op1=ALU.add), [t_sq])
        t2 = p.op("act", lambda e, s_=s_: e.activation(out=s_[:, 1:2], in_=s_[:, 1:2], func=AF.Sqrt), [t1])
        t3 = p.op("dve", lambda e, s_=s_: e.reciprocal(out=s_[:, 1:2], in_=s_[:, 1:2]), [t2])
        t4 = p.op("dve", lambda e, tt=tt, xt=xt, s_=s_: e.scalar_tensor_tensor(out=tt[:], in0=xt[:], scalar=s_[:, 1:2], in1=nwb[:], op0=ALU.mult, op1=ALU.mult), [t3, t_nw])
        xb.done(kx, t4); st.done(ks, t4)
        t5 = p.dma("act", f"o{kt}", y[rows, :], tt[:], [t4])
        tf.done(kt, t5)
        fin.append(t5)
    return fin[-2:]


def emit_gather(p, ck_d, cv_d, ci_d, pt_d, Kp, Vp, ip):
    pt = p.sb([128, 1], I32)
    idx = p.sb([128, 8], I32)
    bufs = Ring([p.sb([128, 8192], F32) for _ in range(3)])
    t_pt = p.dma("sp", "pt", pt[:], pt_d)
    t_i = None
    for e_ in range(8):
        t_i = p.op("dve", lambda e, e_=e_: e.tensor_scalar(
            out=idx[:, e_:e_ + 1], in0=pt[:], scalar1=8, scalar2=e_, op0=ALU.mult, op1=ALU.add), [t_pt])
    outs = []
    jobs = [(ci_d, pt, 0, ip)] + [(ck_d, idx, e_, Kp[:, e_, :]) for e_ in range(8)] + \
           [(cv_d, idx, e_, Vp[:, e_, :]) for e_ in range(8)]
    for n, (src, it, col, dst) in enumerate(jobs):
        kb, bt, db = bufs.next()
        tok = p.lane_op("pool", f"g{kb}", lambda e, bt=bt, src=src, it=it, col=col: e.indirect_dma_start(
            out=bt[:], out_offset=None, in_=src, in_offset=bass.IndirectOffsetOnAxis(ap=it[:, col:col + 1], axis=0)),
            [t_i, t_pt] + db)
        t_o = p.dma("sp", f"go{kb}", dst, bt[:], [tok])
        bufs.done(kb, t_o)
        outs.append(t_o)
    return outs[-3:]


class PromptSrc:
    def __init__(self, p, qT_s, qiT_s, wT_s, gKT, gV, gKi, idxK_d, idxI_d):
        self.p = p
        self.qT_s, self.qiT_s, self.wT_s, self.gKT, self.gV, self.gKi = qT_s, qiT_s, wT_s, gKT, gV, gKi
        self.idxK = p.sb([128, 4 * 144], I32)
        self.idxI = p.sb([128, 144], I32)
        self.t_ik = p.dma("sp", "ik", self.idxK[:], idxK_d)
        self.t_ii = p.dma("sp", "ii", self.idxI[:], idxI_d)
        self.u0 = [sum(P_SMAX[:j]) for j in range(8)]
        self.n = 0

    def load_q(self, j, t, deps):
        return self.p.dma("pool", "lq", t[:, :].rearrange("p (h t) -> p h t", h=16),
                          self.qT_s[:, j * 128:(j + 1) * 128].rearrange("(h d) t -> d h t", d=128), deps)

    def load_qi(self, j, t, deps):
        return self.p.dma("pool", "lqi", t[:, :].rearrange("p (h t) -> p h t", h=16),
                          self.qiT_s[:, j * 128:(j + 1) * 128].rearrange("(h d) t -> d h t", d=64), deps)

    def load_w(self, j, t, deps):
        return self.p.dma("sp", "lw", t[:, :].rearrange("p (h t) -> p h t", h=16),
                          self.wT_s[:, j * 128:(j + 1) * 128].partition_broadcast(128), deps)

    def _ind(self, t, table, idx_ap, deps, dep2):
        self.n += 1
        return self.p.lane_op("pool", f"in{self.n % 6}", lambda e: e.indirect_dma_start(
            out=t[:, :], out_offset=None, in_=table, in_offset=bass.IndirectOffsetOnAxis(ap=idx_ap, axis=0)),
            list(deps) + [dep2])

    def load_ki(self, j, dl, t, deps):
        u = self.u0[j] + dl
        return self._ind(t, self.gKi, self.idxI[:, u:u + 1], deps, self.t_ii)

    def load_k(self, j, g, dl, t, deps):
        u = self.u0[j] + dl
        return self._ind(t, self.gKT, self.idxK[:, g * 144 + u:g * 144 + u + 1], deps, self.t_ik)

    def load_v(self, j, g, dl, t, deps):
        u = self.u0[j] + dl
        return self._ind(t, self.gV, self.idxK[:, g * 144 + u:g * 144 + u + 1], deps, self.t_ik)


class SampleSrc:
    def __init__(self, p, qT_s, qiT_s, wT_s, kT_s, kiT_s, z, KTs, kiTs, Vp):
        self.p = p
        self.qT_s, self.qiT_s, self.wT_s, self.kT_s, self.kiT_s, self.z = qT_s, qiT_s, wT_s, kT_s, kiT_s, z
        self.KTs, self.kiTs, self.Vp = KTs, kiTs, Vp
        self.c = slice(1024, 1028)

    def load_q(self, j, t, deps):
        return self.p.dma("pool", "lq", t[:, :].rearrange("p (h t) -> p h t", h=16),
                          self.qT_s[:, self.c].rearrange("(h d) t -> d h t", d=128), deps)

    def load_qi(self, j, t, deps):
        return self.p.dma("pool", "lqi", t[:, :].rearrange("p (h t) -> p h t", h=16),
                          self.qiT_s[:, self.c].rearrange("(h d) t -> d h t", d=64), deps)

    def load_w(self, j, t, deps):
        return self.p.dma("sp", "lw", t[:, :].rearrange("p (h t) -> p h t", h=16),
                          self.wT_s[:, self.c].partition_broadcast(128), deps)

    def _new(self, t, dst, src, deps):
        z_ = self.p.op("dve", lambda e: e.memset(t[:, :], 0.0), deps)
        return self.p.dma("pool", "ln", dst, src, [z_])

    def load_ki(self, j, dl, t, deps):
        if dl == 0:
            return self._new(t, t[0:64, 0:4], self.kiT_s[0:64, self.c], deps)
        pg = 128 - dl
        return self.p.dma("pool", "lki", t[0:64, :], self.kiTs[:, pg * 128:(pg + 1) * 128], deps)

    def load_k(self, j, g, dl, t, deps):
        if dl == 0:
            return self._new(t, t[:, 0:4], self.kT_s[g * 128:(g + 1) * 128, self.c], deps)
        pg = 128 - dl
        return self.p.dma("pool", "lk", t[:, :], self.KTs[g * 128:(g + 1) * 128, pg * 128:(pg + 1) * 128], deps)

    def load_v(self, j, g, dl, t, deps):
        if dl == 0:
            return self._new(t, t[0:4, :], self.z[1024:1028, 2560 + g * 128:2560 + (g + 1) * 128], deps)
        pg = 128 - dl
        return self.p.dma("pool", "lv", t[:, :], self.Vp[pg * 128:(pg + 1) * 128, g * 128:(g + 1) * 128], deps)


def build_fused():
    nc = bass.Bass("TRN2", target_bir_lowering=False)
    EI = lambda n, s, dt=F32: nc.dram_tensor(n, list(s), dt, kind="ExternalInput").ap()
    EO = lambda n, s, dt=F32: nc.dram_tensor(n, list(s), dt, kind="ExternalOutput").ap()
    IT = lambda n, s, dt=F32: nc.dram_tensor(n, list(s), dt)
    xrows = EI("xrows", [R_, 2048]); prows = EI("prows", [4, R_, 256])
    norm_w = EI("norm_w", [4, 1, 2048]); ple_nw = EI("ple_nw", [4, 1, 2048]); fnw = EI("fnw", [1, 2048])
    a_win = EI("a_win", [2, 2048, 6224]); a_wout = EI("a_wout", [2, 2048, 2048])
    s_win = EI("s_win", [2, 2048, 4096]); s_wglu = EI("s_wglu", [2, 2048, 2048]); s_bglu = EI("s_bglu", [2, 1, 2048])
    s_wout = EI("s_wout", [2, 2048, 2048]); p_wg = EI("p_wg", [4, 2048, 2048]); p_wp = EI("p_wp", [4, 256, 2048])
    ident = EI("ident", [128, 128])
    ck = [EI(f"ck{l}", [10240, 8192]) for l in range(2)]; cv = [EI(f"cv{l}", [10240, 8192]) for l in range(2)]
    ci = [EI(f"ci{l}", [1280, 8192]) for l in range(2)]
    pt = EI("pt", [128, 1], I32)
    vtp = EI("vtp", [1, 144]); Dp = EI("Dp", [2, 128, 2048]); C0p = EI("C0p", [128, 128])
    vts = EI("vts", [1, 129]); Ds = EI("Ds", [2, 128, 64]); C0s = EI("C0s", [128, 4]); cbv = EI("cbv", [1, 16])
    idxK = EI("idxK", [128, 576], I32); idxI = EI("idxI", [128, 144], I32)
    SP = {}
    for nm, shp in [("lre_row", [1, 1024]), ("lim_row", [1, 1024]), ("ldt_row", [1, 1024]),
                    ("lreT2", [128, 16]), ("limT2", [128, 16]), ("ldtT2", [128, 16]),
                    ("lre_gi", [2, 128, 64]), ("lim_gi", [2, 128, 64]), ("ldt_gi", [2, 128, 64]),
                    ("bre_gi", [2, 128, 64]), ("bim_gi", [2, 128, 64]),
                    ("CcP", [2, 128, 1024]), ("Ccc", [128, 256]), ("dvec", [128, 2]),
                    ("H0A", [4, 128, 16]), ("H0B", [4, 128, 16])]:
        SP[nm] = EI("sp_" + nm, [2, 2] + shp)
    tri = EI("tri", [128, 128]); cst = EI("cst", [128, 4]); tvec = EI("tvec", [1, 129]); mask = EI("mask", [128, 8, 64])
    uidx = EI("uidx", [2, 128, 8], I32); yidx = EI("yidx", [128, 36], I32)
    yout = EO("yout", [R_, 2048]); kvk = EO("kvk", [2, R_, 1088])
    HFp = EO("HFp", [2, 2, 1, 128, 16]); HFs = EO("HFs", [2, 2, 4, 128, 16])

    hA = IT("hA", [R_, 2048]).ap(); hB = IT("hB", [R_, 2048]).ap()
    z = IT("z", [R_, 6224]).ap()
    qT_s = IT("qT_s", [2048, R_]).ap(); kT_s = IT("kT_s", [512, R_]).ap(); qiT_s = IT("qiT_s", [1024, R_]).ap()
    kiT_s = IT("kiT_s", [128, R_]).ap(); wT_s = IT("wT_s", [16, R_]).ap()
    pubKT = IT("pubKT", [4 * 8 * 128, 128]); pubV = IT("pubV", [4 * 8 * 128, 128]); pubKi = IT("pubKi", [8 * 128, 128])
    gKT = IT("gKT", [4 * 4 * 8 * 128, 128]); gV = IT("gV", [4 * 4 * 8 * 128, 128]); gKi = IT("gKi", [4 * 8 * 128, 128])
    oT_s = IT("oT_s", [2048, R_]).ap(); o_s = IT("o_s", [R_, 2048]).ap(); g_s = IT("g_s", [R_, 2048]).ap()
    Kp = IT("Kp", [16384, 512]).ap(); Vp = IT("Vp", [16384, 512]).ap(); ip = IT("ip", [16384, 64]).ap()
    KTs = IT("KTs", [512, 16384]).ap(); kiTs = IT("kiTs", [64, 16384]).ap()
    uT_s = IT("uT_s", [2048, R_]); gU = IT("gU", [4 * 2048, R_])
    pubY = IT("pubY", [4608, 512]); gY = IT("gY", [4 * 4608, 512])
    y_s = IT("y_s", [R_, 2048]).ap(); y3 = IT("y3", [R_, 2048]).ap()
    RG = [[0, 1, 2, 3], [4, 5, 6, 7]]

    def allgather(p, src, dst, deps, CR=None):
        rows = src.ap().shape[0]
        CR = CR or rows
        prev = list(deps)
        for c_ in range(rows // CR):
            cc = p.lane_op("pool", "cc", lambda e, c_=c_: e.collective_compute(
                "AllGather", ALU.bypass, replica_groups=RG, ins=[src.ap()[c_ * CR:(c_ + 1) * CR, :].opt()],
                outs=[dst.ap()[c_ * 4 * CR:(c_ + 1) * 4 * CR, :].opt()]), prev, incv=1)
            prev = [cc]
        return prev[0]

    with ExitStack() as es:
        p = Prog(nc, es)
        h, h1 = None, None
        cur = xrows
        bufs = [hA, hB]
        for i in range(4):
            l = i // 2
            hn1 = bufs[0] if cur is not bufs[0] else bufs[1]
            if i % 2 == 0:
                with p.stage():
                    emit_linear(p, RT_, 2048, 6224, "rms", "none", cur, a_win[l], z, ident, nw=norm_w[i])
                for (src, dst, C) in [(z[:, 0:2048], qT_s, 2048), (z[:, 2048:2560], kT_s, 512),
                                      (z[:, 5120:6144], qiT_s, 1024), (z[:, 6144:6208], kiT_s, 64),
                                      (z[:, 6208:6224], wT_s, 16)]:
                    with p.stage():
                        emit_transpose(p, src, dst, R_, C, ident)
                with p.stage():
                    d0 = p.dma("sp", "k0", kvk[l][:, 0:1024], z[:, 2048:3072])
                    d1 = p.dma("sp", "k1", kvk[l][:, 1024:1088], z[:, 6144:6208])
                    pk = pubKT.ap().rearrange("(g j d) s -> g j d s", g=4, j=8)
                    d2 = [p.dma("act", f"k2{g}", pk[g], kT_s[g * 128:(g + 1) * 128, 0:1024].rearrange("d (j s) -> j d s", s=128))
                          for g in range(4)]
                    pv = pubV.ap().rearrange("(g j s) d -> g j s d", g=4, j=8)
                    d3 = [p.dma("act", f"k3{g}", pv[g], z[0:1024, 2560 + g * 128:2560 + (g + 1) * 128].rearrange("(j s) d -> j s d", s=128))
                          for g in range(4)]
                    d4 = p.dma("sp", "k4", pubKi.ap().rearrange("(j d) s -> j d s", j=8),
                               kiT_s[:, 0:1024].rearrange("d (j s) -> j d s", s=128))
                    c1 = allgather(p, pubKT, gKT, d2, 2048)
                    c2_ = allgather(p, pubV, gV, d3 + [c1], 2048)
                    c3 = allgather(p, pubKi, gKi, [d4, c2_])
                    p.op("sp", None, [d0, d1, c3], inc=False)
                with p.stage():
                    src = PromptSrc(p, qT_s, qiT_s, wT_s, gKT.ap(), gV.ap(), gKi.ap(), idxK, idxI)
                    emit_attn(p, 8, 128, P_SMAX, src, vtp, Dp, cbv, C0p,
                              lambda j, g: oT_s[g * 512:(g + 1) * 512, j * 128:(j + 1) * 128].rearrange("(r d) t -> d r t", d=128))
                with p.stage():
                    emit_gather(p, ck[l], cv[l], ci[l], pt, Kp.rearrange("(pg e s) c -> pg e (s c)", pg=128, e=8),
                                Vp.rearrange("(pg e s) c -> pg e (s c)", pg=128, e=8), ip.rearrange("(pg s) c -> pg (s c)", pg=128))
                with p.stage():
                    emit_transpose(p, Kp, KTs, 16384, 512, ident)
                with p.stage():
                    emit_transpose(p, ip, kiTs, 16384, 64, ident)
                with p.stage():
                    src = SampleSrc(p, qT_s, qiT_s, wT_s, kT_s, kiT_s, z, KTs, kiTs, Vp)
                    emit_attn(p, 1, 4, [129], src, vts, Ds, cbv, C0s,
                              lambda j, g: oT_s[g * 512:(g + 1) * 512, 1024:1028].rearrange("(r d) t -> d r t", d=128))
                with p.stage():
                    emit_transpose(p, oT_s, o_s, 2048, R_, ident)
                with p.stage():
                    emit_linear(p, RT_, 2048, 2048, "gate", "res", o_s, a_wout[l], hn1, ident, x2=z[:, 3072:5120], e1=cur)
            else:
                with p.stage():
                    emit_linear(p, RT_, 2048, 4096, "rms", "none", cur, s_win[l], z[:, 0:4096], ident, nw=norm_w[i])
                with p.stage():
                    emit_transpose(p, z[:, 0:2048], uT_s.ap(), R_, 2048, ident)
                for c2 in range(2):
                    I = {k_: v_[l, c2] for k_, v_ in SP.items()}
                    I.update({"tri": tri, "cst": cst, "tvec": tvec, "mask": mask})
                    pY = lambda yrow0, ck_, L, c2=c2: pubY.ap()[yrow0:yrow0 + L, c2 * 256 + ck_ * 128:c2 * 256 + (ck_ + 1) * 128]
                    pblocks = []
                    for S in range(32):
                        r, jj = _zz_owner(S)
                        pblocks.append((r, jj * 128, 128, S * 128))
                    with p.stage():
                        pre = [allgather(p, uT_s, gU, [], 128)] if c2 == 0 else []
                        seqs = [(pblocks, None)] + [([(i_, 1024, 4, 4096 + 128 * i_)], i_) for i_ in range(4)]
                        HFl = [HFp[l, c2][0]] + [HFs[l, c2][i_] for i_ in range(4)]
                        emit_scan(p, seqs, I, gU.ap(), uidx[c2], pY, HFl, ident, pre=pre)
                with p.stage():
                    c1 = allgather(p, pubY, gY, [], 512)
                    yix = p.sb([128, 36], I32)
                    t_yi = p.dma("sp", "yi", yix[:], yidx)
                    yb = Ring([p.sb([128, 512], F32) for _ in range(4)])
                    fin = []
                    for jj in range(9):
                        for r in range(4):
                            kb, bt, db = yb.next()
                            col = jj * 4 + r
                            tg = p.lane_op("pool", f"yg{kb}", lambda e, bt=bt, col=col: e.indirect_dma_start(
                                out=bt[:], out_offset=None, in_=gY.ap(),
                                in_offset=bass.IndirectOffsetOnAxis(ap=yix[:, col:col + 1], axis=0)), [c1, t_yi] + db)
                            to = p.dma("sp", f"yo{kb}", y_s[jj * 128:(jj + 1) * 128, r * 512:(r + 1) * 512], bt[:], [tg])
                            yb.done(kb, to)
                            fin.append(to)
                    p.op("sp", None, fin[-4:], inc=False)
                with p.stage():
                    emit_linear(p, RT_, 2048, 2048, "gelu", "glu", y_s, s_wglu[l], y3, ident, e1=y_s, bv=s_bglu[l])
                with p.stage():
                    emit_linear(p, RT_, 2048, 2048, "gate", "res", y3, s_wout[l], hn1, ident, x2=z[:, 2048:4096], e1=cur)
            with p.stage():
                emit_linear(p, RT_, 2048, 2048, "rms", "sigmoid", hn1, p_wg[i], g_s, ident, nw=ple_nw[i])
            hn2 = bufs[0] if hn1 is not bufs[0] else bufs[1]
            with p.stage():
                emit_linear(p, RT_, 256, 2048, "plain", "gres", prows[i], p_wp[i], hn2, ident, e1=hn1, e2=g_s)
            cur = hn2
        with p.stage():
            fin = emit_rmsout(p, cur, fnw, yout)
            p.op("sp", None, fin, inc=False)
        with p.stage():
            for e_ in ("pe", "act", "dve", "pool", "sp"):
                p.op(e_, None, [], inc=False)
    return nc


def _t5_bucket_np(n):
    n = np.asarray(n, dtype=np.int32)
    nf = np.maximum(n, 1).astype(np.float32)
    large = 16 + (np.log(nf / np.float32(16)) / np.float32(math.log(128 / 16)) * np.float32(16)).astype(np.int32)
    large = np.minimum(large, 31)
    return np.where(n < 16, n, large)


def _bias_tiles(rel_bias, NT):
    s_l = np.arange(128)[:, None]
    t_l = np.arange(NT)[None, :]
    d0 = t_l - s_l
    b0 = rel_bias[_t5_bucket_np(np.maximum(d0, 0))]
    b0 = np.where((d0 >= 0)[:, :, None], b0, np.float32(NEG))
    b1 = rel_bias[_t5_bucket_np(128 + d0)]
    D = np.stack([b0, b1]).transpose(0, 1, 3, 2).reshape(2, 128, 16 * NT)
    C0 = np.where(d0 >= 0, np.float32(0), np.float32(NEG)).astype(np.float32)
    return np.ascontiguousarray(D, dtype=np.float32), C0


def _scan_inputs(m, b, k, T, pi, ssm_lambda_re, ssm_lambda_im, ssm_log_dt, ssm_b_re, ssm_b_im, ssm_c_re, ssm_c_im, ssm_d, state_ssm_re, state_ssm_im):
    sp = {nm: [] for nm in ("lre_row", "lim_row", "ldt_row", "lreT2", "limT2", "ldtT2", "lre_gi", "lim_gi", "ldt_gi",
                            "bre_gi", "bim_gi", "CcP", "Ccc", "dvec", "H0A", "H0B")}
    for l in range(2):
        for c2 in range(2):
            g0 = 32 * k + 16 * c2
            gs = slice(g0, g0 + 16)
            lre, lim, ldt = ssm_lambda_re[l][gs], ssm_lambda_im[l][gs], ssm_log_dt[l][gs]
            sp["lre_row"].append(lre.reshape(1, 1024)); sp["lim_row"].append(lim.reshape(1, 1024))
            sp["ldt_row"].append(np.broadcast_to(ldt[:, None], (16, 64)).reshape(1, 1024))
            sp["lreT2"].append(np.concatenate([lre.T, lre.T], 0)); sp["limT2"].append(np.concatenate([lim.T, lim.T], 0))
            sp["ldtT2"].append(np.broadcast_to(ldt[None, :], (128, 16)))
            rep = lambda a: np.stack([np.repeat(a[ck * 8:(ck + 1) * 8], 16, axis=0) for ck in range(2)])
            sp["lre_gi"].append(rep(lre)); sp["lim_gi"].append(rep(lim))
            sp["ldt_gi"].append(rep(np.broadcast_to(ldt[:, None], (16, 64))))
            tb = lambda a: np.stack([a[ck * 8:(ck + 1) * 8].transpose(0, 2, 1).reshape(128, 64) for ck in range(2)])
            sp["bre_gi"].append(tb(ssm_b_re[l][gs])); sp["bim_gi"].append(tb(ssm_b_im[l][gs]))
            CcP = np.zeros((2, 128, 8, 128), np.float32)
            Ccc = np.zeros((128, 16, 16), np.float32)
            for G in range(16):
                ck_, g = G // 8, G % 8
                cc = np.concatenate([ssm_c_re[l][g0 + G].T, ssm_c_im[l][g0 + G].T], axis=0)
                CcP[ck_, :, g, g * 16:(g + 1) * 16] = cc
                Ccc[:, G, :] = cc
            sp["CcP"].append(CcP.reshape(2, 128, 1024)); sp["Ccc"].append(Ccc.reshape(128, 256))
            sp["dvec"].append(ssm_d[l][512 * k + 256 * c2:512 * k + 256 * c2 + 256].reshape(2, 128).T)
            hre = state_ssm_re[l][4 * b:4 * b + 4, gs].transpose(0, 2, 1)
            him = state_ssm_im[l][4 * b:4 * b + 4, gs].transpose(0, 2, 1)
            sp["H0A"].append(np.concatenate([hre, him], 1)); sp["H0B"].append(np.concatenate([him, hre], 1))
    for nm, lst in sp.items():
        a = np.stack([np.ascontiguousarray(x_, dtype=np.float32) for x_ in lst])
        m["sp_" + nm] = np.ascontiguousarray(a.reshape((2, 2) + a.shape[1:]))
    uidx = np.zeros((2, 128, 8), np.int32)
    for c2 in range(2):
        for ck_ in range(2):
            for r in range(4):
                uidx[c2, :, ck_ * 4 + r] = (4 * k + 2 * c2 + ck_) * 512 + r * 128 + pi
    m["uidx"] = uidx
    yidx = np.zeros((128, 36), np.int32)
    for jj in range(9):
        for r in range(4):
            if jj < 8:
                yidx[:, jj * 4 + r] = (T[jj] // 4) * 2048 + r * 512 + (T[jj] % 4) * 128 + pi
            else:
                yidx[:, jj * 4 + r] = 8 * 2048 + r * 512 + 128 * k + pi
    m["yidx"] = yidx


_IDENT = np.eye(128, dtype=np.float32)
_TRI = np.triu(np.ones((128, 128), np.float32))
_CST = np.stack([np.arange(128), -np.arange(128), np.where(np.arange(128) < 64, -1.0, 1.0),
                 np.where(np.arange(128) < 64, 1.0, -1.0)], axis=1).astype(np.float32)
_TVEC = np.arange(129, dtype=np.float32).reshape(1, 129)
_MASK = (np.arange(128)[:, None, None] // 16 == np.arange(8)[None, :, None]).astype(np.float32) * np.ones((1, 1, 64), np.float32)
_NC = {}


def kernel(x_prompt, x_sample, cache_k, cache_v, cache_kidx, state_ssm_re, state_ssm_im, page_table,
           p_prompt, p_sample, norm_w, final_norm_w, rel_bias, attn_w_in, attn_w_out, ssm_w_in,
           ssm_lambda_re, ssm_lambda_im, ssm_log_dt, ssm_b_re, ssm_b_im, ssm_c_re, ssm_c_im, ssm_d,
           ssm_w_glu, ssm_b_glu, ssm_w_out, ple_norm_w, ple_w_gate, ple_w_proj):
    f32 = lambda a: np.ascontiguousarray(np.asarray(a), dtype=np.float32)
    (x_prompt, x_sample, cache_k, cache_v, cache_kidx, state_ssm_re, state_ssm_im, p_prompt, p_sample, norm_w,
     final_norm_w, rel_bias, attn_w_in, attn_w_out, ssm_w_in, ssm_lambda_re, ssm_lambda_im, ssm_log_dt, ssm_b_re,
     ssm_b_im, ssm_c_re, ssm_c_im, ssm_d, ssm_w_glu, ssm_b_glu, ssm_w_out, ple_norm_w, ple_w_gate, ple_w_proj) = [
        f32(a) for a in (x_prompt, x_sample, cache_k, cache_v, cache_kidx, state_ssm_re, state_ssm_im, p_prompt,
                         p_sample, norm_w, final_norm_w, rel_bias, attn_w_in, attn_w_out, ssm_w_in, ssm_lambda_re,
                         ssm_lambda_im, ssm_log_dt, ssm_b_re, ssm_b_im, ssm_c_re, ssm_c_im, ssm_d, ssm_w_glu,
                         ssm_b_glu, ssm_w_out, ple_norm_w, ple_w_gate, ple_w_proj)]
    page_table = np.asarray(page_table).astype(np.int32)
    if "nc" not in _NC:
        _NC["nc"] = build_fused()
    nc = _NC["nc"]
    Dp, C0p = _bias_tiles(rel_bias, 128)
    Ds, C0s = _bias_tiles(rel_bias, 4)
    shared = {
        "norm_w": norm_w.reshape(4, 1, 2048), "ple_nw": ple_norm_w.reshape(4, 1, 2048), "fnw": final_norm_w.reshape(1, 2048),
        "a_win": attn_w_in, "a_wout": attn_w_out, "s_win": ssm_w_in, "s_wglu": ssm_w_glu,
        "s_bglu": ssm_b_glu.reshape(2, 1, 2048), "s_wout": ssm_w_out, "p_wg": ple_w_gate, "p_wp": ple_w_proj,
        "ident": _IDENT, "Dp": Dp, "C0p": C0p, "Ds": Ds, "C0s": C0s, "vts": np.zeros((1, 129), np.float32),
        "cbv": np.ascontiguousarray(rel_bias[31].reshape(1, 16)), "tri": _TRI, "cst": _CST, "tvec": _TVEC,
        "mask": np.ascontiguousarray(_MASK),
    }
    for l in range(2):
        shared[f"ck{l}"] = cache_k[l].reshape(10240, 8192)
        shared[f"cv{l}"] = cache_v[l].reshape(10240, 8192)
        shared[f"ci{l}"] = cache_kidx[l].reshape(1280, 8192)
    pi = np.arange(128, dtype=np.int32)
    in_maps = []
    for c in range(NCORES):
        b, k = c // 4, c % 4
        T = [_zz_block(k, j) for j in range(8)]
        rows = np.concatenate([np.arange(t * 128, (t + 1) * 128) for t in T])
        m = dict(shared)
        xr = np.zeros((R_, 2048), np.float32)
        xr[0:1024] = x_prompt[b][rows]
        xr[1024:1028] = x_sample[c]
        pr = np.zeros((4, R_, 256), np.float32)
        pr[:, 0:1024] = p_prompt[:, b][:, rows]
        pr[:, 1024:1028] = p_sample[:, c]
        m["xrows"] = xr
        m["prows"] = pr
        m["pt"] = np.ascontiguousarray(page_table[c].reshape(128, 1))
        m["vtp"] = np.concatenate([np.where(np.arange(P_SMAX[j]) <= T[j], 0.0, NEG) for j in range(8)]).reshape(1, 144).astype(np.float32)
        idxK = np.zeros((128, 4, 144), np.int32)
        idxI = np.zeros((128, 144), np.int32)
        u = 0
        for j in range(8):
            for dl in range(P_SMAX[j]):
                r, jj = _zz_owner(max(T[j] - dl, 0))
                for g in range(4):
                    idxK[:, g, u] = (g // 2) * 8192 + r * 2048 + (g % 2) * 1024 + jj * 128 + pi
                idxI[:, u] = r * 1024 + jj * 128 + pi
                u += 1
        m["idxK"] = idxK.reshape(128, 576)
        m["idxI"] = idxI
        _scan_inputs(m, b, k, T, pi, ssm_lambda_re, ssm_lambda_im, ssm_log_dt, ssm_b_re, ssm_b_im, ssm_c_re, ssm_c_im, ssm_d, state_ssm_re, state_ssm_im)
        in_maps.append(m)
    res = run_bass_kernel_spmd(nc, in_maps, core_ids=list(range(NCORES)))
    y_p = np.zeros((2, 4096, 2048), np.float32); y_s = np.zeros((8, 4, 2048), np.float32)
    k_p = np.zeros((2, 2, 4096, 4, 128), np.float32); v_p = np.zeros_like(k_p); ki_p = np.zeros((2, 2, 4096, 64), np.float32)
    k_s = np.zeros((2, 8, 4, 4, 128), np.float32); v_s = np.zeros_like(k_s); ki_s = np.zeros((2, 8, 4, 64), np.float32)
    hr_p = np.zeros((2, 2, 128, 64), np.float32); hi_p = np.zeros_like(hr_p)
    hr_s = np.zeros((2, 8, 128, 64), np.float32); hi_s = np.zeros_like(hr_s)
    for c in range(NCORES):
        b, k = c // 4, c % 4
        r_ = res.results[c]
        yo, kv = r_["yout"], r_["kvk"]
        for j in range(8):
            t = _zz_block(k, j)
            sl = slice(t * 128, (t + 1) * 128)
            y_p[b, sl] = yo[j * 128:(j + 1) * 128]
            for l in range(2):
                blk = kv[l][j * 128:(j + 1) * 128]
                k_p[l, b, sl] = blk[:, 0:512].reshape(128, 4, 128)
                v_p[l, b, sl] = blk[:, 512:1024].reshape(128, 4, 128)
                ki_p[l, b, sl] = blk[:, 1024:1088]
        y_s[c] = yo[1024:1028]
        for l in range(2):
            blk = kv[l][1024:1028]
            k_s[l, c] = blk[:, 0:512].reshape(4, 4, 128); v_s[l, c] = blk[:, 512:1024].reshape(4, 4, 128)
            ki_s[l, c] = blk[:, 1024:1088]
            for c2 in range(2):
                gs = slice(32 * k + 16 * c2, 32 * k + 16 * c2 + 16)
                H = r_["HFp"][l, c2, 0]
                hr_p[l, b, gs] = H[0:64].T; hi_p[l, b, gs] = H[64:128].T
                for i_ in range(4):
                    H = r_["HFs"][l, c2, i_]
                    hr_s[l, 4 * b + i_, gs] = H[0:64].T; hi_s[l, 4 * b + i_, gs] = H[64:128].T
    return (y_p, y_s, k_p, v_p, ki_p, hr_p, hi_p, k_s, v_s, ki_s, hr_s, hi_s)
```
